# Optimizing a Trainium2 kernel written in Bass

```python
import math
import jax, jax.numpy as jnp
from jax import lax
import numpy as np

D_MODEL = 1024
BATCH = 8
SEQ = 8192
DEPTH = 4

N_BRANCH = 4
MIX_WIDTH = D_MODEL // 4
S5_GROUP_CH = 16
S5_GROUPS = MIX_WIDTH // S5_GROUP_CH
S5_STATE = 64
SGU_CHUNK = 128
SGU_GROUPS = MIX_WIDTH // 64
DN_HEAD_DIM = 64
DN_HEADS = MIX_WIDTH // DN_HEAD_DIM
DN_CONV = 4
DN_CHUNK = 64
DIFF_V_DIM = 64
DIFF_HEADS = MIX_WIDTH // DIFF_V_DIM
DIFF_QK_DIM = DIFF_V_DIM // 2
ATTN_BLOCK = 128
D_FF = 2816
FFN_CONV = 3
EPS = 1e-6

IN_SIZES = (
    MIX_WIDTH,
    2 * MIX_WIDTH,
    3 * MIX_WIDTH,
    MIX_WIDTH,
    DN_HEADS,
    DN_HEADS,
    MIX_WIDTH,
    MIX_WIDTH,
    MIX_WIDTH,
    N_BRANCH * D_MODEL,
)
N_IN = sum(IN_SIZES)
IN_SPLITS = tuple(int(s) for s in np.cumsum(IN_SIZES)[:-1])

kernel_name = 'hybrid_parallel_gated_block'


def rms_norm(x, g):
    xf = x.astype(jnp.float32)
    y = xf * lax.rsqrt(jnp.mean(xf * xf, axis=-1, keepdims=True) + EPS)
    return (y * g.astype(jnp.float32)).astype(x.dtype)


def layer_norm(x, g):
    xf = x.astype(jnp.float32)
    mu = jnp.mean(xf, axis=-1, keepdims=True)
    xc = xf - mu
    y = xc * lax.rsqrt(jnp.mean(xc * xc, axis=-1, keepdims=True) + EPS)
    return (y * g.astype(jnp.float32)).astype(x.dtype)


def l2_normalize(t):
    return t * lax.rsqrt(jnp.sum(t * t, axis=-1, keepdims=True) + EPS)


def causal_dwconv(x, w):
    K = w.shape[0]
    L = x.shape[1]
    xp = jnp.pad(x, ((0, 0), (K - 1, 0), (0, 0)))
    return sum(xp[:, k:k + L] * w[k] for k in range(K))


def s5_mixer(u, lam_re, lam_im, log_step, b_re, b_im, c_re, c_im, d, w_glu, b_glu):
    Bsz, L, _ = u.shape
    f32 = jnp.float32
    uf = u.astype(f32).reshape(Bsz, L, S5_GROUPS, S5_GROUP_CH)
    step = jnp.exp(log_step.astype(f32))[:, None]
    lr, li = lam_re.astype(f32), lam_im.astype(f32)
    er = jnp.exp(lr * step)
    abar_re = er * jnp.cos(li * step)
    abar_im = er * jnp.sin(li * step)
    den = lr * lr + li * li
    nr, ni = abar_re - 1.0, abar_im
    coef_re = ((nr * lr + ni * li) / den)[..., None]
    coef_im = ((ni * lr - nr * li) / den)[..., None]
    br, bi = b_re.astype(f32), b_im.astype(f32)
    bb_re = coef_re * br - coef_im * bi
    bb_im = coef_re * bi + coef_im * br
    bu_re = jnp.einsum('blgh,gph->blgp', uf, bb_re)
    bu_im = jnp.einsum('blgh,gph->blgp', uf, bb_im)
    a_re = jnp.broadcast_to(abar_re, bu_re.shape)
    a_im = jnp.broadcast_to(abar_im, bu_im.shape)

    def combine(e1, e2):
        a1r, a1i, b1r, b1i = e1
        a2r, a2i, b2r, b2i = e2
        return (a2r * a1r - a2i * a1i,
                a2r * a1i + a2i * a1r,
                a2r * b1r - a2i * b1i + b2r,
                a2r * b1i + a2i * b1r + b2i)

    _, _, xr, xi = lax.associative_scan(combine, (a_re, a_im, bu_re, bu_im), axis=1)
    y = (jnp.einsum('blgp,ghp->blgh', xr, c_re.astype(f32))
         - jnp.einsum('blgp,ghp->blgh', xi, c_im.astype(f32)))
    y = y + d.astype(f32).reshape(S5_GROUPS, S5_GROUP_CH) * uf
    z = jax.nn.gelu(y.reshape(Bsz, L, MIX_WIDTH))
    out = z * jax.nn.sigmoid(z @ w_glu.astype(f32) + b_glu.astype(f32))
    return out.astype(u.dtype)


def sgu_mixer(p, norm_g, w_s, b_s):
    z = jax.nn.gelu(p)
    u, v = jnp.split(z, 2, axis=-1)
    v = layer_norm(v, norm_g)
    Bsz, L, _ = v.shape
    nc = L // SGU_CHUNK
    vc = v.reshape(Bsz, nc, SGU_CHUNK, SGU_GROUPS, MIX_WIDTH // SGU_GROUPS)
    ws = jnp.tril(w_s)
    mixed = jnp.einsum('gts,bnsgc->bntgc', ws, vc) + b_s.T[None, None, :, :, None]
    return u * mixed.reshape(Bsz, L, MIX_WIDTH)


def gated_deltanet(qkv, z, a, b, conv_w, a_log, dt_bias, norm_g):
    f32 = jnp.float32
    Bsz, L, _ = qkv.shape
    H, D, C = DN_HEADS, DN_HEAD_DIM, DN_CHUNK
    N = L // C
    qkv = jax.nn.silu(causal_dwconv(qkv, conv_w))
    q, k, v = jnp.split(qkv.astype(f32), 3, axis=-1)
    q = l2_normalize(q.reshape(Bsz, L, H, D)) * (D ** -0.5)
    k = l2_normalize(k.reshape(Bsz, L, H, D))
    v = v.reshape(Bsz, L, H, D)
    beta = jax.nn.sigmoid(b.astype(f32))
    g = -jnp.exp(a_log.astype(f32)) * jax.nn.softplus(a.astype(f32) + dt_bias.astype(f32))

    def chunk4(t):
        return t.reshape(Bsz, N, C, H, D).transpose(0, 3, 1, 2, 4)

    def chunk3(t):
        return t.reshape(Bsz, N, C, H).transpose(0, 3, 1, 2)

    q, k, v = chunk4(q), chunk4(k), chunk4(v)
    beta, G = chunk3(beta), jnp.cumsum(chunk3(g), axis=-1)
    tril = jnp.tril(jnp.ones((C, C), dtype=bool))
    strict = jnp.tril(jnp.ones((C, C), dtype=bool), -1)
    decay = jnp.exp(jnp.where(tril, G[..., :, None] - G[..., None, :], -jnp.inf))
    kb = k * beta[..., None]
    vb = v * beta[..., None]
    lmat = jnp.where(strict, jnp.einsum('bhnid,bhnjd->bhnij', kb, k) * decay, 0.0)
    rhs = jnp.concatenate([vb, kb * jnp.exp(G)[..., None]], axis=-1)
    sol = lax.linalg.triangular_solve(jnp.eye(C, dtype=f32) + lmat, rhs,
                                      left_side=True, lower=True, unit_diagonal=True)
    u_w, w_w = sol[..., :D], sol[..., D:]
    aqk = jnp.einsum('bhnid,bhnjd->bhnij', q, k) * decay
    q_dec = q * jnp.exp(G)[..., None]
    k_dec = k * jnp.exp(G[..., -1:] - G)[..., None]
    g_last = jnp.exp(G[..., -1])
    xs = tuple(jnp.moveaxis(t, 2, 0) for t in (u_w, w_w, aqk, q_dec, k_dec, g_last))

    def step(S, inp):
        u_c, w_c, a_c, qd, kd, gl = inp
        v_new = u_c - jnp.einsum('bhcd,bhde->bhce', w_c, S)
        o = jnp.einsum('bhcd,bhde->bhce', qd, S) + jnp.einsum('bhij,bhje->bhie', a_c, v_new)
        S = S * gl[..., None, None] + jnp.einsum('bhcd,bhce->bhde', kd, v_new)
        return S, o

    S0 = jnp.zeros((Bsz, H, D, D), dtype=f32)
    _, o = lax.scan(step, S0, xs)
    o = o.transpose(1, 0, 3, 2, 4).reshape(Bsz, L, H, D)
    o = rms_norm(o, norm_g) * jax.nn.silu(z.astype(f32).reshape(Bsz, L, H, D))
    return o.reshape(Bsz, L, MIX_WIDTH).astype(qkv.dtype)


def diff_attention(q, k, v, lq1, lk1, lq2, lk2, norm_g, lam_init):
    f32 = jnp.float32
    Bsz, L, _ = q.shape
    H, Dq, Dv, Q = DIFF_HEADS, DIFF_QK_DIM, DIFF_V_DIM, ATTN_BLOCK
    qf = q.astype(f32).reshape(Bsz, L, H, 2, Dq) * (Dq ** -0.5)
    kf = k.astype(f32).reshape(Bsz, L, H, 2, Dq)
    vf = v.astype(f32).reshape(Bsz, L, H, Dv)
    lam = (jnp.exp(jnp.sum(lq1.astype(f32) * lk1.astype(f32)))
           - jnp.exp(jnp.sum(lq2.astype(f32) * lk2.astype(f32))) + lam_init)
    slopes = jnp.exp2(-8.0 * jnp.arange(1, H + 1, dtype=f32) / H)
    nb = L // Q
    qb = qf.reshape(Bsz, nb, Q, H, 2, Dq).transpose(1, 0, 2, 3, 4, 5)
    kpos = jnp.arange(L)

    def block(args):
        qblk, i = args
        qpos = i * Q + jnp.arange(Q)
        s = jnp.einsum('bqhmd,bkhmd->bhmqk', qblk, kf)
        dist = (qpos[:, None] - kpos[None, :]).astype(f32)
        s = jnp.where(dist >= 0, s - slopes[:, None, None, None] * dist, -jnp.inf)
        p = jax.nn.softmax(s, axis=-1)
        pd = p[:, :, 0] - lam * p[:, :, 1]
        return jnp.einsum('bhqk,bkhd->bqhd', pd, vf)

    o = lax.map(block, (qb, jnp.arange(nb)))
    o = o.transpose(1, 0, 2, 3, 4).reshape(Bsz, L, H, Dv)
    o = rms_norm(o, norm_g) * (1.0 - lam_init)
    return o.reshape(Bsz, L, MIX_WIDTH).astype(q.dtype)


def conv_ffn(h, w_up, conv_w, conv_b, w_down):
    up = causal_dwconv(h @ w_up, conv_w) + conv_b
    val, gate = jnp.split(up, 2, axis=-1)
    return (jax.nn.silu(gate) * val) @ w_down


def setup_inputs(seed: int = 0) -> dict:
    key = jax.random.key(seed)
    ks = jax.random.split(key, 40)
    f32 = jnp.float32

    def nrm(k, shape, scale):
        return jax.random.normal(k, shape, dtype=f32) * scale

    def gain(k, shape):
        return 1.0 + 0.02 * jax.random.normal(k, shape, dtype=f32)

    G, P, Hc = S5_GROUPS, S5_STATE, S5_GROUP_CH
    inp = {}
    inp['x'] = jax.random.normal(ks[0], (BATCH, SEQ, D_MODEL), dtype=f32)
    inp['norm_mix_g'] = gain(ks[1], (DEPTH, D_MODEL))
    inp['w_in'] = nrm(ks[2], (DEPTH, D_MODEL, N_IN), D_MODEL ** -0.5)
    inp['s5_lam_re'] = -0.5 * jnp.exp(0.05 * jax.random.normal(ks[3], (DEPTH, G, P), dtype=f32))
    inp['s5_lam_im'] = math.pi * jnp.arange(P, dtype=f32) + 0.01 * jax.random.normal(ks[4], (DEPTH, G, P), dtype=f32)
    inp['s5_log_step'] = jax.random.uniform(ks[5], (DEPTH, G), dtype=f32,
                                            minval=math.log(1e-3), maxval=math.log(1e-1))
    inp['s5_b_re'] = nrm(ks[6], (DEPTH, G, P, Hc), (2.0 * Hc) ** -0.5)
    inp['s5_b_im'] = nrm(ks[7], (DEPTH, G, P, Hc), (2.0 * Hc) ** -0.5)
    inp['s5_c_re'] = nrm(ks[8], (DEPTH, G, Hc, P), (2.0 * P) ** -0.5)
    inp['s5_c_im'] = nrm(ks[9], (DEPTH, G, Hc, P), (2.0 * P) ** -0.5)
    inp['s5_d'] = nrm(ks[10], (DEPTH, MIX_WIDTH), 1.0)
    inp['s5_w_glu'] = nrm(ks[11], (DEPTH, MIX_WIDTH, MIX_WIDTH), MIX_WIDTH ** -0.5)
    inp['s5_b_glu'] = nrm(ks[12], (DEPTH, MIX_WIDTH), 0.01)
    inp['sgu_norm_g'] = gain(ks[13], (DEPTH, MIX_WIDTH))
    inp['sgu_w_s'] = nrm(ks[14], (DEPTH, SGU_GROUPS, SGU_CHUNK, SGU_CHUNK), SGU_CHUNK ** -0.5)
    inp['sgu_b_s'] = gain(ks[15], (DEPTH, SGU_GROUPS, SGU_CHUNK))
    inp['dn_conv_w'] = nrm(ks[16], (DEPTH, DN_CONV, 3 * MIX_WIDTH), DN_CONV ** -0.5)
    inp['dn_a_log'] = jnp.log(jax.random.uniform(ks[17], (DEPTH, DN_HEADS), dtype=f32, minval=1.0, maxval=16.0))
    dt0 = jnp.exp(jax.random.uniform(ks[18], (DEPTH, DN_HEADS), dtype=f32,
                                     minval=math.log(1e-3), maxval=math.log(1e-1)))
    inp['dn_dt_bias'] = dt0 + jnp.log(-jnp.expm1(-dt0))
    inp['dn_norm_g'] = gain(ks[19], (DEPTH, DN_HEAD_DIM))
    inp['diff_lq1'] = nrm(ks[20], (DEPTH, DIFF_QK_DIM), 0.1)
    inp['diff_lk1'] = nrm(ks[21], (DEPTH, DIFF_QK_DIM), 0.1)
    inp['diff_lq2'] = nrm(ks[22], (DEPTH, DIFF_QK_DIM), 0.1)
    inp['diff_lk2'] = nrm(ks[23], (DEPTH, DIFF_QK_DIM), 0.1)
    inp['diff_norm_g'] = gain(ks[24], (DEPTH, DIFF_V_DIM))
    inp['w_br_s5'] = nrm(ks[25], (DEPTH, MIX_WIDTH, D_MODEL), MIX_WIDTH ** -0.5)
    inp['w_br_sgu'] = nrm(ks[26], (DEPTH, MIX_WIDTH, D_MODEL), MIX_WIDTH ** -0.5)
    inp['w_br_dn'] = nrm(ks[27], (DEPTH, MIX_WIDTH, D_MODEL), MIX_WIDTH ** -0.5)
    inp['w_br_diff'] = nrm(ks[28], (DEPTH, MIX_WIDTH, D_MODEL), MIX_WIDTH ** -0.5)
    inp['w_out'] = nrm(ks[29], (DEPTH, D_MODEL, D_MODEL), D_MODEL ** -0.5)
    inp['norm_ffn_g'] = gain(ks[30], (DEPTH, D_MODEL))
    inp['ffn_w_up'] = nrm(ks[31], (DEPTH, D_MODEL, 2 * D_FF), D_MODEL ** -0.5)
    inp['ffn_conv_w'] = nrm(ks[32], (DEPTH, FFN_CONV, 2 * D_FF), FFN_CONV ** -0.5)
    inp['ffn_conv_b'] = nrm(ks[33], (DEPTH, 2 * D_FF), 0.01)
    inp['ffn_w_down'] = nrm(ks[34], (DEPTH, D_FF, D_MODEL), D_FF ** -0.5)
    inp['norm_final_g'] = gain(ks[35], (D_MODEL,))
    return inp


def reference(x, norm_mix_g, w_in, s5_lam_re, s5_lam_im, s5_log_step, s5_b_re, s5_b_im,
              s5_c_re, s5_c_im, s5_d, s5_w_glu, s5_b_glu, sgu_norm_g, sgu_w_s, sgu_b_s,
              dn_conv_w, dn_a_log, dn_dt_bias, dn_norm_g, diff_lq1, diff_lk1, diff_lq2,
              diff_lk2, diff_norm_g, w_br_s5, w_br_sgu, w_br_dn, w_br_diff, w_out,
              norm_ffn_g, ffn_w_up, ffn_conv_w, ffn_conv_b, ffn_w_down, norm_final_g):
    Bsz, L, _ = x.shape
    for l in range(DEPTH):
        lam_init = 0.8 - 0.6 * math.exp(-0.3 * l)
        h = rms_norm(x, norm_mix_g[l])
        proj = h @ w_in[l]
        (p_s5, p_sgu, p_dn_qkv, p_dn_z, p_dn_a, p_dn_b,
         p_dq, p_dk, p_dv, p_gate) = jnp.split(proj, IN_SPLITS, axis=-1)
        y_s5 = s5_mixer(p_s5, s5_lam_re[l], s5_lam_im[l], s5_log_step[l], s5_b_re[l], s5_b_im[l],
                        s5_c_re[l], s5_c_im[l], s5_d[l], s5_w_glu[l], s5_b_glu[l]) @ w_br_s5[l]
        y_sgu = sgu_mixer(p_sgu, sgu_norm_g[l], sgu_w_s[l], sgu_b_s[l]) @ w_br_sgu[l]
        y_dn = gated_deltanet(p_dn_qkv, p_dn_z, p_dn_a, p_dn_b, dn_conv_w[l], dn_a_log[l],
                              dn_dt_bias[l], dn_norm_g[l]) @ w_br_dn[l]
        y_diff = diff_attention(p_dq, p_dk, p_dv, diff_lq1[l], diff_lk1[l], diff_lq2[l],
                                diff_lk2[l], diff_norm_g[l], lam_init) @ w_br_diff[l]
        gates = jax.nn.sigmoid(p_gate.astype(jnp.float32)).astype(x.dtype)
        gates = gates.reshape(Bsz, L, N_BRANCH, D_MODEL)
        merged = (gates[:, :, 0] * y_s5 + gates[:, :, 1] * y_sgu
                  + gates[:, :, 2] * y_dn + gates[:, :, 3] * y_diff)
        x = x + merged @ w_out[l]
        h = rms_norm(x, norm_ffn_g[l])
        x = x + conv_ffn(h, ffn_w_up[l], ffn_conv_w[l], ffn_conv_b[l], ffn_w_down[l])
    return rms_norm(x, norm_final_g)
```

```python
import math
from contextlib import ExitStack

import numpy as np
import ml_dtypes
import concourse.bass as bass
import concourse.mybir as mybir
from concourse.bass_utils import run_bass_kernel_spmd

F32 = mybir.dt.float32
BF16 = mybir.dt.bfloat16
AF = mybir.ActivationFunctionType
ALU = mybir.AluOpType
AX = mybir.AxisListType
SAME_ENGINE_SYNC = True

D = 1024
DEPTH = 4
NMIX = 2568
DFF = 2816
EPS = 1e-6


class KB:
    ENG = ('pe', 'act', 'dve', 'pool', 'sp')

    def __init__(self):
        self.nc = bass.Bass("TRN2", target_bir_lowering=False)
        nc = self.nc
        self.e = {'pe': nc.tensor, 'act': nc.scalar, 'dve': nc.vector, 'pool': nc.gpsimd, 'sp': nc.sync}
        self.semh = {}
        self.semv = {}
        for k in self.ENG:
            self.semh['E:' + k] = nc.alloc_semaphore(name="s_" + k)
            self.semv['E:' + k] = 0
        self.waited = {k: {} for k in self.ENG}
        self.lastw = {}
        self.reads = {}
        self.nins = {k: 0 for k in self.ENG}
        self.nwait = 0
        self._uid = 0

    def uid(self):
        self._uid += 1
        return self._uid

    @staticmethod
    def R(x):
        if isinstance(x, (str, tuple)):
            return x
        return x.name

    def _wait(self, eng, key, val):
        w = self.waited[eng]
        if w.get(key, 0) >= val:
            return
        self.e[eng].wait_ge(self.semh[key], val)
        self.nwait += 1
        w[key] = val

    def _deps(self, eng, reads, writes):
        best = {}
        for r in reads:
            ev = self.lastw.get(r)
            if ev is not None:
                best[ev[0]] = max(best.get(ev[0], 0), ev[1])
        for w_ in writes:
            ev = self.lastw.get(w_)
            if ev is not None:
                best[ev[0]] = max(best.get(ev[0], 0), ev[1])
            for k, v in self.reads.get(w_, {}).items():
                best[k] = max(best.get(k, 0), v)
        for k, v in best.items():
            if k == 'E:' + eng and (eng == 'pe' or not SAME_ENGINE_SYNC):
                continue
            self._wait(eng, k, v)

    def _commit(self, ev, reads, writes):
        for w_ in writes:
            self.lastw[w_] = ev
            self.reads[w_] = {}
        for r in reads:
            d = self.reads.setdefault(r, {})
            d[ev[0]] = max(d.get(ev[0], 0), ev[1])

    def op(self, eng, fn, reads, writes):
        pw = [r for r in reads if r is not None and not isinstance(r, (str, tuple)) and self._is_ps(r)]
        reads = [self.R(r) for r in reads if r is not None]
        writes = [self.R(w) for w in writes if w is not None] + [self.R(r) for r in pw]
        self._deps(eng, reads, writes)
        ins = fn()
        key = 'E:' + eng
        self.semv[key] += 1
        ins.then_inc(self.semh[key], 1)
        self.nins[eng] += 1
        self._commit((key, self.semv[key]), reads, writes)
        return ins

    def dma(self, q, out, in_, sem, **kw):
        reads = [self.R(in_)] if self._is_sb(in_) else []
        writes = [self.R(out)] if self._is_sb(out) else []
        self._deps(q, reads, writes)
        key = 'D:' + sem
        if key not in self.semh:
            self.semh[key] = self.nc.alloc_semaphore(name="d%d" % len(self.semh))
            self.semv[key] = 0
        ins = self.e[q].dma_start(out=out, in_=in_, **kw)
        self.semv[key] += 16
        ins.then_inc(self.semh[key], 16)
        self.nins[q] += 1
        self._commit((key, self.semv[key]), reads, writes)
        return ins

    @staticmethod
    def _is_ps(ap):
        t = getattr(ap, 'tensor', ap)
        return type(t).__name__.startswith('PS')

    @staticmethod
    def _is_sb(ap):
        return type(ap.tensor).__name__.startswith('SB')

    def barrier(self, engs=None):
        for eng in (engs or self.ENG):
            for key, val in self.semv.items():
                if val > 0:
                    self._wait(eng, key, val)

    def mm(self, out, lhsT, rhs, start=True, stop=True, **kw):
        nc = self.nc
        return self.op('pe', lambda: nc.tensor.matmul(out, lhsT, rhs, start=start, stop=stop,
                                                      skip_group_check=True, **kw), [lhsT, rhs], [out])

    def tr(self, out, in_, ident):
        nc = self.nc
        return self.op('pe', lambda: nc.tensor.transpose(out, in_, ident), [in_, ident], [out])

    def act(self, out, in_, func, bias=None, scale=None):
        nc = self.nc
        kw = {}
        rr = [in_]
        if bias is not None:
            kw['bias'] = bias
            if not isinstance(bias, (int, float)):
                rr.append(bias)
        if scale is not None:
            kw['scale'] = scale
            if not isinstance(scale, (int, float)):
                rr.append(scale)
        return self.op('act', lambda: nc.scalar.activation(out, in_, func, **kw), rr, [out])

    def tt(self, out, in0, in1, op, eng='dve'):
        e = self.e[eng]
        return self.op(eng, lambda: e.tensor_tensor(out, in0, in1, op), [in0, in1], [out])

    def ts(self, out, in0, s1, op0, s2=None, op1=None, eng='dve'):
        e = self.e[eng]
        rr = [in0] + [s for s in (s1, s2) if s is not None and not isinstance(s, (int, float))]
        if op1 is None:
            return self.op(eng, lambda: e.tensor_scalar(out, in0, s1, None, op0), rr, [out])
        return self.op(eng, lambda: e.tensor_scalar(out, in0, s1, s2, op0, op1), rr, [out])

    def stt(self, out, in0, scalar, in1, op0, op1):
        nc = self.nc
        rr = [in0, in1] + ([scalar] if not isinstance(scalar, (int, float)) else [])
        return self.op('dve', lambda: nc.vector.scalar_tensor_tensor(out, in0, scalar, in1, op0, op1), rr, [out])

    def sumsq(self, junk, x, accum):
        nc = self.nc
        self.act(junk, x, AF.Square)
        return self.op('dve', lambda: nc.vector.tensor_reduce(accum, junk, AX.X, ALU.add), [junk], [accum])

    def cp(self, out, in_, eng='dve'):
        if eng == 'act':
            nc = self.nc
            return self.op('act', lambda: nc.scalar.copy(out, in_), [in_], [out])
        e = self.e[eng]
        return self.op(eng, lambda: e.tensor_copy(out, in_), [in_], [out])

    def memset(self, ap, val, eng='dve'):
        e = self.e[eng]
        return self.op(eng, lambda: e.memset(ap, val), [], [ap])

    def recip(self, out, in_):
        nc = self.nc
        return self.op('dve', lambda: nc.vector.reciprocal(out, in_), [in_], [out])


class Scope:
    def __init__(self, k):
        self.k = k
        self.es = ExitStack()

    def sb(self, name, shape, dt):
        return self.es.enter_context(self.k.nc.sbuf_tensor("%s_%d" % (name, self.k.uid()), list(shape), dt))

    def close(self):
        self.k.barrier()
        self.es.close()


class Rot:
    def __init__(self, S, name, n, shape, dt):
        self.bufs = [(S.sb("%s%d" % (name, i), shape, dt), "%s%d" % (name, i)) for i in range(n)]
        self.i = 0

    def next(self):
        b = self.bufs[self.i % len(self.bufs)]
        self.i += 1
        return b


WEIGHT_SHAPES = {
    'norm_mix_g': (4, 1024), 'w_in': (4, 1024, 6664), 's5_lam_re': (4, 16, 64), 's5_lam_im': (4, 16, 64),
    's5_log_step': (4, 16), 's5_b_re': (4, 16, 64, 16), 's5_b_im': (4, 16, 64, 16), 's5_c_re': (4, 16, 16, 64),
    's5_c_im': (4, 16, 16, 64), 's5_d': (4, 256), 's5_w_glu': (4, 256, 256), 's5_b_glu': (4, 256),
    'sgu_norm_g': (4, 256), 'sgu_w_s': (4, 4, 128, 128), 'sgu_b_s': (4, 4, 128), 'dn_conv_w': (4, 4, 768),
    'dn_a_log': (4, 4), 'dn_dt_bias': (4, 4), 'dn_norm_g': (4, 64), 'diff_lq1': (4, 32), 'diff_lk1': (4, 32),
    'diff_lq2': (4, 32), 'diff_lk2': (4, 32), 'diff_norm_g': (4, 64), 'w_br_s5': (4, 256, 1024),
    'w_br_sgu': (4, 256, 1024), 'w_br_dn': (4, 256, 1024), 'w_br_diff': (4, 256, 1024), 'w_out': (4, 1024, 1024),
    'norm_ffn_g': (4, 1024), 'ffn_w_up': (4, 1024, 5632), 'ffn_conv_w': (4, 3, 5632), 'ffn_conv_b': (4, 5632),
    'ffn_w_down': (4, 2816, 1024), 'norm_final_g': (1024,),
}


def host_consts():
    c = {}
    c['c_identb'] = np.eye(128).astype(ml_dtypes.bfloat16)
    c['c_identf'] = np.eye(128, dtype=np.float32)
    i = np.arange(128)
    c['c_mui'] = (i[:, None] <= i[None, :]).astype(np.float32)
    c['c_msu'] = (i[:, None] < i[None, :]).astype(np.float32)
    c['c_msl'] = (i[:, None] > i[None, :]).astype(np.float32)
    slopes = [2.0 ** (-8.0 * (h + 1) / 4) for h in range(4)]
    c['c_ekey'] = np.stack([np.exp(sl * i) for sl in slopes], axis=1).astype(np.float32)
    d = np.arange(67) - 3
    c['c_abias'] = np.tile(np.concatenate([-sl * 128.0 * d for sl in slopes])[None, :], (128, 1)).astype(np.float32)
    q = np.arange(512)
    c['c_cmask'] = np.stack([(q[None, :] >= 128 * j + i[:, None]) for j in range(4)], axis=1).astype(ml_dtypes.bfloat16)
    c['c_m32'] = np.stack([((i // 32) == j) for j in range(4)], axis=1).astype(np.float32)
    c['c_m64'] = np.stack([((i // 64) == j) for j in range(2)], axis=1).astype(np.float32)
    c['c_sel8'] = (np.arange(8)[:, None] == (i[None, :] // 16)).astype(np.float32)
    c['c_bm16'] = ((i[:, None] // 16) == (i[None, :] // 16)).astype(np.float32)
    c['c_w1mask'] = np.stack([((i // 16) == mh) for mh in range(8)], axis=1).astype(np.float32)
    half = i // 64
    c['c_w2mask'] = np.stack([((i[None, :] // 16) == (2 * mp + half[:, None])) for mp in range(4)], axis=1).astype(np.float32)
    return c


class Prog:
    def __init__(self, L, depth, flags):
        self.L = L
        self.depth = depth
        self.flags = flags
        self.lvl = 99
        for f in flags:
            if f.startswith('lvl'):
                self.lvl = int(f[3:])
        self.k = KB()
        k = self.k
        nc = k.nc
        self.x_in = nc.dram_tensor("x", [L, D], F32, kind="ExternalInput")
        self.y = nc.dram_tensor("y", [L, D], F32, kind="ExternalOutput")
        self.w = {}
        for name, shp in WEIGHT_SHAPES.items():
            self.w[name] = nc.dram_tensor(name, list(shp), F32, kind="ExternalInput")
        self.c = {}
        for name, arr in host_consts().items():
            dt = BF16 if arr.dtype == ml_dtypes.bfloat16 else F32
            self.c[name] = nc.dram_tensor(name, list(arr.shape), dt, kind="ExternalInput")
        self.hT_scr = nc.dram_tensor("hT_scr", [D, L], BF16, kind=("ExternalOutput" if "dbg" in flags else "Internal"))
        self.uT_scr = nc.dram_tensor("uT_scr", [256, L], BF16, kind="Internal")
        self.qkvc_scr = nc.dram_tensor("qkvc_scr", [768, L], BF16, kind="Internal")
        self.ab_scr = nc.dram_tensor("ab_scr", [128, (L // 128) * 8], F32, kind="Internal")
        self.z_scr = nc.dram_tensor("z_scr", [L, 256], BF16, kind="Internal")
        self.aq_scr = nc.dram_tensor("aq_scr", [256, L], BF16, kind="Internal")
        self.ak_scr = nc.dram_tensor("ak_scr", [256, L], BF16, kind="Internal")
        self.av_scr = nc.dram_tensor("av_scr", [L, 256], BF16, kind="Internal")
        self.mix_scr = nc.dram_tensor("mix_scr", [4, 256, L], BF16, kind=("ExternalOutput" if "dbg" in flags else "Internal"))
        self.banks = [nc.alloc_psum_tensor("bank%d" % i, [128, 512], F32) for i in range(8)]
        self.bi = 0
        self.bank_pool = list(range(8))
        self.G = Scope(k)
        self.identb = self.G.sb('identb', [128, 128], BF16)
        self.identf = self.G.sb('identf', [128, 128], F32)
        k.dma('sp', self.identb[:], self.c['c_identb'].ap(), 'identb')
        k.dma('sp', self.identf[:], self.c['c_identf'].ap(), 'identf')
        self.mui = self.G.sb('mui', [128, 128], F32)
        k.dma('sp', self.mui[:], self.c['c_mui'].ap(), 'mui')
        self.ones1 = self.G.sb('ones1', [1, 128], F32)
        k.memset(self.ones1[:], 1.0)

    def bank(self):
        pool = self.bank_pool
        b = self.banks[pool[self.bi % len(pool)]]
        self.bi += 1
        return b

    def load_cols(self, S, src2d, n, name):
        k = self.k
        dst = S.sb(name, [128, n], F32)
        done = 0
        while done < n:
            m = min(128, n - done)
            tmp = S.sb(name + 'r', [m, 128], F32)
            k.dma('sp', tmp[:], src2d[done:done + m, :], name + 'r%d' % done)
            pb = self.bank()
            k.tr(pb[:, 0:m], tmp[:], self.identf[0:m, 0:m])
            k.cp(dst[:, done:done + m], pb[:, 0:m])
            done += m
        return dst

    def load_row_bcast(self, S, src1d, n, name):
        k = self.k
        row = S.sb(name + 'r', [1, n], F32)
        k.dma('sp', row[:], src1d.unsqueeze(0), name + 'r')
        dst = S.sb(name, [128, n], F32)
        for c0 in range(0, n, 512):
            m = min(512, n - c0)
            pb = self.bank()
            k.mm(pb[:, 0:m], self.ones1[:, :], row[:, c0:c0 + m])
            k.cp(dst[:, c0:c0 + m], pb[:, 0:m])
        return dst

    def rms_block(self, xt, gcol, hT, junk, ss, hs):
        k = self.k
        for t in range(4):
            k.sumsq(junk[:], xt[:, t, :], ss[:, t:t + 1])
        k.ts(ss[:, 4:8], ss[:, 0:4], 1.0 / D, ALU.mult, EPS, ALU.add)
        k.act(ss[:, 8:12], ss[:, 4:8], AF.Sqrt)
        k.recip(ss[:, 12:16], ss[:, 8:12])
        for t in range(4):
            k.act(hs[:, t, :], xt[:, t, :], AF.Identity, scale=ss[:, 12 + t:13 + t])
        for kk in range(8):
            pb = self.bank()
            pv = pb[:].bitcast(BF16)
            for t in range(4):
                k.tr(pv[:, t * 128:(t + 1) * 128], hs[:, t, kk * 128:(kk + 1) * 128], self.identb[:])
            if kk % 2 == 0:
                k.ts(hT[:, kk, :], pv[:, 0:512], gcol[:, kk:kk + 1], ALU.mult)
            else:
                k.act(hT[:, kk, :], pv[:, 0:512], AF.Identity, scale=gcol[:, kk:kk + 1])

    def phase_A(self, l):
        k = self.k
        L = self.L
        NB = L // 512
        S = Scope(k)
        w = self.w
        xsrc = self.x_in if l == 0 else self.y
        WinA = S.sb('WinA', [128, 8, NMIX], BF16)
        for kk in range(8):
            k.dma('pool', WinA[:, kk, :], w['w_in'].ap()[l, kk * 128:(kk + 1) * 128, 0:NMIX], 'WinA')
        g1col = self.load_cols(S, w['norm_mix_g'].ap()[l].rearrange("(k p) -> k p", p=128), 8, 'g1col')
        wsT = S.sb('wsT', [128, 4, 128], BF16)
        wtmp = S.sb('wtmp', [128, 4, 128], F32)
        k.dma('sp', wtmp[:], w['sgu_w_s'].ap()[l].rearrange("g t s -> t g s"), 'wtmp')
        for g in range(4):
            pb = self.bank()
            k.tr(pb[:, 0:128], wtmp[:, g, :], self.identf[:])
            k.tt(wsT[:, g, :], pb[:, 0:128], self.mui[:], ALU.mult)
        sgng = self.load_row_bcast(S, w['sgu_norm_g'].ap()[l], 256, 'sgng')
        bsT = self.load_cols(S, w['sgu_b_s'].ap()[l], 4, 'bsT')
        cwc = self.load_cols(S, w['dn_conv_w'].ap()[l].rearrange("k (j p) -> (k j) p", p=128), 24, 'cwc')
        diagW = S.sb('diagW', [128, 6, 4, 128], BF16)
        for j in range(6):
            for kk in range(4):
                k.ts(diagW[:, j, kk, :], self.identf[:], cwc[:, kk * 6 + j:kk * 6 + j + 1], ALU.mult)
        xts = Rot(S, 'xt', 2, [128, 4, D], F32)
        hTs = Rot(S, 'hT', 2, [128, 8, 512], BF16)
        hs = S.sb('hs', [128, 4, D], BF16)
        junk = S.sb('junk', [128, D], F32)
        sss = Rot(S, 'ss', 2, [128, 16], F32)
        stg = Rot(S, 'stg', 4, [128, 512], BF16)
        abt = Rot(S, 'abt', 2, [128, 8], F32)
        qraw = S.sb('qraw', [128, 6, 515], BF16)
        k.memset(qraw[:, :, 0:3], 0.0)
        uv = S.sb('uv', [128, 512], F32)
        vn = S.sb('vn', [128, 256], F32)
        vnb = S.sb('vnb', [128, 256], BF16)
        st6 = S.sb('st6', [128, 8], F32)
        mxs = S.sb('mxs', [128, 256], F32)
        smix = S.sb('smix', [128, 256], BF16)
        smT = Rot(S, 'smT', 2, [128, 2, 512], BF16)
        ztm = Rot(S, 'ztm', 2, [128, 256], BF16)
        vtm = Rot(S, 'vtm', 2, [128, 256], BF16)

        for b in range(NB):
            t0 = b * 512
            xt, xsem = xts.next()
            k.dma('sp', xt[:], xsrc.ap()[t0:t0 + 512, :].rearrange("(t p) d -> p t d", p=128), xsem)
            hT, hsem = hTs.next()
            ss, _ = sss.next()
            self.rms_block(xt, g1col, hT, junk, ss, hs)
            k.dma('sp', self.hT_scr.ap()[:, t0:t0 + 512].rearrange("(k p) t -> p k t", p=128), hT[:], hsem)

            def fm(c0, M):
                pb = self.bank()
                for kk in range(8):
                    k.mm(pb[0:M, :], WinA[:, kk, c0:c0 + M], hT[:, kk, :], start=(kk == 0), stop=(kk == 7))
                return pb

            if self.lvl < 2:
                continue
            for j in range(2):
                pb = fm(j * 128, 128)
                sg, sgs = stg.next()
                k.cp(sg[:], pb[:], eng='act')
                k.dma('sp', self.uT_scr.ap()[j * 128:(j + 1) * 128, t0:t0 + 512], sg[:], sgs)
            if self.lvl < 3:
                continue
            for j in range(6):
                pb = fm(768 + j * 128, 128)
                k.cp(qraw[:, j, 3:515], pb[:], eng=('act' if j % 2 else 'dve'))
            for j in range(6):
                pb = self.bank()
                for kk in range(4):
                    k.mm(pb[:], diagW[:, j, kk, :], qraw[:, j, kk:kk + 512], start=(kk == 0), stop=(kk == 3))
                sg, sgs = stg.next()
                k.act(sg[:], pb[:], AF.Silu)
                k.dma('sp', self.qkvc_scr.ap()[j * 128:(j + 1) * 128, t0:t0 + 512], sg[:], sgs)
            k.cp(qraw[:, :, 0:3], qraw[:, :, 512:515], eng='pool')
            if self.lvl < 5:
                continue
            for (c0, scr) in ((1800, self.aq_scr), (2056, self.ak_scr)):
                for j in range(2):
                    pb = fm(c0 + j * 128, 128)
                    sg, sgs = stg.next()
                    k.cp(sg[:], pb[:], eng=('act' if j else 'dve'))
                    k.dma('sp', scr.ap()[j * 128:(j + 1) * 128, t0:t0 + 512], sg[:], sgs)
            if self.lvl < 6:
                continue
            sm, sms = smT.next()
            for t in range(4):
                tok = slice(t * 128, (t + 1) * 128)
                pb = self.bank()
                for kk in range(8):
                    k.mm(pb[:], hT[:, kk, tok], WinA[:, kk, 256:768], start=(kk == 0), stop=(kk == 7))
                k.act(uv[:], pb[:], AF.Gelu_apprx_tanh)
                k.op('dve', lambda: k.nc.vector.bn_stats(st6[:, 0:6], uv[:, 256:512]), [uv], [st6])
                k.op('dve', lambda: k.nc.vector.bn_aggr(st6[:, 6:8], st6[:, 0:6]), [st6], [st6])
                k.ts(st6[:, 7:8], st6[:, 7:8], EPS, ALU.add)
                k.act(st6[:, 7:8], st6[:, 7:8], AF.Sqrt)
                k.recip(st6[:, 7:8], st6[:, 7:8])
                k.ts(vn[:], uv[:, 256:512], st6[:, 6:7], ALU.subtract, st6[:, 7:8], ALU.mult)
                k.tt(vnb[:], vn[:], sgng[:], ALU.mult)
                pm = self.bank()
                for g in range(4):
                    k.mm(pm[:, g * 64:(g + 1) * 64], wsT[:, g, :], vnb[:, g * 64:(g + 1) * 64])
                k.tt(mxs[:].rearrange("p (g c) -> p g c", g=4), pm[:, 0:256].rearrange("p (g c) -> p g c", g=4),
                     bsT[:].unsqueeze(2).broadcast_to([128, 4, 64]), ALU.add)
                k.tt(smix[:], mxs[:], uv[:, 0:256], ALU.mult)
                pt = self.bank()
                ptv = pt[:].bitcast(BF16)
                for j in range(2):
                    k.tr(ptv[:, j * 128:(j + 1) * 128], smix[:, j * 128:(j + 1) * 128], self.identb[:])
                k.cp(sm[:, :, tok], ptv[:, 0:256].rearrange("p (j t) -> p j t", j=2), eng='act')
                if self.lvl < 7:
                    continue
                pz = self.bank()
                for kk in range(8):
                    k.mm(pz[:, 0:256], hT[:, kk, tok], WinA[:, kk, 1536:1792], start=(kk == 0), stop=(kk == 7))
                for kk in range(8):
                    k.mm(pz[:, 256:512], hT[:, kk, tok], WinA[:, kk, 2312:2568], start=False, stop=(kk == 7))
                zt, zts = ztm.next()
                k.cp(zt[:], pz[:, 0:256], eng='act')
                k.dma('sp', self.z_scr.ap()[t0 + t * 128:t0 + (t + 1) * 128, :], zt[:], zts)
                vt, vts = vtm.next()
                k.cp(vt[:], pz[:, 256:512])
                k.dma('sp', self.av_scr.ap()[t0 + t * 128:t0 + (t + 1) * 128, :], vt[:], vts)
                pab = self.bank()
                for kk in range(8):
                    k.mm(pab[:, 0:8], hT[:, kk, tok], WinA[:, kk, 1792:1800], start=(kk == 0), stop=(kk == 7))
                at_, ats = abt.next()
                k.cp(at_[:], pab[:, 0:8])
                nn = b * 4 + t
                k.dma('sp', self.ab_scr.ap()[:, nn * 8:(nn + 1) * 8], at_[:], ats)
            k.dma('sp', self.mix_scr.ap()[1, :, t0:t0 + 512].rearrange("(j p) t -> p j t", p=128), sm[:], sms)
        S.close()

    ATT_WIN = (6, 17, 1 << 20, 1 << 20)

    def phase_att(self, l):
        k = self.k
        nc = k.nc
        L = self.L
        NT = L // 128
        NQ = L // 512
        S = Scope(k)
        w = self.w
        lam_init = 0.8 - 0.6 * math.exp(-0.3 * l)
        KT = S.sb('KT', [128, 2, L], BF16)
        QT = S.sb('QT', [128, 2, L], BF16)
        for j in range(2):
            k.dma('sp', KT[:, j, :], self.ak_scr.ap()[j * 128:(j + 1) * 128, :], 'KT')
            k.dma('sp', QT[:, j, :], self.aq_scr.ap()[j * 128:(j + 1) * 128, :], 'QT')
        ekey = S.sb('ekey', [128, 4], F32)
        k.dma('sp', ekey[:], self.c['c_ekey'].ap(), 'ekey')
        abias = S.sb('abias', [128, 4 * 67], F32)
        k.dma('sp', abias[:], self.c['c_abias'].ap(), 'abias')
        cmask = S.sb('cmask', [128, 4, 512], BF16)
        k.dma('sp', cmask[:], self.c['c_cmask'].ap(), 'cmask')
        m32 = S.sb('m32', [128, 4], F32)
        k.dma('sp', m32[:], self.c['c_m32'].ap(), 'm32')
        Vaug = S.sb('Vaug', [128, NT, 4, 65], BF16)
        onesn = S.sb('onesn', [128, NT], F32)
        k.memset(onesn[:], 1.0)
        CH = min(16, NT)
        vtmp = Rot(S, 'vtmp', 2, [128, CH, 256], BF16)
        for n0 in range(0, NT, CH):
            vt, vts = vtmp.next()
            k.dma('sp', vt[:], self.av_scr.ap()[n0 * 128:(n0 + CH) * 128, :].rearrange("(n p) c -> p n c", p=128), vts)
            for h in range(4):
                k.ts(Vaug[:, n0:n0 + CH, h, 0:64], vt[:, :, h * 64:(h + 1) * 64], ekey[:, h:h + 1], ALU.mult)
        for h in range(4):
            k.act(Vaug[:, :, h, 64:65], onesn[:].unsqueeze(2), AF.Identity, scale=ekey[:, h:h + 1])
        lq = [self.load_row_bcast(S, w[n].ap()[l], 32, n) for n in ('diff_lq1', 'diff_lk1', 'diff_lq2', 'diff_lk2')]
        lt = S.sb('lt', [128, 32], F32)
        lv = S.sb('lv', [128, 8], F32)
        for i in range(2):
            k.tt(lt[:], lq[2 * i][:], lq[2 * i + 1][:], ALU.mult)
            k.op('dve', lambda: nc.vector.tensor_reduce(lv[:, i:i + 1], lt[:], AX.X, ALU.add), [lt], [lv])
        k.act(lv[:, 2:4], lv[:, 0:2], AF.Exp)
        k.tt(lv[:, 4:5], lv[:, 2:3], lv[:, 3:4], ALU.subtract)
        k.ts(lv[:, 5:6], lv[:, 4:5], lam_init, ALU.add)
        gd = self.load_row_bcast(S, w['diff_norm_g'].ap()[l], 64, 'gd')
        k.ts(gd[:], gd[:], 1.0 - lam_init, ALU.mult)
        QTm = Rot(S, 'QTm', 4, [128, 512], BF16)
        Pb = Rot(S, 'Pb', 4, [128, 512], BF16)
        rr = S.sb('rr', [128, 16], F32)
        t1 = S.sb('t1', [128, 4, 64], F32)
        t2 = S.sb('t2', [128, 4, 64], F32)
        sq = S.sb('sq', [128, 4, 64], F32)
        omix = Rot(S, 'omix', 2, [128, 4, 256], BF16)
        ostg = Rot(S, 'ostg', 2, [128, 2, 512], BF16)
        O = [self.banks[6], self.banks[7]]
        self.bank_pool = list(range(6))
        scale = 32.0 ** -0.5
        for qb in range(NQ):
            om, _ = omix.next()
            for h in range(4):
                t2i = h // 2
                qm = []
                for m in range(2):
                    qt_, _ = QTm.next()
                    k.ts(qt_[:], QT[:, t2i, qb * 512:(qb + 1) * 512], m32[:, (h % 2) * 2 + m:(h % 2) * 2 + m + 1], ALU.mult,
                         eng=('pool' if m else 'dve'))
                    qm.append(qt_)
                for m in range(2):
                    k.memset(O[m][:, 0:260], 0.0)
                blocks = [(s_, 128) for s_ in range(4)] if h == 0 else [(None, 512)]
                for (s_, QB) in blocks:
                    qt_first = 4 * qb + (s_ if s_ is not None else 0)
                    qt_last = 4 * qb + (s_ if s_ is not None else 3)
                    kt_lo = max(0, qt_first - self.ATT_WIN[h])
                    for kt in range(kt_lo, qt_last + 1):
                        jd = kt - qt_first
                        for m in range(2):
                            ps = self.bank()
                            qcols = slice(s_ * 128, (s_ + 1) * 128) if s_ is not None else slice(0, 512)
                            k.mm(ps[:, 0:QB], KT[:, t2i, kt * 128:(kt + 1) * 128], qm[m][:, qcols])
                            pT, _ = Pb.next()
                            dd = (qt_first - kt) + 3
                            k.act(pT[:, 0:QB], ps[:, 0:QB], AF.Exp, scale=scale, bias=abias[:, h * 67 + dd:h * 67 + dd + 1])
                            if jd >= 0:
                                k.tt(pT[:, 0:QB], pT[:, 0:QB], cmask[:, jd, 0:QB], ALU.mult, eng='pool')
                            for ss_ in ([s_] if s_ is not None else range(4)):
                                if s_ is None and ss_ < jd:
                                    continue
                                col = 0 if s_ is not None else ss_ * 128
                                k.mm(O[m][:, ss_ * 65:(ss_ + 1) * 65], pT[:, col:col + 128], Vaug[:, kt, h, :],
                                     start=False, stop=True)
                for m in range(2):
                    k.recip(rr[:, 4 * m:4 * m + 4], O[m][:, 0:260].rearrange("p (s e) -> p s e", e=65)[:, :, 64])
                k.ts(rr[:, 4:8], rr[:, 4:8], lv[:, 5:6], ALU.mult)
                k.tt(t1[:], O[0][:, 0:260].rearrange("p (s e) -> p s e", e=65)[:, :, 0:64],
                     rr[:, 0:4].unsqueeze(2).broadcast_to([128, 4, 64]), ALU.mult)
                k.tt(t2[:], O[1][:, 0:260].rearrange("p (s e) -> p s e", e=65)[:, :, 0:64],
                     rr[:, 4:8].unsqueeze(2).broadcast_to([128, 4, 64]), ALU.mult)
                k.tt(t1[:], t1[:], t2[:], ALU.subtract)
                k.tt(sq[:], t1[:], t1[:], ALU.mult)
                k.op('dve', lambda: nc.vector.tensor_reduce(rr[:, 8:12], sq[:], AX.X, ALU.add), [sq], [rr])
                k.ts(rr[:, 8:12], rr[:, 8:12], 1.0 / 64, ALU.mult, EPS, ALU.add)
                k.act(rr[:, 8:12], rr[:, 8:12], AF.Sqrt)
                k.recip(rr[:, 12:16], rr[:, 8:12])
                k.tt(t1[:], t1[:], rr[:, 12:16].unsqueeze(2).broadcast_to([128, 4, 64]), ALU.mult)
                k.tt(om[:, :, h * 64:(h + 1) * 64], t1[:], gd[:].unsqueeze(1).broadcast_to([128, 4, 64]), ALU.mult)
            og, ogs = ostg.next()
            for s_ in range(4):
                pt = self.bank()
                ptv = pt[:].bitcast(BF16)
                for j in range(2):
                    k.tr(ptv[:, j * 128:(j + 1) * 128], om[:, s_, j * 128:(j + 1) * 128], self.identb[:])
                k.cp(og[:, :, s_ * 128:(s_ + 1) * 128], ptv[:, 0:256].rearrange("p (j t) -> p j t", j=2), eng='act')
            k.dma('sp', self.mix_scr.ap()[3, :, qb * 512:(qb + 1) * 512].rearrange("(j p) t -> p j t", p=128), og[:], ogs)
        self.bank_pool = list(range(8))
        S.close()

    def phase_dn(self, l):
        k = self.k
        nc = k.nc
        L = self.L
        NT = L // 128
        NB = L // 512
        S = Scope(k)
        w = self.w
        onesf = S.sb('onesf', [128, 128], F32)
        k.memset(onesf[:], 1.0)
        nmsl = S.sb('nmsl', [128, 128], F32)
        nmsu = S.sb('nmsu', [128, 128], F32)
        k.dma('sp', nmsl[:], self.c['c_msl'].ap(), 'nmsl')
        k.dma('sp', nmsu[:], self.c['c_msu'].ap(), 'nmsu')
        k.ts(nmsl[:], nmsl[:], -1.0, ALU.mult)
        k.ts(nmsu[:], nmsu[:], -1.0, ALU.mult)
        alog = self.load_row_bcast(S, w['dn_a_log'].ap()[l], 4, 'alog')
        dtb = self.load_row_bcast(S, w['dn_dt_bias'].ap()[l], 4, 'dtb')
        k.act(alog[:], alog[:], AF.Exp)
        k.ts(alog[:], alog[:], -1.0, ALU.mult)
        gn = self.load_row_bcast(S, w['dn_norm_g'].ap()[l], 64, 'gn')
        ab = S.sb('ab', [128, NT, 8], F32)
        k.dma('sp', ab[:].rearrange("p n c -> p (n c)"), self.ab_scr.ap(), 'ab')
        gt = S.sb('gt', [128, NT, 4], F32)
        bt = S.sb('bt', [128, NT, 4], F32)
        Gt = S.sb('Gt', [128, NT, 4], F32)
        GL = S.sb('GL', [128, NT, 4], F32)
        eG = S.sb('eG', [128, NT, 4], F32)
        kds = S.sb('kds', [128, NT, 4], F32)
        gla = S.sb('gla', [128, NT, 4], F32)
        k.tt(gt[:], ab[:, :, 0:4], dtb[:].unsqueeze(1).broadcast_to([128, NT, 4]), ALU.add)
        k.act(gt[:], gt[:], AF.Exp)
        k.ts(gt[:], gt[:], 1.0, ALU.add)
        k.act(gt[:], gt[:], AF.Ln)
        k.tt(gt[:], gt[:], alog[:].unsqueeze(1).broadcast_to([128, NT, 4]), ALU.mult)
        k.act(bt[:], ab[:, :, 4:8], AF.Sigmoid)
        gflat = gt[:].rearrange("p n c -> p (n c)")
        for c0 in range(0, NT * 4, 512):
            m = min(512, NT * 4 - c0)
            pc = self.bank()
            k.mm(pc[:, 0:m], self.mui[:], gflat[:, c0:c0 + m])
            k.cp(Gt[:].rearrange("p n c -> p (n c)")[:, c0:c0 + m], pc[:, 0:m])
            pc2 = self.bank()
            k.mm(pc2[:, 0:m], onesf[:], gflat[:, c0:c0 + m])
            k.cp(GL[:].rearrange("p n c -> p (n c)")[:, c0:c0 + m], pc2[:, 0:m])
        k.act(eG[:], Gt[:], AF.Exp)
        k.tt(kds[:], GL[:], Gt[:], ALU.subtract)
        k.act(kds[:], kds[:], AF.Exp)
        k.act(gla[:], GL[:], AF.Exp)
        Sf = S.sb('Sf', [64, 4, 64], F32)
        Sb = S.sb('Sb', [64, 4, 64], BF16)
        k.memset(Sf[:], 0.0)
        k.memset(Sb[:], 0.0)
        qfs = Rot(S, 'qf', 2, [128, 6, 512], BF16)
        zts = Rot(S, 'zt', 2, [128, 4, 256], BF16)
        qkv = S.sb('qkv', [128, 768], F32)
        sq = S.sb('sq', [128, 512], F32)
        r8 = S.sb('r8', [128, 16], F32)
        qn_f = S.sb('qn_f', [128, 4, 64], F32)
        kn_f = S.sb('kn_f', [128, 4, 64], F32)
        kb_f = S.sb('kb_f', [128, 4, 64], F32)
        tmb = {n: S.sb(n, [128, 4, 64], BF16) for n in ('qn_b', 'kn_b', 'kb_b', 'qd_b', 'kbg_b', 'kdec_b', 'vb_b')}
        fmT = {n: S.sb(n + 'T', [64, 4, 128], BF16) for n in ('kn_b', 'kb_b', 'qn_b', 'qd_b')}
        dG = S.sb('dG', [128, 4, 128], F32)
        Xn = S.sb('Xn', [128, 128], F32)
        Xm = S.sb('Xm', [128, 128], F32)
        Dt = S.sb('Dt', [128, 128], F32)
        Dl = S.sb('Dl', [128, 128], F32)
        Dlm = S.sb('Dlm', [128, 128], F32)
        Dtm1 = S.sb('Dtm1', [128, 128], F32)
        Dtm2 = S.sb('Dtm2', [128, 128], F32)
        Pm = [S.sb('Pm%d' % i, [128, 4, 128], F32) for i in range(2)]
        PTm = [S.sb('PTm%d' % i, [128, 4, 128], F32) for i in range(2)]
        TTm = [S.sb('TTm%d' % i, [128, 4, 128], F32) for i in range(2)]
        TTb = S.sb('TTb', [128, 4, 128], BF16)
        Aq = S.sb('Aq', [128, 4, 128], BF16)
        u_sb = S.sb('u_sb', [128, 4, 64], F32)
        wwT = S.sb('wwT', [64, 4, 128], BF16)
        vnew = S.sb('vnew', [128, 4, 64], BF16)
        o_sb = S.sb('o_sb', [128, 4, 64], F32)
        osq = S.sb('osq', [128, 4, 64], F32)
        zs = S.sb('zs', [128, 256], F32)
        dmix = S.sb('dmix', [128, 256], BF16)
        dstg = Rot(S, 'dstg', 2, [128, 2, 512], BF16)
        for b in range(NB):
            t0 = b * 512
            qf, qfsem = qfs.next()
            k.dma('sp', qf[:], self.qkvc_scr.ap()[:, t0:t0 + 512].rearrange("(j p) t -> p j t", p=128), qfsem)
            zt, ztsem = zts.next()
            k.dma('sp', zt[:], self.z_scr.ap()[t0:t0 + 512, :].rearrange("(t p) c -> p t c", p=128), ztsem)
            dg, dgs = dstg.next()
            for t in range(4):
                n = b * 4 + t
                tok = slice(t * 128, (t + 1) * 128)
                pt = self.bank()
                ptv = pt[:].bitcast(BF16)
                for j in range(6):
                    k.tr(ptv[:, j * 128:(j + 1) * 128], qf[:, j, tok], self.identb[:])
                k.cp(qkv[:], ptv[:, 0:768], eng='act')
                k.tt(sq[:], qkv[:, 0:512], qkv[:, 0:512], ALU.mult)
                k.op('dve', lambda: nc.vector.tensor_reduce(r8[:, 0:8], sq[:].rearrange("p (a d) -> p a d", d=64), AX.X, ALU.add),
                     [sq], [r8])
                k.ts(r8[:, 0:8], r8[:, 0:8], EPS, ALU.add)
                k.act(r8[:, 0:8], r8[:, 0:8], AF.Sqrt)
                k.recip(r8[:, 8:16], r8[:, 0:8])
                k.ts(r8[:, 8:12], r8[:, 8:12], 0.125, ALU.mult)
                q3 = qkv[:, 0:256].rearrange("p (h d) -> p h d", h=4)
                k3 = qkv[:, 256:512].rearrange("p (h d) -> p h d", h=4)
                v3 = qkv[:, 512:768].rearrange("p (h d) -> p h d", h=4)

                def bc(ap2):
                    return ap2.unsqueeze(2).broadcast_to([128, 4, 64])
                k.tt(qn_f[:], q3, bc(r8[:, 8:12]), ALU.mult)
                k.tt(kn_f[:], k3, bc(r8[:, 12:16]), ALU.mult)
                k.tt(kb_f[:], kn_f[:], bc(bt[:, n, :]), ALU.mult)
                k.cp(tmb['qn_b'][:], qn_f[:], eng='act')
                k.cp(tmb['kn_b'][:], kn_f[:], eng='act')
                k.cp(tmb['kb_b'][:], kb_f[:], eng='act')
                k.tt(tmb['qd_b'][:], qn_f[:], bc(eG[:, n, :]), ALU.mult)
                k.tt(tmb['kbg_b'][:], kb_f[:], bc(eG[:, n, :]), ALU.mult)
                k.tt(tmb['kdec_b'][:], kn_f[:], bc(kds[:, n, :]), ALU.mult)
                k.tt(tmb['vb_b'][:], v3, bc(bt[:, n, :]), ALU.mult)
                for gi, (n1, n2) in enumerate((('kn_b', 'kb_b'), ('qn_b', 'qd_b'))):
                    pf = self.bank()
                    pfv = pf[:].bitcast(BF16)
                    for ii, nm in enumerate((n1, n2)):
                        for h in range(4):
                            k.tr(pfv[0:64, ii * 512 + h * 128:ii * 512 + (h + 1) * 128], tmb[nm][:, h, :], self.identb[:])
                    k.cp(fmT[n1][:].rearrange("p h t -> p (h t)"), pfv[0:64, 0:512], eng='act')
                    k.cp(fmT[n2][:].rearrange("p h t -> p (h t)"), pfv[0:64, 512:1024])
                knT, kbT, qnT, qdT = fmT['kn_b'], fmT['kb_b'], fmT['qn_b'], fmT['qd_b']
                for h in range(4):
                    k.ts(dG[:, h, :], self.identf[:], Gt[:, n, h:h + 1], ALU.mult)
                P, PT, TT = Pm[0], PTm[0], TTm[0]
                for h in range(4):
                    ps = self.bank()
                    k.mm(ps[:, 0:128], kbT[:, h, :], knT[:, h, :])
                    k.mm(ps[:, 128:256], knT[:, h, :], kbT[:, h, :])
                    k.mm(ps[:, 256:384], knT[:, h, :], qnT[:, h, :])
                    k.mm(ps[:, 384:512], onesf[:], dG[:, h, :])
                    k.ts(Xn[:], ps[:, 384:512], Gt[:, n, h:h + 1], ALU.subtract, 0.0, ALU.min)
                    k.ts(Xm[:], ps[:, 384:512], Gt[:, n, h:h + 1], ALU.subtract, 0.0, ALU.max)
                    k.act(Dt[:], Xn[:], AF.Exp)
                    k.act(Dl[:], Xm[:], AF.Exp, scale=-1.0)
                    k.tt(Dlm[:], Dl[:], nmsl[:], ALU.mult, eng='pool')
                    k.tt(Dtm1[:], Dt[:], nmsu[:], ALU.mult, eng='pool')
                    k.tt(Dtm2[:], Dt[:], self.mui[:], ALU.mult, eng='pool')
                    k.tt(P[:, h, :], ps[:, 0:128], Dlm[:], ALU.mult)
                    k.tt(PT[:, h, :], ps[:, 128:256], Dtm1[:], ALU.mult)
                    k.tt(Aq[:, h, :], ps[:, 256:384], Dtm2[:], ALU.mult)
                    k.tt(TT[:, h, :], PT[:, h, :], self.identf[:], ALU.add, eng='pool')
                cur = 0
                for lev in range(6):
                    nxt = 1 - cur
                    pP = self.bank()
                    for h in range(4):
                        k.mm(pP[:, h * 128:(h + 1) * 128], PTm[cur][:, h, :], Pm[cur][:, h, :])
                    k.cp(Pm[nxt][:].rearrange("p h t -> p (h t)"), pP[:], eng='act')
                    if lev < 5:
                        pQ = self.bank()
                        for h in range(4):
                            k.mm(pQ[:, h * 128:(h + 1) * 128], Pm[cur][:, h, :], PTm[cur][:, h, :])
                        k.cp(PTm[nxt][:].rearrange("p h t -> p (h t)"), pQ[:])
                    pT_ = self.bank()
                    for h in range(4):
                        k.mm(pT_[:, h * 128:(h + 1) * 128], self.identf[:], TTm[cur][:, h, :], start=True, stop=False)
                        k.mm(pT_[:, h * 128:(h + 1) * 128], Pm[nxt][:, h, :], TTm[cur][:, h, :], start=False, stop=True)
                    k.cp(TTm[nxt][:].rearrange("p h t -> p (h t)"), pT_[:], eng=('act' if lev % 2 else 'dve'))
                    cur = nxt
                k.cp(TTb[:], TTm[cur][:], eng='pool')
                TT = TTb
                pu = self.bank()
                for h in range(4):
                    k.mm(pu[:, h * 64:(h + 1) * 64], TT[:, h, :], tmb['vb_b'][:, h, :])
                k.cp(u_sb[:].rearrange("p h e -> p (h e)"), pu[:, 0:256], eng='act')
                pw = self.bank()
                for h in range(4):
                    k.mm(pw[0:64, h * 128:(h + 1) * 128], tmb['kbg_b'][:, h, :], TT[:, h, :])
                k.cp(wwT[:].rearrange("p h t -> p (h t)"), pw[0:64, :])
                pS = self.bank()
                for h in range(4):
                    k.mm(pS[:, h * 64:(h + 1) * 64], wwT[:, h, :], Sb[:, h, :])
                k.tt(vnew[:].rearrange("p h e -> p (h e)"), u_sb[:].rearrange("p h e -> p (h e)"), pS[:, 0:256], ALU.subtract)
                po = self.bank()
                for h in range(4):
                    k.mm(po[:, h * 64:(h + 1) * 64], qdT[:, h, :], Sb[:, h, :], start=True, stop=False)
                    k.mm(po[:, h * 64:(h + 1) * 64], Aq[:, h, :], vnew[:, h, :], start=False, stop=True)
                pK = self.bank()
                for h in range(4):
                    k.mm(pK[0:64, h * 64:(h + 1) * 64], tmb['kdec_b'][:, h, :], vnew[:, h, :])
                k.tt(Sf[:], Sf[:], gla[0:64, n, :].unsqueeze(2).broadcast_to([64, 4, 64]), ALU.mult)
                k.tt(Sf[:].rearrange("p h e -> p (h e)"), Sf[:].rearrange("p h e -> p (h e)"), pK[0:64, 0:256], ALU.add)
                k.cp(Sb[:], Sf[:], eng='act')
                k.cp(o_sb[:].rearrange("p h e -> p (h e)"), po[:, 0:256], eng='act')
                k.tt(osq[:], o_sb[:], o_sb[:], ALU.mult)
                k.op('dve', lambda: nc.vector.tensor_reduce(r8[:, 0:4], osq[:], AX.X, ALU.add), [osq], [r8])
                k.ts(r8[:, 0:4], r8[:, 0:4], 1.0 / 64, ALU.mult, EPS, ALU.add)
                k.act(r8[:, 0:4], r8[:, 0:4], AF.Sqrt)
                k.recip(r8[:, 4:8], r8[:, 0:4])
                k.tt(o_sb[:], o_sb[:], bc(r8[:, 4:8]), ALU.mult)
                k.tt(o_sb[:], o_sb[:], gn[:].unsqueeze(1).broadcast_to([128, 4, 64]), ALU.mult)
                k.act(zs[:], zt[:, t, :], AF.Silu)
                k.tt(dmix[:], o_sb[:].rearrange("p h e -> p (h e)"), zs[:], ALU.mult)
                px = self.bank()
                pxv = px[:].bitcast(BF16)
                for j in range(2):
                    k.tr(pxv[:, j * 128:(j + 1) * 128], dmix[:, j * 128:(j + 1) * 128], self.identb[:])
                k.cp(dg[:, :, tok], pxv[:, 0:256].rearrange("p (j t) -> p j t", j=2), eng='act')
            k.dma('sp', self.mix_scr.ap()[2, :, t0:t0 + 512].rearrange("(j p) t -> p j t", p=128), dg[:], dgs)
        S.close()

    def phase_s5(self, l):
        k = self.k
        nc = k.nc
        L = self.L
        NB = L // 512
        nch = L // 8
        S = Scope(k)
        w = self.w
        nlev = 0
        while (1 << nlev) < nch:
            nlev += 1
        W1 = S.sb('W1pad', [128, 2, 4, 8, 2, 128], BF16)
        Kbd = S.sb('Kbd', [128, 2, 8, 128], BF16)
        W2 = S.sb('W2bd', [128, 2, 4, 8, 2, 128], BF16)
        A8 = [S.sb('A8_%d' % i, [128, 3, 8], F32) for i in range(nlev)]
        wglu = S.sb('wglu', [128, 2, 256], BF16)
        k.dma('pool', wglu[:], w['s5_w_glu'].ap()[l].rearrange("(j p) n -> p j n", p=128), 'wglu')
        dcol = S.sb('dcol', [128, 2], F32)
        bglu = S.sb('bglu', [128, 2], F32)
        SP = Scope(k)
        dcol_t = self.load_cols(SP, w['s5_d'].ap()[l].rearrange("(j p) -> j p", p=128), 2, 'dcolt')
        bglu_t = self.load_cols(SP, w['s5_b_glu'].ap()[l].rearrange("(j p) -> j p", p=128), 2, 'bglut')
        k.cp(dcol[:], dcol_t[:])
        k.cp(bglu[:], bglu_t[:])
        ctr = [0]

        def T(shape=(128, 128), dt=F32):
            ctr[0] += 1
            return SP.sb('s5t%d' % ctr[0], list(shape), dt)

        def cl(name, shape):
            t_ = SP.sb(name, list(shape), F32)
            k.dma('sp', t_[:], self.c[name].ap(), name)
            return t_
        sel8 = cl('c_sel8', [8, 128])
        bm16 = cl('c_bm16', [128, 128])
        w1mask = cl('c_w1mask', [128, 8])
        w2mask = cl('c_w2mask', [128, 4, 128])
        lam8 = SP.sb('lam8', [8, 2, 2, 64], F32)
        k.dma('sp', lam8[:, 0, :, :], w['s5_lam_re'].ap()[l].rearrange("(j g) p -> g j p", g=8), 'lam8')
        k.dma('sp', lam8[:, 1, :, :], w['s5_lam_im'].ap()[l].rearrange("(j g) p -> g j p", g=8), 'lam8')
        ls8 = SP.sb('ls8', [8, 2], F32)
        for j in range(2):
            k.dma('sp', ls8[:, j:j + 1], w['s5_log_step'].ap()[l, j * 8:(j + 1) * 8].unsqueeze(1), 'ls8')
        lr, li, st = T(), T(), T((128, 2))
        pb = self.bank()
        k.mm(pb[:, 0:256], sel8[:], lam8[:].rearrange("g r j p -> g (r j p)"))
        k.mm(pb[:, 256:258], sel8[:], ls8[:])
        k.cp(lr[:], pb[:, 0:128])
        k.cp(li[:], pb[:, 128:256])
        k.act(st[:], pb[:, 256:258], AF.Exp)
        stb = st[:].unsqueeze(2).broadcast_to([128, 2, 64])

        def v3(t_):
            return t_[:].rearrange("q (j p) -> q j p", j=2)
        th, lrs = T(), T()
        k.tt(v3(th), v3(li), stb, ALU.mult)
        k.tt(v3(lrs), v3(lr), stb, ALU.mult)
        er, s32, cs, sn, t1, t2 = T(), T(), T(), T(), T(), T()
        k.act(er[:], lrs[:], AF.Exp)
        k.act(s32[:], th[:], AF.Sin, scale=1.0 / 32)
        k.act(sn[:], th[:], AF.Sin, scale=1.0 / 16)
        k.tt(t1[:], s32[:], s32[:], ALU.mult)
        k.ts(cs[:], t1[:], -2.0, ALU.mult, 1.0, ALU.add)
        for _ in range(4):
            k.tt(t1[:], cs[:], cs[:], ALU.mult)
            k.tt(t2[:], sn[:], sn[:], ALU.mult)
            k.stt(sn[:], cs[:], 2.0, sn[:], ALU.mult, ALU.mult)
            k.tt(cs[:], t1[:], t2[:], ALU.subtract)
        ar, ai = T(), T()
        k.tt(ar[:], er[:], cs[:], ALU.mult)
        k.tt(ai[:], er[:], sn[:], ALU.mult)
        den, nr, cre, cim = T(), T(), T(), T()
        k.tt(t1[:], lr[:], lr[:], ALU.mult)
        k.tt(t2[:], li[:], li[:], ALU.mult)
        k.tt(den[:], t1[:], t2[:], ALU.add)
        k.recip(den[:], den[:])
        k.ts(nr[:], ar[:], -1.0, ALU.add)
        k.tt(t1[:], nr[:], lr[:], ALU.mult)
        k.tt(t2[:], ai[:], li[:], ALU.mult)
        k.tt(cre[:], t1[:], t2[:], ALU.add)
        k.tt(cre[:], cre[:], den[:], ALU.mult)
        k.tt(t1[:], ai[:], lr[:], ALU.mult)
        k.tt(t2[:], nr[:], li[:], ALU.mult)
        k.tt(cim[:], t1[:], t2[:], ALU.subtract)
        k.tt(cim[:], cim[:], den[:], ALU.mult)

        def cmul(o_r, o_i, a_r, a_i, b_r, b_i, neg_im=False):
            k.tt(t1[:], a_r, b_r, ALU.mult)
            k.tt(t2[:], a_i, b_i, ALU.mult)
            k.tt(o_r, t1[:], t2[:], ALU.subtract)
            k.tt(t1[:], a_r, b_i, ALU.mult)
            k.tt(t2[:], a_i, b_r, ALU.mult)
            if neg_im:
                k.stt(o_i, t1[:], -1.0, t2[:], ALU.mult, ALU.subtract)
            else:
                k.tt(o_i, t1[:], t2[:], ALU.add)
        Bre, Bim = T(), T()
        for (nm, dst) in (('s5_b_re', Bre), ('s5_b_im', Bim)):
            bn = SP.sb(nm + 'n', [64, 16, 16], F32)
            k.dma('sp', bn[:], w[nm].ap()[l].rearrange("g p h -> p g h"), nm + 'n')
            for j in range(2):
                pb = self.bank()
                k.tr(pb[:, 0:64], bn[:, j * 8:(j + 1) * 8, :].rearrange("p g h -> p (g h)"), self.identf[0:64, 0:64])
                k.cp(dst[:, j * 64:(j + 1) * 64], pb[:, 0:64])
        Cre, Cim = T(), T()
        k.dma('sp', v3(Cre), w['s5_c_re'].ap()[l].rearrange("(j g) h p -> (g h) j p", g=8), 'Cre')
        k.dma('sp', v3(Cim), w['s5_c_im'].ap()[l].rearrange("(j g) h p -> (g h) j p", g=8), 'Cim')
        Bbr, Bbi = T(), T()
        cmul(Bbr[:], Bbi[:], cre[:], cim[:], Bre[:], Bim[:])
        Pr = [T() for _ in range(9)]
        Pi = [T() for _ in range(9)]
        k.memset(Pr[0][:], 1.0)
        k.memset(Pi[0][:], 0.0)
        k.cp(Pr[1][:], ar[:])
        k.cp(Pi[1][:], ai[:])
        for kk in range(2, 9):
            cmul(Pr[kk][:], Pi[kk][:], Pr[kk - 1][:], Pi[kk - 1][:], ar[:], ai[:])
        AB = [SP.sb('AB%d' % tau, [128, 2, 2, 64], F32) for tau in range(8)]
        abr, abi = T(), T()
        for tau in range(8):
            cmul(abr[:], abi[:], Pr[tau][:], Pi[tau][:], Bbr[:], Bbi[:])
            k.cp(AB[tau][:, :, 0, :], v3(abr), eng='act')
            k.cp(AB[tau][:, :, 1, :], v3(abi), eng='act')
        w1m = w1mask[:].rearrange("q (m h) -> q m h", h=2).unsqueeze(3).broadcast_to([128, 4, 2, 64])
        for s_ in range(8):
            for j in range(2):
                for ri in range(2):
                    src = AB[7 - s_][:, j, ri, :].unsqueeze(1).unsqueeze(1).broadcast_to([128, 4, 2, 64])
                    k.tt(W1[:, j, :, s_, ri, :].rearrange("q m (h p) -> q m h p", h=2), src, w1m, ALU.mult,
                         eng=('pool' if ri else 'dve'))
        CT0 = SP.sb('CT0', [128, 2, 2, 64], F32)
        k.cp(CT0[:, :, 0, :], v3(Cre))
        k.ts(CT0[:, :, 1, :], v3(Cim), -1.0, ALU.mult)
        CT0T = SP.sb('CT0T', [128, 2, 128], F32)
        for j in range(2):
            pb = self.bank()
            k.tr(pb[:, 0:128], CT0[:, j, :, :].rearrange("q r p -> q (r p)"), self.identf[:])
            k.cp(CT0T[:, j, :], pb[:, 0:128])
        abT = Rot(SP, 'abT', 2, [128, 128], F32)
        for tau in range(8):
            for j in range(2):
                pb = self.bank()
                k.tr(pb[:, 0:128], AB[tau][:, j, :, :].rearrange("q r p -> q (r p)"), self.identf[:])
                at_, _ = abT.next()
                k.cp(at_[:], pb[:, 0:128], eng='act')
                k.mm(pb[:, 128:256], at_[:], CT0T[:, j, :])
                k.tt(Kbd[:, j, tau, :], pb[:, 128:256], bm16[:], ALU.mult)
        CA = SP.sb('CA', [128, 2, 2, 2, 64], F32)
        car, cai = T(), T()
        for s_ in range(8):
            cmul(car[:], cai[:], Cre[:], Cim[:], Pr[s_ + 1][:], Pi[s_ + 1][:], neg_im=True)
            for dup in range(2):
                k.cp(CA[:, :, 0, dup, :], v3(car), eng='act')
                k.cp(CA[:, :, 1, dup, :], v3(cai), eng='pool')
            for j in range(2):
                for ri in range(2):
                    pb = self.bank()
                    k.tr(pb[:, 0:128], CA[:, j, ri, :, :].rearrange("q d p -> q (d p)"), self.identf[:])
                    k.tt(W2[:, j, :, s_, ri, :], pb[:, 0:128].unsqueeze(1).broadcast_to([128, 4, 128]), w2mask[:], ALU.mult)
        a8d = SP.sb('a8d', [128, 2, 2, 2, 64], F32)
        for dup in range(2):
            k.cp(a8d[:, :, 0, dup, :], v3(Pr[8]))
            k.cp(a8d[:, :, 1, dup, :], v3(Pi[8]))
        for j in range(2):
            for ri in range(2):
                pb = self.bank()
                k.tr(pb[:, 0:128], a8d[:, j, ri, :, :].rearrange("q d p -> q (d p)"), self.identf[:])
                k.cp(A8[0][0:64, ri, 4 * j:4 * j + 4], pb[0:64, 0:128:32])
                k.cp(A8[0][64:128, ri, 4 * j:4 * j + 4], pb[64:128, 16:128:32])
        s1, s2 = T((128, 8)), T((128, 8))
        for i in range(nlev):
            if i > 0:
                k.tt(s1[:], A8[i - 1][:, 0, :], A8[i - 1][:, 0, :], ALU.mult)
                k.tt(s2[:], A8[i - 1][:, 1, :], A8[i - 1][:, 1, :], ALU.mult)
                k.tt(A8[i][:, 0, :], s1[:], s2[:], ALU.subtract)
                k.stt(A8[i][:, 1, :], A8[i - 1][:, 0, :], 2.0, A8[i - 1][:, 1, :], ALU.mult, ALU.mult)
            k.ts(A8[i][:, 2, :], A8[i][:, 1, :], -1.0, ALU.mult)
        SP.close()
        uT = S.sb('uT', [128, 2, L], BF16)
        for j in range(2):
            k.dma('sp', uT[:, j, :], self.uT_scr.ap()[j * 128:(j + 1) * 128, :], 'uT')
        Xbf = S.sb('Xbf', [128, 2, 8, nch + 1], BF16)
        k.memset(Xbf[:, :, :, 0:1], 0.0)
        XA = [S.sb('XA%d' % i, [128, nch], F32) for i in range(2)]
        XB = [S.sb('XB%d' % i, [128, nch], F32) for i in range(2)]
        XT = S.sb('XT', [128, nch], F32)
        for m in range(8):
            j, mp = m // 4, m % 4
            for ri in range(2):
                for c0 in range(0, nch, 512):
                    n = min(512, nch - c0)
                    ps = self.bank()
                    for s_ in range(8):
                        k.mm(ps[:, 0:n], W1[:, j, mp, s_, ri, :], uT[:, j, 8 * c0 + s_:8 * (c0 + n):8],
                             start=(s_ == 0), stop=(s_ == 7))
                    k.cp(XA[ri][:, c0:c0 + n], ps[:, 0:n], eng=('act' if ri else 'dve'))
            cur, oth = XA, XB
            for lev in range(nlev):
                d = 1 << lev
                Ar = A8[lev][:, 0, m:m + 1]
                Ai = A8[lev][:, 1, m:m + 1]
                nAi = A8[lev][:, 2, m:m + 1]
                k.stt(XT[:, d:], cur[0][:, 0:nch - d], Ar, cur[0][:, d:], ALU.mult, ALU.add)
                k.stt(oth[0][:, d:], cur[1][:, 0:nch - d], nAi, XT[:, d:], ALU.mult, ALU.add)
                k.stt(XT[:, d:], cur[1][:, 0:nch - d], Ar, cur[1][:, d:], ALU.mult, ALU.add)
                k.stt(oth[1][:, d:], cur[0][:, 0:nch - d], Ai, XT[:, d:], ALU.mult, ALU.add)
                k.cp(oth[0][:, 0:d], cur[0][:, 0:d], eng='act')
                k.cp(oth[1][:, 0:d], cur[1][:, 0:d], eng='pool')
                cur, oth = oth, cur
            k.cp(Xbf[:, 0, m, 1:nch + 1], cur[0][:], eng='act')
            k.cp(Xbf[:, 1, m, 1:nch + 1], cur[1][:], eng='pool')
        yv = S.sb('yv', [128, 2, 512], F32)
        zf = S.sb('zf', [128, 2, 512], F32)
        zb = S.sb('zb', [128, 2, 512], BF16)
        sgm = S.sb('sgm', [128, 512], F32)
        sstg = Rot(S, 's5stg', 2, [128, 2, 512], BF16)
        for b in range(NB):
            t0 = b * 512
            cb0 = b * 64
            for j in range(2):
                ps = self.bank()
                k.memset(ps[:], 0.0)
                for s_ in range(8):
                    out = ps[:, s_:512:8]
                    for mp in range(4):
                        for ri in range(2):
                            k.mm(out, W2[:, j, mp, s_, ri, :], Xbf[:, ri, 4 * j + mp, cb0:cb0 + 64], start=False, stop=False)
                    for tau in range(s_ + 1):
                        k.mm(out, Kbd[:, j, tau, :], uT[:, j, t0 + s_ - tau:t0 + 512:8], start=False, stop=(tau == s_))
                k.stt(yv[:, j, :], uT[:, j, t0:t0 + 512], dcol[:, j:j + 1], ps[:], ALU.mult, ALU.add)
                k.act(zf[:, j, :], yv[:, j, :], AF.Gelu_apprx_tanh)
                k.cp(zb[:, j, :], zf[:, j, :], eng='pool')
            sg_, sgs = sstg.next()
            for jo in range(2):
                pg = self.bank()
                for ji in range(2):
                    k.mm(pg[:], wglu[:, ji, jo * 128:(jo + 1) * 128], zb[:, ji, :], start=(ji == 0), stop=(ji == 1))
                k.act(sgm[:], pg[:], AF.Sigmoid, bias=bglu[:, jo:jo + 1])
                k.tt(sg_[:, jo, :], zf[:, jo, :], sgm[:], ALU.mult)
            k.dma('sp', self.mix_scr.ap()[0, :, t0:t0 + 512].rearrange("(j p) t -> p j t", p=128), sg_[:], sgs)
        S.close()

    def zero_branch(self, br):
        k = self.k
        S = Scope(k)
        zt = S.sb('zt', [128, 2, 512], BF16)
        k.memset(zt[:], 0.0)
        for b in range(self.L // 512):
            k.dma('sp', self.mix_scr.ap()[br, :, b * 512:(b + 1) * 512].rearrange("(j p) t -> p j t", p=128), zt[:], 'zt')
        S.close()

    def phase_C(self, l):
        k = self.k
        L = self.L
        NB = L // 512
        S = Scope(k)
        w = self.w
        xsrc = self.x_in if l == 0 else self.y
        Wg = S.sb('Wg', [128, 8, 4096], BF16)
        for kk in range(8):
            k.dma('pool', Wg[:, kk, :], w['w_in'].ap()[l, kk * 128:(kk + 1) * 128, NMIX:NMIX + 4096], 'Wg')
        wbr = S.sb('wbr', [128, 4, 2, D], BF16)
        for i, nm in enumerate(('w_br_s5', 'w_br_sgu', 'w_br_dn', 'w_br_diff')):
            k.dma('pool', wbr[:, i, :, :], w[nm].ap()[l].rearrange("(j p) n -> p j n", p=128), 'wbr')
        wout = S.sb('wout', [128, 8, D], BF16)
        k.dma('pool', wout[:], w['w_out'].ap()[l].rearrange("(j p) n -> p j n", p=128), 'wout')
        xts = Rot(S, 'xt', 2, [128, 4, D], F32)
        hTs = Rot(S, 'hT', 2, [128, 8, 512], BF16)
        mixs = Rot(S, 'mx', 2, [128, 4, 2, 512], BF16)
        sig = Rot(S, 'sig', 2, [128, 512], F32)
        mg = S.sb('mg', [128, 512], F32)
        tmpm = S.sb('tmpm', [128, 512], F32)
        mgT = S.sb('mgT', [128, 8, 512], BF16)
        for b in range(NB):
            t0 = b * 512
            xt, xsem = xts.next()
            k.dma('sp', xt[:], xsrc.ap()[t0:t0 + 512, :].rearrange("(t p) d -> p t d", p=128), xsem)
            hT, hsem = hTs.next()
            k.dma('sp', hT[:], self.hT_scr.ap()[:, t0:t0 + 512].rearrange("(k p) t -> p k t", p=128), hsem)
            mx, msem = mixs.next()
            for br in range(4):
                k.dma('sp', mx[:, br, :, :], self.mix_scr.ap()[br, :, t0:t0 + 512].rearrange("(j p) t -> p j t", p=128), msem)
            for mo in range(8):
                for br in range(4):
                    pg = self.bank()
                    c0 = br * D + mo * 128
                    for kk in range(8):
                        k.mm(pg[:], Wg[:, kk, c0:c0 + 128], hT[:, kk, :], start=(kk == 0), stop=(kk == 7))
                    sg, _ = sig.next()
                    k.act(sg[:], pg[:], AF.Sigmoid)
                    py = self.bank()
                    for j in range(2):
                        k.mm(py[:], wbr[:, br, j, mo * 128:(mo + 1) * 128], mx[:, br, j, :], start=(j == 0), stop=(j == 1))
                    if br == 0:
                        k.tt(mg[:], py[:], sg[:], ALU.mult)
                    elif br < 3:
                        k.tt(tmpm[:], py[:], sg[:], ALU.mult)
                        k.tt(mg[:], mg[:], tmpm[:], ALU.add, eng='pool')
                    else:
                        k.tt(tmpm[:], py[:], sg[:], ALU.mult)
                        k.tt(mgT[:, mo, :], mg[:], tmpm[:], ALU.add, eng='pool')
            xn, xns = xt, xsem
            for t in range(4):
                for hf in range(2):
                    po = self.bank()
                    for kk in range(8):
                        k.mm(po[:], mgT[:, kk, t * 128:(t + 1) * 128], wout[:, kk, hf * 512:(hf + 1) * 512],
                             start=(kk == 0), stop=(kk == 7))
                    k.tt(xn[:, t, hf * 512:(hf + 1) * 512], po[:], xt[:, t, hf * 512:(hf + 1) * 512], ALU.add)
            k.dma('sp', self.y.ap()[t0:t0 + 512, :].rearrange("(t p) d -> p t d", p=128), xn[:], xns)
        S.close()

    def phase_D(self, l, p, last):
        k = self.k
        if 'bi0' in self.flags:
            self.bi = 0
        L = self.L
        NB = L // 512
        S = Scope(k)
        w = self.w
        HC = DFF // 2
        Wup = S.sb('Wup', [128, 8, 2, HC], BF16)
        for kk in range(8):
            for which in range(2):
                c0 = which * DFF + p * HC
                k.dma('pool', Wup[:, kk, which, :], w['ffn_w_up'].ap()[l, kk * 128:(kk + 1) * 128, c0:c0 + HC], 'Wup')
        Wdn = S.sb('Wdn', [128, 11, D], BF16)
        k.dma('pool', Wdn[:], w['ffn_w_down'].ap()[l, p * HC:(p + 1) * HC, :].rearrange("(j p) n -> p j n", p=128), 'Wdn')
        g2col = self.load_cols(S, w['norm_ffn_g'].ap()[l].rearrange("(k p) -> k p", p=128), 8, 'g2col')
        cw = self.load_cols(S, w['ffn_conv_w'].ap()[l].rearrange("k (j p) -> (k j) p", p=128), 132, 'cw')
        cb = self.load_cols(S, w['ffn_conv_b'].ap()[l].rearrange("(j p) -> j p", p=128), 44, 'cb')
        fin = last and p == 1
        if fin:
            gfin = self.load_row_bcast(S, w['norm_final_g'].ap(), D, 'gfin')
        xts = Rot(S, 'xt', 2, [128, 4, D], F32)
        hTs = Rot(S, 'hT', 2, [128, 8, 512], BF16)
        hs = S.sb('hs', [128, 4, D], BF16)
        junk = S.sb('junk', [128, D], F32)
        sss = Rot(S, 'ss', 2, [128, 16], F32)
        ups = Rot(S, 'ups', 3, [128, 514], F32)
        halo = S.sb('halo', [128, 22, 2], F32)
        k.memset(halo[:], 0.0)
        cv = Rot(S, 'cv', 2, [128, 512], F32)
        cg = Rot(S, 'cg', 2, [128, 512], F32)
        sgl = Rot(S, 'sgl', 2, [128, 512], F32)
        actT = S.sb('actT', [128, 11, 512], BF16)
        for b in range(NB):
            t0 = b * 512
            xt, xsem = xts.next()
            k.dma('sp', xt[:], self.y.ap()[t0:t0 + 512, :].rearrange("(t p) d -> p t d", p=128), xsem)
            hT, hsem = hTs.next()
            ss, _ = sss.next()
            hview = self.hT_scr.ap()[:, t0:t0 + 512].rearrange("(k p) t -> p k t", p=128)
            if p == 0:
                self.rms_block(xt, g2col, hT, junk, ss, hs)
                k.dma('sp', hview, hT[:], hsem)
            else:
                k.dma('sp', hT[:], hview, hsem)
            for i in range(11):
                res = []
                for which in range(2):
                    j = which * 22 + p * 11 + i
                    hj = which * 11 + i
                    pu = self.bank()
                    for kk in range(8):
                        k.mm(pu[:], Wup[:, kk, which, i * 128:(i + 1) * 128], hT[:, kk, :], start=(kk == 0), stop=(kk == 7))
                    up, _ = ups.next()
                    k.cp(up[:, 2:514], pu[:], eng='act')
                    k.cp(up[:, 0:2], halo[:, hj, :], eng='pool')
                    c, _ = (cv if which == 0 else cg).next()
                    k.act(c[:], pu[:], AF.Identity, scale=cw[:, 2 * 44 + j:2 * 44 + j + 1], bias=cb[:, j:j + 1])
                    k.stt(c[:], up[:, 1:513], cw[:, 44 + j:44 + j + 1], c[:], ALU.mult, ALU.add)
                    k.stt(c[:], up[:, 0:512], cw[:, j:j + 1], c[:], ALU.mult, ALU.add)
                    k.cp(halo[:, hj, :], up[:, 512:514], eng='pool')
                    res.append(c)
                sl, _ = sgl.next()
                k.act(sl[:], res[1][:], AF.Silu)
                k.tt(actT[:, i, :], sl[:], res[0][:], ALU.mult, eng='pool')
            for t in range(4):
                for hf in range(2):
                    po = self.bank()
                    for i in range(11):
                        k.mm(po[:], actT[:, i, t * 128:(t + 1) * 128], Wdn[:, i, hf * 512:(hf + 1) * 512],
                             start=(i == 0), stop=(i == 10))
                    k.tt(xt[:, t, hf * 512:(hf + 1) * 512], po[:], xt[:, t, hf * 512:(hf + 1) * 512], ALU.add)
            if fin:
                ss2, _ = sss.next()
                for t in range(4):
                    k.sumsq(junk[:], xt[:, t, :], ss2[:, t:t + 1])
                k.ts(ss2[:, 4:8], ss2[:, 0:4], 1.0 / D, ALU.mult, EPS, ALU.add)
                k.act(ss2[:, 8:12], ss2[:, 4:8], AF.Sqrt)
                k.recip(ss2[:, 12:16], ss2[:, 8:12])
                for t in range(4):
                    k.stt(xt[:, t, :], xt[:, t, :], ss2[:, 12 + t:13 + t], gfin[:], ALU.mult, ALU.mult)
            k.dma('sp', self.y.ap()[t0:t0 + 512, :].rearrange("(t p) d -> p t d", p=128), xt[:], xsem)
        S.close()

    def build(self):
        fl = self.flags
        if 'onlyA' in fl:
            self.phase_A(0)
            self.k.barrier()
            return self.k.nc
        for l in range(self.depth):
            self.phase_A(l)
            for br, nm in ((0, 's5'), (2, 'dn'), (3, 'att')):
                if nm in fl:
                    getattr(self, 'phase_' + nm)(l)
                else:
                    self.zero_branch(br)
            self.phase_C(l)
            if 'stopC' in fl:
                break
            self.phase_D(l, 0, last=(l == self.depth - 1))
            if 'stopD0' in fl:
                break
            self.phase_D(l, 1, last=(l == self.depth - 1))
        self.k.barrier()
        return self.k.nc


_CACHE = {}


def run_cores(xs, weights, L, depth, flags):
    key = (L, depth, tuple(sorted(flags)))
    if key not in _CACHE:
        _CACHE[key] = Prog(L, depth, flags).build()
    nc = _CACHE[key]
    consts = host_consts()
    in_maps = []
    for x in xs:
        m = {"x": np.ascontiguousarray(x, dtype=np.float32)}
        for name in WEIGHT_SHAPES:
            m[name] = np.ascontiguousarray(weights[name], dtype=np.float32)
        m.update(consts)
        in_maps.append(m)
    res = run_bass_kernel_spmd(nc, in_maps, core_ids=list(range(len(xs))))
    if "dbg" in flags:
        return [(r["y"], r["mix_scr"], r["hT_scr"]) for r in res.results]
    return [r["y"] for r in res.results]


def kernel(**inputs):
    x = np.asarray(inputs['x'])
    B, L, _ = x.shape
    weights = {n: np.asarray(inputs[n]) for n in WEIGHT_SHAPES}
    outs = run_cores([x[b] for b in range(B)], weights, L, DEPTH, ('s5', 'dn', 'att'))
    return np.stack(outs, axis=0).astype(np.float32)
```

```python
import math
from contextlib import ExitStack

import numpy as np
import ml_dtypes
import concourse.bass as bass
import concourse.mybir as mybir
from concourse.bass_utils import run_bass_kernel_spmd

F32 = mybir.dt.float32
BF16 = mybir.dt.bfloat16
AF = mybir.ActivationFunctionType
ALU = mybir.AluOpType
AX = mybir.AxisListType
SAME_ENGINE_SYNC = True

D = 1024
DEPTH = 4
NMIX = 2568
DFF = 2816
EPS = 1e-6


class KB:
    ENG = ('pe', 'act', 'dve', 'pool', 'sp')

    def __init__(self):
        self.nc = bass.Bass("TRN2", target_bir_lowering=False)
        nc = self.nc
        self.e = {'pe': nc.tensor, 'act': nc.scalar, 'dve': nc.vector, 'pool': nc.gpsimd, 'sp': nc.sync}
        self.semh = {}
        self.semv = {}
        for k in self.ENG:
            self.semh['E:' + k] = nc.alloc_semaphore(name="s_" + k)
            self.semv['E:' + k] = 0
        self.waited = {k: {} for k in self.ENG}
        self.lastw = {}
        self.reads = {}
        self.nins = {k: 0 for k in self.ENG}
        self.nwait = 0
        self._uid = 0

    def uid(self):
        self._uid += 1
        return self._uid

    @staticmethod
    def R(x):
        if isinstance(x, (str, tuple)):
            return x
        return x.name

    def _wait(self, eng, key, val):
        w = self.waited[eng]
        if w.get(key, 0) >= val:
            return
        self.e[eng].wait_ge(self.semh[key], val)
        self.nwait += 1
        w[key] = val

    def _deps(self, eng, reads, writes):
        best = {}
        for r in reads:
            ev = self.lastw.get(r)
            if ev is not None:
                best[ev[0]] = max(best.get(ev[0], 0), ev[1])
        for w_ in writes:
            ev = self.lastw.get(w_)
            if ev is not None:
                best[ev[0]] = max(best.get(ev[0], 0), ev[1])
            for k, v in self.reads.get(w_, {}).items():
                best[k] = max(best.get(k, 0), v)
        for k, v in best.items():
            if k == 'E:' + eng and (eng == 'pe' or not SAME_ENGINE_SYNC):
                continue
            self._wait(eng, k, v)

    def _commit(self, ev, reads, writes):
        for w_ in writes:
            self.lastw[w_] = ev
            self.reads[w_] = {}
        for r in reads:
            d = self.reads.setdefault(r, {})
            d[ev[0]] = max(d.get(ev[0], 0), ev[1])

    def op(self, eng, fn, reads, writes):
        pw = [r for r in reads if r is not None and not isinstance(r, (str, tuple)) and self._is_ps(r)]
        reads = [self.R(r) for r in reads if r is not None]
        writes = [self.R(w) for w in writes if w is not None] + [self.R(r) for r in pw]
        self._deps(eng, reads, writes)
        ins = fn()
        key = 'E:' + eng
        self.semv[key] += 1
        ins.then_inc(self.semh[key], 1)
        self.nins[eng] += 1
        self._commit((key, self.semv[key]), reads, writes)
        return ins

    def dma(self, q, out, in_, sem, **kw):
        reads = [self.R(in_)] if self._is_sb(in_) else []
        writes = [self.R(out)] if self._is_sb(out) else []
        self._deps(q, reads, writes)
        key = 'D:' + sem
        if key not in self.semh:
            self.semh[key] = self.nc.alloc_semaphore(name="d%d" % len(self.semh))
            self.semv[key] = 0
        ins = self.e[q].dma_start(out=out, in_=in_, **kw)
        self.semv[key] += 16
        ins.then_inc(self.semh[key], 16)
        self.nins[q] += 1
        self._commit((key, self.semv[key]), reads, writes)
        return ins

    @staticmethod
    def _is_ps(ap):
        t = getattr(ap, 'tensor', ap)
        return type(t).__name__.startswith('PS')

    @staticmethod
    def _is_sb(ap):
        return type(ap.tensor).__name__.startswith('SB')

    def barrier(self, engs=None):
        for eng in (engs or self.ENG):
            for key, val in self.semv.items():
                if val > 0:
                    self._wait(eng, key, val)

    def mm(self, out, lhsT, rhs, start=True, stop=True, **kw):
        nc = self.nc
        return self.op('pe', lambda: nc.tensor.matmul(out, lhsT, rhs, start=start, stop=stop,
                                                      skip_group_check=True, **kw), [lhsT, rhs], [out])

    def tr(self, out, in_, ident):
        nc = self.nc
        return self.op('pe', lambda: nc.tensor.transpose(out, in_, ident), [in_, ident], [out])

    def act(self, out, in_, func, bias=None, scale=None):
        nc = self.nc
        kw = {}
        rr = [in_]
        if bias is not None:
            kw['bias'] = bias
            if not isinstance(bias, (int, float)):
                rr.append(bias)
        if scale is not None:
            kw['scale'] = scale
            if not isinstance(scale, (int, float)):
                rr.append(scale)
        return self.op('act', lambda: nc.scalar.activation(out, in_, func, **kw), rr, [out])

    def tt(self, out, in0, in1, op, eng='dve'):
        e = self.e[eng]
        return self.op(eng, lambda: e.tensor_tensor(out, in0, in1, op), [in0, in1], [out])

    def ts(self, out, in0, s1, op0, s2=None, op1=None, eng='dve'):
        e = self.e[eng]
        rr = [in0] + [s for s in (s1, s2) if s is not None and not isinstance(s, (int, float))]
        if op1 is None:
            return self.op(eng, lambda: e.tensor_scalar(out, in0, s1, None, op0), rr, [out])
        return self.op(eng, lambda: e.tensor_scalar(out, in0, s1, s2, op0, op1), rr, [out])

    def stt(self, out, in0, scalar, in1, op0, op1):
        nc = self.nc
        rr = [in0, in1] + ([scalar] if not isinstance(scalar, (int, float)) else [])
        return self.op('dve', lambda: nc.vector.scalar_tensor_tensor(out, in0, scalar, in1, op0, op1), rr, [out])

    def sumsq(self, junk, x, accum):
        nc = self.nc
        self.act(junk, x, AF.Square)
        return self.op('dve', lambda: nc.vector.tensor_reduce(accum, junk, AX.X, ALU.add), [junk], [accum])

    def cp(self, out, in_, eng='dve'):
        if eng == 'act':
            nc = self.nc
            return self.op('act', lambda: nc.scalar.copy(out, in_), [in_], [out])
        e = self.e[eng]
        return self.op(eng, lambda: e.tensor_copy(out, in_), [in_], [out])

    def memset(self, ap, val, eng='dve'):
        e = self.e[eng]
        return self.op(eng, lambda: e.memset(ap, val), [], [ap])

    def recip(self, out, in_):
        nc = self.nc
        return self.op('dve', lambda: nc.vector.reciprocal(out, in_), [in_], [out])


class Scope:
    def __init__(self, k):
        self.k = k
        self.es = ExitStack()

    def sb(self, name, shape, dt):
        return self.es.enter_context(self.k.nc.sbuf_tensor("%s_%d" % (name, self.k.uid()), list(shape), dt))

    def close(self):
        self.k.barrier()
        self.es.close()


class Rot:
    def __init__(self, S, name, n, shape, dt):
        self.bufs = [(S.sb("%s%d" % (name, i), shape, dt), "%s%d" % (name, i)) for i in range(n)]
        self.i = 0

    def next(self):
        b = self.bufs[self.i % len(self.bufs)]
        self.i += 1
        return b


WEIGHT_SHAPES = {
    'norm_mix_g': (4, 1024), 'w_in': (4, 1024, 6664), 's5_lam_re': (4, 16, 64), 's5_lam_im': (4, 16, 64),
    's5_log_step': (4, 16), 's5_b_re': (4, 16, 64, 16), 's5_b_im': (4, 16, 64, 16), 's5_c_re': (4, 16, 16, 64),
    's5_c_im': (4, 16, 16, 64), 's5_d': (4, 256), 's5_w_glu': (4, 256, 256), 's5_b_glu': (4, 256),
    'sgu_norm_g': (4, 256), 'sgu_w_s': (4, 4, 128, 128), 'sgu_b_s': (4, 4, 128), 'dn_conv_w': (4, 4, 768),
    'dn_a_log': (4, 4), 'dn_dt_bias': (4, 4), 'dn_norm_g': (4, 64), 'diff_lq1': (4, 32), 'diff_lk1': (4, 32),
    'diff_lq2': (4, 32), 'diff_lk2': (4, 32), 'diff_norm_g': (4, 64), 'w_br_s5': (4, 256, 1024),
    'w_br_sgu': (4, 256, 1024), 'w_br_dn': (4, 256, 1024), 'w_br_diff': (4, 256, 1024), 'w_out': (4, 1024, 1024),
    'norm_ffn_g': (4, 1024), 'ffn_w_up': (4, 1024, 5632), 'ffn_conv_w': (4, 3, 5632), 'ffn_conv_b': (4, 5632),
    'ffn_w_down': (4, 2816, 1024), 'norm_final_g': (1024,),
}


def host_consts():
    c = {}
    c['c_identb'] = np.eye(128).astype(ml_dtypes.bfloat16)
    c['c_identf'] = np.eye(128, dtype=np.float32)
    i = np.arange(128)
    c['c_mui'] = (i[:, None] <= i[None, :]).astype(np.float32)
    c['c_msu'] = (i[:, None] < i[None, :]).astype(np.float32)
    c['c_msl'] = (i[:, None] > i[None, :]).astype(np.float32)
    slopes = [2.0 ** (-8.0 * (h + 1) / 4) for h in range(4)]
    c['c_ekey'] = np.stack([np.exp(sl * i) for sl in slopes], axis=1).astype(np.float32)
    d = np.arange(67) - 3
    c['c_abias'] = np.tile(np.concatenate([-sl * 128.0 * d for sl in slopes])[None, :], (128, 1)).astype(np.float32)
    q = np.arange(512)
    c['c_cmask'] = np.stack([(q[None, :] >= 128 * j + i[:, None]) for j in range(4)], axis=1).astype(ml_dtypes.bfloat16)
    c['c_m32'] = np.stack([((i // 32) == j) for j in range(4)], axis=1).astype(np.float32)
    c['c_m64'] = np.stack([((i // 64) == j) for j in range(2)], axis=1).astype(np.float32)
    c['c_sel8'] = (np.arange(8)[:, None] == (i[None, :] // 16)).astype(np.float32)
    c['c_bm16'] = ((i[:, None] // 16) == (i[None, :] // 16)).astype(np.float32)
    c['c_w1mask'] = np.stack([((i // 16) == mh) for mh in range(8)], axis=1).astype(np.float32)
    half = i // 64
    c['c_w2mask'] = np.stack([((i[None, :] // 16) == (2 * mp + half[:, None])) for mp in range(4)], axis=1).astype(np.float32)
    return c


class Prog:
    def __init__(self, L, depth, flags):
        self.L = L
        self.depth = depth
        self.flags = flags
        self.lvl = 99
        for f in flags:
            if f.startswith('lvl'):
                self.lvl = int(f[3:])
        self.k = KB()
        k = self.k
        nc = k.nc
        self.x_in = nc.dram_tensor("x", [L, D], F32, kind="ExternalInput")
        self.y = nc.dram_tensor("y", [L, D], F32, kind="ExternalOutput")
        self.w = {}
        for name, shp in WEIGHT_SHAPES.items():
            self.w[name] = nc.dram_tensor(name, list(shp), F32, kind="ExternalInput")
        self.c = {}
        for name, arr in host_consts().items():
            dt = BF16 if arr.dtype == ml_dtypes.bfloat16 else F32
            self.c[name] = nc.dram_tensor(name, list(arr.shape), dt, kind="ExternalInput")
        self.hT_scr = nc.dram_tensor("hT_scr", [D, L], BF16, kind=("ExternalOutput" if "dbg" in flags else "Internal"))
        self.uT_scr = nc.dram_tensor("uT_scr", [256, L], BF16, kind="Internal")
        self.qkvc_scr = nc.dram_tensor("qkvc_scr", [768, L], BF16, kind="Internal")
        self.ab_scr = nc.dram_tensor("ab_scr", [128, (L // 128) * 8], F32, kind="Internal")
        self.z_scr = nc.dram_tensor("z_scr", [L, 256], BF16, kind="Internal")
        self.aq_scr = nc.dram_tensor("aq_scr", [256, L], BF16, kind="Internal")
        self.ak_scr = nc.dram_tensor("ak_scr", [256, L], BF16, kind="Internal")
        self.av_scr = nc.dram_tensor("av_scr", [L, 256], BF16, kind="Internal")
        self.mix_scr = nc.dram_tensor("mix_scr", [4, 256, L], BF16, kind=("ExternalOutput" if "dbg" in flags else "Internal"))
        self.banks = [nc.alloc_psum_tensor("bank%d" % i, [128, 512], F32) for i in range(8)]
        self.bi = 0
        self.bank_pool = list(range(8))
        self.G = Scope(k)
        self.identb = self.G.sb('identb', [128, 128], BF16)
        self.identf = self.G.sb('identf', [128, 128], F32)
        k.dma('sp', self.identb[:], self.c['c_identb'].ap(), 'identb')
        k.dma('sp', self.identf[:], self.c['c_identf'].ap(), 'identf')
        self.mui = self.G.sb('mui', [128, 128], F32)
        k.dma('sp', self.mui[:], self.c['c_mui'].ap(), 'mui')
        self.ones1 = self.G.sb('ones1', [1, 128], F32)
        k.memset(self.ones1[:], 1.0)

    def bank(self):
        pool = self.bank_pool
        b = self.banks[pool[self.bi % len(pool)]]
        self.bi += 1
        return b

    def load_cols(self, S, src2d, n, name):
        k = self.k
        dst = S.sb(name, [128, n], F32)
        done = 0
        while done < n:
            m = min(128, n - done)
            tmp = S.sb(name + 'r', [m, 128], F32)
            k.dma('sp', tmp[:], src2d[done:done + m, :], name + 'r%d' % done)
            pb = self.bank()
            k.tr(pb[:, 0:m], tmp[:], self.identf[0:m, 0:m])
            k.cp(dst[:, done:done + m], pb[:, 0:m])
            done += m
        return dst

    def load_row_bcast(self, S, src1d, n, name):
        k = self.k
        row = S.sb(name + 'r', [1, n], F32)
        k.dma('sp', row[:], src1d.unsqueeze(0), name + 'r')
        dst = S.sb(name, [128, n], F32)
        for c0 in range(0, n, 512):
            m = min(512, n - c0)
            pb = self.bank()
            k.mm(pb[:, 0:m], self.ones1[:, :], row[:, c0:c0 + m])
            k.cp(dst[:, c0:c0 + m], pb[:, 0:m])
        return dst

    def rms_block(self, xt, gcol, hT, junk, ss, hs):
        k = self.k
        for t in range(4):
            k.sumsq(junk[:], xt[:, t, :], ss[:, t:t + 1])
        k.ts(ss[:, 4:8], ss[:, 0:4], 1.0 / D, ALU.mult, EPS, ALU.add)
        k.act(ss[:, 8:12], ss[:, 4:8], AF.Sqrt)
        k.recip(ss[:, 12:16], ss[:, 8:12])
        for t in range(4):
            k.act(hs[:, t, :], xt[:, t, :], AF.Identity, scale=ss[:, 12 + t:13 + t])
        for kk in range(8):
            pb = self.bank()
            pv = pb[:].bitcast(BF16)
            for t in range(4):
                k.tr(pv[:, t * 128:(t + 1) * 128], hs[:, t, kk * 128:(kk + 1) * 128], self.identb[:])
            if kk % 2 == 0:
                k.ts(hT[:, kk, :], pv[:, 0:512], gcol[:, kk:kk + 1], ALU.mult)
            else:
                k.act(hT[:, kk, :], pv[:, 0:512], AF.Identity, scale=gcol[:, kk:kk + 1])

    def phase_A(self, l):
        k = self.k
        L = self.L
        NB = L // 512
        S = Scope(k)
        w = self.w
        xsrc = self.x_in if l == 0 else self.y
        WinA = S.sb('WinA', [128, 8, NMIX], BF16)
        for kk in range(8):
            k.dma('pool', WinA[:, kk, :], w['w_in'].ap()[l, kk * 128:(kk + 1) * 128, 0:NMIX], 'WinA')
        g1col = self.load_cols(S, w['norm_mix_g'].ap()[l].rearrange("(k p) -> k p", p=128), 8, 'g1col')
        wsT = S.sb('wsT', [128, 4, 128], BF16)
        wtmp = S.sb('wtmp', [128, 4, 128], F32)
        k.dma('sp', wtmp[:], w['sgu_w_s'].ap()[l].rearrange("g t s -> t g s"), 'wtmp')
        for g in range(4):
            pb = self.bank()
            k.tr(pb[:, 0:128], wtmp[:, g, :], self.identf[:])
            k.tt(wsT[:, g, :], pb[:, 0:128], self.mui[:], ALU.mult)
        sgng = self.load_row_bcast(S, w['sgu_norm_g'].ap()[l], 256, 'sgng')
        bsT = self.load_cols(S, w['sgu_b_s'].ap()[l], 4, 'bsT')
        cwc = self.load_cols(S, w['dn_conv_w'].ap()[l].rearrange("k (j p) -> (k j) p", p=128), 24, 'cwc')
        diagW = S.sb('diagW', [128, 6, 4, 128], BF16)
        for j in range(6):
            for kk in range(4):
                k.ts(diagW[:, j, kk, :], self.identf[:], cwc[:, kk * 6 + j:kk * 6 + j + 1], ALU.mult)
        xts = Rot(S, 'xt', 2, [128, 4, D], F32)
        hTs = Rot(S, 'hT', 2, [128, 8, 512], BF16)
        hs = S.sb('hs', [128, 4, D], BF16)
        junk = S.sb('junk', [128, D], F32)
        sss = Rot(S, 'ss', 2, [128, 16], F32)
        stg = Rot(S, 'stg', 4, [128, 512], BF16)
        abt = Rot(S, 'abt', 2, [128, 8], F32)
        qraw = S.sb('qraw', [128, 6, 515], BF16)
        k.memset(qraw[:, :, 0:3], 0.0)
        uv = S.sb('uv', [128, 512], F32)
        vn = S.sb('vn', [128, 256], F32)
        vnb = S.sb('vnb', [128, 256], BF16)
        st6 = S.sb('st6', [128, 8], F32)
        mxs = S.sb('mxs', [128, 256], F32)
        smix = S.sb('smix', [128, 256], BF16)
        smT = Rot(S, 'smT', 2, [128, 2, 512], BF16)
        ztm = Rot(S, 'ztm', 2, [128, 256], BF16)
        vtm = Rot(S, 'vtm', 2, [128, 256], BF16)

        for b in range(NB):
            t0 = b * 512
            xt, xsem = xts.next()
            k.dma('sp', xt[:], xsrc.ap()[t0:t0 + 512, :].rearrange("(t p) d -> p t d", p=128), xsem)
            hT, hsem = hTs.next()
            ss, _ = sss.next()
            self.rms_block(xt, g1col, hT, junk, ss, hs)
            k.dma('sp', self.hT_scr.ap()[:, t0:t0 + 512].rearrange("(k p) t -> p k t", p=128), hT[:], hsem)

            def fm(c0, M):
                pb = self.bank()
                for kk in range(8):
                    k.mm(pb[0:M, :], WinA[:, kk, c0:c0 + M], hT[:, kk, :], start=(kk == 0), stop=(kk == 7))
                return pb

            if self.lvl < 2:
                continue
            for j in range(2):
                pb = fm(j * 128, 128)
                sg, sgs = stg.next()
                k.cp(sg[:], pb[:], eng='act')
                k.dma('sp', self.uT_scr.ap()[j * 128:(j + 1) * 128, t0:t0 + 512], sg[:], sgs)
            if self.lvl < 3:
                continue
            for j in range(6):
                pb = fm(768 + j * 128, 128)
                k.cp(qraw[:, j, 3:515], pb[:], eng=('act' if j % 2 else 'dve'))
            for j in range(6):
                pb = self.bank()
                for kk in range(4):
                    k.mm(pb[:], diagW[:, j, kk, :], qraw[:, j, kk:kk + 512], start=(kk == 0), stop=(kk == 3))
                sg, sgs = stg.next()
                k.act(sg[:], pb[:], AF.Silu)
                k.dma('sp', self.qkvc_scr.ap()[j * 128:(j + 1) * 128, t0:t0 + 512], sg[:], sgs)
            k.cp(qraw[:, :, 0:3], qraw[:, :, 512:515], eng='pool')
            if self.lvl < 5:
                continue
            for (c0, scr) in ((1800, self.aq_scr), (2056, self.ak_scr)):
                for j in range(2):
                    pb = fm(c0 + j * 128, 128)
                    sg, sgs = stg.next()
                    k.cp(sg[:], pb[:], eng=('act' if j else 'dve'))
                    k.dma('sp', scr.ap()[j * 128:(j + 1) * 128, t0:t0 + 512], sg[:], sgs)
            if self.lvl < 6:
                continue
            sm, sms = smT.next()
            for t in range(4):
                tok = slice(t * 128, (t + 1) * 128)
                pb = self.bank()
                for kk in range(8):
                    k.mm(pb[:], hT[:, kk, tok], WinA[:, kk, 256:768], start=(kk == 0), stop=(kk == 7))
                k.act(uv[:], pb[:], AF.Gelu_apprx_tanh)
                k.op('dve', lambda: k.nc.vector.bn_stats(st6[:, 0:6], uv[:, 256:512]), [uv], [st6])
                k.op('dve', lambda: k.nc.vector.bn_aggr(st6[:, 6:8], st6[:, 0:6]), [st6], [st6])
                k.ts(st6[:, 7:8], st6[:, 7:8], EPS, ALU.add)
                k.act(st6[:, 7:8], st6[:, 7:8], AF.Sqrt)
                k.recip(st6[:, 7:8], st6[:, 7:8])
                k.ts(vn[:], uv[:, 256:512], st6[:, 6:7], ALU.subtract, st6[:, 7:8], ALU.mult)
                k.tt(vnb[:], vn[:], sgng[:], ALU.mult)
                pm = self.bank()
                for g in range(4):
                    k.mm(pm[:, g * 64:(g + 1) * 64], wsT[:, g, :], vnb[:, g * 64:(g + 1) * 64])
                k.tt(mxs[:].rearrange("p (g c) -> p g c", g=4), pm[:, 0:256].rearrange("p (g c) -> p g c", g=4),
                     bsT[:].unsqueeze(2).broadcast_to([128, 4, 64]), ALU.add)
                k.tt(smix[:], mxs[:], uv[:, 0:256], ALU.mult)
                pt = self.bank()
                ptv = pt[:].bitcast(BF16)
                for j in range(2):
                    k.tr(ptv[:, j * 128:(j + 1) * 128], smix[:, j * 128:(j + 1) * 128], self.identb[:])
                k.cp(sm[:, :, tok], ptv[:, 0:256].rearrange("p (j t) -> p j t", j=2), eng='act')
                if self.lvl < 7:
                    continue
                pz = self.bank()
                for kk in range(8):
                    k.mm(pz[:, 0:256], hT[:, kk, tok], WinA[:, kk, 1536:1792], start=(kk == 0), stop=(kk == 7))
                for kk in range(8):
                    k.mm(pz[:, 256:512], hT[:, kk, tok], WinA[:, kk, 2312:2568], start=False, stop=(kk == 7))
                zt, zts = ztm.next()
                k.cp(zt[:], pz[:, 0:256], eng='act')
                k.dma('sp', self.z_scr.ap()[t0 + t * 128:t0 + (t + 1) * 128, :], zt[:], zts)
                vt, vts = vtm.next()
                k.cp(vt[:], pz[:, 256:512])
                k.dma('sp', self.av_scr.ap()[t0 + t * 128:t0 + (t + 1) * 128, :], vt[:], vts)
                pab = self.bank()
                for kk in range(8):
                    k.mm(pab[:, 0:8], hT[:, kk, tok], WinA[:, kk, 1792:1800], start=(kk == 0), stop=(kk == 7))
                at_, ats = abt.next()
                k.cp(at_[:], pab[:, 0:8])
                nn = b * 4 + t
                k.dma('sp', self.ab_scr.ap()[:, nn * 8:(nn + 1) * 8], at_[:], ats)
            k.dma('sp', self.mix_scr.ap()[1, :, t0:t0 + 512].rearrange("(j p) t -> p j t", p=128), sm[:], sms)
        S.close()

    ATT_WIN = (6, 17, 1 << 20, 1 << 20)

    def phase_att(self, l):
        k = self.k
        nc = k.nc
        L = self.L
        NT = L // 128
        NQ = L // 512
        S = Scope(k)
        w = self.w
        lam_init = 0.8 - 0.6 * math.exp(-0.3 * l)
        KT = S.sb('KT', [128, 2, L], BF16)
        QT = S.sb('QT', [128, 2, L], BF16)
        for j in range(2):
            k.dma('sp', KT[:, j, :], self.ak_scr.ap()[j * 128:(j + 1) * 128, :], 'KT')
            k.dma('sp', QT[:, j, :], self.aq_scr.ap()[j * 128:(j + 1) * 128, :], 'QT')
        ekey = S.sb('ekey', [128, 4], F32)
        k.dma('sp', ekey[:], self.c['c_ekey'].ap(), 'ekey')
        abias = S.sb('abias', [128, 4 * 67], F32)
        k.dma('sp', abias[:], self.c['c_abias'].ap(), 'abias')
        cmask = S.sb('cmask', [128, 4, 512], BF16)
        k.dma('sp', cmask[:], self.c['c_cmask'].ap(), 'cmask')
        m32 = S.sb('m32', [128, 4], F32)
        k.dma('sp', m32[:], self.c['c_m32'].ap(), 'm32')
        Vaug = S.sb('Vaug', [128, NT, 4, 65], BF16)
        onesn = S.sb('onesn', [128, NT], F32)
        k.memset(onesn[:], 1.0)
        CH = min(16, NT)
        vtmp = Rot(S, 'vtmp', 2, [128, CH, 256], BF16)
        for n0 in range(0, NT, CH):
            vt, vts = vtmp.next()
            k.dma('sp', vt[:], self.av_scr.ap()[n0 * 128:(n0 + CH) * 128, :].rearrange("(n p) c -> p n c", p=128), vts)
            for h in range(4):
                k.ts(Vaug[:, n0:n0 + CH, h, 0:64], vt[:, :, h * 64:(h + 1) * 64], ekey[:, h:h + 1], ALU.mult)
        for h in range(4):
            k.act(Vaug[:, :, h, 64:65], onesn[:].unsqueeze(2), AF.Identity, scale=ekey[:, h:h + 1])
        lq = [self.load_row_bcast(S, w[n].ap()[l], 32, n) for n in ('diff_lq1', 'diff_lk1', 'diff_lq2', 'diff_lk2')]
        lt = S.sb('lt', [128, 32], F32)
        lv = S.sb('lv', [128, 8], F32)
        for i in range(2):
            k.tt(lt[:], lq[2 * i][:], lq[2 * i + 1][:], ALU.mult)
            k.op('dve', lambda: nc.vector.tensor_reduce(lv[:, i:i + 1], lt[:], AX.X, ALU.add), [lt], [lv])
        k.act(lv[:, 2:4], lv[:, 0:2], AF.Exp)
        k.tt(lv[:, 4:5], lv[:, 2:3], lv[:, 3:4], ALU.subtract)
        k.ts(lv[:, 5:6], lv[:, 4:5], lam_init, ALU.add)
        gd = self.load_row_bcast(S, w['diff_norm_g'].ap()[l], 64, 'gd')
        k.ts(gd[:], gd[:], 1.0 - lam_init, ALU.mult)
        QTm = Rot(S, 'QTm', 4, [128, 512], BF16)
        Pb = Rot(S, 'Pb', 5, [128, 512], BF16)
        rr = S.sb('rr', [128, 16], F32)
        t1 = S.sb('t1', [128, 4, 64], F32)
        t2 = S.sb('t2', [128, 4, 64], F32)
        sq = S.sb('sq', [128, 4, 64], F32)
        omix = Rot(S, 'omix', 2, [128, 4, 256], BF16)
        ostg = Rot(S, 'ostg', 2, [128, 2, 512], BF16)
        Osets = [[self.banks[4], self.banks[5]], [self.banks[6], self.banks[7]]]
        self.bank_pool = list(range(4))
        scale = 32.0 ** -0.5
        LA = 2
        gi = 0
        for qb in range(NQ):
            om, _ = omix.next()
            for h in range(4):
                t2i = h // 2
                O = Osets[gi % 2]
                gi += 1
                qm = []
                for m in range(2):
                    qt_, _ = QTm.next()
                    k.ts(qt_[:], QT[:, t2i, qb * 512:(qb + 1) * 512], m32[:, (h % 2) * 2 + m:(h % 2) * 2 + m + 1], ALU.mult,
                         eng=('pool' if m else 'dve'))
                    qm.append(qt_)
                for m in range(2):
                    k.memset(O[m][:, 0:260], 0.0)
                blocks = [(s_, 128) for s_ in range(4)] if h == 0 else [(None, 512)]
                steps = []
                for (s_, QB) in blocks:
                    qt_first = 4 * qb + (s_ if s_ is not None else 0)
                    qt_last = 4 * qb + (s_ if s_ is not None else 3)
                    kt_lo = max(0, qt_first - self.ATT_WIN[h])
                    for kt in range(kt_lo, qt_last + 1):
                        for m in range(2):
                            steps.append((s_, QB, qt_first, kt, m))
                live = {}

                def front(i):
                    s_, QB, qt_first, kt, m = steps[i]
                    jd = kt - qt_first
                    ps = self.bank()
                    qcols = slice(s_ * 128, (s_ + 1) * 128) if s_ is not None else slice(0, 512)
                    k.mm(ps[:, 0:QB], KT[:, t2i, kt * 128:(kt + 1) * 128], qm[m][:, qcols])
                    pT, _ = Pb.next()
                    dd = (qt_first - kt) + 3
                    k.act(pT[:, 0:QB], ps[:, 0:QB], AF.Exp, scale=scale, bias=abias[:, h * 67 + dd:h * 67 + dd + 1])
                    if jd >= 0:
                        k.tt(pT[:, 0:QB], pT[:, 0:QB], cmask[:, jd, 0:QB], ALU.mult, eng='pool')
                    live[i] = pT

                def back(i):
                    s_, QB, qt_first, kt, m = steps[i]
                    jd = kt - qt_first
                    pT = live.pop(i)
                    for ss_ in ([s_] if s_ is not None else range(4)):
                        if s_ is None and ss_ < jd:
                            continue
                        col = 0 if s_ is not None else ss_ * 128
                        k.mm(O[m][:, ss_ * 65:(ss_ + 1) * 65], pT[:, col:col + 128], Vaug[:, kt, h, :],
                             start=False, stop=True)
                for i in range(len(steps) + LA):
                    if i < len(steps):
                        front(i)
                    if i >= LA:
                        back(i - LA)
                for m in range(2):
                    k.recip(rr[:, 4 * m:4 * m + 4], O[m][:, 0:260].rearrange("p (s e) -> p s e", e=65)[:, :, 64])
                k.ts(rr[:, 4:8], rr[:, 4:8], lv[:, 5:6], ALU.mult)
                k.tt(t1[:], O[0][:, 0:260].rearrange("p (s e) -> p s e", e=65)[:, :, 0:64],
                     rr[:, 0:4].unsqueeze(2).broadcast_to([128, 4, 64]), ALU.mult)
                k.tt(t2[:], O[1][:, 0:260].rearrange("p (s e) -> p s e", e=65)[:, :, 0:64],
                     rr[:, 4:8].unsqueeze(2).broadcast_to([128, 4, 64]), ALU.mult)
                k.tt(t1[:], t1[:], t2[:], ALU.subtract)
                k.tt(sq[:], t1[:], t1[:], ALU.mult)
                k.op('dve', lambda: nc.vector.tensor_reduce(rr[:, 8:12], sq[:], AX.X, ALU.add), [sq], [rr])
                k.ts(rr[:, 8:12], rr[:, 8:12], 1.0 / 64, ALU.mult, EPS, ALU.add)
                k.act(rr[:, 8:12], rr[:, 8:12], AF.Sqrt)
                k.recip(rr[:, 12:16], rr[:, 8:12])
                k.tt(t1[:], t1[:], rr[:, 12:16].unsqueeze(2).broadcast_to([128, 4, 64]), ALU.mult)
                k.tt(om[:, :, h * 64:(h + 1) * 64], t1[:], gd[:].unsqueeze(1).broadcast_to([128, 4, 64]), ALU.mult)
            og, ogs = ostg.next()
            for s_ in range(4):
                pt = self.bank()
                ptv = pt[:].bitcast(BF16)
                for j in range(2):
                    k.tr(ptv[:, j * 128:(j + 1) * 128], om[:, s_, j * 128:(j + 1) * 128], self.identb[:])
                k.cp(og[:, :, s_ * 128:(s_ + 1) * 128], ptv[:, 0:256].rearrange("p (j t) -> p j t", j=2), eng='act')
            k.dma('sp', self.mix_scr.ap()[3, :, qb * 512:(qb + 1) * 512].rearrange("(j p) t -> p j t", p=128), og[:], ogs)
        self.bank_pool = list(range(8))
        S.close()

    def phase_dn(self, l):
        k = self.k
        nc = k.nc
        L = self.L
        NT = L // 128
        NB = L // 512
        S = Scope(k)
        w = self.w
        onesf = S.sb('onesf', [128, 128], F32)
        k.memset(onesf[:], 1.0)
        nmsl = S.sb('nmsl', [128, 128], F32)
        nmsu = S.sb('nmsu', [128, 128], F32)
        k.dma('sp', nmsl[:], self.c['c_msl'].ap(), 'nmsl')
        k.dma('sp', nmsu[:], self.c['c_msu'].ap(), 'nmsu')
        k.ts(nmsl[:], nmsl[:], -1.0, ALU.mult)
        k.ts(nmsu[:], nmsu[:], -1.0, ALU.mult)
        alog = self.load_row_bcast(S, w['dn_a_log'].ap()[l], 4, 'alog')
        dtb = self.load_row_bcast(S, w['dn_dt_bias'].ap()[l], 4, 'dtb')
        k.act(alog[:], alog[:], AF.Exp)
        k.ts(alog[:], alog[:], -1.0, ALU.mult)
        gn = self.load_row_bcast(S, w['dn_norm_g'].ap()[l], 64, 'gn')
        ab = S.sb('ab', [128, NT, 8], F32)
        k.dma('sp', ab[:].rearrange("p n c -> p (n c)"), self.ab_scr.ap(), 'ab')
        gt = S.sb('gt', [128, NT, 4], F32)
        bt = S.sb('bt', [128, NT, 4], F32)
        Gt = S.sb('Gt', [128, NT, 4], F32)
        GL = S.sb('GL', [128, NT, 4], F32)
        eG = S.sb('eG', [128, NT, 4], F32)
        kds = S.sb('kds', [128, NT, 4], F32)
        gla = S.sb('gla', [128, NT, 4], F32)
        k.tt(gt[:], ab[:, :, 0:4], dtb[:].unsqueeze(1).broadcast_to([128, NT, 4]), ALU.add)
        k.act(gt[:], gt[:], AF.Exp)
        k.ts(gt[:], gt[:], 1.0, ALU.add)
        k.act(gt[:], gt[:], AF.Ln)
        k.tt(gt[:], gt[:], alog[:].unsqueeze(1).broadcast_to([128, NT, 4]), ALU.mult)
        k.act(bt[:], ab[:, :, 4:8], AF.Sigmoid)
        gflat = gt[:].rearrange("p n c -> p (n c)")
        for c0 in range(0, NT * 4, 512):
            m = min(512, NT * 4 - c0)
            pc = self.bank()
            k.mm(pc[:, 0:m], self.mui[:], gflat[:, c0:c0 + m])
            k.cp(Gt[:].rearrange("p n c -> p (n c)")[:, c0:c0 + m], pc[:, 0:m])
            pc2 = self.bank()
            k.mm(pc2[:, 0:m], onesf[:], gflat[:, c0:c0 + m])
            k.cp(GL[:].rearrange("p n c -> p (n c)")[:, c0:c0 + m], pc2[:, 0:m])
        k.act(eG[:], Gt[:], AF.Exp)
        k.tt(kds[:], GL[:], Gt[:], ALU.subtract)
        k.act(kds[:], kds[:], AF.Exp)
        k.act(gla[:], GL[:], AF.Exp)
        Sf = S.sb('Sf', [64, 4, 64], F32)
        Sb = S.sb('Sb', [64, 4, 64], BF16)
        k.memset(Sf[:], 0.0)
        k.memset(Sb[:], 0.0)
        qfs = Rot(S, 'qf', 2, [128, 6, 512], BF16)
        zts = Rot(S, 'zt', 2, [128, 4, 256], BF16)
        zsb = S.sb('zsb', [128, 4, 256], F32)

        def alloc_set(i):
            B = {}
            def a(name, shape, dt):
                B[name] = S.sb('%s_%d' % (name, i), shape, dt)
            a('qkv', [128, 768], F32); a('sq', [128, 512], F32); a('r8', [128, 16], F32)
            a('qn_f', [128, 4, 64], F32); a('kn_f', [128, 4, 64], F32); a('kb_f', [128, 4, 64], F32)
            for n_ in ('qn_b', 'kn_b', 'kb_b', 'qd_b', 'kbg_b', 'kdec_b', 'vb_b'):
                a(n_, [128, 4, 64], BF16)
            for n_ in ('kn_b', 'kb_b', 'qn_b', 'qd_b'):
                a(n_ + 'T', [64, 4, 128], BF16)
            a('dG', [128, 4, 128], F32)
            for n_ in ('Xn', 'Xm', 'Dt', 'Dl', 'Dlm', 'Dtm1', 'Dtm2'):
                a(n_, [128, 128], F32)
            for n_ in ('P0', 'P1', 'PT0', 'PT1', 'TT0', 'TT1'):
                a(n_, [128, 4, 128], F32)
            a('TTb', [128, 4, 128], BF16); a('Aq', [128, 4, 128], BF16)
            a('u_sb', [128, 4, 64], F32); a('wwT', [64, 4, 128], BF16); a('vnew', [128, 4, 64], BF16)
            a('o_sb', [128, 4, 64], F32); a('osq', [128, 4, 64], F32); a('dmix', [128, 256], BF16)
            return B
        sets = [alloc_set(0), alloc_set(1)]
        dstg = Rot(S, 'dstg', 2, [128, 2, 512], BF16)

        def bc(ap2):
            return ap2.unsqueeze(2).broadcast_to([128, 4, 64])

        def rsq(out, in_, mult):
            k.ts(out, in_, mult, ALU.mult, EPS, ALU.add)
            k.act(out, out, AF.Ln)
            k.act(out, out, AF.Exp, scale=-0.5)

        def dn_tile(B, n, t, qf, dg):
            tok = slice(t * 128, (t + 1) * 128)
            qkv, sq, r8 = B['qkv'], B['sq'], B['r8']
            qn_f, kn_f, kb_f = B['qn_f'], B['kn_f'], B['kb_f']
            pt = self.bank()
            ptv = pt[:].bitcast(BF16)
            for j in range(6):
                k.tr(ptv[:, j * 128:(j + 1) * 128], qf[:, j, tok], self.identb[:])
            k.cp(qkv[:], ptv[:, 0:768], eng='act')
            yield
            k.tt(sq[:], qkv[:, 0:512], qkv[:, 0:512], ALU.mult)
            k.op('dve', lambda: nc.vector.tensor_reduce(r8[:, 0:8], sq[:].rearrange("p (a d) -> p a d", d=64), AX.X, ALU.add),
                 [sq], [r8])
            rsq(r8[:, 8:16], r8[:, 0:8], 1.0)
            k.ts(r8[:, 8:12], r8[:, 8:12], 0.125, ALU.mult)
            yield
            q3 = qkv[:, 0:256].rearrange("p (h d) -> p h d", h=4)
            k3 = qkv[:, 256:512].rearrange("p (h d) -> p h d", h=4)
            v3 = qkv[:, 512:768].rearrange("p (h d) -> p h d", h=4)
            k.tt(qn_f[:], q3, bc(r8[:, 8:12]), ALU.mult)
            k.tt(kn_f[:], k3, bc(r8[:, 12:16]), ALU.mult)
            k.tt(kb_f[:], kn_f[:], bc(bt[:, n, :]), ALU.mult)
            k.cp(B['qn_b'][:], qn_f[:], eng='act')
            k.cp(B['kn_b'][:], kn_f[:], eng='act')
            k.cp(B['kb_b'][:], kb_f[:], eng='act')
            yield
            k.tt(B['qd_b'][:], qn_f[:], bc(eG[:, n, :]), ALU.mult)
            k.tt(B['kbg_b'][:], kb_f[:], bc(eG[:, n, :]), ALU.mult, eng='pool')
            k.tt(B['kdec_b'][:], kn_f[:], bc(kds[:, n, :]), ALU.mult)
            k.tt(B['vb_b'][:], v3, bc(bt[:, n, :]), ALU.mult, eng='pool')
            for gi_, (n1, n2) in enumerate((('kn_b', 'kb_b'), ('qn_b', 'qd_b'))):
                pf = self.bank()
                pfv = pf[:].bitcast(BF16)
                for ii, nm in enumerate((n1, n2)):
                    for h in range(4):
                        k.tr(pfv[0:64, ii * 512 + h * 128:ii * 512 + (h + 1) * 128], B[nm][:, h, :], self.identb[:])
                k.cp(B[n1 + 'T'][:].rearrange("p h t -> p (h t)"), pfv[0:64, 0:512], eng='act')
                k.cp(B[n2 + 'T'][:].rearrange("p h t -> p (h t)"), pfv[0:64, 512:1024])
                yield
            knT, kbT, qnT, qdT = B['kn_bT'], B['kb_bT'], B['qn_bT'], B['qd_bT']
            dG, Xn, Xm, Dt, Dl, Dlm, Dtm1, Dtm2 = (B[x_] for x_ in ('dG', 'Xn', 'Xm', 'Dt', 'Dl', 'Dlm', 'Dtm1', 'Dtm2'))
            for h in range(4):
                k.ts(dG[:, h, :], self.identf[:], Gt[:, n, h:h + 1], ALU.mult, eng='pool')
            Pm = [B['P0'], B['P1']]
            PTm = [B['PT0'], B['PT1']]
            TTm = [B['TT0'], B['TT1']]
            Aq = B['Aq']
            P, PT, TT = Pm[0], PTm[0], TTm[0]
            for h in range(4):
                ps = self.bank()
                k.mm(ps[:, 0:128], kbT[:, h, :], knT[:, h, :])
                k.mm(ps[:, 128:256], knT[:, h, :], kbT[:, h, :])
                k.mm(ps[:, 256:384], knT[:, h, :], qnT[:, h, :])
                k.mm(ps[:, 384:512], onesf[:], dG[:, h, :])
                k.ts(Xn[:], ps[:, 384:512], Gt[:, n, h:h + 1], ALU.subtract, 0.0, ALU.min)
                k.ts(Xm[:], ps[:, 384:512], Gt[:, n, h:h + 1], ALU.subtract, 0.0, ALU.max)
                k.act(Dt[:], Xn[:], AF.Exp)
                k.act(Dl[:], Xm[:], AF.Exp, scale=-1.0)
                k.tt(Dlm[:], Dl[:], nmsl[:], ALU.mult, eng='pool')
                k.tt(Dtm1[:], Dt[:], nmsu[:], ALU.mult, eng='pool')
                k.tt(Dtm2[:], Dt[:], self.mui[:], ALU.mult, eng='pool')
                k.tt(P[:, h, :], ps[:, 0:128], Dlm[:], ALU.mult)
                k.tt(PT[:, h, :], ps[:, 128:256], Dtm1[:], ALU.mult)
                k.tt(Aq[:, h, :], ps[:, 256:384], Dtm2[:], ALU.mult)
                k.tt(TT[:, h, :], PT[:, h, :], self.identf[:], ALU.add, eng='pool')
                yield
            cur = 0
            for lev in range(6):
                nxt = 1 - cur
                pP = self.bank()
                for h in range(4):
                    k.mm(pP[:, h * 128:(h + 1) * 128], PTm[cur][:, h, :], Pm[cur][:, h, :])
                k.cp(Pm[nxt][:].rearrange("p h t -> p (h t)"), pP[:], eng='act')
                if lev < 5:
                    pQ = self.bank()
                    for h in range(4):
                        k.mm(pQ[:, h * 128:(h + 1) * 128], Pm[cur][:, h, :], PTm[cur][:, h, :])
                    k.cp(PTm[nxt][:].rearrange("p h t -> p (h t)"), pQ[:])
                yield
                pT_ = self.bank()
                for h in range(4):
                    k.mm(pT_[:, h * 128:(h + 1) * 128], Pm[nxt][:, h, :], TTm[cur][:, h, :])
                k.tt(TTm[nxt][:].rearrange("p h t -> p (h t)"), pT_[:], TTm[cur][:].rearrange("p h t -> p (h t)"), ALU.add)
                cur = nxt
                yield
            TT = B['TTb']
            k.cp(TT[:], TTm[cur][:], eng='pool')
            u_sb, wwT, vnew, o_sb, osq, dmix = (B[x_] for x_ in ('u_sb', 'wwT', 'vnew', 'o_sb', 'osq', 'dmix'))
            pu = self.bank()
            for h in range(4):
                k.mm(pu[:, h * 64:(h + 1) * 64], TT[:, h, :], B['vb_b'][:, h, :])
            k.cp(u_sb[:].rearrange("p h e -> p (h e)"), pu[:, 0:256], eng='act')
            pw = self.bank()
            for h in range(4):
                k.mm(pw[0:64, h * 128:(h + 1) * 128], B['kbg_b'][:, h, :], TT[:, h, :])
            k.cp(wwT[:].rearrange("p h t -> p (h t)"), pw[0:64, :])
            yield
            pS = self.bank()
            for h in range(4):
                k.mm(pS[:, h * 64:(h + 1) * 64], wwT[:, h, :], Sb[:, h, :])
            k.tt(vnew[:].rearrange("p h e -> p (h e)"), u_sb[:].rearrange("p h e -> p (h e)"), pS[:, 0:256], ALU.subtract)
            po = self.bank()
            for h in range(4):
                k.mm(po[:, h * 64:(h + 1) * 64], qdT[:, h, :], Sb[:, h, :], start=True, stop=False)
                k.mm(po[:, h * 64:(h + 1) * 64], Aq[:, h, :], vnew[:, h, :], start=False, stop=True)
            pK = self.bank()
            for h in range(4):
                k.mm(pK[0:64, h * 64:(h + 1) * 64], B['kdec_b'][:, h, :], vnew[:, h, :])
            k.tt(Sf[:], Sf[:], gla[0:64, n, :].unsqueeze(2).broadcast_to([64, 4, 64]), ALU.mult)
            k.tt(Sf[:].rearrange("p h e -> p (h e)"), Sf[:].rearrange("p h e -> p (h e)"), pK[0:64, 0:256], ALU.add)
            k.cp(Sb[:], Sf[:], eng='act')
            k.cp(o_sb[:].rearrange("p h e -> p (h e)"), po[:, 0:256], eng='act')
            yield
            k.tt(osq[:], o_sb[:], o_sb[:], ALU.mult)
            k.op('dve', lambda: nc.vector.tensor_reduce(r8[:, 0:4], osq[:], AX.X, ALU.add), [osq], [r8])
            rsq(r8[:, 4:8], r8[:, 0:4], 1.0 / 64)
            k.tt(o_sb[:], o_sb[:], bc(r8[:, 4:8]), ALU.mult)
            k.tt(o_sb[:], o_sb[:], gn[:].unsqueeze(1).broadcast_to([128, 4, 64]), ALU.mult, eng='pool')
            k.tt(dmix[:], o_sb[:].rearrange("p h e -> p (h e)"), zsb[:, t, :], ALU.mult)
            yield
            px = self.bank()
            pxv = px[:].bitcast(BF16)
            for j in range(2):
                k.tr(pxv[:, j * 128:(j + 1) * 128], dmix[:, j * 128:(j + 1) * 128], self.identb[:])
            k.cp(dg[:, :, tok], pxv[:, 0:256].rearrange("p (j t) -> p j t", j=2), eng='act')

        for b in range(NB):
            t0 = b * 512
            qf, qfsem = qfs.next()
            k.dma('sp', qf[:], self.qkvc_scr.ap()[:, t0:t0 + 512].rearrange("(j p) t -> p j t", p=128), qfsem)
            zt, ztsem = zts.next()
            k.dma('sp', zt[:], self.z_scr.ap()[t0:t0 + 512, :].rearrange("(t p) c -> p t c", p=128), ztsem)
            k.act(zsb[:], zt[:], AF.Silu)
            dg, dgs = dstg.next()
            for tp in range(2):
                gens = [dn_tile(sets[i], b * 4 + tp * 2 + i, tp * 2 + i, qf, dg) for i in range(2)]
                alive = [True, True]
                while any(alive):
                    for i in range(2):
                        if alive[i]:
                            try:
                                next(gens[i])
                            except StopIteration:
                                alive[i] = False
            k.dma('sp', self.mix_scr.ap()[2, :, t0:t0 + 512].rearrange("(j p) t -> p j t", p=128), dg[:], dgs)
        S.close()

    def phase_s5(self, l):
        k = self.k
        nc = k.nc
        L = self.L
        NB = L // 512
        nch = L // 8
        S = Scope(k)
        w = self.w
        nlev = 0
        while (1 << nlev) < nch:
            nlev += 1
        W1 = S.sb('W1pad', [128, 2, 4, 8, 2, 128], BF16)
        Kbd = S.sb('Kbd', [128, 2, 8, 128], BF16)
        W2 = S.sb('W2bd', [128, 2, 4, 8, 2, 128], BF16)
        A8 = [S.sb('A8_%d' % i, [128, 3, 8], F32) for i in range(nlev)]
        wglu = S.sb('wglu', [128, 2, 256], BF16)
        k.dma('pool', wglu[:], w['s5_w_glu'].ap()[l].rearrange("(j p) n -> p j n", p=128), 'wglu')
        dcol = S.sb('dcol', [128, 2], F32)
        bglu = S.sb('bglu', [128, 2], F32)
        SP = Scope(k)
        dcol_t = self.load_cols(SP, w['s5_d'].ap()[l].rearrange("(j p) -> j p", p=128), 2, 'dcolt')
        bglu_t = self.load_cols(SP, w['s5_b_glu'].ap()[l].rearrange("(j p) -> j p", p=128), 2, 'bglut')
        k.cp(dcol[:], dcol_t[:])
        k.cp(bglu[:], bglu_t[:])
        ctr = [0]

        def T(shape=(128, 128), dt=F32):
            ctr[0] += 1
            return SP.sb('s5t%d' % ctr[0], list(shape), dt)

        def cl(name, shape):
            t_ = SP.sb(name, list(shape), F32)
            k.dma('sp', t_[:], self.c[name].ap(), name)
            return t_
        sel8 = cl('c_sel8', [8, 128])
        bm16 = cl('c_bm16', [128, 128])
        w1mask = cl('c_w1mask', [128, 8])
        w2mask = cl('c_w2mask', [128, 4, 128])
        lam8 = SP.sb('lam8', [8, 2, 2, 64], F32)
        k.dma('sp', lam8[:, 0, :, :], w['s5_lam_re'].ap()[l].rearrange("(j g) p -> g j p", g=8), 'lam8')
        k.dma('sp', lam8[:, 1, :, :], w['s5_lam_im'].ap()[l].rearrange("(j g) p -> g j p", g=8), 'lam8')
        ls8 = SP.sb('ls8', [8, 2], F32)
        for j in range(2):
            k.dma('sp', ls8[:, j:j + 1], w['s5_log_step'].ap()[l, j * 8:(j + 1) * 8].unsqueeze(1), 'ls8')
        lr, li, st = T(), T(), T((128, 2))
        pb = self.bank()
        k.mm(pb[:, 0:256], sel8[:], lam8[:].rearrange("g r j p -> g (r j p)"))
        k.mm(pb[:, 256:258], sel8[:], ls8[:])
        k.cp(lr[:], pb[:, 0:128])
        k.cp(li[:], pb[:, 128:256])
        k.act(st[:], pb[:, 256:258], AF.Exp)
        stb = st[:].unsqueeze(2).broadcast_to([128, 2, 64])

        def v3(t_):
            return t_[:].rearrange("q (j p) -> q j p", j=2)
        th, lrs = T(), T()
        k.tt(v3(th), v3(li), stb, ALU.mult)
        k.tt(v3(lrs), v3(lr), stb, ALU.mult)
        er, s32, cs, sn, t1, t2 = T(), T(), T(), T(), T(), T()
        k.act(er[:], lrs[:], AF.Exp)
        k.act(s32[:], th[:], AF.Sin, scale=1.0 / 32)
        k.act(sn[:], th[:], AF.Sin, scale=1.0 / 16)
        k.tt(t1[:], s32[:], s32[:], ALU.mult)
        k.ts(cs[:], t1[:], -2.0, ALU.mult, 1.0, ALU.add)
        for _ in range(4):
            k.tt(t1[:], cs[:], cs[:], ALU.mult)
            k.tt(t2[:], sn[:], sn[:], ALU.mult)
            k.stt(sn[:], cs[:], 2.0, sn[:], ALU.mult, ALU.mult)
            k.tt(cs[:], t1[:], t2[:], ALU.subtract)
        ar, ai = T(), T()
        k.tt(ar[:], er[:], cs[:], ALU.mult)
        k.tt(ai[:], er[:], sn[:], ALU.mult)
        den, nr, cre, cim = T(), T(), T(), T()
        k.tt(t1[:], lr[:], lr[:], ALU.mult)
        k.tt(t2[:], li[:], li[:], ALU.mult)
        k.tt(den[:], t1[:], t2[:], ALU.add)
        k.recip(den[:], den[:])
        k.ts(nr[:], ar[:], -1.0, ALU.add)
        k.tt(t1[:], nr[:], lr[:], ALU.mult)
        k.tt(t2[:], ai[:], li[:], ALU.mult)
        k.tt(cre[:], t1[:], t2[:], ALU.add)
        k.tt(cre[:], cre[:], den[:], ALU.mult)
        k.tt(t1[:], ai[:], lr[:], ALU.mult)
        k.tt(t2[:], nr[:], li[:], ALU.mult)
        k.tt(cim[:], t1[:], t2[:], ALU.subtract)
        k.tt(cim[:], cim[:], den[:], ALU.mult)

        def cmul(o_r, o_i, a_r, a_i, b_r, b_i, neg_im=False):
            k.tt(t1[:], a_r, b_r, ALU.mult)
            k.tt(t2[:], a_i, b_i, ALU.mult)
            k.tt(o_r, t1[:], t2[:], ALU.subtract)
            k.tt(t1[:], a_r, b_i, ALU.mult)
            k.tt(t2[:], a_i, b_r, ALU.mult)
            if neg_im:
                k.stt(o_i, t1[:], -1.0, t2[:], ALU.mult, ALU.subtract)
            else:
                k.tt(o_i, t1[:], t2[:], ALU.add)
        Bre, Bim = T(), T()
        for (nm, dst) in (('s5_b_re', Bre), ('s5_b_im', Bim)):
            bn = SP.sb(nm + 'n', [64, 16, 16], F32)
            k.dma('sp', bn[:], w[nm].ap()[l].rearrange("g p h -> p g h"), nm + 'n')
            for j in range(2):
                pb = self.bank()
                k.tr(pb[:, 0:64], bn[:, j * 8:(j + 1) * 8, :].rearrange("p g h -> p (g h)"), self.identf[0:64, 0:64])
                k.cp(dst[:, j * 64:(j + 1) * 64], pb[:, 0:64])
        Cre, Cim = T(), T()
        k.dma('sp', v3(Cre), w['s5_c_re'].ap()[l].rearrange("(j g) h p -> (g h) j p", g=8), 'Cre')
        k.dma('sp', v3(Cim), w['s5_c_im'].ap()[l].rearrange("(j g) h p -> (g h) j p", g=8), 'Cim')
        Bbr, Bbi = T(), T()
        cmul(Bbr[:], Bbi[:], cre[:], cim[:], Bre[:], Bim[:])
        Pr = [T() for _ in range(9)]
        Pi = [T() for _ in range(9)]
        k.memset(Pr[0][:], 1.0)
        k.memset(Pi[0][:], 0.0)
        k.cp(Pr[1][:], ar[:])
        k.cp(Pi[1][:], ai[:])
        for kk in range(2, 9):
            cmul(Pr[kk][:], Pi[kk][:], Pr[kk - 1][:], Pi[kk - 1][:], ar[:], ai[:])
        AB = [SP.sb('AB%d' % tau, [128, 2, 2, 64], F32) for tau in range(8)]
        abr, abi = T(), T()
        for tau in range(8):
            cmul(abr[:], abi[:], Pr[tau][:], Pi[tau][:], Bbr[:], Bbi[:])
            k.cp(AB[tau][:, :, 0, :], v3(abr), eng='act')
            k.cp(AB[tau][:, :, 1, :], v3(abi), eng='act')
        w1m = w1mask[:].rearrange("q (m h) -> q m h", h=2).unsqueeze(3).broadcast_to([128, 4, 2, 64])
        for s_ in range(8):
            for j in range(2):
                for ri in range(2):
                    src = AB[7 - s_][:, j, ri, :].unsqueeze(1).unsqueeze(1).broadcast_to([128, 4, 2, 64])
                    k.tt(W1[:, j, :, s_, ri, :].rearrange("q m (h p) -> q m h p", h=2), src, w1m, ALU.mult,
                         eng=('pool' if ri else 'dve'))
        CT0 = SP.sb('CT0', [128, 2, 2, 64], F32)
        k.cp(CT0[:, :, 0, :], v3(Cre))
        k.ts(CT0[:, :, 1, :], v3(Cim), -1.0, ALU.mult)
        CT0T = SP.sb('CT0T', [128, 2, 128], F32)
        for j in range(2):
            pb = self.bank()
            k.tr(pb[:, 0:128], CT0[:, j, :, :].rearrange("q r p -> q (r p)"), self.identf[:])
            k.cp(CT0T[:, j, :], pb[:, 0:128])
        abT = Rot(SP, 'abT', 2, [128, 128], F32)
        for tau in range(8):
            for j in range(2):
                pb = self.bank()
                k.tr(pb[:, 0:128], AB[tau][:, j, :, :].rearrange("q r p -> q (r p)"), self.identf[:])
                at_, _ = abT.next()
                k.cp(at_[:], pb[:, 0:128], eng='act')
                k.mm(pb[:, 128:256], at_[:], CT0T[:, j, :])
                k.tt(Kbd[:, j, tau, :], pb[:, 128:256], bm16[:], ALU.mult)
        CA = SP.sb('CA', [128, 2, 2, 2, 64], F32)
        car, cai = T(), T()
        for s_ in range(8):
            cmul(car[:], cai[:], Cre[:], Cim[:], Pr[s_ + 1][:], Pi[s_ + 1][:], neg_im=True)
            for dup in range(2):
                k.cp(CA[:, :, 0, dup, :], v3(car), eng='act')
                k.cp(CA[:, :, 1, dup, :], v3(cai), eng='pool')
            for j in range(2):
                for ri in range(2):
                    pb = self.bank()
                    k.tr(pb[:, 0:128], CA[:, j, ri, :, :].rearrange("q d p -> q (d p)"), self.identf[:])
                    k.tt(W2[:, j, :, s_, ri, :], pb[:, 0:128].unsqueeze(1).broadcast_to([128, 4, 128]), w2mask[:], ALU.mult)
        a8d = SP.sb('a8d', [128, 2, 2, 2, 64], F32)
        for dup in range(2):
            k.cp(a8d[:, :, 0, dup, :], v3(Pr[8]))
            k.cp(a8d[:, :, 1, dup, :], v3(Pi[8]))
        for j in range(2):
            for ri in range(2):
                pb = self.bank()
                k.tr(pb[:, 0:128], a8d[:, j, ri, :, :].rearrange("q d p -> q (d p)"), self.identf[:])
                k.cp(A8[0][0:64, ri, 4 * j:4 * j + 4], pb[0:64, 0:128:32])
                k.cp(A8[0][64:128, ri, 4 * j:4 * j + 4], pb[64:128, 16:128:32])
        s1, s2 = T((128, 8)), T((128, 8))
        for i in range(nlev):
            if i > 0:
                k.tt(s1[:], A8[i - 1][:, 0, :], A8[i - 1][:, 0, :], ALU.mult)
                k.tt(s2[:], A8[i - 1][:, 1, :], A8[i - 1][:, 1, :], ALU.mult)
                k.tt(A8[i][:, 0, :], s1[:], s2[:], ALU.subtract)
                k.stt(A8[i][:, 1, :], A8[i - 1][:, 0, :], 2.0, A8[i - 1][:, 1, :], ALU.mult, ALU.mult)
            k.ts(A8[i][:, 2, :], A8[i][:, 1, :], -1.0, ALU.mult)
        SP.close()
        uT = S.sb('uT', [128, 2, L], BF16)
        for j in range(2):
            k.dma('sp', uT[:, j, :], self.uT_scr.ap()[j * 128:(j + 1) * 128, :], 'uT')
        Xbf = S.sb('Xbf', [128, 2, 8, nch + 1], BF16)
        k.memset(Xbf[:, :, :, 0:1], 0.0)
        XA = [S.sb('XA%d' % i, [128, nch], F32) for i in range(2)]
        XB = [S.sb('XB%d' % i, [128, nch], F32) for i in range(2)]
        XT = S.sb('XT', [128, nch], F32)
        for m in range(8):
            j, mp = m // 4, m % 4
            for ri in range(2):
                for c0 in range(0, nch, 512):
                    n = min(512, nch - c0)
                    ps = self.bank()
                    for s_ in range(8):
                        k.mm(ps[:, 0:n], W1[:, j, mp, s_, ri, :], uT[:, j, 8 * c0 + s_:8 * (c0 + n):8],
                             start=(s_ == 0), stop=(s_ == 7))
                    k.cp(XA[ri][:, c0:c0 + n], ps[:, 0:n], eng=('act' if ri else 'dve'))
            cur, oth = XA, XB
            for lev in range(nlev):
                d = 1 << lev
                Ar = A8[lev][:, 0, m:m + 1]
                Ai = A8[lev][:, 1, m:m + 1]
                nAi = A8[lev][:, 2, m:m + 1]
                k.stt(XT[:, d:], cur[0][:, 0:nch - d], Ar, cur[0][:, d:], ALU.mult, ALU.add)
                k.stt(oth[0][:, d:], cur[1][:, 0:nch - d], nAi, XT[:, d:], ALU.mult, ALU.add)
                k.stt(XT[:, d:], cur[1][:, 0:nch - d], Ar, cur[1][:, d:], ALU.mult, ALU.add)
                k.stt(oth[1][:, d:], cur[0][:, 0:nch - d], Ai, XT[:, d:], ALU.mult, ALU.add)
                k.cp(oth[0][:, 0:d], cur[0][:, 0:d], eng='act')
                k.cp(oth[1][:, 0:d], cur[1][:, 0:d], eng='pool')
                cur, oth = oth, cur
            k.cp(Xbf[:, 0, m, 1:nch + 1], cur[0][:], eng='act')
            k.cp(Xbf[:, 1, m, 1:nch + 1], cur[1][:], eng='pool')
        yv = S.sb('yv', [128, 2, 512], F32)
        zf = S.sb('zf', [128, 2, 512], F32)
        zb = S.sb('zb', [128, 2, 512], BF16)
        sgm = S.sb('sgm', [128, 512], F32)
        sstg = Rot(S, 's5stg', 2, [128, 2, 512], BF16)
        for b in range(NB):
            t0 = b * 512
            cb0 = b * 64
            for j in range(2):
                ps = self.bank()
                k.memset(ps[:], 0.0)
                for s_ in range(8):
                    out = ps[:, s_:512:8]
                    for mp in range(4):
                        for ri in range(2):
                            k.mm(out, W2[:, j, mp, s_, ri, :], Xbf[:, ri, 4 * j + mp, cb0:cb0 + 64], start=False, stop=False)
                    for tau in range(s_ + 1):
                        k.mm(out, Kbd[:, j, tau, :], uT[:, j, t0 + s_ - tau:t0 + 512:8], start=False, stop=(tau == s_))
                k.stt(yv[:, j, :], uT[:, j, t0:t0 + 512], dcol[:, j:j + 1], ps[:], ALU.mult, ALU.add)
                k.act(zf[:, j, :], yv[:, j, :], AF.Gelu_apprx_tanh)
                k.cp(zb[:, j, :], zf[:, j, :], eng='pool')
            sg_, sgs = sstg.next()
            for jo in range(2):
                pg = self.bank()
                for ji in range(2):
                    k.mm(pg[:], wglu[:, ji, jo * 128:(jo + 1) * 128], zb[:, ji, :], start=(ji == 0), stop=(ji == 1))
                k.act(sgm[:], pg[:], AF.Sigmoid, bias=bglu[:, jo:jo + 1])
                k.tt(sg_[:, jo, :], zf[:, jo, :], sgm[:], ALU.mult)
            k.dma('sp', self.mix_scr.ap()[0, :, t0:t0 + 512].rearrange("(j p) t -> p j t", p=128), sg_[:], sgs)
        S.close()

    def zero_branch(self, br):
        k = self.k
        S = Scope(k)
        zt = S.sb('zt', [128, 2, 512], BF16)
        k.memset(zt[:], 0.0)
        for b in range(self.L // 512):
            k.dma('sp', self.mix_scr.ap()[br, :, b * 512:(b + 1) * 512].rearrange("(j p) t -> p j t", p=128), zt[:], 'zt')
        S.close()

    def phase_C(self, l):
        k = self.k
        L = self.L
        NB = L // 512
        S = Scope(k)
        w = self.w
        xsrc = self.x_in if l == 0 else self.y
        Wg = S.sb('Wg', [128, 8, 4096], BF16)
        for kk in range(8):
            k.dma('pool', Wg[:, kk, :], w['w_in'].ap()[l, kk * 128:(kk + 1) * 128, NMIX:NMIX + 4096], 'Wg')
        wbr = S.sb('wbr', [128, 4, 2, D], BF16)
        for i, nm in enumerate(('w_br_s5', 'w_br_sgu', 'w_br_dn', 'w_br_diff')):
            k.dma('pool', wbr[:, i, :, :], w[nm].ap()[l].rearrange("(j p) n -> p j n", p=128), 'wbr')
        wout = S.sb('wout', [128, 8, D], BF16)
        k.dma('pool', wout[:], w['w_out'].ap()[l].rearrange("(j p) n -> p j n", p=128), 'wout')
        xts = Rot(S, 'xt', 2, [128, 4, D], F32)
        hTs = Rot(S, 'hT', 2, [128, 8, 512], BF16)
        mixs = Rot(S, 'mx', 2, [128, 4, 2, 512], BF16)
        sig = Rot(S, 'sig', 2, [128, 512], F32)
        mg = S.sb('mg', [128, 512], F32)
        tmpm = S.sb('tmpm', [128, 512], F32)
        mgT = S.sb('mgT', [128, 8, 512], BF16)
        for b in range(NB):
            t0 = b * 512
            xt, xsem = xts.next()
            k.dma('sp', xt[:], xsrc.ap()[t0:t0 + 512, :].rearrange("(t p) d -> p t d", p=128), xsem)
            hT, hsem = hTs.next()
            k.dma('sp', hT[:], self.hT_scr.ap()[:, t0:t0 + 512].rearrange("(k p) t -> p k t", p=128), hsem)
            mx, msem = mixs.next()
            for br in range(4):
                k.dma('sp', mx[:, br, :, :], self.mix_scr.ap()[br, :, t0:t0 + 512].rearrange("(j p) t -> p j t", p=128), msem)
            for mo in range(8):
                for br in range(4):
                    pg = self.bank()
                    c0 = br * D + mo * 128
                    for kk in range(8):
                        k.mm(pg[:], Wg[:, kk, c0:c0 + 128], hT[:, kk, :], start=(kk == 0), stop=(kk == 7))
                    sg, _ = sig.next()
                    k.act(sg[:], pg[:], AF.Sigmoid)
                    py = self.bank()
                    for j in range(2):
                        k.mm(py[:], wbr[:, br, j, mo * 128:(mo + 1) * 128], mx[:, br, j, :], start=(j == 0), stop=(j == 1))
                    if br == 0:
                        k.tt(mg[:], py[:], sg[:], ALU.mult)
                    elif br < 3:
                        k.tt(tmpm[:], py[:], sg[:], ALU.mult)
                        k.tt(mg[:], mg[:], tmpm[:], ALU.add, eng='pool')
                    else:
                        k.tt(tmpm[:], py[:], sg[:], ALU.mult)
                        k.tt(mgT[:, mo, :], mg[:], tmpm[:], ALU.add, eng='pool')
            xn, xns = xt, xsem
            for t in range(4):
                for hf in range(2):
                    po = self.bank()
                    for kk in range(8):
                        k.mm(po[:], mgT[:, kk, t * 128:(t + 1) * 128], wout[:, kk, hf * 512:(hf + 1) * 512],
                             start=(kk == 0), stop=(kk == 7))
                    k.tt(xn[:, t, hf * 512:(hf + 1) * 512], po[:], xt[:, t, hf * 512:(hf + 1) * 512], ALU.add)
            k.dma('sp', self.y.ap()[t0:t0 + 512, :].rearrange("(t p) d -> p t d", p=128), xn[:], xns)
        S.close()

    def phase_D(self, l, p, last):
        k = self.k
        if 'bi0' in self.flags:
            self.bi = 0
        L = self.L
        NB = L // 512
        S = Scope(k)
        w = self.w
        HC = DFF // 2
        Wup = S.sb('Wup', [128, 8, 2, HC], BF16)
        for kk in range(8):
            for which in range(2):
                c0 = which * DFF + p * HC
                k.dma('pool', Wup[:, kk, which, :], w['ffn_w_up'].ap()[l, kk * 128:(kk + 1) * 128, c0:c0 + HC], 'Wup')
        Wdn = S.sb('Wdn', [128, 11, D], BF16)
        k.dma('pool', Wdn[:], w['ffn_w_down'].ap()[l, p * HC:(p + 1) * HC, :].rearrange("(j p) n -> p j n", p=128), 'Wdn')
        g2col = self.load_cols(S, w['norm_ffn_g'].ap()[l].rearrange("(k p) -> k p", p=128), 8, 'g2col')
        cw = self.load_cols(S, w['ffn_conv_w'].ap()[l].rearrange("k (j p) -> (k j) p", p=128), 132, 'cw')
        cb = self.load_cols(S, w['ffn_conv_b'].ap()[l].rearrange("(j p) -> j p", p=128), 44, 'cb')
        fin = last and p == 1
        if fin:
            gfin = self.load_row_bcast(S, w['norm_final_g'].ap(), D, 'gfin')
        xts = Rot(S, 'xt', 2, [128, 4, D], F32)
        hTs = Rot(S, 'hT', 2, [128, 8, 512], BF16)
        hs = S.sb('hs', [128, 4, D], BF16)
        junk = S.sb('junk', [128, D], F32)
        sss = Rot(S, 'ss', 2, [128, 16], F32)
        ups = Rot(S, 'ups', 3, [128, 514], F32)
        halo = S.sb('halo', [128, 22, 2], F32)
        k.memset(halo[:], 0.0)
        cv = Rot(S, 'cv', 3, [128, 512], F32)
        cg = Rot(S, 'cg', 3, [128, 512], F32)
        sgl = Rot(S, 'sgl', 2, [128, 512], F32)
        actT = S.sb('actT', [128, 11, 512], BF16)
        for b in range(NB):
            t0 = b * 512
            xt, xsem = xts.next()
            k.dma('sp', xt[:], self.y.ap()[t0:t0 + 512, :].rearrange("(t p) d -> p t d", p=128), xsem)
            hT, hsem = hTs.next()
            ss, _ = sss.next()
            hview = self.hT_scr.ap()[:, t0:t0 + 512].rearrange("(k p) t -> p k t", p=128)
            if p == 0:
                self.rms_block(xt, g2col, hT, junk, ss, hs)
                k.dma('sp', hview, hT[:], hsem)
            else:
                k.dma('sp', hT[:], hview, hsem)
            pend = []

            def tail(i_, res_):
                sl, _ = sgl.next()
                k.act(sl[:], res_[1][:], AF.Silu)
                k.tt(actT[:, i_, :], sl[:], res_[0][:], ALU.mult, eng='pool')
            for i in range(11):
                res = []
                for which in range(2):
                    j = which * 22 + p * 11 + i
                    hj = which * 11 + i
                    pu = self.bank()
                    for kk in range(8):
                        k.mm(pu[:], Wup[:, kk, which, i * 128:(i + 1) * 128], hT[:, kk, :], start=(kk == 0), stop=(kk == 7))
                    up, _ = ups.next()
                    k.cp(up[:, 2:514], pu[:], eng='act')
                    k.cp(up[:, 0:2], halo[:, hj, :], eng='pool')
                    c, _ = (cv if which == 0 else cg).next()
                    k.act(c[:], pu[:], AF.Identity, scale=cw[:, 2 * 44 + j:2 * 44 + j + 1], bias=cb[:, j:j + 1])
                    k.stt(c[:], up[:, 1:513], cw[:, 44 + j:44 + j + 1], c[:], ALU.mult, ALU.add)
                    k.stt(c[:], up[:, 0:512], cw[:, j:j + 1], c[:], ALU.mult, ALU.add)
                    k.cp(halo[:, hj, :], up[:, 512:514], eng='pool')
                    res.append(c)
                pend.append((i, res))
                if len(pend) > 1:
                    tail(*pend.pop(0))
            while pend:
                tail(*pend.pop(0))
            for t in range(4):
                for hf in range(2):
                    po = self.bank()
                    for i in range(11):
                        k.mm(po[:], actT[:, i, t * 128:(t + 1) * 128], Wdn[:, i, hf * 512:(hf + 1) * 512],
                             start=(i == 0), stop=(i == 10))
                    k.tt(xt[:, t, hf * 512:(hf + 1) * 512], po[:], xt[:, t, hf * 512:(hf + 1) * 512], ALU.add)
            if fin:
                ss2, _ = sss.next()
                for t in range(4):
                    k.sumsq(junk[:], xt[:, t, :], ss2[:, t:t + 1])
                k.ts(ss2[:, 4:8], ss2[:, 0:4], 1.0 / D, ALU.mult, EPS, ALU.add)
                k.act(ss2[:, 8:12], ss2[:, 4:8], AF.Sqrt)
                k.recip(ss2[:, 12:16], ss2[:, 8:12])
                for t in range(4):
                    k.stt(xt[:, t, :], xt[:, t, :], ss2[:, 12 + t:13 + t], gfin[:], ALU.mult, ALU.mult)
            k.dma('sp', self.y.ap()[t0:t0 + 512, :].rearrange("(t p) d -> p t d", p=128), xt[:], xsem)
        S.close()

    def build(self):
        fl = self.flags
        if 'onlyA' in fl:
            self.phase_A(0)
            self.k.barrier()
            return self.k.nc
        for l in range(self.depth):
            self.phase_A(l)
            for br, nm in ((0, 's5'), (2, 'dn'), (3, 'att')):
                if nm in fl:
                    getattr(self, 'phase_' + nm)(l)
                else:
                    self.zero_branch(br)
            if 'noCD' in fl:
                continue
            self.phase_C(l)
            if 'stopC' in fl:
                break
            self.phase_D(l, 0, last=(l == self.depth - 1))
            if 'stopD0' in fl:
                break
            self.phase_D(l, 1, last=(l == self.depth - 1))
        self.k.barrier()
        return self.k.nc


_CACHE = {}


def run_cores(xs, weights, L, depth, flags):
    key = (L, depth, tuple(sorted(flags)))
    if key not in _CACHE:
        _CACHE[key] = Prog(L, depth, flags).build()
    nc = _CACHE[key]
    consts = host_consts()
    in_maps = []
    for x in xs:
        m = {"x": np.ascontiguousarray(x, dtype=np.float32)}
        for name in WEIGHT_SHAPES:
            m[name] = np.ascontiguousarray(weights[name], dtype=np.float32)
        m.update(consts)
        in_maps.append(m)
    res = run_bass_kernel_spmd(nc, in_maps, core_ids=list(range(len(xs))))
    if "dbg" in flags:
        return [(r["y"], r["mix_scr"], r["hT_scr"]) for r in res.results]
    return [r["y"] for r in res.results]


def kernel(**inputs):
    x = np.asarray(inputs['x'])
    B, L, _ = x.shape
    weights = {n: np.asarray(inputs[n]) for n in WEIGHT_SHAPES}
    outs = run_cores([x[b] for b in range(B)], weights, L, DEPTH, ('s5', 'dn', 'att'))
    return np.stack(outs, axis=0).astype(np.float32)
```

```python
import math
from contextlib import ExitStack

import numpy as np
import ml_dtypes
import concourse.bass as bass
import concourse.mybir as mybir
from concourse.bass_utils import run_bass_kernel_spmd

F32 = mybir.dt.float32
BF16 = mybir.dt.bfloat16
AF = mybir.ActivationFunctionType
ALU = mybir.AluOpType
AX = mybir.AxisListType
SAME_ENGINE_SYNC = True

D = 1024
DEPTH = 4
NMIX = 2568
DFF = 2816
EPS = 1e-6


class KB:
    ENG = ('pe', 'act', 'dve', 'pool', 'sp')

    def __init__(self):
        self.nc = bass.Bass("TRN2", target_bir_lowering=False)
        nc = self.nc
        self.e = {'pe': nc.tensor, 'act': nc.scalar, 'dve': nc.vector, 'pool': nc.gpsimd, 'sp': nc.sync}
        self.semh = {}
        self.semv = {}
        for k in self.ENG:
            self.semh['E:' + k] = nc.alloc_semaphore(name="s_" + k)
            self.semv['E:' + k] = 0
        self.waited = {k: {} for k in self.ENG}
        self.lastw = {}
        self.reads = {}
        self.nins = {k: 0 for k in self.ENG}
        self.nwait = 0
        self._uid = 0

    def uid(self):
        self._uid += 1
        return self._uid

    @staticmethod
    def R(x):
        if isinstance(x, (str, tuple)):
            return x
        return x.name

    def _wait(self, eng, key, val):
        w = self.waited[eng]
        if w.get(key, 0) >= val:
            return
        self.e[eng].wait_ge(self.semh[key], val)
        self.nwait += 1
        w[key] = val

    def _deps(self, eng, reads, writes):
        best = {}
        for r in reads:
            ev = self.lastw.get(r)
            if ev is not None:
                best[ev[0]] = max(best.get(ev[0], 0), ev[1])
        for w_ in writes:
            ev = self.lastw.get(w_)
            if ev is not None:
                best[ev[0]] = max(best.get(ev[0], 0), ev[1])
            for k, v in self.reads.get(w_, {}).items():
                best[k] = max(best.get(k, 0), v)
        for k, v in best.items():
            if k == 'E:' + eng and (eng == 'pe' or not SAME_ENGINE_SYNC):
                continue
            self._wait(eng, k, v)

    def _commit(self, ev, reads, writes):
        for w_ in writes:
            self.lastw[w_] = ev
            self.reads[w_] = {}
        for r in reads:
            d = self.reads.setdefault(r, {})
            d[ev[0]] = max(d.get(ev[0], 0), ev[1])

    def op(self, eng, fn, reads, writes):
        pw = [r for r in reads if r is not None and not isinstance(r, (str, tuple)) and self._is_ps(r)]
        reads = [self.R(r) for r in reads if r is not None]
        writes = [self.R(w) for w in writes if w is not None] + [self.R(r) for r in pw]
        self._deps(eng, reads, writes)
        ins = fn()
        key = 'E:' + eng
        self.semv[key] += 1
        ins.then_inc(self.semh[key], 1)
        self.nins[eng] += 1
        self._commit((key, self.semv[key]), reads, writes)
        return ins

    def dma(self, q, out, in_, sem, **kw):
        reads = [self.R(in_)] if self._is_sb(in_) else []
        writes = [self.R(out)] if self._is_sb(out) else []
        self._deps(q, reads, writes)
        key = 'D:' + sem
        if key not in self.semh:
            self.semh[key] = self.nc.alloc_semaphore(name="d%d" % len(self.semh))
            self.semv[key] = 0
        ins = self.e[q].dma_start(out=out, in_=in_, **kw)
        self.semv[key] += 16
        ins.then_inc(self.semh[key], 16)
        self.nins[q] += 1
        self._commit((key, self.semv[key]), reads, writes)
        return ins

    @staticmethod
    def _is_ps(ap):
        t = getattr(ap, 'tensor', ap)
        return type(t).__name__.startswith('PS')

    @staticmethod
    def _is_sb(ap):
        return type(ap.tensor).__name__.startswith('SB')

    def barrier(self, engs=None):
        for eng in (engs or self.ENG):
            for key, val in self.semv.items():
                if val > 0:
                    self._wait(eng, key, val)

    def mm(self, out, lhsT, rhs, start=True, stop=True, **kw):
        nc = self.nc
        return self.op('pe', lambda: nc.tensor.matmul(out, lhsT, rhs, start=start, stop=stop,
                                                      skip_group_check=True, **kw), [lhsT, rhs], [out])

    def tr(self, out, in_, ident):
        nc = self.nc
        return self.op('pe', lambda: nc.tensor.transpose(out, in_, ident), [in_, ident], [out])

    def act(self, out, in_, func, bias=None, scale=None):
        nc = self.nc
        kw = {}
        rr = [in_]
        if bias is not None:
            kw['bias'] = bias
            if not isinstance(bias, (int, float)):
                rr.append(bias)
        if scale is not None:
            kw['scale'] = scale
            if not isinstance(scale, (int, float)):
                rr.append(scale)
        return self.op('act', lambda: nc.scalar.activation(out, in_, func, **kw), rr, [out])

    def tt(self, out, in0, in1, op, eng='dve'):
        e = self.e[eng]
        return self.op(eng, lambda: e.tensor_tensor(out, in0, in1, op), [in0, in1], [out])

    def ts(self, out, in0, s1, op0, s2=None, op1=None, eng='dve'):
        e = self.e[eng]
        rr = [in0] + [s for s in (s1, s2) if s is not None and not isinstance(s, (int, float))]
        if op1 is None:
            return self.op(eng, lambda: e.tensor_scalar(out, in0, s1, None, op0), rr, [out])
        return self.op(eng, lambda: e.tensor_scalar(out, in0, s1, s2, op0, op1), rr, [out])

    def stt(self, out, in0, scalar, in1, op0, op1):
        nc = self.nc
        rr = [in0, in1] + ([scalar] if not isinstance(scalar, (int, float)) else [])
        return self.op('dve', lambda: nc.vector.scalar_tensor_tensor(out, in0, scalar, in1, op0, op1), rr, [out])

    def sumsq(self, junk, x, accum):
        nc = self.nc
        self.act(junk, x, AF.Square)
        return self.op('dve', lambda: nc.vector.tensor_reduce(accum, junk, AX.X, ALU.add), [junk], [accum])

    def cp(self, out, in_, eng='dve'):
        if eng == 'act':
            nc = self.nc
            return self.op('act', lambda: nc.scalar.copy(out, in_), [in_], [out])
        e = self.e[eng]
        return self.op(eng, lambda: e.tensor_copy(out, in_), [in_], [out])

    def memset(self, ap, val, eng='dve'):
        e = self.e[eng]
        return self.op(eng, lambda: e.memset(ap, val), [], [ap])

    def recip(self, out, in_):
        nc = self.nc
        return self.op('dve', lambda: nc.vector.reciprocal(out, in_), [in_], [out])


class Scope:
    def __init__(self, k):
        self.k = k
        self.es = ExitStack()

    def sb(self, name, shape, dt):
        return self.es.enter_context(self.k.nc.sbuf_tensor("%s_%d" % (name, self.k.uid()), list(shape), dt))

    def close(self):
        self.k.barrier()
        self.es.close()


class Rot:
    def __init__(self, S, name, n, shape, dt):
        self.bufs = [(S.sb("%s%d" % (name, i), shape, dt), "%s%d" % (name, i)) for i in range(n)]
        self.i = 0

    def next(self):
        b = self.bufs[self.i % len(self.bufs)]
        self.i += 1
        return b


WEIGHT_SHAPES = {
    'norm_mix_g': (4, 1024), 'w_in': (4, 1024, 6664), 's5_lam_re': (4, 16, 64), 's5_lam_im': (4, 16, 64),
    's5_log_step': (4, 16), 's5_b_re': (4, 16, 64, 16), 's5_b_im': (4, 16, 64, 16), 's5_c_re': (4, 16, 16, 64),
    's5_c_im': (4, 16, 16, 64), 's5_d': (4, 256), 's5_w_glu': (4, 256, 256), 's5_b_glu': (4, 256),
    'sgu_norm_g': (4, 256), 'sgu_w_s': (4, 4, 128, 128), 'sgu_b_s': (4, 4, 128), 'dn_conv_w': (4, 4, 768),
    'dn_a_log': (4, 4), 'dn_dt_bias': (4, 4), 'dn_norm_g': (4, 64), 'diff_lq1': (4, 32), 'diff_lk1': (4, 32),
    'diff_lq2': (4, 32), 'diff_lk2': (4, 32), 'diff_norm_g': (4, 64), 'w_br_s5': (4, 256, 1024),
    'w_br_sgu': (4, 256, 1024), 'w_br_dn': (4, 256, 1024), 'w_br_diff': (4, 256, 1024), 'w_out': (4, 1024, 1024),
    'norm_ffn_g': (4, 1024), 'ffn_w_up': (4, 1024, 5632), 'ffn_conv_w': (4, 3, 5632), 'ffn_conv_b': (4, 5632),
    'ffn_w_down': (4, 2816, 1024), 'norm_final_g': (1024,),
}


def host_consts():
    c = {}
    c['c_identb'] = np.eye(128).astype(ml_dtypes.bfloat16)
    c['c_identf'] = np.eye(128, dtype=np.float32)
    i = np.arange(128)
    c['c_mui'] = (i[:, None] <= i[None, :]).astype(np.float32)
    c['c_msu'] = (i[:, None] < i[None, :]).astype(np.float32)
    c['c_msl'] = (i[:, None] > i[None, :]).astype(np.float32)
    slopes = [2.0 ** (-8.0 * (h + 1) / 4) for h in range(4)]
    c['c_ekey'] = np.stack([np.exp(sl * i) for sl in slopes], axis=1).astype(np.float32)
    d = np.arange(67) - 3
    c['c_abias'] = np.tile(np.concatenate([-sl * 128.0 * d for sl in slopes])[None, :], (128, 1)).astype(np.float32)
    q = np.arange(512)
    c['c_cmask'] = np.stack([(q[None, :] >= 128 * j + i[:, None]) for j in range(4)], axis=1).astype(ml_dtypes.bfloat16)
    c['c_m32'] = np.stack([((i // 32) == j) for j in range(4)], axis=1).astype(np.float32)
    c['c_m64'] = np.stack([((i // 64) == j) for j in range(2)], axis=1).astype(np.float32)
    c['c_sel8'] = (np.arange(8)[:, None] == (i[None, :] // 16)).astype(np.float32)
    c['c_bm16'] = ((i[:, None] // 16) == (i[None, :] // 16)).astype(np.float32)
    c['c_w1mask'] = np.stack([((i // 16) == mh) for mh in range(8)], axis=1).astype(np.float32)
    half = i // 64
    c['c_w2mask'] = np.stack([((i[None, :] // 16) == (2 * mp + half[:, None])) for mp in range(4)], axis=1).astype(np.float32)
    return c


class Prog:
    def __init__(self, L, depth, flags):
        self.L = L
        self.depth = depth
        self.flags = flags
        self.lvl = 99
        for f in flags:
            if f.startswith('lvl'):
                self.lvl = int(f[3:])
        self.k = KB()
        k = self.k
        nc = k.nc
        self.x_in = nc.dram_tensor("x", [L, D], F32, kind="ExternalInput")
        self.y = nc.dram_tensor("y", [L, D], F32, kind="ExternalOutput")
        self.w = {}
        for name, shp in WEIGHT_SHAPES.items():
            self.w[name] = nc.dram_tensor(name, list(shp), F32, kind="ExternalInput")
        self.c = {}
        for name, arr in host_consts().items():
            dt = BF16 if arr.dtype == ml_dtypes.bfloat16 else F32
            self.c[name] = nc.dram_tensor(name, list(arr.shape), dt, kind="ExternalInput")
        self.hT_scr = nc.dram_tensor("hT_scr", [D, L], BF16, kind=("ExternalOutput" if "dbg" in flags else "Internal"))
        self.uT_scr = nc.dram_tensor("uT_scr", [256, L], BF16, kind="Internal")
        self.qkvc_scr = nc.dram_tensor("qkvc_scr", [768, L], BF16, kind="Internal")
        self.ab_scr = nc.dram_tensor("ab_scr", [128, (L // 128) * 8], F32, kind="Internal")
        self.z_scr = nc.dram_tensor("z_scr", [L, 256], BF16, kind="Internal")
        self.aq_scr = nc.dram_tensor("aq_scr", [256, L], BF16, kind="Internal")
        self.ak_scr = nc.dram_tensor("ak_scr", [256, L], BF16, kind="Internal")
        self.av_scr = nc.dram_tensor("av_scr", [L, 256], BF16, kind="Internal")
        self.mix_scr = nc.dram_tensor("mix_scr", [4, 256, L], BF16, kind=("ExternalOutput" if "dbg" in flags else "Internal"))
        self.banks = [nc.alloc_psum_tensor("bank%d" % i, [128, 512], F32) for i in range(8)]
        self.bi = 0
        self.bank_pool = list(range(8))
        self.G = Scope(k)
        self.identb = self.G.sb('identb', [128, 128], BF16)
        self.identf = self.G.sb('identf', [128, 128], F32)
        k.dma('sp', self.identb[:], self.c['c_identb'].ap(), 'identb')
        k.dma('sp', self.identf[:], self.c['c_identf'].ap(), 'identf')
        self.mui = self.G.sb('mui', [128, 128], F32)
        k.dma('sp', self.mui[:], self.c['c_mui'].ap(), 'mui')
        self.ones1 = self.G.sb('ones1', [1, 128], F32)
        k.memset(self.ones1[:], 1.0)

    def bank(self):
        pool = self.bank_pool
        b = self.banks[pool[self.bi % len(pool)]]
        self.bi += 1
        return b

    def load_cols(self, S, src2d, n, name):
        k = self.k
        dst = S.sb(name, [128, n], F32)
        done = 0
        while done < n:
            m = min(128, n - done)
            tmp = S.sb(name + 'r', [m, 128], F32)
            k.dma('sp', tmp[:], src2d[done:done + m, :], name + 'r%d' % done)
            pb = self.bank()
            k.tr(pb[:, 0:m], tmp[:], self.identf[0:m, 0:m])
            k.cp(dst[:, done:done + m], pb[:, 0:m])
            done += m
        return dst

    def load_row_bcast(self, S, src1d, n, name):
        k = self.k
        row = S.sb(name + 'r', [1, n], F32)
        k.dma('sp', row[:], src1d.unsqueeze(0), name + 'r')
        dst = S.sb(name, [128, n], F32)
        for c0 in range(0, n, 512):
            m = min(512, n - c0)
            pb = self.bank()
            k.mm(pb[:, 0:m], self.ones1[:, :], row[:, c0:c0 + m])
            k.cp(dst[:, c0:c0 + m], pb[:, 0:m])
        return dst

    def rms_block(self, xt, gcol, hT, junk, ss, hs):
        k = self.k
        for t in range(4):
            k.sumsq(junk[:], xt[:, t, :], ss[:, t:t + 1])
        k.ts(ss[:, 4:8], ss[:, 0:4], 1.0 / D, ALU.mult, EPS, ALU.add)
        k.act(ss[:, 8:12], ss[:, 4:8], AF.Sqrt)
        k.recip(ss[:, 12:16], ss[:, 8:12])
        for t in range(4):
            k.act(hs[:, t, :], xt[:, t, :], AF.Identity, scale=ss[:, 12 + t:13 + t])
        for kk in range(8):
            pb = self.bank()
            pv = pb[:].bitcast(BF16)
            for t in range(4):
                k.tr(pv[:, t * 128:(t + 1) * 128], hs[:, t, kk * 128:(kk + 1) * 128], self.identb[:])
            if kk % 2 == 0:
                k.ts(hT[:, kk, :], pv[:, 0:512], gcol[:, kk:kk + 1], ALU.mult)
            else:
                k.act(hT[:, kk, :], pv[:, 0:512], AF.Identity, scale=gcol[:, kk:kk + 1])

    def phase_A(self, l):
        k = self.k
        L = self.L
        NB = L // 512
        S = Scope(k)
        w = self.w
        xsrc = self.x_in if l == 0 else self.y
        WinA = S.sb('WinA', [128, 8, NMIX], BF16)
        for kk in range(8):
            k.dma('pool', WinA[:, kk, :], w['w_in'].ap()[l, kk * 128:(kk + 1) * 128, 0:NMIX], 'WinA')
        g1col = self.load_cols(S, w['norm_mix_g'].ap()[l].rearrange("(k p) -> k p", p=128), 8, 'g1col')
        wsT = S.sb('wsT', [128, 4, 128], BF16)
        wtmp = S.sb('wtmp', [128, 4, 128], F32)
        k.dma('sp', wtmp[:], w['sgu_w_s'].ap()[l].rearrange("g t s -> t g s"), 'wtmp')
        for g in range(4):
            pb = self.bank()
            k.tr(pb[:, 0:128], wtmp[:, g, :], self.identf[:])
            k.tt(wsT[:, g, :], pb[:, 0:128], self.mui[:], ALU.mult)
        sgng = self.load_row_bcast(S, w['sgu_norm_g'].ap()[l], 256, 'sgng')
        bsT = self.load_cols(S, w['sgu_b_s'].ap()[l], 4, 'bsT')
        cwc = self.load_cols(S, w['dn_conv_w'].ap()[l].rearrange("k (j p) -> (k j) p", p=128), 24, 'cwc')
        diagW = S.sb('diagW', [128, 6, 4, 128], BF16)
        for j in range(6):
            for kk in range(4):
                k.ts(diagW[:, j, kk, :], self.identf[:], cwc[:, kk * 6 + j:kk * 6 + j + 1], ALU.mult)
        xts = Rot(S, 'xt', 2, [128, 4, D], F32)
        hTs = Rot(S, 'hT', 2, [128, 8, 512], BF16)
        hs = S.sb('hs', [128, 4, D], BF16)
        junk = S.sb('junk', [128, D], F32)
        sss = Rot(S, 'ss', 2, [128, 16], F32)
        stg = Rot(S, 'stg', 4, [128, 512], BF16)
        abt = Rot(S, 'abt', 2, [128, 8], F32)
        qraw = S.sb('qraw', [128, 6, 515], BF16)
        k.memset(qraw[:, :, 0:3], 0.0)
        uvs = Rot(S, 'uv', 3, [128, 512], F32)
        vn = S.sb('vn', [128, 256], F32)
        vnbs = Rot(S, 'vnb', 3, [128, 256], BF16)
        st6 = S.sb('st6', [128, 8], F32)
        mxs = S.sb('mxs', [128, 256], F32)
        smix = S.sb('smix', [128, 256], BF16)
        smT = Rot(S, 'smT', 2, [128, 2, 512], BF16)
        ztm = Rot(S, 'ztm', 2, [128, 256], BF16)
        vtm = Rot(S, 'vtm', 2, [128, 256], BF16)

        for b in range(NB):
            t0 = b * 512
            xt, xsem = xts.next()
            k.dma('sp', xt[:], xsrc.ap()[t0:t0 + 512, :].rearrange("(t p) d -> p t d", p=128), xsem)
            hT, hsem = hTs.next()
            ss, _ = sss.next()
            self.rms_block(xt, g1col, hT, junk, ss, hs)
            k.dma('sp', self.hT_scr.ap()[:, t0:t0 + 512].rearrange("(k p) t -> p k t", p=128), hT[:], hsem)

            def fm(c0, M):
                pb = self.bank()
                for kk in range(8):
                    k.mm(pb[0:M, :], WinA[:, kk, c0:c0 + M], hT[:, kk, :], start=(kk == 0), stop=(kk == 7))
                return pb

            if self.lvl < 2:
                continue
            for j in range(2):
                pb = fm(j * 128, 128)
                sg, sgs = stg.next()
                k.cp(sg[:], pb[:], eng='act')
                k.dma('sp', self.uT_scr.ap()[j * 128:(j + 1) * 128, t0:t0 + 512], sg[:], sgs)
            if self.lvl < 3:
                continue
            for j in range(6):
                pb = fm(768 + j * 128, 128)
                k.cp(qraw[:, j, 3:515], pb[:], eng=('act' if j % 2 else 'dve'))
            for j in range(6):
                pb = self.bank()
                for kk in range(4):
                    k.mm(pb[:], diagW[:, j, kk, :], qraw[:, j, kk:kk + 512], start=(kk == 0), stop=(kk == 3))
                sg, sgs = stg.next()
                k.act(sg[:], pb[:], AF.Silu)
                k.dma('sp', self.qkvc_scr.ap()[j * 128:(j + 1) * 128, t0:t0 + 512], sg[:], sgs)
            k.cp(qraw[:, :, 0:3], qraw[:, :, 512:515], eng='pool')
            if self.lvl < 5:
                continue
            for (c0, scr) in ((1800, self.aq_scr), (2056, self.ak_scr)):
                for j in range(2):
                    pb = fm(c0 + j * 128, 128)
                    sg, sgs = stg.next()
                    k.cp(sg[:], pb[:], eng=('act' if j else 'dve'))
                    k.dma('sp', scr.ap()[j * 128:(j + 1) * 128, t0:t0 + 512], sg[:], sgs)
            if self.lvl < 6:
                continue
            sm, sms = smT.next()

            def stage1(t):
                tok = slice(t * 128, (t + 1) * 128)
                uv_, _ = uvs.next()
                vnb_, _ = vnbs.next()
                pb = self.bank()
                for kk in range(8):
                    k.mm(pb[:], hT[:, kk, tok], WinA[:, kk, 256:768], start=(kk == 0), stop=(kk == 7))
                k.act(uv_[:], pb[:], AF.Gelu_apprx_tanh)
                pz = self.bank()
                for kk in range(8):
                    k.mm(pz[:, 0:256], hT[:, kk, tok], WinA[:, kk, 1536:1792], start=(kk == 0), stop=(kk == 7))
                for kk in range(8):
                    k.mm(pz[:, 256:512], hT[:, kk, tok], WinA[:, kk, 2312:2568], start=False, stop=(kk == 7))
                pab = self.bank()
                for kk in range(8):
                    k.mm(pab[:, 0:8], hT[:, kk, tok], WinA[:, kk, 1792:1800], start=(kk == 0), stop=(kk == 7))
                k.op('dve', lambda: k.nc.vector.bn_stats(st6[:, 0:6], uv_[:, 256:512]), [uv_], [st6])
                k.op('dve', lambda: k.nc.vector.bn_aggr(st6[:, 6:8], st6[:, 0:6]), [st6], [st6])
                k.ts(st6[:, 7:8], st6[:, 7:8], EPS, ALU.add)
                k.act(st6[:, 7:8], st6[:, 7:8], AF.Sqrt)
                k.recip(st6[:, 7:8], st6[:, 7:8])
                k.ts(vn[:], uv_[:, 256:512], st6[:, 6:7], ALU.subtract, st6[:, 7:8], ALU.mult)
                k.tt(vnb_[:], vn[:], sgng[:], ALU.mult)
                zt, zts = ztm.next()
                k.cp(zt[:], pz[:, 0:256], eng='act')
                k.dma('sp', self.z_scr.ap()[t0 + t * 128:t0 + (t + 1) * 128, :], zt[:], zts)
                vt, vts = vtm.next()
                k.cp(vt[:], pz[:, 256:512], eng='pool' if False else 'dve')
                k.dma('sp', self.av_scr.ap()[t0 + t * 128:t0 + (t + 1) * 128, :], vt[:], vts)
                at_, ats = abt.next()
                k.cp(at_[:], pab[:, 0:8], eng='act')
                nn = b * 4 + t
                k.dma('sp', self.ab_scr.ap()[:, nn * 8:(nn + 1) * 8], at_[:], ats)
                return (t, uv_, vnb_)

            def stage2(t, uv_, vnb_):
                tok = slice(t * 128, (t + 1) * 128)
                pm = self.bank()
                for g in range(4):
                    k.mm(pm[:, g * 64:(g + 1) * 64], wsT[:, g, :], vnb_[:, g * 64:(g + 1) * 64])
                k.tt(mxs[:].rearrange("p (g c) -> p g c", g=4), pm[:, 0:256].rearrange("p (g c) -> p g c", g=4),
                     bsT[:].unsqueeze(2).broadcast_to([128, 4, 64]), ALU.add)
                k.tt(smix[:], mxs[:], uv_[:, 0:256], ALU.mult)
                pt = self.bank()
                ptv = pt[:].bitcast(BF16)
                for j in range(2):
                    k.tr(ptv[:, j * 128:(j + 1) * 128], smix[:, j * 128:(j + 1) * 128], self.identb[:])
                k.cp(sm[:, :, tok], ptv[:, 0:256].rearrange("p (j t) -> p j t", j=2), eng='act')
            pend = []
            for t in range(4):
                pend.append(stage1(t))
                if len(pend) > 1:
                    stage2(*pend.pop(0))
            while pend:
                stage2(*pend.pop(0))
            k.dma('sp', self.mix_scr.ap()[1, :, t0:t0 + 512].rearrange("(j p) t -> p j t", p=128), sm[:], sms)
        S.close()

    ATT_WIN = (6, 17, 1 << 20, 1 << 20)

    def att_gen(self, l, osets):
        k = self.k
        nc = k.nc
        L = self.L
        NT = L // 128
        NQ = L // 512
        S = Scope(k)
        w = self.w
        lam_init = 0.8 - 0.6 * math.exp(-0.3 * l)
        KT = S.sb('KT', [128, 2, L], BF16)
        QTs = Rot(S, 'QT', 2, [128, 2, 512], BF16)
        for j in range(2):
            k.dma('sp', KT[:, j, :], self.ak_scr.ap()[j * 128:(j + 1) * 128, :], 'KT')
        ekey = S.sb('ekey', [128, 4], F32)
        k.dma('sp', ekey[:], self.c['c_ekey'].ap(), 'ekey')
        abias = S.sb('abias', [128, 4 * 67], F32)
        k.dma('sp', abias[:], self.c['c_abias'].ap(), 'abias')
        cmask = S.sb('cmask', [128, 4, 512], BF16)
        k.dma('sp', cmask[:], self.c['c_cmask'].ap(), 'cmask')
        m32 = S.sb('m32', [128, 4], F32)
        k.dma('sp', m32[:], self.c['c_m32'].ap(), 'm32')
        Vaug = S.sb('Vaug', [128, NT, 4, 65], BF16)
        onesn = S.sb('onesn', [128, NT], F32)
        k.memset(onesn[:], 1.0)
        CH = min(8, NT)
        vtmp = Rot(S, 'vtmp', 2, [128, CH, 256], BF16)
        for n0 in range(0, NT, CH):
            vt, vts = vtmp.next()
            k.dma('sp', vt[:], self.av_scr.ap()[n0 * 128:(n0 + CH) * 128, :].rearrange("(n p) c -> p n c", p=128), vts)
            for h in range(4):
                k.ts(Vaug[:, n0:n0 + CH, h, 0:64], vt[:, :, h * 64:(h + 1) * 64], ekey[:, h:h + 1], ALU.mult)
        for h in range(4):
            k.act(Vaug[:, :, h, 64:65], onesn[:].unsqueeze(2), AF.Identity, scale=ekey[:, h:h + 1])
        lq = [self.load_row_bcast(S, w[n].ap()[l], 32, n) for n in ('diff_lq1', 'diff_lk1', 'diff_lq2', 'diff_lk2')]
        lt = S.sb('lt', [128, 32], F32)
        lv = S.sb('lv', [128, 8], F32)
        for i in range(2):
            k.tt(lt[:], lq[2 * i][:], lq[2 * i + 1][:], ALU.mult)
            k.op('dve', lambda: nc.vector.tensor_reduce(lv[:, i:i + 1], lt[:], AX.X, ALU.add), [lt], [lv])
        k.act(lv[:, 2:4], lv[:, 0:2], AF.Exp)
        k.tt(lv[:, 4:5], lv[:, 2:3], lv[:, 3:4], ALU.subtract)
        k.ts(lv[:, 5:6], lv[:, 4:5], lam_init, ALU.add)
        gd = self.load_row_bcast(S, w['diff_norm_g'].ap()[l], 64, 'gd')
        k.ts(gd[:], gd[:], 1.0 - lam_init, ALU.mult)
        QTm = Rot(S, 'QTm', 4, [128, 512], BF16)
        Pb = Rot(S, 'Pb', 5, [128, 512], BF16)
        rr = S.sb('rr', [128, 16], F32)
        t1 = S.sb('t1', [128, 4, 64], F32)
        t2 = S.sb('t2', [128, 4, 64], F32)
        sq = S.sb('sq', [128, 4, 64], F32)
        omix = Rot(S, 'omix', 2, [128, 4, 256], BF16)
        ostg = Rot(S, 'ostg', 2, [128, 2, 512], BF16)
        Osets = [[self.banks[a_], self.banks[b_]] for (a_, b_) in osets]
        scale = 32.0 ** -0.5
        LA = 2
        gi = 0
        for qb in range(NQ):
            om, _ = omix.next()
            QT, qts_ = QTs.next()
            k.dma('sp', QT[:], self.aq_scr.ap()[:, qb * 512:(qb + 1) * 512].rearrange("(j p) t -> p j t", p=128), qts_)
            for h in range(4):
                t2i = h // 2
                O = Osets[gi % len(Osets)]
                gi += 1
                qm = []
                for m in range(2):
                    qt_, _ = QTm.next()
                    k.ts(qt_[:], QT[:, t2i, :], m32[:, (h % 2) * 2 + m:(h % 2) * 2 + m + 1], ALU.mult,
                         eng=('pool' if m else 'dve'))
                    qm.append(qt_)
                for m in range(2):
                    k.memset(O[m][:, 0:260], 0.0)
                blocks = [(s_, 128) for s_ in range(4)] if h == 0 else [(None, 512)]
                steps = []
                for (s_, QB) in blocks:
                    qt_first = 4 * qb + (s_ if s_ is not None else 0)
                    qt_last = 4 * qb + (s_ if s_ is not None else 3)
                    kt_lo = max(0, qt_first - self.ATT_WIN[h])
                    for kt in range(kt_lo, qt_last + 1):
                        for m in range(2):
                            steps.append((s_, QB, qt_first, kt, m))
                live = {}

                def front(i):
                    s_, QB, qt_first, kt, m = steps[i]
                    jd = kt - qt_first
                    ps = self.bank()
                    qcols = slice(s_ * 128, (s_ + 1) * 128) if s_ is not None else slice(0, 512)
                    k.mm(ps[:, 0:QB], KT[:, t2i, kt * 128:(kt + 1) * 128], qm[m][:, qcols])
                    pT, _ = Pb.next()
                    dd = (qt_first - kt) + 3
                    k.act(pT[:, 0:QB], ps[:, 0:QB], AF.Exp, scale=scale, bias=abias[:, h * 67 + dd:h * 67 + dd + 1])
                    if jd >= 0:
                        k.tt(pT[:, 0:QB], pT[:, 0:QB], cmask[:, jd, 0:QB], ALU.mult, eng='pool')
                    live[i] = pT

                def back(i):
                    s_, QB, qt_first, kt, m = steps[i]
                    jd = kt - qt_first
                    pT = live.pop(i)
                    for ss_ in ([s_] if s_ is not None else range(4)):
                        if s_ is None and ss_ < jd:
                            continue
                        col = 0 if s_ is not None else ss_ * 128
                        k.mm(O[m][:, ss_ * 65:(ss_ + 1) * 65], pT[:, col:col + 128], Vaug[:, kt, h, :],
                             start=False, stop=True)
                for i in range(len(steps) + LA):
                    if i < len(steps):
                        front(i)
                    if i >= LA:
                        back(i - LA)
                    yield
                for m in range(2):
                    k.recip(rr[:, 4 * m:4 * m + 4], O[m][:, 0:260].rearrange("p (s e) -> p s e", e=65)[:, :, 64])
                k.ts(rr[:, 4:8], rr[:, 4:8], lv[:, 5:6], ALU.mult)
                k.tt(t1[:], O[0][:, 0:260].rearrange("p (s e) -> p s e", e=65)[:, :, 0:64],
                     rr[:, 0:4].unsqueeze(2).broadcast_to([128, 4, 64]), ALU.mult)
                k.tt(t2[:], O[1][:, 0:260].rearrange("p (s e) -> p s e", e=65)[:, :, 0:64],
                     rr[:, 4:8].unsqueeze(2).broadcast_to([128, 4, 64]), ALU.mult)
                k.tt(t1[:], t1[:], t2[:], ALU.subtract)
                k.tt(sq[:], t1[:], t1[:], ALU.mult)
                k.op('dve', lambda: nc.vector.tensor_reduce(rr[:, 8:12], sq[:], AX.X, ALU.add), [sq], [rr])
                k.ts(rr[:, 8:12], rr[:, 8:12], 1.0 / 64, ALU.mult, EPS, ALU.add)
                k.act(rr[:, 8:12], rr[:, 8:12], AF.Ln)
                k.act(rr[:, 12:16], rr[:, 8:12], AF.Exp, scale=-0.5)
                yield
                k.tt(t1[:], t1[:], rr[:, 12:16].unsqueeze(2).broadcast_to([128, 4, 64]), ALU.mult)
                k.tt(om[:, :, h * 64:(h + 1) * 64], t1[:], gd[:].unsqueeze(1).broadcast_to([128, 4, 64]), ALU.mult)
            og, ogs = ostg.next()
            for s_ in range(4):
                pt = self.bank()
                ptv = pt[:].bitcast(BF16)
                for j in range(2):
                    k.tr(ptv[:, j * 128:(j + 1) * 128], om[:, s_, j * 128:(j + 1) * 128], self.identb[:])
                k.cp(og[:, :, s_ * 128:(s_ + 1) * 128], ptv[:, 0:256].rearrange("p (j t) -> p j t", j=2), eng='act')
            k.dma('sp', self.mix_scr.ap()[3, :, qb * 512:(qb + 1) * 512].rearrange("(j p) t -> p j t", p=128), og[:], ogs)
            yield
        return S

    def run_gens(self, specs, ratio=None):
        st = [{'g': g, 'pool': pool, 'bi': 0, 'alive': True, 'S': None} for (g, pool) in specs]
        ratio = ratio or [1] * len(st)
        while any(x['alive'] for x in st):
            for x, r in zip(st, ratio):
                for _ in range(r):
                    if not x['alive']:
                        break
                    self.bank_pool = x['pool']
                    self.bi = x['bi']
                    try:
                        next(x['g'])
                    except StopIteration as e:
                        x['alive'] = False
                        x['S'] = e.value
                    x['bi'] = self.bi
        self.bank_pool = list(range(8))
        self.bi = 0
        self.k.barrier()
        for x in reversed(st):
            x['S'].es.close()

    def phase_att(self, l):
        self.run_gens([(self.att_gen(l, [(4, 5), (6, 7)]), [0, 1, 2, 3])])

    def phase_dn(self, l):
        self.run_gens([(self.dn_gen(l), list(range(8)))])

    def dn_gen(self, l, nsets=2):
        k = self.k
        nc = k.nc
        L = self.L
        NT = L // 128
        NB = L // 512
        S = Scope(k)
        w = self.w
        onesf = S.sb('onesf', [128, 128], F32)
        k.memset(onesf[:], 1.0)
        nmsl = S.sb('nmsl', [128, 128], F32)
        nmsu = S.sb('nmsu', [128, 128], F32)
        k.dma('sp', nmsl[:], self.c['c_msl'].ap(), 'nmsl')
        k.dma('sp', nmsu[:], self.c['c_msu'].ap(), 'nmsu')
        k.ts(nmsl[:], nmsl[:], -1.0, ALU.mult)
        k.ts(nmsu[:], nmsu[:], -1.0, ALU.mult)
        alog = self.load_row_bcast(S, w['dn_a_log'].ap()[l], 4, 'alog')
        dtb = self.load_row_bcast(S, w['dn_dt_bias'].ap()[l], 4, 'dtb')
        k.act(alog[:], alog[:], AF.Exp)
        k.ts(alog[:], alog[:], -1.0, ALU.mult)
        gn = self.load_row_bcast(S, w['dn_norm_g'].ap()[l], 64, 'gn')
        ab = S.sb('ab', [128, NT, 8], F32)
        k.dma('sp', ab[:].rearrange("p n c -> p (n c)"), self.ab_scr.ap(), 'ab')
        gt = S.sb('gt', [128, NT, 4], F32)
        bt = S.sb('bt', [128, NT, 4], F32)
        Gt = S.sb('Gt', [128, NT, 4], F32)
        GL = S.sb('GL', [128, NT, 4], F32)
        eG = S.sb('eG', [128, NT, 4], F32)
        kds = S.sb('kds', [128, NT, 4], F32)
        gla = S.sb('gla', [128, NT, 4], F32)
        k.tt(gt[:], ab[:, :, 0:4], dtb[:].unsqueeze(1).broadcast_to([128, NT, 4]), ALU.add)
        k.act(gt[:], gt[:], AF.Exp)
        k.ts(gt[:], gt[:], 1.0, ALU.add)
        k.act(gt[:], gt[:], AF.Ln)
        k.tt(gt[:], gt[:], alog[:].unsqueeze(1).broadcast_to([128, NT, 4]), ALU.mult)
        k.act(bt[:], ab[:, :, 4:8], AF.Sigmoid)
        gflat = gt[:].rearrange("p n c -> p (n c)")
        for c0 in range(0, NT * 4, 512):
            m = min(512, NT * 4 - c0)
            pc = self.bank()
            k.mm(pc[:, 0:m], self.mui[:], gflat[:, c0:c0 + m])
            k.cp(Gt[:].rearrange("p n c -> p (n c)")[:, c0:c0 + m], pc[:, 0:m])
            pc2 = self.bank()
            k.mm(pc2[:, 0:m], onesf[:], gflat[:, c0:c0 + m])
            k.cp(GL[:].rearrange("p n c -> p (n c)")[:, c0:c0 + m], pc2[:, 0:m])
        k.act(eG[:], Gt[:], AF.Exp)
        k.tt(kds[:], GL[:], Gt[:], ALU.subtract)
        k.act(kds[:], kds[:], AF.Exp)
        k.act(gla[:], GL[:], AF.Exp)
        Sf = S.sb('Sf', [64, 4, 64], F32)
        Sb = S.sb('Sb', [64, 4, 64], BF16)
        k.memset(Sf[:], 0.0)
        k.memset(Sb[:], 0.0)
        qfs = Rot(S, 'qf', 2, [128, 6, 512], BF16)
        zts = Rot(S, 'zt', 2, [128, 4, 256], BF16)
        zsb = S.sb('zsb', [128, 4, 256], F32)

        def alloc_set(i):
            B = {}
            def a(name, shape, dt):
                B[name] = S.sb('%s_%d' % (name, i), shape, dt)
            a('qkv', [128, 768], F32); a('sq', [128, 512], F32); a('r8', [128, 16], F32)
            a('qn_f', [128, 4, 64], F32); a('kn_f', [128, 4, 64], F32); a('kb_f', [128, 4, 64], F32)
            for n_ in ('qn_b', 'kn_b', 'kb_b', 'qd_b', 'kbg_b', 'kdec_b', 'vb_b'):
                a(n_, [128, 4, 64], BF16)
            for n_ in ('kn_b', 'kb_b', 'qn_b', 'qd_b'):
                a(n_ + 'T', [64, 4, 128], BF16)
            a('dG', [128, 4, 128], F32)
            for n_ in ('Xn', 'Xm', 'Dt', 'Dl', 'Dlm', 'Dtm1', 'Dtm2'):
                a(n_, [128, 128], F32)
            for n_ in ('P0', 'P1', 'PT0', 'PT1', 'TT0', 'TT1'):
                a(n_, [128, 4, 128], F32)
            a('TTb', [128, 4, 128], BF16); a('Aq', [128, 4, 128], BF16)
            a('u_sb', [128, 4, 64], F32); a('wwT', [64, 4, 128], BF16); a('vnew', [128, 4, 64], BF16)
            a('o_sb', [128, 4, 64], F32); a('osq', [128, 4, 64], F32); a('dmix', [128, 256], BF16)
            return B
        sets = [alloc_set(i) for i in range(nsets)]
        dstg = Rot(S, 'dstg', 2, [128, 2, 512], BF16)

        def bc(ap2):
            return ap2.unsqueeze(2).broadcast_to([128, 4, 64])

        def rsq(out, in_, mult):
            k.ts(out, in_, mult, ALU.mult, EPS, ALU.add)
            k.act(out, out, AF.Ln)
            k.act(out, out, AF.Exp, scale=-0.5)

        def dn_tile(B, n, t, qf, dg):
            tok = slice(t * 128, (t + 1) * 128)
            qkv, sq, r8 = B['qkv'], B['sq'], B['r8']
            qn_f, kn_f, kb_f = B['qn_f'], B['kn_f'], B['kb_f']
            pt = self.bank()
            ptv = pt[:].bitcast(BF16)
            for j in range(6):
                k.tr(ptv[:, j * 128:(j + 1) * 128], qf[:, j, tok], self.identb[:])
            k.cp(qkv[:], ptv[:, 0:768], eng='act')
            yield
            k.tt(sq[:], qkv[:, 0:512], qkv[:, 0:512], ALU.mult)
            k.op('dve', lambda: nc.vector.tensor_reduce(r8[:, 0:8], sq[:].rearrange("p (a d) -> p a d", d=64), AX.X, ALU.add),
                 [sq], [r8])
            rsq(r8[:, 8:16], r8[:, 0:8], 1.0)
            k.ts(r8[:, 8:12], r8[:, 8:12], 0.125, ALU.mult)
            yield
            q3 = qkv[:, 0:256].rearrange("p (h d) -> p h d", h=4)
            k3 = qkv[:, 256:512].rearrange("p (h d) -> p h d", h=4)
            v3 = qkv[:, 512:768].rearrange("p (h d) -> p h d", h=4)
            k.tt(qn_f[:], q3, bc(r8[:, 8:12]), ALU.mult)
            k.tt(kn_f[:], k3, bc(r8[:, 12:16]), ALU.mult)
            k.tt(kb_f[:], kn_f[:], bc(bt[:, n, :]), ALU.mult)
            k.cp(B['qn_b'][:], qn_f[:], eng='act')
            k.cp(B['kn_b'][:], kn_f[:], eng='act')
            k.cp(B['kb_b'][:], kb_f[:], eng='act')
            yield
            k.tt(B['qd_b'][:], qn_f[:], bc(eG[:, n, :]), ALU.mult)
            k.tt(B['kbg_b'][:], kb_f[:], bc(eG[:, n, :]), ALU.mult, eng='pool')
            k.tt(B['kdec_b'][:], kn_f[:], bc(kds[:, n, :]), ALU.mult)
            k.tt(B['vb_b'][:], v3, bc(bt[:, n, :]), ALU.mult, eng='pool')
            for gi_, (n1, n2) in enumerate((('kn_b', 'kb_b'), ('qn_b', 'qd_b'))):
                pf = self.bank()
                pfv = pf[:].bitcast(BF16)
                for ii, nm in enumerate((n1, n2)):
                    for h in range(4):
                        k.tr(pfv[0:64, ii * 512 + h * 128:ii * 512 + (h + 1) * 128], B[nm][:, h, :], self.identb[:])
                k.cp(B[n1 + 'T'][:].rearrange("p h t -> p (h t)"), pfv[0:64, 0:512], eng='act')
                k.cp(B[n2 + 'T'][:].rearrange("p h t -> p (h t)"), pfv[0:64, 512:1024])
                yield
            knT, kbT, qnT, qdT = B['kn_bT'], B['kb_bT'], B['qn_bT'], B['qd_bT']
            dG, Xn, Xm, Dt, Dl, Dlm, Dtm1, Dtm2 = (B[x_] for x_ in ('dG', 'Xn', 'Xm', 'Dt', 'Dl', 'Dlm', 'Dtm1', 'Dtm2'))
            for h in range(4):
                k.ts(dG[:, h, :], self.identf[:], Gt[:, n, h:h + 1], ALU.mult, eng='pool')
            Pm = [B['P0'], B['P1']]
            PTm = [B['PT0'], B['PT1']]
            TTm = [B['TT0'], B['TT1']]
            Aq = B['Aq']
            P, PT, TT = Pm[0], PTm[0], TTm[0]
            for h in range(4):
                ps = self.bank()
                k.mm(ps[:, 0:128], kbT[:, h, :], knT[:, h, :])
                k.mm(ps[:, 128:256], knT[:, h, :], kbT[:, h, :])
                k.mm(ps[:, 256:384], knT[:, h, :], qnT[:, h, :])
                k.mm(ps[:, 384:512], onesf[:], dG[:, h, :])
                k.ts(Xn[:], ps[:, 384:512], Gt[:, n, h:h + 1], ALU.subtract, 0.0, ALU.min)
                k.ts(Xm[:], ps[:, 384:512], Gt[:, n, h:h + 1], ALU.subtract, 0.0, ALU.max)
                k.act(Dt[:], Xn[:], AF.Exp)
                k.act(Dl[:], Xm[:], AF.Exp, scale=-1.0)
                k.tt(Dlm[:], Dl[:], nmsl[:], ALU.mult, eng='pool')
                k.tt(Dtm1[:], Dt[:], nmsu[:], ALU.mult, eng='pool')
                k.tt(Dtm2[:], Dt[:], self.mui[:], ALU.mult, eng='pool')
                k.tt(P[:, h, :], ps[:, 0:128], Dlm[:], ALU.mult)
                k.tt(PT[:, h, :], ps[:, 128:256], Dtm1[:], ALU.mult)
                k.tt(Aq[:, h, :], ps[:, 256:384], Dtm2[:], ALU.mult)
                k.tt(TT[:, h, :], PT[:, h, :], self.identf[:], ALU.add, eng='pool')
                yield
            cur = 0
            for lev in range(6):
                nxt = 1 - cur
                pP = self.bank()
                for h in range(4):
                    k.mm(pP[:, h * 128:(h + 1) * 128], PTm[cur][:, h, :], Pm[cur][:, h, :])
                k.cp(Pm[nxt][:].rearrange("p h t -> p (h t)"), pP[:], eng='act')
                if lev < 5:
                    pQ = self.bank()
                    for h in range(4):
                        k.mm(pQ[:, h * 128:(h + 1) * 128], Pm[cur][:, h, :], PTm[cur][:, h, :])
                    k.cp(PTm[nxt][:].rearrange("p h t -> p (h t)"), pQ[:])
                yield
                pT_ = self.bank()
                for h in range(4):
                    k.mm(pT_[:, h * 128:(h + 1) * 128], Pm[nxt][:, h, :], TTm[cur][:, h, :])
                k.tt(TTm[nxt][:].rearrange("p h t -> p (h t)"), pT_[:], TTm[cur][:].rearrange("p h t -> p (h t)"), ALU.add)
                cur = nxt
                yield
            TT = B['TTb']
            k.cp(TT[:], TTm[cur][:], eng='pool')
            u_sb, wwT, vnew, o_sb, osq, dmix = (B[x_] for x_ in ('u_sb', 'wwT', 'vnew', 'o_sb', 'osq', 'dmix'))
            pu = self.bank()
            for h in range(4):
                k.mm(pu[:, h * 64:(h + 1) * 64], TT[:, h, :], B['vb_b'][:, h, :])
            k.cp(u_sb[:].rearrange("p h e -> p (h e)"), pu[:, 0:256], eng='act')
            pw = self.bank()
            for h in range(4):
                k.mm(pw[0:64, h * 128:(h + 1) * 128], B['kbg_b'][:, h, :], TT[:, h, :])
            k.cp(wwT[:].rearrange("p h t -> p (h t)"), pw[0:64, :])
            yield
            pS = self.bank()
            for h in range(4):
                k.mm(pS[:, h * 64:(h + 1) * 64], wwT[:, h, :], Sb[:, h, :])
            k.tt(vnew[:].rearrange("p h e -> p (h e)"), u_sb[:].rearrange("p h e -> p (h e)"), pS[:, 0:256], ALU.subtract)
            po = self.bank()
            for h in range(4):
                k.mm(po[:, h * 64:(h + 1) * 64], qdT[:, h, :], Sb[:, h, :], start=True, stop=False)
                k.mm(po[:, h * 64:(h + 1) * 64], Aq[:, h, :], vnew[:, h, :], start=False, stop=True)
            pK = self.bank()
            for h in range(4):
                k.mm(pK[0:64, h * 64:(h + 1) * 64], B['kdec_b'][:, h, :], vnew[:, h, :])
            k.tt(Sf[:], Sf[:], gla[0:64, n, :].unsqueeze(2).broadcast_to([64, 4, 64]), ALU.mult)
            k.tt(Sf[:].rearrange("p h e -> p (h e)"), Sf[:].rearrange("p h e -> p (h e)"), pK[0:64, 0:256], ALU.add)
            k.cp(Sb[:], Sf[:], eng='act')
            k.cp(o_sb[:].rearrange("p h e -> p (h e)"), po[:, 0:256], eng='act')
            yield
            k.tt(osq[:], o_sb[:], o_sb[:], ALU.mult)
            k.op('dve', lambda: nc.vector.tensor_reduce(r8[:, 0:4], osq[:], AX.X, ALU.add), [osq], [r8])
            rsq(r8[:, 4:8], r8[:, 0:4], 1.0 / 64)
            k.tt(o_sb[:], o_sb[:], bc(r8[:, 4:8]), ALU.mult)
            k.tt(o_sb[:], o_sb[:], gn[:].unsqueeze(1).broadcast_to([128, 4, 64]), ALU.mult, eng='pool')
            k.tt(dmix[:], o_sb[:].rearrange("p h e -> p (h e)"), zsb[:, t, :], ALU.mult)
            yield
            px = self.bank()
            pxv = px[:].bitcast(BF16)
            for j in range(2):
                k.tr(pxv[:, j * 128:(j + 1) * 128], dmix[:, j * 128:(j + 1) * 128], self.identb[:])
            k.cp(dg[:, :, tok], pxv[:, 0:256].rearrange("p (j t) -> p j t", j=2), eng='act')

        for b in range(NB):
            t0 = b * 512
            qf, qfsem = qfs.next()
            k.dma('sp', qf[:], self.qkvc_scr.ap()[:, t0:t0 + 512].rearrange("(j p) t -> p j t", p=128), qfsem)
            zt, ztsem = zts.next()
            k.dma('sp', zt[:], self.z_scr.ap()[t0:t0 + 512, :].rearrange("(t p) c -> p t c", p=128), ztsem)
            k.act(zsb[:], zt[:], AF.Silu)
            dg, dgs = dstg.next()
            for tp in range(4 // nsets):
                gens = [dn_tile(sets[i], b * 4 + tp * nsets + i, tp * nsets + i, qf, dg) for i in range(nsets)]
                alive = [True] * nsets
                while any(alive):
                    for i in range(nsets):
                        if alive[i]:
                            try:
                                next(gens[i])
                            except StopIteration:
                                alive[i] = False
                    yield
            k.dma('sp', self.mix_scr.ap()[2, :, t0:t0 + 512].rearrange("(j p) t -> p j t", p=128), dg[:], dgs)
        return S

    def phase_s5(self, l):
        k = self.k
        nc = k.nc
        L = self.L
        NB = L // 512
        nch = L // 8
        S = Scope(k)
        w = self.w
        nlev = 0
        while (1 << nlev) < nch:
            nlev += 1
        W1 = S.sb('W1pad', [128, 2, 4, 8, 2, 128], BF16)
        Kbd = S.sb('Kbd', [128, 2, 8, 128], BF16)
        W2 = S.sb('W2bd', [128, 2, 4, 8, 2, 128], BF16)
        A8 = [S.sb('A8_%d' % i, [128, 3, 8], F32) for i in range(nlev)]
        wglu = S.sb('wglu', [128, 2, 256], BF16)
        k.dma('pool', wglu[:], w['s5_w_glu'].ap()[l].rearrange("(j p) n -> p j n", p=128), 'wglu')
        dcol = S.sb('dcol', [128, 2], F32)
        bglu = S.sb('bglu', [128, 2], F32)
        SP = Scope(k)
        dcol_t = self.load_cols(SP, w['s5_d'].ap()[l].rearrange("(j p) -> j p", p=128), 2, 'dcolt')
        bglu_t = self.load_cols(SP, w['s5_b_glu'].ap()[l].rearrange("(j p) -> j p", p=128), 2, 'bglut')
        k.cp(dcol[:], dcol_t[:])
        k.cp(bglu[:], bglu_t[:])
        ctr = [0]

        def T(shape=(128, 128), dt=F32):
            ctr[0] += 1
            return SP.sb('s5t%d' % ctr[0], list(shape), dt)

        def cl(name, shape):
            t_ = SP.sb(name, list(shape), F32)
            k.dma('sp', t_[:], self.c[name].ap(), name)
            return t_
        sel8 = cl('c_sel8', [8, 128])
        bm16 = cl('c_bm16', [128, 128])
        w1mask = cl('c_w1mask', [128, 8])
        w2mask = cl('c_w2mask', [128, 4, 128])
        lam8 = SP.sb('lam8', [8, 2, 2, 64], F32)
        k.dma('sp', lam8[:, 0, :, :], w['s5_lam_re'].ap()[l].rearrange("(j g) p -> g j p", g=8), 'lam8')
        k.dma('sp', lam8[:, 1, :, :], w['s5_lam_im'].ap()[l].rearrange("(j g) p -> g j p", g=8), 'lam8')
        ls8 = SP.sb('ls8', [8, 2], F32)
        for j in range(2):
            k.dma('sp', ls8[:, j:j + 1], w['s5_log_step'].ap()[l, j * 8:(j + 1) * 8].unsqueeze(1), 'ls8')
        lr, li, st = T(), T(), T((128, 2))
        pb = self.bank()
        k.mm(pb[:, 0:256], sel8[:], lam8[:].rearrange("g r j p -> g (r j p)"))
        k.mm(pb[:, 256:258], sel8[:], ls8[:])
        k.cp(lr[:], pb[:, 0:128])
        k.cp(li[:], pb[:, 128:256])
        k.act(st[:], pb[:, 256:258], AF.Exp)
        stb = st[:].unsqueeze(2).broadcast_to([128, 2, 64])

        def v3(t_):
            return t_[:].rearrange("q (j p) -> q j p", j=2)
        th, lrs = T(), T()
        k.tt(v3(th), v3(li), stb, ALU.mult)
        k.tt(v3(lrs), v3(lr), stb, ALU.mult)
        er, s32, cs, sn, t1, t2 = T(), T(), T(), T(), T(), T()
        k.act(er[:], lrs[:], AF.Exp)
        k.act(s32[:], th[:], AF.Sin, scale=1.0 / 32)
        k.act(sn[:], th[:], AF.Sin, scale=1.0 / 16)
        k.tt(t1[:], s32[:], s32[:], ALU.mult)
        k.ts(cs[:], t1[:], -2.0, ALU.mult, 1.0, ALU.add)
        for _ in range(4):
            k.tt(t1[:], cs[:], cs[:], ALU.mult)
            k.tt(t2[:], sn[:], sn[:], ALU.mult)
            k.stt(sn[:], cs[:], 2.0, sn[:], ALU.mult, ALU.mult)
            k.tt(cs[:], t1[:], t2[:], ALU.subtract)
        ar, ai = T(), T()
        k.tt(ar[:], er[:], cs[:], ALU.mult)
        k.tt(ai[:], er[:], sn[:], ALU.mult)
        den, nr, cre, cim = T(), T(), T(), T()
        k.tt(t1[:], lr[:], lr[:], ALU.mult)
        k.tt(t2[:], li[:], li[:], ALU.mult)
        k.tt(den[:], t1[:], t2[:], ALU.add)
        k.recip(den[:], den[:])
        k.ts(nr[:], ar[:], -1.0, ALU.add)
        k.tt(t1[:], nr[:], lr[:], ALU.mult)
        k.tt(t2[:], ai[:], li[:], ALU.mult)
        k.tt(cre[:], t1[:], t2[:], ALU.add)
        k.tt(cre[:], cre[:], den[:], ALU.mult)
        k.tt(t1[:], ai[:], lr[:], ALU.mult)
        k.tt(t2[:], nr[:], li[:], ALU.mult)
        k.tt(cim[:], t1[:], t2[:], ALU.subtract)
        k.tt(cim[:], cim[:], den[:], ALU.mult)

        def cmul(o_r, o_i, a_r, a_i, b_r, b_i, neg_im=False):
            k.tt(t1[:], a_r, b_r, ALU.mult)
            k.tt(t2[:], a_i, b_i, ALU.mult)
            k.tt(o_r, t1[:], t2[:], ALU.subtract)
            k.tt(t1[:], a_r, b_i, ALU.mult)
            k.tt(t2[:], a_i, b_r, ALU.mult)
            if neg_im:
                k.stt(o_i, t1[:], -1.0, t2[:], ALU.mult, ALU.subtract)
            else:
                k.tt(o_i, t1[:], t2[:], ALU.add)
        Bre, Bim = T(), T()
        for (nm, dst) in (('s5_b_re', Bre), ('s5_b_im', Bim)):
            bn = SP.sb(nm + 'n', [64, 16, 16], F32)
            k.dma('sp', bn[:], w[nm].ap()[l].rearrange("g p h -> p g h"), nm + 'n')
            for j in range(2):
                pb = self.bank()
                k.tr(pb[:, 0:64], bn[:, j * 8:(j + 1) * 8, :].rearrange("p g h -> p (g h)"), self.identf[0:64, 0:64])
                k.cp(dst[:, j * 64:(j + 1) * 64], pb[:, 0:64])
        Cre, Cim = T(), T()
        k.dma('sp', v3(Cre), w['s5_c_re'].ap()[l].rearrange("(j g) h p -> (g h) j p", g=8), 'Cre')
        k.dma('sp', v3(Cim), w['s5_c_im'].ap()[l].rearrange("(j g) h p -> (g h) j p", g=8), 'Cim')
        Bbr, Bbi = T(), T()
        cmul(Bbr[:], Bbi[:], cre[:], cim[:], Bre[:], Bim[:])
        Pr = [T() for _ in range(9)]
        Pi = [T() for _ in range(9)]
        k.memset(Pr[0][:], 1.0)
        k.memset(Pi[0][:], 0.0)
        k.cp(Pr[1][:], ar[:])
        k.cp(Pi[1][:], ai[:])
        for kk in range(2, 9):
            cmul(Pr[kk][:], Pi[kk][:], Pr[kk - 1][:], Pi[kk - 1][:], ar[:], ai[:])
        AB = [SP.sb('AB%d' % tau, [128, 2, 2, 64], F32) for tau in range(8)]
        abr, abi = T(), T()
        for tau in range(8):
            cmul(abr[:], abi[:], Pr[tau][:], Pi[tau][:], Bbr[:], Bbi[:])
            k.cp(AB[tau][:, :, 0, :], v3(abr), eng='act')
            k.cp(AB[tau][:, :, 1, :], v3(abi), eng='act')
        w1m = w1mask[:].rearrange("q (m h) -> q m h", h=2).unsqueeze(3).broadcast_to([128, 4, 2, 64])
        for s_ in range(8):
            for j in range(2):
                for ri in range(2):
                    src = AB[7 - s_][:, j, ri, :].unsqueeze(1).unsqueeze(1).broadcast_to([128, 4, 2, 64])
                    k.tt(W1[:, j, :, s_, ri, :].rearrange("q m (h p) -> q m h p", h=2), src, w1m, ALU.mult,
                         eng=('pool' if ri else 'dve'))
        CT0 = SP.sb('CT0', [128, 2, 2, 64], F32)
        k.cp(CT0[:, :, 0, :], v3(Cre))
        k.ts(CT0[:, :, 1, :], v3(Cim), -1.0, ALU.mult)
        CT0T = SP.sb('CT0T', [128, 2, 128], F32)
        for j in range(2):
            pb = self.bank()
            k.tr(pb[:, 0:128], CT0[:, j, :, :].rearrange("q r p -> q (r p)"), self.identf[:])
            k.cp(CT0T[:, j, :], pb[:, 0:128])
        abT = Rot(SP, 'abT', 2, [128, 128], F32)
        for tau in range(8):
            for j in range(2):
                pb = self.bank()
                k.tr(pb[:, 0:128], AB[tau][:, j, :, :].rearrange("q r p -> q (r p)"), self.identf[:])
                at_, _ = abT.next()
                k.cp(at_[:], pb[:, 0:128], eng='act')
                k.mm(pb[:, 128:256], at_[:], CT0T[:, j, :])
                k.tt(Kbd[:, j, tau, :], pb[:, 128:256], bm16[:], ALU.mult)
        CA = SP.sb('CA', [128, 2, 2, 2, 64], F32)
        car, cai = T(), T()
        for s_ in range(8):
            cmul(car[:], cai[:], Cre[:], Cim[:], Pr[s_ + 1][:], Pi[s_ + 1][:], neg_im=True)
            for dup in range(2):
                k.cp(CA[:, :, 0, dup, :], v3(car), eng='act')
                k.cp(CA[:, :, 1, dup, :], v3(cai), eng='pool')
            for j in range(2):
                for ri in range(2):
                    pb = self.bank()
                    k.tr(pb[:, 0:128], CA[:, j, ri, :, :].rearrange("q d p -> q (d p)"), self.identf[:])
                    k.tt(W2[:, j, :, s_, ri, :], pb[:, 0:128].unsqueeze(1).broadcast_to([128, 4, 128]), w2mask[:], ALU.mult)
        a8d = SP.sb('a8d', [128, 2, 2, 2, 64], F32)
        for dup in range(2):
            k.cp(a8d[:, :, 0, dup, :], v3(Pr[8]))
            k.cp(a8d[:, :, 1, dup, :], v3(Pi[8]))
        for j in range(2):
            for ri in range(2):
                pb = self.bank()
                k.tr(pb[:, 0:128], a8d[:, j, ri, :, :].rearrange("q d p -> q (d p)"), self.identf[:])
                k.cp(A8[0][0:64, ri, 4 * j:4 * j + 4], pb[0:64, 0:128:32])
                k.cp(A8[0][64:128, ri, 4 * j:4 * j + 4], pb[64:128, 16:128:32])
        s1, s2 = T((128, 8)), T((128, 8))
        for i in range(nlev):
            if i > 0:
                k.tt(s1[:], A8[i - 1][:, 0, :], A8[i - 1][:, 0, :], ALU.mult)
                k.tt(s2[:], A8[i - 1][:, 1, :], A8[i - 1][:, 1, :], ALU.mult)
                k.tt(A8[i][:, 0, :], s1[:], s2[:], ALU.subtract)
                k.stt(A8[i][:, 1, :], A8[i - 1][:, 0, :], 2.0, A8[i - 1][:, 1, :], ALU.mult, ALU.mult)
            k.ts(A8[i][:, 2, :], A8[i][:, 1, :], -1.0, ALU.mult)
        SP.close()
        uT = S.sb('uT', [128, 2, L], BF16)
        for j in range(2):
            k.dma('sp', uT[:, j, :], self.uT_scr.ap()[j * 128:(j + 1) * 128, :], 'uT')
        Xbf = S.sb('Xbf', [128, 2, 8, nch + 1], BF16)
        k.memset(Xbf[:, :, :, 0:1], 0.0)
        XA = [S.sb('XA%d' % i, [128, nch], F32) for i in range(2)]
        XB = [S.sb('XB%d' % i, [128, nch], F32) for i in range(2)]
        XT = S.sb('XT', [128, nch], F32)
        for m in range(8):
            j, mp = m // 4, m % 4
            for ri in range(2):
                for c0 in range(0, nch, 512):
                    n = min(512, nch - c0)
                    ps = self.bank()
                    for s_ in range(8):
                        k.mm(ps[:, 0:n], W1[:, j, mp, s_, ri, :], uT[:, j, 8 * c0 + s_:8 * (c0 + n):8],
                             start=(s_ == 0), stop=(s_ == 7))
                    k.cp(XA[ri][:, c0:c0 + n], ps[:, 0:n], eng=('act' if ri else 'dve'))
            cur, oth = XA, XB
            for lev in range(nlev):
                d = 1 << lev
                Ar = A8[lev][:, 0, m:m + 1]
                Ai = A8[lev][:, 1, m:m + 1]
                nAi = A8[lev][:, 2, m:m + 1]
                k.stt(XT[:, d:], cur[0][:, 0:nch - d], Ar, cur[0][:, d:], ALU.mult, ALU.add)
                k.stt(oth[0][:, d:], cur[1][:, 0:nch - d], nAi, XT[:, d:], ALU.mult, ALU.add)
                k.stt(XT[:, d:], cur[1][:, 0:nch - d], Ar, cur[1][:, d:], ALU.mult, ALU.add)
                k.stt(oth[1][:, d:], cur[0][:, 0:nch - d], Ai, XT[:, d:], ALU.mult, ALU.add)
                k.cp(oth[0][:, 0:d], cur[0][:, 0:d], eng='act')
                k.cp(oth[1][:, 0:d], cur[1][:, 0:d], eng='pool')
                cur, oth = oth, cur
            k.cp(Xbf[:, 0, m, 1:nch + 1], cur[0][:], eng='act')
            k.cp(Xbf[:, 1, m, 1:nch + 1], cur[1][:], eng='pool')
        yv = S.sb('yv', [128, 2, 512], F32)
        zf = S.sb('zf', [128, 2, 512], F32)
        zb = S.sb('zb', [128, 2, 512], BF16)
        sgm = S.sb('sgm', [128, 512], F32)
        sstg = Rot(S, 's5stg', 2, [128, 2, 512], BF16)
        for b in range(NB):
            t0 = b * 512
            cb0 = b * 64
            for j in range(2):
                ps = self.bank()
                k.memset(ps[:], 0.0)
                for s_ in range(8):
                    out = ps[:, s_:512:8]
                    for mp in range(4):
                        for ri in range(2):
                            k.mm(out, W2[:, j, mp, s_, ri, :], Xbf[:, ri, 4 * j + mp, cb0:cb0 + 64], start=False, stop=False)
                    for tau in range(s_ + 1):
                        k.mm(out, Kbd[:, j, tau, :], uT[:, j, t0 + s_ - tau:t0 + 512:8], start=False, stop=(tau == s_))
                k.stt(yv[:, j, :], uT[:, j, t0:t0 + 512], dcol[:, j:j + 1], ps[:], ALU.mult, ALU.add)
                k.act(zf[:, j, :], yv[:, j, :], AF.Gelu_apprx_tanh)
                k.cp(zb[:, j, :], zf[:, j, :], eng='pool')
            sg_, sgs = sstg.next()
            for jo in range(2):
                pg = self.bank()
                for ji in range(2):
                    k.mm(pg[:], wglu[:, ji, jo * 128:(jo + 1) * 128], zb[:, ji, :], start=(ji == 0), stop=(ji == 1))
                k.act(sgm[:], pg[:], AF.Sigmoid, bias=bglu[:, jo:jo + 1])
                k.tt(sg_[:, jo, :], zf[:, jo, :], sgm[:], ALU.mult)
            k.dma('sp', self.mix_scr.ap()[0, :, t0:t0 + 512].rearrange("(j p) t -> p j t", p=128), sg_[:], sgs)
        S.close()

    def zero_branch(self, br):
        k = self.k
        S = Scope(k)
        zt = S.sb('zt', [128, 2, 512], BF16)
        k.memset(zt[:], 0.0)
        for b in range(self.L // 512):
            k.dma('sp', self.mix_scr.ap()[br, :, b * 512:(b + 1) * 512].rearrange("(j p) t -> p j t", p=128), zt[:], 'zt')
        S.close()

    def phase_C(self, l):
        k = self.k
        L = self.L
        NB = L // 512
        S = Scope(k)
        w = self.w
        xsrc = self.x_in if l == 0 else self.y
        Wg = S.sb('Wg', [128, 8, 4096], BF16)
        for kk in range(8):
            k.dma('pool', Wg[:, kk, :], w['w_in'].ap()[l, kk * 128:(kk + 1) * 128, NMIX:NMIX + 4096], 'Wg')
        wbr = S.sb('wbr', [128, 4, 2, D], BF16)
        for i, nm in enumerate(('w_br_s5', 'w_br_sgu', 'w_br_dn', 'w_br_diff')):
            k.dma('pool', wbr[:, i, :, :], w[nm].ap()[l].rearrange("(j p) n -> p j n", p=128), 'wbr')
        wout = S.sb('wout', [128, 8, D], BF16)
        k.dma('pool', wout[:], w['w_out'].ap()[l].rearrange("(j p) n -> p j n", p=128), 'wout')
        xts = Rot(S, 'xt', 2, [128, 4, D], F32)
        hTs = Rot(S, 'hT', 2, [128, 8, 512], BF16)
        mixs = Rot(S, 'mx', 2, [128, 4, 2, 512], BF16)
        sig = Rot(S, 'sig', 2, [128, 512], F32)
        mg = S.sb('mg', [128, 512], F32)
        tmpm = S.sb('tmpm', [128, 512], F32)
        mgT = S.sb('mgT', [128, 8, 512], BF16)
        for b in range(NB):
            t0 = b * 512
            xt, xsem = xts.next()
            k.dma('sp', xt[:], xsrc.ap()[t0:t0 + 512, :].rearrange("(t p) d -> p t d", p=128), xsem)
            hT, hsem = hTs.next()
            k.dma('sp', hT[:], self.hT_scr.ap()[:, t0:t0 + 512].rearrange("(k p) t -> p k t", p=128), hsem)
            mx, msem = mixs.next()
            for br in range(4):
                k.dma('sp', mx[:, br, :, :], self.mix_scr.ap()[br, :, t0:t0 + 512].rearrange("(j p) t -> p j t", p=128), msem)
            for mo in range(8):
                for br in range(4):
                    pg = self.bank()
                    c0 = br * D + mo * 128
                    for kk in range(8):
                        k.mm(pg[:], Wg[:, kk, c0:c0 + 128], hT[:, kk, :], start=(kk == 0), stop=(kk == 7))
                    sg, _ = sig.next()
                    k.act(sg[:], pg[:], AF.Sigmoid)
                    py = self.bank()
                    for j in range(2):
                        k.mm(py[:], wbr[:, br, j, mo * 128:(mo + 1) * 128], mx[:, br, j, :], start=(j == 0), stop=(j == 1))
                    if br == 0:
                        k.tt(mg[:], py[:], sg[:], ALU.mult)
                    elif br < 3:
                        k.tt(tmpm[:], py[:], sg[:], ALU.mult)
                        k.tt(mg[:], mg[:], tmpm[:], ALU.add, eng='pool')
                    else:
                        k.tt(tmpm[:], py[:], sg[:], ALU.mult)
                        k.tt(mgT[:, mo, :], mg[:], tmpm[:], ALU.add, eng='pool')
            xn, xns = xt, xsem
            for t in range(4):
                for hf in range(2):
                    po = self.bank()
                    for kk in range(8):
                        k.mm(po[:], mgT[:, kk, t * 128:(t + 1) * 128], wout[:, kk, hf * 512:(hf + 1) * 512],
                             start=(kk == 0), stop=(kk == 7))
                    k.tt(xn[:, t, hf * 512:(hf + 1) * 512], po[:], xt[:, t, hf * 512:(hf + 1) * 512], ALU.add)
            k.dma('sp', self.y.ap()[t0:t0 + 512, :].rearrange("(t p) d -> p t d", p=128), xn[:], xns)
        S.close()

    def phase_D(self, l, p, last):
        k = self.k
        if 'bi0' in self.flags:
            self.bi = 0
        L = self.L
        NB = L // 512
        S = Scope(k)
        w = self.w
        HC = DFF // 2
        Wup = S.sb('Wup', [128, 8, 2, HC], BF16)
        for kk in range(8):
            for which in range(2):
                c0 = which * DFF + p * HC
                k.dma('pool', Wup[:, kk, which, :], w['ffn_w_up'].ap()[l, kk * 128:(kk + 1) * 128, c0:c0 + HC], 'Wup')
        Wdn = S.sb('Wdn', [128, 11, D], BF16)
        k.dma('pool', Wdn[:], w['ffn_w_down'].ap()[l, p * HC:(p + 1) * HC, :].rearrange("(j p) n -> p j n", p=128), 'Wdn')
        g2col = self.load_cols(S, w['norm_ffn_g'].ap()[l].rearrange("(k p) -> k p", p=128), 8, 'g2col')
        cw = self.load_cols(S, w['ffn_conv_w'].ap()[l].rearrange("k (j p) -> (k j) p", p=128), 132, 'cw')
        cb = self.load_cols(S, w['ffn_conv_b'].ap()[l].rearrange("(j p) -> j p", p=128), 44, 'cb')
        fin = last and p == 1
        if fin:
            gfin = self.load_row_bcast(S, w['norm_final_g'].ap(), D, 'gfin')
        xts = Rot(S, 'xt', 2, [128, 4, D], F32)
        hTs = Rot(S, 'hT', 2, [128, 8, 512], BF16)
        hs = S.sb('hs', [128, 4, D], BF16)
        junk = S.sb('junk', [128, D], F32)
        sss = Rot(S, 'ss', 2, [128, 16], F32)
        ups = Rot(S, 'ups', 3, [128, 514], F32)
        halo = S.sb('halo', [128, 22, 2], F32)
        k.memset(halo[:], 0.0)
        cv = Rot(S, 'cv', 3, [128, 512], F32)
        cg = Rot(S, 'cg', 3, [128, 512], F32)
        sgl = Rot(S, 'sgl', 2, [128, 512], F32)
        actT = S.sb('actT', [128, 11, 512], BF16)
        for b in range(NB):
            t0 = b * 512
            xt, xsem = xts.next()
            k.dma('sp', xt[:], self.y.ap()[t0:t0 + 512, :].rearrange("(t p) d -> p t d", p=128), xsem)
            hT, hsem = hTs.next()
            ss, _ = sss.next()
            hview = self.hT_scr.ap()[:, t0:t0 + 512].rearrange("(k p) t -> p k t", p=128)
            if p == 0:
                self.rms_block(xt, g2col, hT, junk, ss, hs)
                k.dma('sp', hview, hT[:], hsem)
            else:
                k.dma('sp', hT[:], hview, hsem)
            pend = []

            def tail(i_, res_):
                sl, _ = sgl.next()
                k.act(sl[:], res_[1][:], AF.Silu)
                k.tt(actT[:, i_, :], sl[:], res_[0][:], ALU.mult, eng='pool')
            for i in range(11):
                res = []
                for which in range(2):
                    j = which * 22 + p * 11 + i
                    hj = which * 11 + i
                    pu = self.bank()
                    for kk in range(8):
                        k.mm(pu[:], Wup[:, kk, which, i * 128:(i + 1) * 128], hT[:, kk, :], start=(kk == 0), stop=(kk == 7))
                    up, _ = ups.next()
                    k.cp(up[:, 2:514], pu[:], eng='act')
                    k.cp(up[:, 0:2], halo[:, hj, :], eng='pool')
                    c, _ = (cv if which == 0 else cg).next()
                    k.act(c[:], pu[:], AF.Identity, scale=cw[:, 2 * 44 + j:2 * 44 + j + 1], bias=cb[:, j:j + 1])
                    k.stt(c[:], up[:, 1:513], cw[:, 44 + j:44 + j + 1], c[:], ALU.mult, ALU.add)
                    k.stt(c[:], up[:, 0:512], cw[:, j:j + 1], c[:], ALU.mult, ALU.add)
                    k.cp(halo[:, hj, :], up[:, 512:514], eng='pool')
                    res.append(c)
                pend.append((i, res))
                if len(pend) > 1:
                    tail(*pend.pop(0))
            while pend:
                tail(*pend.pop(0))
            for t in range(4):
                for hf in range(2):
                    po = self.bank()
                    for i in range(11):
                        k.mm(po[:], actT[:, i, t * 128:(t + 1) * 128], Wdn[:, i, hf * 512:(hf + 1) * 512],
                             start=(i == 0), stop=(i == 10))
                    k.tt(xt[:, t, hf * 512:(hf + 1) * 512], po[:], xt[:, t, hf * 512:(hf + 1) * 512], ALU.add)
            if fin:
                ss2, _ = sss.next()
                for t in range(4):
                    k.sumsq(junk[:], xt[:, t, :], ss2[:, t:t + 1])
                k.ts(ss2[:, 4:8], ss2[:, 0:4], 1.0 / D, ALU.mult, EPS, ALU.add)
                k.act(ss2[:, 8:12], ss2[:, 4:8], AF.Sqrt)
                k.recip(ss2[:, 12:16], ss2[:, 8:12])
                for t in range(4):
                    k.stt(xt[:, t, :], xt[:, t, :], ss2[:, 12 + t:13 + t], gfin[:], ALU.mult, ALU.mult)
            k.dma('sp', self.y.ap()[t0:t0 + 512, :].rearrange("(t p) d -> p t d", p=128), xt[:], xsem)
        S.close()

    def build(self):
        fl = self.flags
        if 'onlyA' in fl:
            self.phase_A(0)
            self.k.barrier()
            return self.k.nc
        for l in range(self.depth):
            self.phase_A(l)
            if 'dn' in fl and 'att' in fl and 'seq' not in fl:
                if 's5' in fl:
                    self.phase_s5(l)
                else:
                    self.zero_branch(0)
                self.run_gens([(self.att_gen(l, [(6, 7)]), [3, 4, 5]), (self.dn_gen(l, 1), [0, 1, 2])], ratio=[3, 1])
            else:
                for br, nm in ((0, 's5'), (2, 'dn'), (3, 'att')):
                    if nm in fl:
                        getattr(self, 'phase_' + nm)(l)
                    else:
                        self.zero_branch(br)
            if 'noCD' in fl:
                continue
            self.phase_C(l)
            if 'stopC' in fl:
                break
            self.phase_D(l, 0, last=(l == self.depth - 1))
            if 'stopD0' in fl:
                break
            self.phase_D(l, 1, last=(l == self.depth - 1))
        self.k.barrier()
        return self.k.nc


_CACHE = {}


def run_cores(xs, weights, L, depth, flags):
    key = (L, depth, tuple(sorted(flags)))
    if key not in _CACHE:
        _CACHE[key] = Prog(L, depth, flags).build()
    nc = _CACHE[key]
    consts = host_consts()
    in_maps = []
    for x in xs:
        m = {"x": np.ascontiguousarray(x, dtype=np.float32)}
        for name in WEIGHT_SHAPES:
            m[name] = np.ascontiguousarray(weights[name], dtype=np.float32)
        m.update(consts)
        in_maps.append(m)
    res = run_bass_kernel_spmd(nc, in_maps, core_ids=list(range(len(xs))))
    if "dbg" in flags:
        return [(r["y"], r["mix_scr"], r["hT_scr"]) for r in res.results]
    return [r["y"] for r in res.results]


def kernel(**inputs):
    x = np.asarray(inputs['x'])
    B, L, _ = x.shape
    weights = {n: np.asarray(inputs[n]) for n in WEIGHT_SHAPES}
    outs = run_cores([x[b] for b in range(B)], weights, L, DEPTH, ('s5', 'dn', 'att'))
    return np.stack(outs, axis=0).astype(np.float32)
```

```python
import math
from contextlib import ExitStack

import numpy as np
import ml_dtypes
import concourse.bass as bass
import concourse.mybir as mybir
from concourse.bass_utils import run_bass_kernel_spmd

F32 = mybir.dt.float32
BF16 = mybir.dt.bfloat16
AF = mybir.ActivationFunctionType
ALU = mybir.AluOpType
AX = mybir.AxisListType
SAME_ENGINE_SYNC = True

D = 1024
DEPTH = 4
NMIX = 2568
DFF = 2816
EPS = 1e-6


class KB:
    ENG = ('pe', 'act', 'dve', 'pool', 'sp')

    def __init__(self):
        self.nc = bass.Bass("TRN2", target_bir_lowering=False)
        nc = self.nc
        self.e = {'pe': nc.tensor, 'act': nc.scalar, 'dve': nc.vector, 'pool': nc.gpsimd, 'sp': nc.sync}
        self.semh = {}
        self.semv = {}
        for k in self.ENG:
            self.semh['E:' + k] = nc.alloc_semaphore(name="s_" + k)
            self.semv['E:' + k] = 0
        self.waited = {k: {} for k in self.ENG}
        self.lastw = {}
        self.reads = {}
        self.nins = {k: 0 for k in self.ENG}
        self.nwait = 0
        self._uid = 0

    def uid(self):
        self._uid += 1
        return self._uid

    @staticmethod
    def R(x):
        if isinstance(x, (str, tuple)):
            return x
        return x.name

    def _wait(self, eng, key, val):
        w = self.waited[eng]
        if w.get(key, 0) >= val:
            return
        self.e[eng].wait_ge(self.semh[key], val)
        self.nwait += 1
        w[key] = val

    def _deps(self, eng, reads, writes):
        best = {}
        for r in reads:
            ev = self.lastw.get(r)
            if ev is not None:
                best[ev[0]] = max(best.get(ev[0], 0), ev[1])
        for w_ in writes:
            ev = self.lastw.get(w_)
            if ev is not None:
                best[ev[0]] = max(best.get(ev[0], 0), ev[1])
            for k, v in self.reads.get(w_, {}).items():
                best[k] = max(best.get(k, 0), v)
        for k, v in best.items():
            if k == 'E:' + eng and (eng == 'pe' or not SAME_ENGINE_SYNC):
                continue
            self._wait(eng, k, v)

    def _commit(self, ev, reads, writes):
        for w_ in writes:
            self.lastw[w_] = ev
            self.reads[w_] = {}
        for r in reads:
            d = self.reads.setdefault(r, {})
            d[ev[0]] = max(d.get(ev[0], 0), ev[1])

    def op(self, eng, fn, reads, writes):
        pw = [r for r in reads if r is not None and not isinstance(r, (str, tuple)) and self._is_ps(r)]
        reads = [self.R(r) for r in reads if r is not None]
        writes = [self.R(w) for w in writes if w is not None] + [self.R(r) for r in pw]
        self._deps(eng, reads, writes)
        ins = fn()
        key = 'E:' + eng
        self.semv[key] += 1
        ins.then_inc(self.semh[key], 1)
        self.nins[eng] += 1
        self._commit((key, self.semv[key]), reads, writes)
        return ins

    def dma(self, q, out, in_, sem, **kw):
        reads = [self.R(in_)] if self._is_sb(in_) else []
        writes = [self.R(out)] if self._is_sb(out) else []
        self._deps(q, reads, writes)
        key = 'D:' + sem
        if key not in self.semh:
            self.semh[key] = self.nc.alloc_semaphore(name="d%d" % len(self.semh))
            self.semv[key] = 0
        ins = self.e[q].dma_start(out=out, in_=in_, **kw)
        self.semv[key] += 16
        ins.then_inc(self.semh[key], 16)
        self.nins[q] += 1
        self._commit((key, self.semv[key]), reads, writes)
        return ins

    @staticmethod
    def _is_ps(ap):
        t = getattr(ap, 'tensor', ap)
        return type(t).__name__.startswith('PS')

    @staticmethod
    def _is_sb(ap):
        return type(ap.tensor).__name__.startswith('SB')

    def barrier(self, engs=None):
        for eng in (engs or self.ENG):
            for key, val in self.semv.items():
                if val > 0:
                    self._wait(eng, key, val)

    def mm(self, out, lhsT, rhs, start=True, stop=True, **kw):
        nc = self.nc
        return self.op('pe', lambda: nc.tensor.matmul(out, lhsT, rhs, start=start, stop=stop,
                                                      skip_group_check=True, **kw), [lhsT, rhs], [out])

    def tr(self, out, in_, ident):
        nc = self.nc
        return self.op('pe', lambda: nc.tensor.transpose(out, in_, ident), [in_, ident], [out])

    def act(self, out, in_, func, bias=None, scale=None):
        nc = self.nc
        kw = {}
        rr = [in_]
        if bias is not None:
            kw['bias'] = bias
            if not isinstance(bias, (int, float)):
                rr.append(bias)
        if scale is not None:
            kw['scale'] = scale
            if not isinstance(scale, (int, float)):
                rr.append(scale)
        return self.op('act', lambda: nc.scalar.activation(out, in_, func, **kw), rr, [out])

    def tt(self, out, in0, in1, op, eng='dve'):
        e = self.e[eng]
        return self.op(eng, lambda: e.tensor_tensor(out, in0, in1, op), [in0, in1], [out])

    def ts(self, out, in0, s1, op0, s2=None, op1=None, eng='dve'):
        e = self.e[eng]
        rr = [in0] + [s for s in (s1, s2) if s is not None and not isinstance(s, (int, float))]
        if op1 is None:
            return self.op(eng, lambda: e.tensor_scalar(out, in0, s1, None, op0), rr, [out])
        return self.op(eng, lambda: e.tensor_scalar(out, in0, s1, s2, op0, op1), rr, [out])

    def stt(self, out, in0, scalar, in1, op0, op1):
        nc = self.nc
        rr = [in0, in1] + ([scalar] if not isinstance(scalar, (int, float)) else [])
        return self.op('dve', lambda: nc.vector.scalar_tensor_tensor(out, in0, scalar, in1, op0, op1), rr, [out])

    def sumsq(self, junk, x, accum):
        nc = self.nc
        self.act(junk, x, AF.Square)
        return self.op('dve', lambda: nc.vector.tensor_reduce(accum, junk, AX.X, ALU.add), [junk], [accum])

    def cp(self, out, in_, eng='dve'):
        if eng == 'act':
            nc = self.nc
            return self.op('act', lambda: nc.scalar.copy(out, in_), [in_], [out])
        e = self.e[eng]
        return self.op(eng, lambda: e.tensor_copy(out, in_), [in_], [out])

    def memset(self, ap, val, eng='dve'):
        e = self.e[eng]
        return self.op(eng, lambda: e.memset(ap, val), [], [ap])

    def recip(self, out, in_):
        nc = self.nc
        return self.op('dve', lambda: nc.vector.reciprocal(out, in_), [in_], [out])


class Scope:
    def __init__(self, k):
        self.k = k
        self.es = ExitStack()

    def sb(self, name, shape, dt):
        return self.es.enter_context(self.k.nc.sbuf_tensor("%s_%d" % (name, self.k.uid()), list(shape), dt))

    def close(self):
        self.k.barrier()
        self.es.close()


class Rot:
    def __init__(self, S, name, n, shape, dt):
        self.bufs = [(S.sb("%s%d" % (name, i), shape, dt), "%s%d" % (name, i)) for i in range(n)]
        self.i = 0

    def next(self):
        b = self.bufs[self.i % len(self.bufs)]
        self.i += 1
        return b


WEIGHT_SHAPES = {
    'norm_mix_g': (4, 1024), 'w_in': (4, 1024, 6664), 's5_lam_re': (4, 16, 64), 's5_lam_im': (4, 16, 64),
    's5_log_step': (4, 16), 's5_b_re': (4, 16, 64, 16), 's5_b_im': (4, 16, 64, 16), 's5_c_re': (4, 16, 16, 64),
    's5_c_im': (4, 16, 16, 64), 's5_d': (4, 256), 's5_w_glu': (4, 256, 256), 's5_b_glu': (4, 256),
    'sgu_norm_g': (4, 256), 'sgu_w_s': (4, 4, 128, 128), 'sgu_b_s': (4, 4, 128), 'dn_conv_w': (4, 4, 768),
    'dn_a_log': (4, 4), 'dn_dt_bias': (4, 4), 'dn_norm_g': (4, 64), 'diff_lq1': (4, 32), 'diff_lk1': (4, 32),
    'diff_lq2': (4, 32), 'diff_lk2': (4, 32), 'diff_norm_g': (4, 64), 'w_br_s5': (4, 256, 1024),
    'w_br_sgu': (4, 256, 1024), 'w_br_dn': (4, 256, 1024), 'w_br_diff': (4, 256, 1024), 'w_out': (4, 1024, 1024),
    'norm_ffn_g': (4, 1024), 'ffn_w_up': (4, 1024, 5632), 'ffn_conv_w': (4, 3, 5632), 'ffn_conv_b': (4, 5632),
    'ffn_w_down': (4, 2816, 1024), 'norm_final_g': (1024,),
}


def host_consts():
    c = {}
    c['c_identb'] = np.eye(128).astype(ml_dtypes.bfloat16)
    c['c_identf'] = np.eye(128, dtype=np.float32)
    i = np.arange(128)
    c['c_mui'] = (i[:, None] <= i[None, :]).astype(np.float32)
    c['c_msu'] = (i[:, None] < i[None, :]).astype(np.float32)
    c['c_msl'] = (i[:, None] > i[None, :]).astype(np.float32)
    slopes = [2.0 ** (-8.0 * (h + 1) / 4) for h in range(4)]
    c['c_ekey'] = np.stack([np.exp(sl * i) for sl in slopes], axis=1).astype(np.float32)
    d = np.arange(67) - 3
    c['c_abias'] = np.tile(np.concatenate([-sl * 128.0 * d for sl in slopes])[None, :], (128, 1)).astype(np.float32)
    q = np.arange(512)
    c['c_cmask'] = np.stack([(q[None, :] >= 128 * j + i[:, None]) for j in range(4)], axis=1).astype(ml_dtypes.bfloat16)
    c['c_m32'] = np.stack([((i // 32) == j) for j in range(4)], axis=1).astype(np.float32)
    c['c_m64'] = np.stack([((i // 64) == j) for j in range(2)], axis=1).astype(np.float32)
    c['c_sel8'] = (np.arange(8)[:, None] == (i[None, :] // 16)).astype(np.float32)
    c['c_bm16'] = ((i[:, None] // 16) == (i[None, :] // 16)).astype(np.float32)
    c['c_w1mask'] = np.stack([((i // 16) == mh) for mh in range(8)], axis=1).astype(np.float32)
    half = i // 64
    c['c_w2mask'] = np.stack([((i[None, :] // 16) == (2 * mp + half[:, None])) for mp in range(4)], axis=1).astype(np.float32)
    return c


class Prog:
    def __init__(self, L, depth, flags):
        self.L = L
        self.depth = depth
        self.flags = flags
        self.lvl = 99
        for f in flags:
            if f.startswith('lvl'):
                self.lvl = int(f[3:])
        self.k = KB()
        k = self.k
        nc = k.nc
        self.x_in = nc.dram_tensor("x", [L, D], F32, kind="ExternalInput")
        self.y = nc.dram_tensor("y", [L, D], F32, kind="ExternalOutput")
        self.w = {}
        for name, shp in WEIGHT_SHAPES.items():
            self.w[name] = nc.dram_tensor(name, list(shp), F32, kind="ExternalInput")
        self.c = {}
        for name, arr in host_consts().items():
            dt = BF16 if arr.dtype == ml_dtypes.bfloat16 else F32
            self.c[name] = nc.dram_tensor(name, list(arr.shape), dt, kind="ExternalInput")
        self.hT_scr = nc.dram_tensor("hT_scr", [D, L], BF16, kind=("ExternalOutput" if "dbg" in flags else "Internal"))
        self.uT_scr = nc.dram_tensor("uT_scr", [256, L], BF16, kind="Internal")
        self.qkvc_scr = nc.dram_tensor("qkvc_scr", [768, L], BF16, kind="Internal")
        self.ab_scr = nc.dram_tensor("ab_scr", [128, (L // 128) * 8], F32, kind="Internal")
        self.z_scr = nc.dram_tensor("z_scr", [L, 256], BF16, kind="Internal")
        self.aq_scr = nc.dram_tensor("aq_scr", [256, L], BF16, kind="Internal")
        self.ak_scr = nc.dram_tensor("ak_scr", [256, L], BF16, kind="Internal")
        self.av_scr = nc.dram_tensor("av_scr", [L, 256], BF16, kind="Internal")
        self.mix_scr = nc.dram_tensor("mix_scr", [4, 256, L], BF16, kind=("ExternalOutput" if "dbg" in flags else "Internal"))
        self.banks = [nc.alloc_psum_tensor("bank%d" % i, [128, 512], F32) for i in range(8)]
        self.bi = 0
        self.bank_pool = list(range(8))
        self.G = Scope(k)
        self.identb = self.G.sb('identb', [128, 128], BF16)
        self.identf = self.G.sb('identf', [128, 128], F32)
        k.dma('sp', self.identb[:], self.c['c_identb'].ap(), 'identb')
        k.dma('sp', self.identf[:], self.c['c_identf'].ap(), 'identf')
        self.mui = self.G.sb('mui', [128, 128], F32)
        k.dma('sp', self.mui[:], self.c['c_mui'].ap(), 'mui')
        self.ones1 = self.G.sb('ones1', [1, 128], F32)
        k.memset(self.ones1[:], 1.0)

    def bank(self):
        pool = self.bank_pool
        b = self.banks[pool[self.bi % len(pool)]]
        self.bi += 1
        return b

    def load_cols(self, S, src2d, n, name):
        k = self.k
        dst = S.sb(name, [128, n], F32)
        done = 0
        while done < n:
            m = min(128, n - done)
            tmp = S.sb(name + 'r', [m, 128], F32)
            k.dma('sp', tmp[:], src2d[done:done + m, :], name + 'r%d' % done)
            pb = self.bank()
            k.tr(pb[:, 0:m], tmp[:], self.identf[0:m, 0:m])
            k.cp(dst[:, done:done + m], pb[:, 0:m])
            done += m
        return dst

    def load_row_bcast(self, S, src1d, n, name):
        k = self.k
        row = S.sb(name + 'r', [1, n], F32)
        k.dma('sp', row[:], src1d.unsqueeze(0), name + 'r')
        dst = S.sb(name, [128, n], F32)
        for c0 in range(0, n, 512):
            m = min(512, n - c0)
            pb = self.bank()
            k.mm(pb[:, 0:m], self.ones1[:, :], row[:, c0:c0 + m])
            k.cp(dst[:, c0:c0 + m], pb[:, 0:m])
        return dst

    def rms_block(self, xt, gcol, hT, junk, ss, hs):
        k = self.k
        for t in range(4):
            k.sumsq(junk[:], xt[:, t, :], ss[:, t:t + 1])
        k.ts(ss[:, 4:8], ss[:, 0:4], 1.0 / D, ALU.mult, EPS, ALU.add)
        k.act(ss[:, 8:12], ss[:, 4:8], AF.Sqrt)
        k.recip(ss[:, 12:16], ss[:, 8:12])
        for t in range(4):
            k.act(hs[:, t, :], xt[:, t, :], AF.Identity, scale=ss[:, 12 + t:13 + t])
        for kk in range(8):
            pb = self.bank()
            pv = pb[:].bitcast(BF16)
            for t in range(4):
                k.tr(pv[:, t * 128:(t + 1) * 128], hs[:, t, kk * 128:(kk + 1) * 128], self.identb[:])
            if kk % 2 == 0:
                k.ts(hT[:, kk, :], pv[:, 0:512], gcol[:, kk:kk + 1], ALU.mult)
            else:
                k.act(hT[:, kk, :], pv[:, 0:512], AF.Identity, scale=gcol[:, kk:kk + 1])

    def phase_A(self, l):
        k = self.k
        L = self.L
        NB = L // 512
        S = Scope(k)
        w = self.w
        xsrc = self.x_in if l == 0 else self.y
        WinA = S.sb('WinA', [128, 8, NMIX], BF16)
        for kk in range(8):
            k.dma('pool', WinA[:, kk, :], w['w_in'].ap()[l, kk * 128:(kk + 1) * 128, 0:NMIX], 'WinA')
        g1col = self.load_cols(S, w['norm_mix_g'].ap()[l].rearrange("(k p) -> k p", p=128), 8, 'g1col')
        wsT = S.sb('wsT', [128, 4, 128], BF16)
        wtmp = S.sb('wtmp', [128, 4, 128], F32)
        k.dma('sp', wtmp[:], w['sgu_w_s'].ap()[l].rearrange("g t s -> t g s"), 'wtmp')
        for g in range(4):
            pb = self.bank()
            k.tr(pb[:, 0:128], wtmp[:, g, :], self.identf[:])
            k.tt(wsT[:, g, :], pb[:, 0:128], self.mui[:], ALU.mult)
        sgng = self.load_row_bcast(S, w['sgu_norm_g'].ap()[l], 256, 'sgng')
        bsT = self.load_cols(S, w['sgu_b_s'].ap()[l], 4, 'bsT')
        cwc = self.load_cols(S, w['dn_conv_w'].ap()[l].rearrange("k (j p) -> (k j) p", p=128), 24, 'cwc')
        diagW = S.sb('diagW', [128, 6, 4, 128], BF16)
        for j in range(6):
            for kk in range(4):
                k.ts(diagW[:, j, kk, :], self.identf[:], cwc[:, kk * 6 + j:kk * 6 + j + 1], ALU.mult)
        xts = Rot(S, 'xt', 2, [128, 4, D], F32)
        hTs = Rot(S, 'hT', 2, [128, 8, 512], BF16)
        hs = S.sb('hs', [128, 4, D], BF16)
        junk = S.sb('junk', [128, D], F32)
        sss = Rot(S, 'ss', 2, [128, 16], F32)
        stg = Rot(S, 'stg', 4, [128, 512], BF16)
        abt = Rot(S, 'abt', 2, [128, 8], F32)
        qraw = S.sb('qraw', [128, 6, 515], BF16)
        k.memset(qraw[:, :, 0:3], 0.0)
        uvs = Rot(S, 'uv', 3, [128, 512], F32)
        vn = S.sb('vn', [128, 256], F32)
        vnbs = Rot(S, 'vnb', 3, [128, 256], BF16)
        st6 = S.sb('st6', [128, 8], F32)
        mxs = S.sb('mxs', [128, 256], F32)
        smix = S.sb('smix', [128, 256], BF16)
        smT = Rot(S, 'smT', 2, [128, 2, 512], BF16)
        ztm = Rot(S, 'ztm', 2, [128, 256], BF16)
        vtm = Rot(S, 'vtm', 2, [128, 256], BF16)

        def ldx(bb):
            xt_, xsem_ = xts.next()
            k.dma('sp', xt_[:], xsrc.ap()[bb * 512:bb * 512 + 512, :].rearrange("(t p) d -> p t d", p=128), xsem_)
            return xt_, xsem_
        nxt_x = ldx(0)
        for b in range(NB):
            t0 = b * 512
            xt, xsem = nxt_x
            if b + 1 < NB:
                nxt_x = ldx(b + 1)
            hT, hsem = hTs.next()
            ss, _ = sss.next()
            self.rms_block(xt, g1col, hT, junk, ss, hs)
            k.dma('sp', self.hT_scr.ap()[:, t0:t0 + 512].rearrange("(k p) t -> p k t", p=128), hT[:], hsem)

            def fm(c0, M):
                pb = self.bank()
                for kk in range(8):
                    k.mm(pb[0:M, :], WinA[:, kk, c0:c0 + M], hT[:, kk, :], start=(kk == 0), stop=(kk == 7))
                return pb

            if self.lvl < 2:
                continue
            for j in range(2):
                pb = fm(j * 128, 128)
                sg, sgs = stg.next()
                k.cp(sg[:], pb[:], eng='act')
                k.dma('sp', self.uT_scr.ap()[j * 128:(j + 1) * 128, t0:t0 + 512], sg[:], sgs)
            if self.lvl < 3:
                continue
            for j in range(6):
                pb = fm(768 + j * 128, 128)
                k.cp(qraw[:, j, 3:515], pb[:], eng=('act' if j % 2 else 'dve'))
            for j in range(6):
                pb = self.bank()
                for kk in range(4):
                    k.mm(pb[:], diagW[:, j, kk, :], qraw[:, j, kk:kk + 512], start=(kk == 0), stop=(kk == 3))
                sg, sgs = stg.next()
                k.act(sg[:], pb[:], AF.Silu)
                k.dma('sp', self.qkvc_scr.ap()[j * 128:(j + 1) * 128, t0:t0 + 512], sg[:], sgs)
            k.cp(qraw[:, :, 0:3], qraw[:, :, 512:515], eng='pool')
            if self.lvl < 5:
                continue
            for (c0, scr) in ((1800, self.aq_scr), (2056, self.ak_scr)):
                for j in range(2):
                    pb = fm(c0 + j * 128, 128)
                    sg, sgs = stg.next()
                    k.cp(sg[:], pb[:], eng=('act' if j else 'dve'))
                    k.dma('sp', scr.ap()[j * 128:(j + 1) * 128, t0:t0 + 512], sg[:], sgs)
            if self.lvl < 6:
                continue
            sm, sms = smT.next()

            def stage1(t):
                tok = slice(t * 128, (t + 1) * 128)
                uv_, _ = uvs.next()
                vnb_, _ = vnbs.next()
                pb = self.bank()
                for kk in range(8):
                    k.mm(pb[:], hT[:, kk, tok], WinA[:, kk, 256:768], start=(kk == 0), stop=(kk == 7))
                k.act(uv_[:], pb[:], AF.Gelu_apprx_tanh)
                pz = self.bank()
                for kk in range(8):
                    k.mm(pz[:, 0:256], hT[:, kk, tok], WinA[:, kk, 1536:1792], start=(kk == 0), stop=(kk == 7))
                for kk in range(8):
                    k.mm(pz[:, 256:512], hT[:, kk, tok], WinA[:, kk, 2312:2568], start=False, stop=(kk == 7))
                pab = self.bank()
                for kk in range(8):
                    k.mm(pab[:, 0:8], hT[:, kk, tok], WinA[:, kk, 1792:1800], start=(kk == 0), stop=(kk == 7))
                k.op('dve', lambda: k.nc.vector.bn_stats(st6[:, 0:6], uv_[:, 256:512]), [uv_], [st6])
                k.op('dve', lambda: k.nc.vector.bn_aggr(st6[:, 6:8], st6[:, 0:6]), [st6], [st6])
                k.ts(st6[:, 7:8], st6[:, 7:8], EPS, ALU.add)
                k.act(st6[:, 7:8], st6[:, 7:8], AF.Sqrt)
                k.recip(st6[:, 7:8], st6[:, 7:8])
                k.ts(vn[:], uv_[:, 256:512], st6[:, 6:7], ALU.subtract, st6[:, 7:8], ALU.mult)
                k.tt(vnb_[:], vn[:], sgng[:], ALU.mult)
                zt, zts = ztm.next()
                k.cp(zt[:], pz[:, 0:256], eng='act')
                k.dma('sp', self.z_scr.ap()[t0 + t * 128:t0 + (t + 1) * 128, :], zt[:], zts)
                vt, vts = vtm.next()
                k.cp(vt[:], pz[:, 256:512], eng='pool' if False else 'dve')
                k.dma('sp', self.av_scr.ap()[t0 + t * 128:t0 + (t + 1) * 128, :], vt[:], vts)
                at_, ats = abt.next()
                k.cp(at_[:], pab[:, 0:8], eng='act')
                nn = b * 4 + t
                k.dma('sp', self.ab_scr.ap()[:, nn * 8:(nn + 1) * 8], at_[:], ats)
                return (t, uv_, vnb_)

            def stage2(t, uv_, vnb_):
                tok = slice(t * 128, (t + 1) * 128)
                pm = self.bank()
                for g in range(4):
                    k.mm(pm[:, g * 64:(g + 1) * 64], wsT[:, g, :], vnb_[:, g * 64:(g + 1) * 64])
                k.tt(mxs[:].rearrange("p (g c) -> p g c", g=4), pm[:, 0:256].rearrange("p (g c) -> p g c", g=4),
                     bsT[:].unsqueeze(2).broadcast_to([128, 4, 64]), ALU.add)
                k.tt(smix[:], mxs[:], uv_[:, 0:256], ALU.mult)
                pt = self.bank()
                ptv = pt[:].bitcast(BF16)
                for j in range(2):
                    k.tr(ptv[:, j * 128:(j + 1) * 128], smix[:, j * 128:(j + 1) * 128], self.identb[:])
                k.cp(sm[:, :, tok], ptv[:, 0:256].rearrange("p (j t) -> p j t", j=2), eng='act')
            pend = []
            for t in range(4):
                pend.append(stage1(t))
                if len(pend) > 1:
                    stage2(*pend.pop(0))
            while pend:
                stage2(*pend.pop(0))
            k.dma('sp', self.mix_scr.ap()[1, :, t0:t0 + 512].rearrange("(j p) t -> p j t", p=128), sm[:], sms)
        S.close()

    ATT_WIN = (6, 17, 1 << 20, 1 << 20)

    def att_gen(self, l, osets):
        k = self.k
        nc = k.nc
        L = self.L
        NT = L // 128
        NQ = L // 512
        S = Scope(k)
        w = self.w
        lam_init = 0.8 - 0.6 * math.exp(-0.3 * l)
        KT = S.sb('KT', [128, 2, L], BF16)
        QTs = Rot(S, 'QT', 2, [128, 2, 512], BF16)
        for j in range(2):
            k.dma('sp', KT[:, j, :], self.ak_scr.ap()[j * 128:(j + 1) * 128, :], 'KT')
        ekey = S.sb('ekey', [128, 4], F32)
        k.dma('sp', ekey[:], self.c['c_ekey'].ap(), 'ekey')
        abias = S.sb('abias', [128, 4 * 67], F32)
        k.dma('sp', abias[:], self.c['c_abias'].ap(), 'abias')
        cmask = S.sb('cmask', [128, 4, 512], BF16)
        k.dma('sp', cmask[:], self.c['c_cmask'].ap(), 'cmask')
        m32 = S.sb('m32', [128, 4], F32)
        k.dma('sp', m32[:], self.c['c_m32'].ap(), 'm32')
        Vaug = S.sb('Vaug', [128, NT, 4, 65], BF16)
        onesn = S.sb('onesn', [128, NT], F32)
        k.memset(onesn[:], 1.0)
        CH = min(8, NT)
        vtmp = Rot(S, 'vtmp', 2, [128, CH, 256], BF16)
        for n0 in range(0, NT, CH):
            vt, vts = vtmp.next()
            k.dma('sp', vt[:], self.av_scr.ap()[n0 * 128:(n0 + CH) * 128, :].rearrange("(n p) c -> p n c", p=128), vts)
            for h in range(4):
                k.ts(Vaug[:, n0:n0 + CH, h, 0:64], vt[:, :, h * 64:(h + 1) * 64], ekey[:, h:h + 1], ALU.mult)
        for h in range(4):
            k.act(Vaug[:, :, h, 64:65], onesn[:].unsqueeze(2), AF.Identity, scale=ekey[:, h:h + 1])
        lq = [self.load_row_bcast(S, w[n].ap()[l], 32, n) for n in ('diff_lq1', 'diff_lk1', 'diff_lq2', 'diff_lk2')]
        lt = S.sb('lt', [128, 32], F32)
        lv = S.sb('lv', [128, 8], F32)
        for i in range(2):
            k.tt(lt[:], lq[2 * i][:], lq[2 * i + 1][:], ALU.mult)
            k.op('dve', lambda: nc.vector.tensor_reduce(lv[:, i:i + 1], lt[:], AX.X, ALU.add), [lt], [lv])
        k.act(lv[:, 2:4], lv[:, 0:2], AF.Exp)
        k.tt(lv[:, 4:5], lv[:, 2:3], lv[:, 3:4], ALU.subtract)
        k.ts(lv[:, 5:6], lv[:, 4:5], lam_init, ALU.add)
        gd = self.load_row_bcast(S, w['diff_norm_g'].ap()[l], 64, 'gd')
        k.ts(gd[:], gd[:], 1.0 - lam_init, ALU.mult)
        QTm = Rot(S, 'QTm', 4, [128, 512], BF16)
        Pb = Rot(S, 'Pb', 5, [128, 512], BF16)
        rr = S.sb('rr', [128, 16], F32)
        t1 = S.sb('t1', [128, 4, 64], F32)
        t2 = S.sb('t2', [128, 4, 64], F32)
        sq = S.sb('sq', [128, 4, 64], F32)
        omix = Rot(S, 'omix', 2, [128, 4, 256], BF16)
        ostg = Rot(S, 'ostg', 2, [128, 2, 512], BF16)
        Osets = [[self.banks[a_], self.banks[b_]] for (a_, b_) in osets]
        scale = 32.0 ** -0.5
        LA = 2
        gi = 0
        for qb in range(NQ):
            om, _ = omix.next()
            QT, qts_ = QTs.next()
            k.dma('sp', QT[:], self.aq_scr.ap()[:, qb * 512:(qb + 1) * 512].rearrange("(j p) t -> p j t", p=128), qts_)
            for h in range(4):
                t2i = h // 2
                O = Osets[gi % len(Osets)]
                gi += 1
                qm = []
                for m in range(2):
                    qt_, _ = QTm.next()
                    k.ts(qt_[:], QT[:, t2i, :], m32[:, (h % 2) * 2 + m:(h % 2) * 2 + m + 1], ALU.mult,
                         eng=('pool' if m else 'dve'))
                    qm.append(qt_)
                for m in range(2):
                    k.memset(O[m][:, 0:260], 0.0)
                blocks = [(s_, 128) for s_ in range(4)] if h == 0 else [(None, 512)]
                steps = []
                for (s_, QB) in blocks:
                    qt_first = 4 * qb + (s_ if s_ is not None else 0)
                    qt_last = 4 * qb + (s_ if s_ is not None else 3)
                    kt_lo = max(0, qt_first - self.ATT_WIN[h])
                    for kt in range(kt_lo, qt_last + 1):
                        for m in range(2):
                            steps.append((s_, QB, qt_first, kt, m))
                live = {}

                def front(i):
                    s_, QB, qt_first, kt, m = steps[i]
                    jd = kt - qt_first
                    ps = self.bank()
                    qcols = slice(s_ * 128, (s_ + 1) * 128) if s_ is not None else slice(0, 512)
                    k.mm(ps[:, 0:QB], KT[:, t2i, kt * 128:(kt + 1) * 128], qm[m][:, qcols])
                    pT, _ = Pb.next()
                    dd = (qt_first - kt) + 3
                    k.act(pT[:, 0:QB], ps[:, 0:QB], AF.Exp, scale=scale, bias=abias[:, h * 67 + dd:h * 67 + dd + 1])
                    if jd >= 0:
                        k.tt(pT[:, 0:QB], pT[:, 0:QB], cmask[:, jd, 0:QB], ALU.mult, eng='pool')
                    live[i] = pT

                def back(i):
                    s_, QB, qt_first, kt, m = steps[i]
                    jd = kt - qt_first
                    pT = live.pop(i)
                    for ss_ in ([s_] if s_ is not None else range(4)):
                        if s_ is None and ss_ < jd:
                            continue
                        col = 0 if s_ is not None else ss_ * 128
                        k.mm(O[m][:, ss_ * 65:(ss_ + 1) * 65], pT[:, col:col + 128], Vaug[:, kt, h, :],
                             start=False, stop=True)
                for i in range(len(steps) + LA):
                    if i < len(steps):
                        front(i)
                    if i >= LA:
                        back(i - LA)
                    yield
                for m in range(2):
                    k.recip(rr[:, 4 * m:4 * m + 4], O[m][:, 0:260].rearrange("p (s e) -> p s e", e=65)[:, :, 64])
                k.ts(rr[:, 4:8], rr[:, 4:8], lv[:, 5:6], ALU.mult)
                k.tt(t1[:], O[0][:, 0:260].rearrange("p (s e) -> p s e", e=65)[:, :, 0:64],
                     rr[:, 0:4].unsqueeze(2).broadcast_to([128, 4, 64]), ALU.mult)
                k.tt(t2[:], O[1][:, 0:260].rearrange("p (s e) -> p s e", e=65)[:, :, 0:64],
                     rr[:, 4:8].unsqueeze(2).broadcast_to([128, 4, 64]), ALU.mult)
                k.tt(t1[:], t1[:], t2[:], ALU.subtract)
                k.tt(sq[:], t1[:], t1[:], ALU.mult)
                k.op('dve', lambda: nc.vector.tensor_reduce(rr[:, 8:12], sq[:], AX.X, ALU.add), [sq], [rr])
                k.ts(rr[:, 8:12], rr[:, 8:12], 1.0 / 64, ALU.mult, EPS, ALU.add)
                k.act(rr[:, 8:12], rr[:, 8:12], AF.Ln)
                k.act(rr[:, 12:16], rr[:, 8:12], AF.Exp, scale=-0.5)
                yield
                k.tt(t1[:], t1[:], rr[:, 12:16].unsqueeze(2).broadcast_to([128, 4, 64]), ALU.mult)
                k.tt(om[:, :, h * 64:(h + 1) * 64], t1[:], gd[:].unsqueeze(1).broadcast_to([128, 4, 64]), ALU.mult)
            og, ogs = ostg.next()
            for s_ in range(4):
                pt = self.bank()
                ptv = pt[:].bitcast(BF16)
                for j in range(2):
                    k.tr(ptv[:, j * 128:(j + 1) * 128], om[:, s_, j * 128:(j + 1) * 128], self.identb[:])
                k.cp(og[:, :, s_ * 128:(s_ + 1) * 128], ptv[:, 0:256].rearrange("p (j t) -> p j t", j=2), eng='act')
            k.dma('sp', self.mix_scr.ap()[3, :, qb * 512:(qb + 1) * 512].rearrange("(j p) t -> p j t", p=128), og[:], ogs)
            yield
        return S

    def run_gens(self, specs, ratio=None):
        st = [{'g': g, 'pool': pool, 'bi': 0, 'alive': True, 'S': None} for (g, pool) in specs]
        ratio = ratio or [1] * len(st)
        while any(x['alive'] for x in st):
            for x, r in zip(st, ratio):
                for _ in range(r):
                    if not x['alive']:
                        break
                    self.bank_pool = x['pool']
                    self.bi = x['bi']
                    try:
                        next(x['g'])
                    except StopIteration as e:
                        x['alive'] = False
                        x['S'] = e.value
                    x['bi'] = self.bi
        self.bank_pool = list(range(8))
        self.bi = 0
        self.k.barrier()
        for x in reversed(st):
            x['S'].es.close()

    def phase_att(self, l):
        self.run_gens([(self.att_gen(l, [(4, 5), (6, 7)]), [0, 1, 2, 3])])

    def phase_dn(self, l):
        self.run_gens([(self.dn_gen(l), list(range(8)))])

    def dn_gen(self, l, nsets=2):
        k = self.k
        nc = k.nc
        L = self.L
        NT = L // 128
        NB = L // 512
        S = Scope(k)
        w = self.w
        onesf = S.sb('onesf', [128, 128], F32)
        k.memset(onesf[:], 1.0)
        nmsl = S.sb('nmsl', [128, 128], F32)
        nmsu = S.sb('nmsu', [128, 128], F32)
        k.dma('sp', nmsl[:], self.c['c_msl'].ap(), 'nmsl')
        k.dma('sp', nmsu[:], self.c['c_msu'].ap(), 'nmsu')
        k.ts(nmsl[:], nmsl[:], -1.0, ALU.mult)
        k.ts(nmsu[:], nmsu[:], -1.0, ALU.mult)
        alog = self.load_row_bcast(S, w['dn_a_log'].ap()[l], 4, 'alog')
        dtb = self.load_row_bcast(S, w['dn_dt_bias'].ap()[l], 4, 'dtb')
        k.act(alog[:], alog[:], AF.Exp)
        k.ts(alog[:], alog[:], -1.0, ALU.mult)
        gn = self.load_row_bcast(S, w['dn_norm_g'].ap()[l], 64, 'gn')
        ab = S.sb('ab', [128, NT, 8], F32)
        k.dma('sp', ab[:].rearrange("p n c -> p (n c)"), self.ab_scr.ap(), 'ab')
        gt = S.sb('gt', [128, NT, 4], F32)
        bt = S.sb('bt', [128, NT, 4], F32)
        Gt = S.sb('Gt', [128, NT, 4], F32)
        GL = S.sb('GL', [128, NT, 4], F32)
        eG = S.sb('eG', [128, NT, 4], F32)
        kds = S.sb('kds', [128, NT, 4], F32)
        gla = S.sb('gla', [128, NT, 4], F32)
        k.tt(gt[:], ab[:, :, 0:4], dtb[:].unsqueeze(1).broadcast_to([128, NT, 4]), ALU.add)
        k.act(gt[:], gt[:], AF.Exp)
        k.ts(gt[:], gt[:], 1.0, ALU.add)
        k.act(gt[:], gt[:], AF.Ln)
        k.tt(gt[:], gt[:], alog[:].unsqueeze(1).broadcast_to([128, NT, 4]), ALU.mult)
        k.act(bt[:], ab[:, :, 4:8], AF.Sigmoid)
        gflat = gt[:].rearrange("p n c -> p (n c)")
        for c0 in range(0, NT * 4, 512):
            m = min(512, NT * 4 - c0)
            pc = self.bank()
            k.mm(pc[:, 0:m], self.mui[:], gflat[:, c0:c0 + m])
            k.cp(Gt[:].rearrange("p n c -> p (n c)")[:, c0:c0 + m], pc[:, 0:m])
            pc2 = self.bank()
            k.mm(pc2[:, 0:m], onesf[:], gflat[:, c0:c0 + m])
            k.cp(GL[:].rearrange("p n c -> p (n c)")[:, c0:c0 + m], pc2[:, 0:m])
        k.act(eG[:], Gt[:], AF.Exp)
        k.tt(kds[:], GL[:], Gt[:], ALU.subtract)
        k.act(kds[:], kds[:], AF.Exp)
        k.act(gla[:], GL[:], AF.Exp)
        Sf = S.sb('Sf', [64, 4, 64], F32)
        Sb = S.sb('Sb', [64, 4, 64], BF16)
        k.memset(Sf[:], 0.0)
        k.memset(Sb[:], 0.0)
        qfs = Rot(S, 'qf', 2, [128, 6, 512], BF16)
        zts = Rot(S, 'zt', 2, [128, 4, 256], BF16)
        zsb = S.sb('zsb', [128, 4, 256], F32)

        def alloc_set(i):
            B = {}
            def a(name, shape, dt):
                B[name] = S.sb('%s_%d' % (name, i), shape, dt)
            a('qkv', [128, 768], F32); a('sq', [128, 512], F32); a('r8', [128, 16], F32)
            a('qn_f', [128, 4, 64], F32); a('kn_f', [128, 4, 64], F32); a('kb_f', [128, 4, 64], F32)
            for n_ in ('qn_b', 'kn_b', 'kb_b', 'qd_b', 'kbg_b', 'kdec_b', 'vb_b'):
                a(n_, [128, 4, 64], BF16)
            for n_ in ('kn_b', 'kb_b', 'qn_b', 'qd_b'):
                a(n_ + 'T', [64, 4, 128], BF16)
            a('dG', [128, 4, 128], F32)
            for n_ in ('Xn', 'Xm', 'Dt', 'Dl', 'Dlm', 'Dtm1', 'Dtm2'):
                a(n_, [128, 128], F32)
            for n_ in ('P0', 'P1', 'PT0', 'PT1', 'TT0', 'TT1'):
                a(n_, [128, 4, 128], F32)
            a('TTb', [128, 4, 128], BF16); a('Aq', [128, 4, 128], BF16)
            a('u_sb', [128, 4, 64], F32); a('wwT', [64, 4, 128], BF16); a('vnew', [128, 4, 64], BF16)
            a('o_sb', [128, 4, 64], F32); a('osq', [128, 4, 64], F32); a('dmix', [128, 256], BF16)
            return B
        sets = [alloc_set(i) for i in range(nsets)]
        dstg = Rot(S, 'dstg', 2, [128, 2, 512], BF16)

        def bc(ap2):
            return ap2.unsqueeze(2).broadcast_to([128, 4, 64])

        def rsq(out, in_, mult):
            k.ts(out, in_, mult, ALU.mult, EPS, ALU.add)
            k.act(out, out, AF.Ln)
            k.act(out, out, AF.Exp, scale=-0.5)

        def dn_tile(B, n, t, qf, dg):
            tok = slice(t * 128, (t + 1) * 128)
            qkv, sq, r8 = B['qkv'], B['sq'], B['r8']
            qn_f, kn_f, kb_f = B['qn_f'], B['kn_f'], B['kb_f']
            pt = self.bank()
            ptv = pt[:].bitcast(BF16)
            for j in range(6):
                k.tr(ptv[:, j * 128:(j + 1) * 128], qf[:, j, tok], self.identb[:])
            k.cp(qkv[:], ptv[:, 0:768], eng='act')
            yield
            k.tt(sq[:], qkv[:, 0:512], qkv[:, 0:512], ALU.mult)
            k.op('dve', lambda: nc.vector.tensor_reduce(r8[:, 0:8], sq[:].rearrange("p (a d) -> p a d", d=64), AX.X, ALU.add),
                 [sq], [r8])
            rsq(r8[:, 8:16], r8[:, 0:8], 1.0)
            k.ts(r8[:, 8:12], r8[:, 8:12], 0.125, ALU.mult)
            yield
            q3 = qkv[:, 0:256].rearrange("p (h d) -> p h d", h=4)
            k3 = qkv[:, 256:512].rearrange("p (h d) -> p h d", h=4)
            v3 = qkv[:, 512:768].rearrange("p (h d) -> p h d", h=4)
            k.tt(qn_f[:], q3, bc(r8[:, 8:12]), ALU.mult)
            k.tt(kn_f[:], k3, bc(r8[:, 12:16]), ALU.mult)
            k.tt(kb_f[:], kn_f[:], bc(bt[:, n, :]), ALU.mult)
            k.cp(B['qn_b'][:], qn_f[:], eng='act')
            k.cp(B['kn_b'][:], kn_f[:], eng='act')
            k.cp(B['kb_b'][:], kb_f[:], eng='act')
            yield
            k.tt(B['qd_b'][:], qn_f[:], bc(eG[:, n, :]), ALU.mult)
            k.tt(B['kbg_b'][:], kb_f[:], bc(eG[:, n, :]), ALU.mult, eng='pool')
            k.tt(B['kdec_b'][:], kn_f[:], bc(kds[:, n, :]), ALU.mult)
            k.tt(B['vb_b'][:], v3, bc(bt[:, n, :]), ALU.mult, eng='pool')
            for gi_, (n1, n2) in enumerate((('kn_b', 'kb_b'), ('qn_b', 'qd_b'))):
                pf = self.bank()
                pfv = pf[:].bitcast(BF16)
                for ii, nm in enumerate((n1, n2)):
                    for h in range(4):
                        k.tr(pfv[0:64, ii * 512 + h * 128:ii * 512 + (h + 1) * 128], B[nm][:, h, :], self.identb[:])
                k.cp(B[n1 + 'T'][:].rearrange("p h t -> p (h t)"), pfv[0:64, 0:512], eng='act')
                k.cp(B[n2 + 'T'][:].rearrange("p h t -> p (h t)"), pfv[0:64, 512:1024])
                yield
            knT, kbT, qnT, qdT = B['kn_bT'], B['kb_bT'], B['qn_bT'], B['qd_bT']
            dG, Xn, Xm, Dt, Dl, Dlm, Dtm1, Dtm2 = (B[x_] for x_ in ('dG', 'Xn', 'Xm', 'Dt', 'Dl', 'Dlm', 'Dtm1', 'Dtm2'))
            for h in range(4):
                k.ts(dG[:, h, :], self.identf[:], Gt[:, n, h:h + 1], ALU.mult, eng='pool')
            Pm = [B['P0'], B['P1']]
            PTm = [B['PT0'], B['PT1']]
            TTm = [B['TT0'], B['TT1']]
            Aq = B['Aq']
            P, PT, TT = Pm[0], PTm[0], TTm[0]
            for h in range(4):
                ps = self.bank()
                k.mm(ps[:, 0:128], kbT[:, h, :], knT[:, h, :])
                k.mm(ps[:, 128:256], knT[:, h, :], kbT[:, h, :])
                k.mm(ps[:, 256:384], knT[:, h, :], qnT[:, h, :])
                k.mm(ps[:, 384:512], onesf[:], dG[:, h, :])
                k.ts(Xn[:], ps[:, 384:512], Gt[:, n, h:h + 1], ALU.subtract, 0.0, ALU.min)
                k.ts(Xm[:], ps[:, 384:512], Gt[:, n, h:h + 1], ALU.subtract, 0.0, ALU.max)
                k.act(Dt[:], Xn[:], AF.Exp)
                k.act(Dl[:], Xm[:], AF.Exp, scale=-1.0)
                k.tt(Dlm[:], Dl[:], nmsl[:], ALU.mult, eng='pool')
                k.tt(Dtm1[:], Dt[:], nmsu[:], ALU.mult, eng='pool')
                k.tt(Dtm2[:], Dt[:], self.mui[:], ALU.mult, eng='pool')
                k.tt(P[:, h, :], ps[:, 0:128], Dlm[:], ALU.mult)
                k.tt(PT[:, h, :], ps[:, 128:256], Dtm1[:], ALU.mult)
                k.tt(Aq[:, h, :], ps[:, 256:384], Dtm2[:], ALU.mult)
                k.tt(TT[:, h, :], PT[:, h, :], self.identf[:], ALU.add, eng='pool')
                yield
            cur = 0
            for lev in range(6):
                nxt = 1 - cur
                pP = self.bank()
                for h in range(4):
                    k.mm(pP[:, h * 128:(h + 1) * 128], PTm[cur][:, h, :], Pm[cur][:, h, :])
                k.cp(Pm[nxt][:].rearrange("p h t -> p (h t)"), pP[:], eng='act')
                if lev < 5:
                    pQ = self.bank()
                    for h in range(4):
                        k.mm(pQ[:, h * 128:(h + 1) * 128], Pm[cur][:, h, :], PTm[cur][:, h, :])
                    k.cp(PTm[nxt][:].rearrange("p h t -> p (h t)"), pQ[:])
                yield
                pT_ = self.bank()
                for h in range(4):
                    k.mm(pT_[:, h * 128:(h + 1) * 128], Pm[nxt][:, h, :], TTm[cur][:, h, :])
                k.tt(TTm[nxt][:].rearrange("p h t -> p (h t)"), pT_[:], TTm[cur][:].rearrange("p h t -> p (h t)"), ALU.add)
                cur = nxt
                yield
            TT = B['TTb']
            k.cp(TT[:], TTm[cur][:], eng='pool')
            u_sb, wwT, vnew, o_sb, osq, dmix = (B[x_] for x_ in ('u_sb', 'wwT', 'vnew', 'o_sb', 'osq', 'dmix'))
            pu = self.bank()
            for h in range(4):
                k.mm(pu[:, h * 64:(h + 1) * 64], TT[:, h, :], B['vb_b'][:, h, :])
            k.cp(u_sb[:].rearrange("p h e -> p (h e)"), pu[:, 0:256], eng='act')
            pw = self.bank()
            for h in range(4):
                k.mm(pw[0:64, h * 128:(h + 1) * 128], B['kbg_b'][:, h, :], TT[:, h, :])
            k.cp(wwT[:].rearrange("p h t -> p (h t)"), pw[0:64, :])
            yield
            pS = self.bank()
            for h in range(4):
                k.mm(pS[:, h * 64:(h + 1) * 64], wwT[:, h, :], Sb[:, h, :])
            k.tt(vnew[:].rearrange("p h e -> p (h e)"), u_sb[:].rearrange("p h e -> p (h e)"), pS[:, 0:256], ALU.subtract)
            po = self.bank()
            for h in range(4):
                k.mm(po[:, h * 64:(h + 1) * 64], qdT[:, h, :], Sb[:, h, :], start=True, stop=False)
                k.mm(po[:, h * 64:(h + 1) * 64], Aq[:, h, :], vnew[:, h, :], start=False, stop=True)
            pK = self.bank()
            for h in range(4):
                k.mm(pK[0:64, h * 64:(h + 1) * 64], B['kdec_b'][:, h, :], vnew[:, h, :])
            k.tt(Sf[:], Sf[:], gla[0:64, n, :].unsqueeze(2).broadcast_to([64, 4, 64]), ALU.mult)
            k.tt(Sf[:].rearrange("p h e -> p (h e)"), Sf[:].rearrange("p h e -> p (h e)"), pK[0:64, 0:256], ALU.add)
            k.cp(Sb[:], Sf[:], eng='act')
            k.cp(o_sb[:].rearrange("p h e -> p (h e)"), po[:, 0:256], eng='act')
            yield
            k.tt(osq[:], o_sb[:], o_sb[:], ALU.mult)
            k.op('dve', lambda: nc.vector.tensor_reduce(r8[:, 0:4], osq[:], AX.X, ALU.add), [osq], [r8])
            rsq(r8[:, 4:8], r8[:, 0:4], 1.0 / 64)
            k.tt(o_sb[:], o_sb[:], bc(r8[:, 4:8]), ALU.mult)
            k.tt(o_sb[:], o_sb[:], gn[:].unsqueeze(1).broadcast_to([128, 4, 64]), ALU.mult, eng='pool')
            k.tt(dmix[:], o_sb[:].rearrange("p h e -> p (h e)"), zsb[:, t, :], ALU.mult)
            yield
            px = self.bank()
            pxv = px[:].bitcast(BF16)
            for j in range(2):
                k.tr(pxv[:, j * 128:(j + 1) * 128], dmix[:, j * 128:(j + 1) * 128], self.identb[:])
            k.cp(dg[:, :, tok], pxv[:, 0:256].rearrange("p (j t) -> p j t", j=2), eng='act')

        for b in range(NB):
            t0 = b * 512
            qf, qfsem = qfs.next()
            k.dma('sp', qf[:], self.qkvc_scr.ap()[:, t0:t0 + 512].rearrange("(j p) t -> p j t", p=128), qfsem)
            zt, ztsem = zts.next()
            k.dma('sp', zt[:], self.z_scr.ap()[t0:t0 + 512, :].rearrange("(t p) c -> p t c", p=128), ztsem)
            k.act(zsb[:], zt[:], AF.Silu)
            dg, dgs = dstg.next()
            for tp in range(4 // nsets):
                gens = [dn_tile(sets[i], b * 4 + tp * nsets + i, tp * nsets + i, qf, dg) for i in range(nsets)]
                alive = [True] * nsets
                while any(alive):
                    for i in range(nsets):
                        if alive[i]:
                            try:
                                next(gens[i])
                            except StopIteration:
                                alive[i] = False
                    yield
            k.dma('sp', self.mix_scr.ap()[2, :, t0:t0 + 512].rearrange("(j p) t -> p j t", p=128), dg[:], dgs)
        return S

    def phase_s5(self, l):
        k = self.k
        nc = k.nc
        L = self.L
        NB = L // 512
        nch = L // 8
        S = Scope(k)
        w = self.w
        nlev = 0
        while (1 << nlev) < nch:
            nlev += 1
        W1 = S.sb('W1pad', [128, 2, 4, 8, 2, 128], BF16)
        Kbd = S.sb('Kbd', [128, 2, 8, 128], BF16)
        W2 = S.sb('W2bd', [128, 2, 4, 8, 2, 128], BF16)
        A8 = [S.sb('A8_%d' % i, [128, 3, 8], F32) for i in range(nlev)]
        wglu = S.sb('wglu', [128, 2, 256], BF16)
        k.dma('pool', wglu[:], w['s5_w_glu'].ap()[l].rearrange("(j p) n -> p j n", p=128), 'wglu')
        dcol = S.sb('dcol', [128, 2], F32)
        bglu = S.sb('bglu', [128, 2], F32)
        SP = Scope(k)
        dcol_t = self.load_cols(SP, w['s5_d'].ap()[l].rearrange("(j p) -> j p", p=128), 2, 'dcolt')
        bglu_t = self.load_cols(SP, w['s5_b_glu'].ap()[l].rearrange("(j p) -> j p", p=128), 2, 'bglut')
        k.cp(dcol[:], dcol_t[:])
        k.cp(bglu[:], bglu_t[:])
        ctr = [0]

        def T(shape=(128, 128), dt=F32):
            ctr[0] += 1
            return SP.sb('s5t%d' % ctr[0], list(shape), dt)

        def cl(name, shape):
            t_ = SP.sb(name, list(shape), F32)
            k.dma('sp', t_[:], self.c[name].ap(), name)
            return t_
        sel8 = cl('c_sel8', [8, 128])
        bm16 = cl('c_bm16', [128, 128])
        w1mask = cl('c_w1mask', [128, 8])
        w2mask = cl('c_w2mask', [128, 4, 128])
        lam8 = SP.sb('lam8', [8, 2, 2, 64], F32)
        k.dma('sp', lam8[:, 0, :, :], w['s5_lam_re'].ap()[l].rearrange("(j g) p -> g j p", g=8), 'lam8')
        k.dma('sp', lam8[:, 1, :, :], w['s5_lam_im'].ap()[l].rearrange("(j g) p -> g j p", g=8), 'lam8')
        ls8 = SP.sb('ls8', [8, 2], F32)
        for j in range(2):
            k.dma('sp', ls8[:, j:j + 1], w['s5_log_step'].ap()[l, j * 8:(j + 1) * 8].unsqueeze(1), 'ls8')
        lr, li, st = T(), T(), T((128, 2))
        pb = self.bank()
        k.mm(pb[:, 0:256], sel8[:], lam8[:].rearrange("g r j p -> g (r j p)"))
        k.mm(pb[:, 256:258], sel8[:], ls8[:])
        k.cp(lr[:], pb[:, 0:128])
        k.cp(li[:], pb[:, 128:256])
        k.act(st[:], pb[:, 256:258], AF.Exp)
        stb = st[:].unsqueeze(2).broadcast_to([128, 2, 64])

        def v3(t_):
            return t_[:].rearrange("q (j p) -> q j p", j=2)
        th, lrs = T(), T()
        k.tt(v3(th), v3(li), stb, ALU.mult)
        k.tt(v3(lrs), v3(lr), stb, ALU.mult)
        er, s32, cs, sn, t1, t2 = T(), T(), T(), T(), T(), T()
        k.act(er[:], lrs[:], AF.Exp)
        k.act(s32[:], th[:], AF.Sin, scale=1.0 / 32)
        k.act(sn[:], th[:], AF.Sin, scale=1.0 / 16)
        k.tt(t1[:], s32[:], s32[:], ALU.mult)
        k.ts(cs[:], t1[:], -2.0, ALU.mult, 1.0, ALU.add)
        for _ in range(4):
            k.tt(t1[:], cs[:], cs[:], ALU.mult)
            k.tt(t2[:], sn[:], sn[:], ALU.mult)
            k.stt(sn[:], cs[:], 2.0, sn[:], ALU.mult, ALU.mult)
            k.tt(cs[:], t1[:], t2[:], ALU.subtract)
        ar, ai = T(), T()
        k.tt(ar[:], er[:], cs[:], ALU.mult)
        k.tt(ai[:], er[:], sn[:], ALU.mult)
        den, nr, cre, cim = T(), T(), T(), T()
        k.tt(t1[:], lr[:], lr[:], ALU.mult)
        k.tt(t2[:], li[:], li[:], ALU.mult)
        k.tt(den[:], t1[:], t2[:], ALU.add)
        k.recip(den[:], den[:])
        k.ts(nr[:], ar[:], -1.0, ALU.add)
        k.tt(t1[:], nr[:], lr[:], ALU.mult)
        k.tt(t2[:], ai[:], li[:], ALU.mult)
        k.tt(cre[:], t1[:], t2[:], ALU.add)
        k.tt(cre[:], cre[:], den[:], ALU.mult)
        k.tt(t1[:], ai[:], lr[:], ALU.mult)
        k.tt(t2[:], nr[:], li[:], ALU.mult)
        k.tt(cim[:], t1[:], t2[:], ALU.subtract)
        k.tt(cim[:], cim[:], den[:], ALU.mult)

        def cmul(o_r, o_i, a_r, a_i, b_r, b_i, neg_im=False):
            k.tt(t1[:], a_r, b_r, ALU.mult)
            k.tt(t2[:], a_i, b_i, ALU.mult)
            k.tt(o_r, t1[:], t2[:], ALU.subtract)
            k.tt(t1[:], a_r, b_i, ALU.mult)
            k.tt(t2[:], a_i, b_r, ALU.mult)
            if neg_im:
                k.stt(o_i, t1[:], -1.0, t2[:], ALU.mult, ALU.subtract)
            else:
                k.tt(o_i, t1[:], t2[:], ALU.add)
        Bre, Bim = T(), T()
        for (nm, dst) in (('s5_b_re', Bre), ('s5_b_im', Bim)):
            bn = SP.sb(nm + 'n', [64, 16, 16], F32)
            k.dma('sp', bn[:], w[nm].ap()[l].rearrange("g p h -> p g h"), nm + 'n')
            for j in range(2):
                pb = self.bank()
                k.tr(pb[:, 0:64], bn[:, j * 8:(j + 1) * 8, :].rearrange("p g h -> p (g h)"), self.identf[0:64, 0:64])
                k.cp(dst[:, j * 64:(j + 1) * 64], pb[:, 0:64])
        Cre, Cim = T(), T()
        k.dma('sp', v3(Cre), w['s5_c_re'].ap()[l].rearrange("(j g) h p -> (g h) j p", g=8), 'Cre')
        k.dma('sp', v3(Cim), w['s5_c_im'].ap()[l].rearrange("(j g) h p -> (g h) j p", g=8), 'Cim')
        Bbr, Bbi = T(), T()
        cmul(Bbr[:], Bbi[:], cre[:], cim[:], Bre[:], Bim[:])
        Pr = [T() for _ in range(9)]
        Pi = [T() for _ in range(9)]
        k.memset(Pr[0][:], 1.0)
        k.memset(Pi[0][:], 0.0)
        k.cp(Pr[1][:], ar[:])
        k.cp(Pi[1][:], ai[:])
        for kk in range(2, 9):
            cmul(Pr[kk][:], Pi[kk][:], Pr[kk - 1][:], Pi[kk - 1][:], ar[:], ai[:])
        AB = [SP.sb('AB%d' % tau, [128, 2, 2, 64], F32) for tau in range(8)]
        abr, abi = T(), T()
        for tau in range(8):
            cmul(abr[:], abi[:], Pr[tau][:], Pi[tau][:], Bbr[:], Bbi[:])
            k.cp(AB[tau][:, :, 0, :], v3(abr), eng='act')
            k.cp(AB[tau][:, :, 1, :], v3(abi), eng='act')
        w1m = w1mask[:].rearrange("q (m h) -> q m h", h=2).unsqueeze(3).broadcast_to([128, 4, 2, 64])
        for s_ in range(8):
            for j in range(2):
                for ri in range(2):
                    src = AB[7 - s_][:, j, ri, :].unsqueeze(1).unsqueeze(1).broadcast_to([128, 4, 2, 64])
                    k.tt(W1[:, j, :, s_, ri, :].rearrange("q m (h p) -> q m h p", h=2), src, w1m, ALU.mult,
                         eng=('pool' if ri else 'dve'))
        CT0 = SP.sb('CT0', [128, 2, 2, 64], F32)
        k.cp(CT0[:, :, 0, :], v3(Cre))
        k.ts(CT0[:, :, 1, :], v3(Cim), -1.0, ALU.mult)
        CT0T = SP.sb('CT0T', [128, 2, 128], F32)
        for j in range(2):
            pb = self.bank()
            k.tr(pb[:, 0:128], CT0[:, j, :, :].rearrange("q r p -> q (r p)"), self.identf[:])
            k.cp(CT0T[:, j, :], pb[:, 0:128])
        abT = Rot(SP, 'abT', 2, [128, 128], F32)
        for tau in range(8):
            for j in range(2):
                pb = self.bank()
                k.tr(pb[:, 0:128], AB[tau][:, j, :, :].rearrange("q r p -> q (r p)"), self.identf[:])
                at_, _ = abT.next()
                k.cp(at_[:], pb[:, 0:128], eng='act')
                k.mm(pb[:, 128:256], at_[:], CT0T[:, j, :])
                k.tt(Kbd[:, j, tau, :], pb[:, 128:256], bm16[:], ALU.mult)
        CA = SP.sb('CA', [128, 2, 2, 2, 64], F32)
        car, cai = T(), T()
        for s_ in range(8):
            cmul(car[:], cai[:], Cre[:], Cim[:], Pr[s_ + 1][:], Pi[s_ + 1][:], neg_im=True)
            for dup in range(2):
                k.cp(CA[:, :, 0, dup, :], v3(car), eng='act')
                k.cp(CA[:, :, 1, dup, :], v3(cai), eng='pool')
            for j in range(2):
                for ri in range(2):
                    pb = self.bank()
                    k.tr(pb[:, 0:128], CA[:, j, ri, :, :].rearrange("q d p -> q (d p)"), self.identf[:])
                    k.tt(W2[:, j, :, s_, ri, :], pb[:, 0:128].unsqueeze(1).broadcast_to([128, 4, 128]), w2mask[:], ALU.mult)
        a8d = SP.sb('a8d', [128, 2, 2, 2, 64], F32)
        for dup in range(2):
            k.cp(a8d[:, :, 0, dup, :], v3(Pr[8]))
            k.cp(a8d[:, :, 1, dup, :], v3(Pi[8]))
        for j in range(2):
            for ri in range(2):
                pb = self.bank()
                k.tr(pb[:, 0:128], a8d[:, j, ri, :, :].rearrange("q d p -> q (d p)"), self.identf[:])
                k.cp(A8[0][0:64, ri, 4 * j:4 * j + 4], pb[0:64, 0:128:32])
                k.cp(A8[0][64:128, ri, 4 * j:4 * j + 4], pb[64:128, 16:128:32])
        s1, s2 = T((128, 8)), T((128, 8))
        for i in range(nlev):
            if i > 0:
                k.tt(s1[:], A8[i - 1][:, 0, :], A8[i - 1][:, 0, :], ALU.mult)
                k.tt(s2[:], A8[i - 1][:, 1, :], A8[i - 1][:, 1, :], ALU.mult)
                k.tt(A8[i][:, 0, :], s1[:], s2[:], ALU.subtract)
                k.stt(A8[i][:, 1, :], A8[i - 1][:, 0, :], 2.0, A8[i - 1][:, 1, :], ALU.mult, ALU.mult)
            k.ts(A8[i][:, 2, :], A8[i][:, 1, :], -1.0, ALU.mult)
        SP.close()
        uT = S.sb('uT', [128, 2, L], BF16)
        for j in range(2):
            k.dma('sp', uT[:, j, :], self.uT_scr.ap()[j * 128:(j + 1) * 128, :], 'uT')
        Xbf = S.sb('Xbf', [128, 2, 8, nch + 1], BF16)
        k.memset(Xbf[:, :, :, 0:1], 0.0)
        XA = [S.sb('XA%d' % i, [128, nch], F32) for i in range(2)]
        XB = [S.sb('XB%d' % i, [128, nch], F32) for i in range(2)]
        XT = S.sb('XT', [128, nch], F32)
        for m in range(8):
            j, mp = m // 4, m % 4
            for ri in range(2):
                for c0 in range(0, nch, 512):
                    n = min(512, nch - c0)
                    ps = self.bank()
                    for s_ in range(8):
                        k.mm(ps[:, 0:n], W1[:, j, mp, s_, ri, :], uT[:, j, 8 * c0 + s_:8 * (c0 + n):8],
                             start=(s_ == 0), stop=(s_ == 7))
                    k.cp(XA[ri][:, c0:c0 + n], ps[:, 0:n], eng=('act' if ri else 'dve'))
            cur, oth = XA, XB
            for lev in range(nlev):
                d = 1 << lev
                Ar = A8[lev][:, 0, m:m + 1]
                Ai = A8[lev][:, 1, m:m + 1]
                nAi = A8[lev][:, 2, m:m + 1]
                k.stt(XT[:, d:], cur[0][:, 0:nch - d], Ar, cur[0][:, d:], ALU.mult, ALU.add)
                k.stt(oth[0][:, d:], cur[1][:, 0:nch - d], nAi, XT[:, d:], ALU.mult, ALU.add)
                k.stt(XT[:, d:], cur[1][:, 0:nch - d], Ar, cur[1][:, d:], ALU.mult, ALU.add)
                k.stt(oth[1][:, d:], cur[0][:, 0:nch - d], Ai, XT[:, d:], ALU.mult, ALU.add)
                k.cp(oth[0][:, 0:d], cur[0][:, 0:d], eng='act')
                k.cp(oth[1][:, 0:d], cur[1][:, 0:d], eng='pool')
                cur, oth = oth, cur
            k.cp(Xbf[:, 0, m, 1:nch + 1], cur[0][:], eng='act')
            k.cp(Xbf[:, 1, m, 1:nch + 1], cur[1][:], eng='pool')
        yv = S.sb('yv', [128, 2, 512], F32)
        zf = S.sb('zf', [128, 2, 512], F32)
        zb = S.sb('zb', [128, 2, 512], BF16)
        sgm = S.sb('sgm', [128, 512], F32)
        sstg = Rot(S, 's5stg', 2, [128, 2, 512], BF16)
        for b in range(NB):
            t0 = b * 512
            cb0 = b * 64
            for j in range(2):
                ps = self.bank()
                k.memset(ps[:], 0.0)
                for s_ in range(8):
                    out = ps[:, s_:512:8]
                    for mp in range(4):
                        for ri in range(2):
                            k.mm(out, W2[:, j, mp, s_, ri, :], Xbf[:, ri, 4 * j + mp, cb0:cb0 + 64], start=False, stop=False)
                    for tau in range(s_ + 1):
                        k.mm(out, Kbd[:, j, tau, :], uT[:, j, t0 + s_ - tau:t0 + 512:8], start=False, stop=(tau == s_))
                k.stt(yv[:, j, :], uT[:, j, t0:t0 + 512], dcol[:, j:j + 1], ps[:], ALU.mult, ALU.add)
                k.act(zf[:, j, :], yv[:, j, :], AF.Gelu_apprx_tanh)
                k.cp(zb[:, j, :], zf[:, j, :], eng='pool')
            sg_, sgs = sstg.next()
            for jo in range(2):
                pg = self.bank()
                for ji in range(2):
                    k.mm(pg[:], wglu[:, ji, jo * 128:(jo + 1) * 128], zb[:, ji, :], start=(ji == 0), stop=(ji == 1))
                k.act(sgm[:], pg[:], AF.Sigmoid, bias=bglu[:, jo:jo + 1])
                k.tt(sg_[:, jo, :], zf[:, jo, :], sgm[:], ALU.mult)
            k.dma('sp', self.mix_scr.ap()[0, :, t0:t0 + 512].rearrange("(j p) t -> p j t", p=128), sg_[:], sgs)
        S.close()

    def zero_branch(self, br):
        k = self.k
        S = Scope(k)
        zt = S.sb('zt', [128, 2, 512], BF16)
        k.memset(zt[:], 0.0)
        for b in range(self.L // 512):
            k.dma('sp', self.mix_scr.ap()[br, :, b * 512:(b + 1) * 512].rearrange("(j p) t -> p j t", p=128), zt[:], 'zt')
        S.close()

    def phase_C(self, l):
        k = self.k
        L = self.L
        NB = L // 512
        S = Scope(k)
        w = self.w
        xsrc = self.x_in if l == 0 else self.y
        Wg = S.sb('Wg', [128, 8, 4096], BF16)
        for kk in range(8):
            k.dma('pool', Wg[:, kk, :], w['w_in'].ap()[l, kk * 128:(kk + 1) * 128, NMIX:NMIX + 4096], 'Wg')
        wbr = S.sb('wbr', [128, 4, 2, D], BF16)
        for i, nm in enumerate(('w_br_s5', 'w_br_sgu', 'w_br_dn', 'w_br_diff')):
            k.dma('pool', wbr[:, i, :, :], w[nm].ap()[l].rearrange("(j p) n -> p j n", p=128), 'wbr')
        wout = S.sb('wout', [128, 8, D], BF16)
        k.dma('pool', wout[:], w['w_out'].ap()[l].rearrange("(j p) n -> p j n", p=128), 'wout')
        xts = Rot(S, 'xt', 2, [128, 4, D], F32)
        hTs = Rot(S, 'hT', 2, [128, 8, 512], BF16)
        mixs = Rot(S, 'mx', 2, [128, 4, 2, 512], BF16)
        sig = Rot(S, 'sig', 2, [128, 512], F32)
        mg = S.sb('mg', [128, 512], F32)
        tmpm = S.sb('tmpm', [128, 512], F32)
        mgT = S.sb('mgT', [128, 8, 512], BF16)
        def ldc(bb):
            tt0 = bb * 512
            xt_, xsem_ = xts.next()
            k.dma('sp', xt_[:], xsrc.ap()[tt0:tt0 + 512, :].rearrange("(t p) d -> p t d", p=128), xsem_)
            hT_, hsem_ = hTs.next()
            k.dma('sp', hT_[:], self.hT_scr.ap()[:, tt0:tt0 + 512].rearrange("(k p) t -> p k t", p=128), hsem_)
            mx_, msem_ = mixs.next()
            for br in range(4):
                k.dma('sp', mx_[:, br, :, :], self.mix_scr.ap()[br, :, tt0:tt0 + 512].rearrange("(j p) t -> p j t", p=128), msem_)
            return xt_, xsem_, hT_, mx_
        nxt_c = ldc(0)
        for b in range(NB):
            t0 = b * 512
            xt, xsem, hT, mx = nxt_c
            if b + 1 < NB:
                nxt_c = ldc(b + 1)
            for mo in range(8):
                for br in range(4):
                    pg = self.bank()
                    c0 = br * D + mo * 128
                    for kk in range(8):
                        k.mm(pg[:], Wg[:, kk, c0:c0 + 128], hT[:, kk, :], start=(kk == 0), stop=(kk == 7))
                    sg, _ = sig.next()
                    k.act(sg[:], pg[:], AF.Sigmoid)
                    py = self.bank()
                    for j in range(2):
                        k.mm(py[:], wbr[:, br, j, mo * 128:(mo + 1) * 128], mx[:, br, j, :], start=(j == 0), stop=(j == 1))
                    if br == 0:
                        k.tt(mg[:], py[:], sg[:], ALU.mult)
                    elif br < 3:
                        k.tt(tmpm[:], py[:], sg[:], ALU.mult)
                        k.tt(mg[:], mg[:], tmpm[:], ALU.add, eng='pool')
                    else:
                        k.tt(tmpm[:], py[:], sg[:], ALU.mult)
                        k.tt(mgT[:, mo, :], mg[:], tmpm[:], ALU.add, eng='pool')
            xn, xns = xt, xsem
            for t in range(4):
                for hf in range(2):
                    po = self.bank()
                    for kk in range(8):
                        k.mm(po[:], mgT[:, kk, t * 128:(t + 1) * 128], wout[:, kk, hf * 512:(hf + 1) * 512],
                             start=(kk == 0), stop=(kk == 7))
                    k.tt(xn[:, t, hf * 512:(hf + 1) * 512], po[:], xt[:, t, hf * 512:(hf + 1) * 512], ALU.add)
            k.dma('sp', self.y.ap()[t0:t0 + 512, :].rearrange("(t p) d -> p t d", p=128), xn[:], xns)
        S.close()

    def phase_D(self, l, p, last):
        k = self.k
        if 'bi0' in self.flags:
            self.bi = 0
        L = self.L
        NB = L // 512
        S = Scope(k)
        w = self.w
        HC = DFF // 2
        Wup = S.sb('Wup', [128, 8, 2, HC], BF16)
        for kk in range(8):
            for which in range(2):
                c0 = which * DFF + p * HC
                k.dma('pool', Wup[:, kk, which, :], w['ffn_w_up'].ap()[l, kk * 128:(kk + 1) * 128, c0:c0 + HC], 'Wup')
        Wdn = S.sb('Wdn', [128, 11, D], BF16)
        k.dma('pool', Wdn[:], w['ffn_w_down'].ap()[l, p * HC:(p + 1) * HC, :].rearrange("(j p) n -> p j n", p=128), 'Wdn')
        g2col = self.load_cols(S, w['norm_ffn_g'].ap()[l].rearrange("(k p) -> k p", p=128), 8, 'g2col')
        cw = self.load_cols(S, w['ffn_conv_w'].ap()[l].rearrange("k (j p) -> (k j) p", p=128), 132, 'cw')
        cb = self.load_cols(S, w['ffn_conv_b'].ap()[l].rearrange("(j p) -> j p", p=128), 44, 'cb')
        fin = last and p == 1
        if fin:
            gfin = self.load_row_bcast(S, w['norm_final_g'].ap(), D, 'gfin')
        xts = Rot(S, 'xt', 2, [128, 4, D], F32)
        hTs = Rot(S, 'hT', 2, [128, 8, 512], BF16)
        hs = S.sb('hs', [128, 4, D], BF16)
        junk = S.sb('junk', [128, D], F32)
        sss = Rot(S, 'ss', 2, [128, 16], F32)
        ups = Rot(S, 'ups', 3, [128, 514], F32)
        halo = S.sb('halo', [128, 22, 2], F32)
        k.memset(halo[:], 0.0)
        cv = Rot(S, 'cv', 3, [128, 512], F32)
        cg = Rot(S, 'cg', 3, [128, 512], F32)
        sgl = Rot(S, 'sgl', 2, [128, 512], F32)
        actT = S.sb('actT', [128, 11, 512], BF16)
        def ldd(bb):
            tt0 = bb * 512
            xt_, xsem_ = xts.next()
            k.dma('sp', xt_[:], self.y.ap()[tt0:tt0 + 512, :].rearrange("(t p) d -> p t d", p=128), xsem_)
            hT_, hsem_ = hTs.next()
            if p == 1:
                k.dma('sp', hT_[:], self.hT_scr.ap()[:, tt0:tt0 + 512].rearrange("(k p) t -> p k t", p=128), hsem_)
            return xt_, xsem_, hT_, hsem_
        nxt_d = ldd(0)
        for b in range(NB):
            t0 = b * 512
            xt, xsem, hT, hsem = nxt_d
            if b + 1 < NB:
                nxt_d = ldd(b + 1)
            ss, _ = sss.next()
            hview = self.hT_scr.ap()[:, t0:t0 + 512].rearrange("(k p) t -> p k t", p=128)
            if p == 0:
                self.rms_block(xt, g2col, hT, junk, ss, hs)
                k.dma('sp', hview, hT[:], hsem)
            pend = []

            def tail(i_, res_):
                sl, _ = sgl.next()
                k.act(sl[:], res_[1][:], AF.Silu)
                k.tt(actT[:, i_, :], sl[:], res_[0][:], ALU.mult, eng='pool')
            for i in range(11):
                res = []
                for which in range(2):
                    j = which * 22 + p * 11 + i
                    hj = which * 11 + i
                    pu = self.bank()
                    for kk in range(8):
                        k.mm(pu[:], Wup[:, kk, which, i * 128:(i + 1) * 128], hT[:, kk, :], start=(kk == 0), stop=(kk == 7))
                    up, _ = ups.next()
                    k.cp(up[:, 2:514], pu[:], eng='act')
                    k.cp(up[:, 0:2], halo[:, hj, :], eng='pool')
                    c, _ = (cv if which == 0 else cg).next()
                    k.act(c[:], pu[:], AF.Identity, scale=cw[:, 2 * 44 + j:2 * 44 + j + 1], bias=cb[:, j:j + 1])
                    k.stt(c[:], up[:, 1:513], cw[:, 44 + j:44 + j + 1], c[:], ALU.mult, ALU.add)
                    k.stt(c[:], up[:, 0:512], cw[:, j:j + 1], c[:], ALU.mult, ALU.add)
                    k.cp(halo[:, hj, :], up[:, 512:514], eng='pool')
                    res.append(c)
                pend.append((i, res))
                if len(pend) > 1:
                    tail(*pend.pop(0))
            while pend:
                tail(*pend.pop(0))
            for t in range(4):
                for hf in range(2):
                    po = self.bank()
                    for i in range(11):
                        k.mm(po[:], actT[:, i, t * 128:(t + 1) * 128], Wdn[:, i, hf * 512:(hf + 1) * 512],
                             start=(i == 0), stop=(i == 10))
                    k.tt(xt[:, t, hf * 512:(hf + 1) * 512], po[:], xt[:, t, hf * 512:(hf + 1) * 512], ALU.add)
            if fin:
                ss2, _ = sss.next()
                for t in range(4):
                    k.sumsq(junk[:], xt[:, t, :], ss2[:, t:t + 1])
                k.ts(ss2[:, 4:8], ss2[:, 0:4], 1.0 / D, ALU.mult, EPS, ALU.add)
                k.act(ss2[:, 8:12], ss2[:, 4:8], AF.Sqrt)
                k.recip(ss2[:, 12:16], ss2[:, 8:12])
                for t in range(4):
                    k.stt(xt[:, t, :], xt[:, t, :], ss2[:, 12 + t:13 + t], gfin[:], ALU.mult, ALU.mult)
            k.dma('sp', self.y.ap()[t0:t0 + 512, :].rearrange("(t p) d -> p t d", p=128), xt[:], xsem)
        S.close()

    def build(self):
        fl = self.flags
        if 'onlyA' in fl:
            self.phase_A(0)
            self.k.barrier()
            return self.k.nc
        for l in range(self.depth):
            self.phase_A(l)
            if 'dn' in fl and 'att' in fl and 'seq' not in fl:
                if 's5' in fl:
                    self.phase_s5(l)
                else:
                    self.zero_branch(0)
                self.run_gens([(self.att_gen(l, [(6, 7)]), [3, 4, 5]), (self.dn_gen(l, 1), [0, 1, 2])], ratio=[3, 1])
            else:
                for br, nm in ((0, 's5'), (2, 'dn'), (3, 'att')):
                    if nm in fl:
                        getattr(self, 'phase_' + nm)(l)
                    else:
                        self.zero_branch(br)
            if 'noCD' in fl:
                continue
            self.phase_C(l)
            if 'stopC' in fl:
                break
            self.phase_D(l, 0, last=(l == self.depth - 1))
            if 'stopD0' in fl:
                break
            self.phase_D(l, 1, last=(l == self.depth - 1))
        self.k.barrier()
        return self.k.nc


_CACHE = {}


def run_cores(xs, weights, L, depth, flags):
    key = (L, depth, tuple(sorted(flags)))
    if key not in _CACHE:
        _CACHE[key] = Prog(L, depth, flags).build()
    nc = _CACHE[key]
    consts = host_consts()
    in_maps = []
    for x in xs:
        m = {"x": np.ascontiguousarray(x, dtype=np.float32)}
        for name in WEIGHT_SHAPES:
            m[name] = np.ascontiguousarray(weights[name], dtype=np.float32)
        m.update(consts)
        in_maps.append(m)
    res = run_bass_kernel_spmd(nc, in_maps, core_ids=list(range(len(xs))))
    if "dbg" in flags:
        return [(r["y"], r["mix_scr"], r["hT_scr"]) for r in res.results]
    return [r["y"] for r in res.results]


def kernel(**inputs):
    x = np.asarray(inputs['x'])
    B, L, _ = x.shape
    weights = {n: np.asarray(inputs[n]) for n in WEIGHT_SHAPES}
    outs = run_cores([x[b] for b in range(B)], weights, L, DEPTH, ('s5', 'dn', 'att'))
    return np.stack(outs, axis=0).astype(np.float32)
```

```python
import math
from contextlib import ExitStack

import numpy as np
import ml_dtypes
import concourse.bass as bass
import concourse.mybir as mybir
from concourse.bass_utils import run_bass_kernel_spmd

F32 = mybir.dt.float32
BF16 = mybir.dt.bfloat16
AF = mybir.ActivationFunctionType
ALU = mybir.AluOpType
AX = mybir.AxisListType
SAME_ENGINE_SYNC = True

D = 1024
DEPTH = 4
NMIX = 2568
DFF = 2816
EPS = 1e-6


class KB:
    ENG = ('pe', 'act', 'dve', 'pool', 'sp')

    def __init__(self):
        self.nc = bass.Bass("TRN2", target_bir_lowering=False)
        nc = self.nc
        self.e = {'pe': nc.tensor, 'act': nc.scalar, 'dve': nc.vector, 'pool': nc.gpsimd, 'sp': nc.sync}
        self.semh = {}
        self.semv = {}
        for k in self.ENG:
            self.semh['E:' + k] = nc.alloc_semaphore(name="s_" + k)
            self.semv['E:' + k] = 0
        self.waited = {k: {} for k in self.ENG}
        self.lastw = {}
        self.reads = {}
        self.nins = {k: 0 for k in self.ENG}
        self.nwait = 0
        self._uid = 0

    def uid(self):
        self._uid += 1
        return self._uid

    @staticmethod
    def R(x):
        if isinstance(x, (str, tuple)):
            return x
        return x.name

    def _wait(self, eng, key, val):
        w = self.waited[eng]
        if w.get(key, 0) >= val:
            return
        self.e[eng].wait_ge(self.semh[key], val)
        self.nwait += 1
        w[key] = val

    def _deps(self, eng, reads, writes):
        best = {}
        for r in reads:
            ev = self.lastw.get(r)
            if ev is not None:
                best[ev[0]] = max(best.get(ev[0], 0), ev[1])
        for w_ in writes:
            ev = self.lastw.get(w_)
            if ev is not None:
                best[ev[0]] = max(best.get(ev[0], 0), ev[1])
            for k, v in self.reads.get(w_, {}).items():
                best[k] = max(best.get(k, 0), v)
        for k, v in best.items():
            if k == 'E:' + eng and (eng == 'pe' or not SAME_ENGINE_SYNC):
                continue
            self._wait(eng, k, v)

    def _commit(self, ev, reads, writes):
        for w_ in writes:
            self.lastw[w_] = ev
            self.reads[w_] = {}
        for r in reads:
            d = self.reads.setdefault(r, {})
            d[ev[0]] = max(d.get(ev[0], 0), ev[1])

    def op(self, eng, fn, reads, writes):
        pw = [r for r in reads if r is not None and not isinstance(r, (str, tuple)) and self._is_ps(r)]
        reads = [self.R(r) for r in reads if r is not None]
        writes = [self.R(w) for w in writes if w is not None] + [self.R(r) for r in pw]
        self._deps(eng, reads, writes)
        ins = fn()
        key = 'E:' + eng
        self.semv[key] += 1
        ins.then_inc(self.semh[key], 1)
        self.nins[eng] += 1
        self._commit((key, self.semv[key]), reads, writes)
        return ins

    def dma(self, q, out, in_, sem, **kw):
        reads = [self.R(in_)] if self._is_sb(in_) else []
        writes = [self.R(out)] if self._is_sb(out) else []
        self._deps(q, reads, writes)
        key = 'D:' + sem
        if key not in self.semh:
            self.semh[key] = self.nc.alloc_semaphore(name="d%d" % len(self.semh))
            self.semv[key] = 0
        ins = self.e[q].dma_start(out=out, in_=in_, **kw)
        self.semv[key] += 16
        ins.then_inc(self.semh[key], 16)
        self.nins[q] += 1
        self._commit((key, self.semv[key]), reads, writes)
        return ins

    @staticmethod
    def _is_ps(ap):
        t = getattr(ap, 'tensor', ap)
        return type(t).__name__.startswith('PS')

    @staticmethod
    def _is_sb(ap):
        return type(ap.tensor).__name__.startswith('SB')

    def barrier(self, engs=None):
        for eng in (engs or self.ENG):
            for key, val in self.semv.items():
                if val > 0:
                    self._wait(eng, key, val)

    def mm(self, out, lhsT, rhs, start=True, stop=True, **kw):
        nc = self.nc
        return self.op('pe', lambda: nc.tensor.matmul(out, lhsT, rhs, start=start, stop=stop,
                                                      skip_group_check=True, **kw), [lhsT, rhs], [out])

    def tr(self, out, in_, ident):
        nc = self.nc
        return self.op('pe', lambda: nc.tensor.transpose(out, in_, ident), [in_, ident], [out])

    def act(self, out, in_, func, bias=None, scale=None):
        nc = self.nc
        kw = {}
        rr = [in_]
        if bias is not None:
            kw['bias'] = bias
            if not isinstance(bias, (int, float)):
                rr.append(bias)
        if scale is not None:
            kw['scale'] = scale
            if not isinstance(scale, (int, float)):
                rr.append(scale)
        return self.op('act', lambda: nc.scalar.activation(out, in_, func, **kw), rr, [out])

    def tt(self, out, in0, in1, op, eng='dve'):
        e = self.e[eng]
        return self.op(eng, lambda: e.tensor_tensor(out, in0, in1, op), [in0, in1], [out])

    def ts(self, out, in0, s1, op0, s2=None, op1=None, eng='dve'):
        e = self.e[eng]
        rr = [in0] + [s for s in (s1, s2) if s is not None and not isinstance(s, (int, float))]
        if op1 is None:
            return self.op(eng, lambda: e.tensor_scalar(out, in0, s1, None, op0), rr, [out])
        return self.op(eng, lambda: e.tensor_scalar(out, in0, s1, s2, op0, op1), rr, [out])

    def stt(self, out, in0, scalar, in1, op0, op1):
        nc = self.nc
        rr = [in0, in1] + ([scalar] if not isinstance(scalar, (int, float)) else [])
        return self.op('dve', lambda: nc.vector.scalar_tensor_tensor(out, in0, scalar, in1, op0, op1), rr, [out])

    def sumsq(self, junk, x, accum):
        nc = self.nc
        self.act(junk, x, AF.Square)
        return self.op('dve', lambda: nc.vector.tensor_reduce(accum, junk, AX.X, ALU.add), [junk], [accum])

    def cp(self, out, in_, eng='dve'):
        if eng == 'act':
            nc = self.nc
            return self.op('act', lambda: nc.scalar.copy(out, in_), [in_], [out])
        e = self.e[eng]
        return self.op(eng, lambda: e.tensor_copy(out, in_), [in_], [out])

    def memset(self, ap, val, eng='dve'):
        e = self.e[eng]
        return self.op(eng, lambda: e.memset(ap, val), [], [ap])

    def recip(self, out, in_):
        nc = self.nc
        return self.op('dve', lambda: nc.vector.reciprocal(out, in_), [in_], [out])


class Scope:
    def __init__(self, k):
        self.k = k
        self.es = ExitStack()

    def sb(self, name, shape, dt):
        return self.es.enter_context(self.k.nc.sbuf_tensor("%s_%d" % (name, self.k.uid()), list(shape), dt))

    def close(self):
        self.k.barrier()
        self.es.close()


class Rot:
    def __init__(self, S, name, n, shape, dt):
        self.bufs = [(S.sb("%s%d" % (name, i), shape, dt), "%s%d" % (name, i)) for i in range(n)]
        self.i = 0

    def next(self):
        b = self.bufs[self.i % len(self.bufs)]
        self.i += 1
        return b


WEIGHT_SHAPES = {
    'norm_mix_g': (4, 1024), 'w_in': (4, 1024, 6664), 's5_lam_re': (4, 16, 64), 's5_lam_im': (4, 16, 64),
    's5_log_step': (4, 16), 's5_b_re': (4, 16, 64, 16), 's5_b_im': (4, 16, 64, 16), 's5_c_re': (4, 16, 16, 64),
    's5_c_im': (4, 16, 16, 64), 's5_d': (4, 256), 's5_w_glu': (4, 256, 256), 's5_b_glu': (4, 256),
    'sgu_norm_g': (4, 256), 'sgu_w_s': (4, 4, 128, 128), 'sgu_b_s': (4, 4, 128), 'dn_conv_w': (4, 4, 768),
    'dn_a_log': (4, 4), 'dn_dt_bias': (4, 4), 'dn_norm_g': (4, 64), 'diff_lq1': (4, 32), 'diff_lk1': (4, 32),
    'diff_lq2': (4, 32), 'diff_lk2': (4, 32), 'diff_norm_g': (4, 64), 'w_br_s5': (4, 256, 1024),
    'w_br_sgu': (4, 256, 1024), 'w_br_dn': (4, 256, 1024), 'w_br_diff': (4, 256, 1024), 'w_out': (4, 1024, 1024),
    'norm_ffn_g': (4, 1024), 'ffn_w_up': (4, 1024, 5632), 'ffn_conv_w': (4, 3, 5632), 'ffn_conv_b': (4, 5632),
    'ffn_w_down': (4, 2816, 1024), 'norm_final_g': (1024,),
}


def host_consts():
    c = {}
    c['c_identb'] = np.eye(128).astype(ml_dtypes.bfloat16)
    c['c_identf'] = np.eye(128, dtype=np.float32)
    i = np.arange(128)
    c['c_mui'] = (i[:, None] <= i[None, :]).astype(np.float32)
    c['c_msu'] = (i[:, None] < i[None, :]).astype(np.float32)
    c['c_msl'] = (i[:, None] > i[None, :]).astype(np.float32)
    slopes = [2.0 ** (-8.0 * (h + 1) / 4) for h in range(4)]
    c['c_ekey'] = np.stack([np.exp(sl * i) for sl in slopes], axis=1).astype(np.float32)
    d = np.arange(67) - 3
    c['c_abias'] = np.tile(np.concatenate([-sl * 128.0 * d for sl in slopes])[None, :], (128, 1)).astype(np.float32)
    q = np.arange(512)
    c['c_cmask'] = np.stack([(q[None, :] >= 128 * j + i[:, None]) for j in range(4)], axis=1).astype(ml_dtypes.bfloat16)
    c['c_m32'] = np.stack([((i // 32) == j) for j in range(4)], axis=1).astype(np.float32)
    c['c_m64'] = np.stack([((i // 64) == j) for j in range(2)], axis=1).astype(np.float32)
    c['c_sel8'] = (np.arange(8)[:, None] == (i[None, :] // 16)).astype(np.float32)
    c['c_bm16'] = ((i[:, None] // 16) == (i[None, :] // 16)).astype(np.float32)
    c['c_w1mask'] = np.stack([((i // 16) == mh) for mh in range(8)], axis=1).astype(np.float32)
    half = i // 64
    c['c_w2mask'] = np.stack([((i[None, :] // 16) == (2 * mp + half[:, None])) for mp in range(4)], axis=1).astype(np.float32)
    return c


class Prog:
    def __init__(self, L, depth, flags):
        self.L = L
        self.depth = depth
        self.flags = flags
        self.lvl = 99
        for f in flags:
            if f.startswith('lvl'):
                self.lvl = int(f[3:])
        self.k = KB()
        k = self.k
        nc = k.nc
        self.x_in = nc.dram_tensor("x", [L, D], F32, kind="ExternalInput")
        self.y = nc.dram_tensor("y", [L, D], F32, kind="ExternalOutput")
        self.w = {}
        for name, shp in WEIGHT_SHAPES.items():
            self.w[name] = nc.dram_tensor(name, list(shp), F32, kind="ExternalInput")
        self.c = {}
        for name, arr in host_consts().items():
            dt = BF16 if arr.dtype == ml_dtypes.bfloat16 else F32
            self.c[name] = nc.dram_tensor(name, list(arr.shape), dt, kind="ExternalInput")
        self.hT_scr = nc.dram_tensor("hT_scr", [D, L], BF16, kind=("ExternalOutput" if "dbg" in flags else "Internal"))
        self.uT_scr = nc.dram_tensor("uT_scr", [256, L], BF16, kind="Internal")
        self.qkvc_scr = nc.dram_tensor("qkvc_scr", [768, L], BF16, kind="Internal")
        self.ab_scr = nc.dram_tensor("ab_scr", [128, (L // 128) * 8], F32, kind="Internal")
        self.z_scr = nc.dram_tensor("z_scr", [L, 256], BF16, kind="Internal")
        self.aq_scr = nc.dram_tensor("aq_scr", [256, L], BF16, kind="Internal")
        self.ak_scr = nc.dram_tensor("ak_scr", [256, L], BF16, kind="Internal")
        self.av_scr = nc.dram_tensor("av_scr", [L, 256], BF16, kind="Internal")
        self.mix_scr = nc.dram_tensor("mix_scr", [4, 256, L], BF16, kind=("ExternalOutput" if "dbg" in flags else "Internal"))
        self.banks = [nc.alloc_psum_tensor("bank%d" % i, [128, 512], F32) for i in range(8)]
        self.bi = 0
        self.bank_pool = list(range(8))
        self.G = Scope(k)
        self.identb = self.G.sb('identb', [128, 128], BF16)
        self.identf = self.G.sb('identf', [128, 128], F32)
        k.dma('sp', self.identb[:], self.c['c_identb'].ap(), 'identb')
        k.dma('sp', self.identf[:], self.c['c_identf'].ap(), 'identf')
        self.mui = self.G.sb('mui', [128, 128], F32)
        k.dma('sp', self.mui[:], self.c['c_mui'].ap(), 'mui')
        self.ones1 = self.G.sb('ones1', [1, 128], F32)
        k.memset(self.ones1[:], 1.0)

    def bank(self):
        pool = self.bank_pool
        b = self.banks[pool[self.bi % len(pool)]]
        self.bi += 1
        return b

    def load_cols(self, S, src2d, n, name):
        k = self.k
        dst = S.sb(name, [128, n], F32)
        done = 0
        while done < n:
            m = min(128, n - done)
            tmp = S.sb(name + 'r', [m, 128], F32)
            k.dma('sp', tmp[:], src2d[done:done + m, :], name + 'r%d' % done)
            pb = self.bank()
            k.tr(pb[:, 0:m], tmp[:], self.identf[0:m, 0:m])
            k.cp(dst[:, done:done + m], pb[:, 0:m])
            done += m
        return dst

    def load_row_bcast(self, S, src1d, n, name):
        k = self.k
        row = S.sb(name + 'r', [1, n], F32)
        k.dma('sp', row[:], src1d.unsqueeze(0), name + 'r')
        dst = S.sb(name, [128, n], F32)
        for c0 in range(0, n, 512):
            m = min(512, n - c0)
            pb = self.bank()
            k.mm(pb[:, 0:m], self.ones1[:, :], row[:, c0:c0 + m])
            k.cp(dst[:, c0:c0 + m], pb[:, 0:m])
        return dst

    def rms_block(self, xt, gcol, hT, junk, ss, hs):
        k = self.k
        for t in range(4):
            k.sumsq(junk[:], xt[:, t, :], ss[:, t:t + 1])
        k.ts(ss[:, 4:8], ss[:, 0:4], 1.0 / D, ALU.mult, EPS, ALU.add)
        k.act(ss[:, 8:12], ss[:, 4:8], AF.Sqrt)
        k.recip(ss[:, 12:16], ss[:, 8:12])
        for t in range(4):
            k.act(hs[:, t, :], xt[:, t, :], AF.Identity, scale=ss[:, 12 + t:13 + t])
        for kk in range(8):
            pb = self.bank()
            pv = pb[:].bitcast(BF16)
            for t in range(4):
                k.tr(pv[:, t * 128:(t + 1) * 128], hs[:, t, kk * 128:(kk + 1) * 128], self.identb[:])
            if kk % 2 == 0:
                k.ts(hT[:, kk, :], pv[:, 0:512], gcol[:, kk:kk + 1], ALU.mult)
            else:
                k.act(hT[:, kk, :], pv[:, 0:512], AF.Identity, scale=gcol[:, kk:kk + 1])

    def phase_A(self, l):
        k = self.k
        L = self.L
        NB = L // 512
        S = Scope(k)
        w = self.w
        xsrc = self.x_in if l == 0 else self.y
        WinA = S.sb('WinA', [128, 8, NMIX], BF16)
        for kk in range(8):
            k.dma('pool', WinA[:, kk, :], w['w_in'].ap()[l, kk * 128:(kk + 1) * 128, 0:NMIX], 'WinA')
        g1col = self.load_cols(S, w['norm_mix_g'].ap()[l].rearrange("(k p) -> k p", p=128), 8, 'g1col')
        wsT = S.sb('wsT', [128, 4, 128], BF16)
        wtmp = S.sb('wtmp', [128, 4, 128], F32)
        k.dma('sp', wtmp[:], w['sgu_w_s'].ap()[l].rearrange("g t s -> t g s"), 'wtmp')
        for g in range(4):
            pb = self.bank()
            k.tr(pb[:, 0:128], wtmp[:, g, :], self.identf[:])
            k.tt(wsT[:, g, :], pb[:, 0:128], self.mui[:], ALU.mult)
        sgng = self.load_row_bcast(S, w['sgu_norm_g'].ap()[l], 256, 'sgng')
        bsT = self.load_cols(S, w['sgu_b_s'].ap()[l], 4, 'bsT')
        cwc = self.load_cols(S, w['dn_conv_w'].ap()[l].rearrange("k (j p) -> (k j) p", p=128), 24, 'cwc')
        diagW = S.sb('diagW', [128, 6, 4, 128], BF16)
        for j in range(6):
            for kk in range(4):
                k.ts(diagW[:, j, kk, :], self.identf[:], cwc[:, kk * 6 + j:kk * 6 + j + 1], ALU.mult)
        xts = Rot(S, 'xt', 2, [128, 4, D], F32)
        hTs = Rot(S, 'hT', 2, [128, 8, 512], BF16)
        hs = S.sb('hs', [128, 4, D], BF16)
        junk = S.sb('junk', [128, D], F32)
        sss = Rot(S, 'ss', 2, [128, 16], F32)
        stg = Rot(S, 'stg', 4, [128, 512], BF16)
        abt = Rot(S, 'abt', 2, [128, 8], F32)
        qraw = S.sb('qraw', [128, 6, 515], BF16)
        k.memset(qraw[:, :, 0:3], 0.0)
        uvs = Rot(S, 'uv', 3, [128, 512], F32)
        vn = S.sb('vn', [128, 256], F32)
        vnbs = Rot(S, 'vnb', 3, [128, 256], BF16)
        st6 = S.sb('st6', [128, 8], F32)
        mxs = S.sb('mxs', [128, 256], F32)
        smix = S.sb('smix', [128, 256], BF16)
        smT = Rot(S, 'smT', 2, [128, 2, 512], BF16)
        ztm = Rot(S, 'ztm', 2, [128, 256], BF16)
        vtm = Rot(S, 'vtm', 2, [128, 256], BF16)

        def ldx(bb):
            xt_, xsem_ = xts.next()
            k.dma('sp', xt_[:], xsrc.ap()[bb * 512:bb * 512 + 512, :].rearrange("(t p) d -> p t d", p=128), xsem_)
            return xt_, xsem_
        nxt_x = ldx(0)
        for b in range(NB):
            t0 = b * 512
            xt, xsem = nxt_x
            if b + 1 < NB:
                nxt_x = ldx(b + 1)
            hT, hsem = hTs.next()
            ss, _ = sss.next()
            self.rms_block(xt, g1col, hT, junk, ss, hs)
            k.dma('sp', self.hT_scr.ap()[:, t0:t0 + 512].rearrange("(k p) t -> p k t", p=128), hT[:], hsem)

            def fm(c0, M):
                pb = self.bank()
                for kk in range(8):
                    k.mm(pb[0:M, :], WinA[:, kk, c0:c0 + M], hT[:, kk, :], start=(kk == 0), stop=(kk == 7))
                return pb

            if self.lvl < 2:
                continue
            for j in range(2):
                pb = fm(j * 128, 128)
                sg, sgs = stg.next()
                k.cp(sg[:], pb[:], eng='act')
                k.dma('sp', self.uT_scr.ap()[j * 128:(j + 1) * 128, t0:t0 + 512], sg[:], sgs)
            if self.lvl < 3:
                continue
            for j in range(6):
                pb = fm(768 + j * 128, 128)
                k.cp(qraw[:, j, 3:515], pb[:], eng=('act' if j % 2 else 'dve'))
            for j in range(6):
                pb = self.bank()
                for kk in range(4):
                    k.mm(pb[:], diagW[:, j, kk, :], qraw[:, j, kk:kk + 512], start=(kk == 0), stop=(kk == 3))
                sg, sgs = stg.next()
                k.act(sg[:], pb[:], AF.Silu)
                k.dma('sp', self.qkvc_scr.ap()[j * 128:(j + 1) * 128, t0:t0 + 512], sg[:], sgs)
            k.cp(qraw[:, :, 0:3], qraw[:, :, 512:515], eng='pool')
            if self.lvl < 5:
                continue
            for (c0, scr) in ((1800, self.aq_scr), (2056, self.ak_scr)):
                for j in range(2):
                    pb = fm(c0 + j * 128, 128)
                    sg, sgs = stg.next()
                    k.cp(sg[:], pb[:], eng=('act' if j else 'dve'))
                    k.dma('sp', scr.ap()[j * 128:(j + 1) * 128, t0:t0 + 512], sg[:], sgs)
            if self.lvl < 6:
                continue
            sm, sms = smT.next()

            def stage1(t):
                tok = slice(t * 128, (t + 1) * 128)
                uv_, _ = uvs.next()
                vnb_, _ = vnbs.next()
                pb = self.bank()
                for kk in range(8):
                    k.mm(pb[:], hT[:, kk, tok], WinA[:, kk, 256:768], start=(kk == 0), stop=(kk == 7))
                k.act(uv_[:], pb[:], AF.Gelu_apprx_tanh)
                pz = self.bank()
                for kk in range(8):
                    k.mm(pz[:, 0:256], hT[:, kk, tok], WinA[:, kk, 1536:1792], start=(kk == 0), stop=(kk == 7))
                for kk in range(8):
                    k.mm(pz[:, 256:512], hT[:, kk, tok], WinA[:, kk, 2312:2568], start=False, stop=(kk == 7))
                pab = self.bank()
                for kk in range(8):
                    k.mm(pab[:, 0:8], hT[:, kk, tok], WinA[:, kk, 1792:1800], start=(kk == 0), stop=(kk == 7))
                k.op('dve', lambda: k.nc.vector.bn_stats(st6[:, 0:6], uv_[:, 256:512]), [uv_], [st6])
                k.op('dve', lambda: k.nc.vector.bn_aggr(st6[:, 6:8], st6[:, 0:6]), [st6], [st6])
                k.ts(st6[:, 7:8], st6[:, 7:8], EPS, ALU.add)
                k.act(st6[:, 7:8], st6[:, 7:8], AF.Sqrt)
                k.recip(st6[:, 7:8], st6[:, 7:8])
                k.ts(vn[:], uv_[:, 256:512], st6[:, 6:7], ALU.subtract, st6[:, 7:8], ALU.mult)
                k.tt(vnb_[:], vn[:], sgng[:], ALU.mult)
                zt, zts = ztm.next()
                k.cp(zt[:], pz[:, 0:256], eng='act')
                k.dma('sp', self.z_scr.ap()[t0 + t * 128:t0 + (t + 1) * 128, :], zt[:], zts)
                vt, vts = vtm.next()
                k.cp(vt[:], pz[:, 256:512], eng='pool' if False else 'dve')
                k.dma('sp', self.av_scr.ap()[t0 + t * 128:t0 + (t + 1) * 128, :], vt[:], vts)
                at_, ats = abt.next()
                k.cp(at_[:], pab[:, 0:8], eng='act')
                nn = b * 4 + t
                k.dma('sp', self.ab_scr.ap()[:, nn * 8:(nn + 1) * 8], at_[:], ats)
                return (t, uv_, vnb_)

            def stage2(t, uv_, vnb_):
                tok = slice(t * 128, (t + 1) * 128)
                pm = self.bank()
                for g in range(4):
                    k.mm(pm[:, g * 64:(g + 1) * 64], wsT[:, g, :], vnb_[:, g * 64:(g + 1) * 64])
                k.tt(mxs[:].rearrange("p (g c) -> p g c", g=4), pm[:, 0:256].rearrange("p (g c) -> p g c", g=4),
                     bsT[:].unsqueeze(2).broadcast_to([128, 4, 64]), ALU.add)
                k.tt(smix[:], mxs[:], uv_[:, 0:256], ALU.mult)
                pt = self.bank()
                ptv = pt[:].bitcast(BF16)
                for j in range(2):
                    k.tr(ptv[:, j * 128:(j + 1) * 128], smix[:, j * 128:(j + 1) * 128], self.identb[:])
                k.cp(sm[:, :, tok], ptv[:, 0:256].rearrange("p (j t) -> p j t", j=2), eng='act')
            pend = []
            for t in range(4):
                pend.append(stage1(t))
                if len(pend) > 1:
                    stage2(*pend.pop(0))
            while pend:
                stage2(*pend.pop(0))
            k.dma('sp', self.mix_scr.ap()[1, :, t0:t0 + 512].rearrange("(j p) t -> p j t", p=128), sm[:], sms)
        S.close()

    ATT_WIN = (6, 17, 1 << 20, 1 << 20)

    def att_gen(self, l, osets):
        k = self.k
        nc = k.nc
        L = self.L
        NT = L // 128
        NQ = L // 512
        S = Scope(k)
        w = self.w
        lam_init = 0.8 - 0.6 * math.exp(-0.3 * l)
        KT = S.sb('KT', [128, 2, L], BF16)
        QTs = Rot(S, 'QT', 2, [128, 2, 512], BF16)
        for j in range(2):
            k.dma('sp', KT[:, j, :], self.ak_scr.ap()[j * 128:(j + 1) * 128, :], 'KT')
        ekey = S.sb('ekey', [128, 4], F32)
        k.dma('sp', ekey[:], self.c['c_ekey'].ap(), 'ekey')
        abias = S.sb('abias', [128, 4 * 67], F32)
        k.dma('sp', abias[:], self.c['c_abias'].ap(), 'abias')
        cmask = S.sb('cmask', [128, 4, 512], BF16)
        k.dma('sp', cmask[:], self.c['c_cmask'].ap(), 'cmask')
        m32 = S.sb('m32', [128, 4], F32)
        k.dma('sp', m32[:], self.c['c_m32'].ap(), 'm32')
        Vaug = S.sb('Vaug', [128, NT, 4, 65], BF16)
        onesn = S.sb('onesn', [128, NT], F32)
        k.memset(onesn[:], 1.0)
        CH = min(8, NT)
        vtmp = Rot(S, 'vtmp', 2, [128, CH, 256], BF16)
        for n0 in range(0, NT, CH):
            vt, vts = vtmp.next()
            k.dma('sp', vt[:], self.av_scr.ap()[n0 * 128:(n0 + CH) * 128, :].rearrange("(n p) c -> p n c", p=128), vts)
            for h in range(4):
                k.ts(Vaug[:, n0:n0 + CH, h, 0:64], vt[:, :, h * 64:(h + 1) * 64], ekey[:, h:h + 1], ALU.mult)
        for h in range(4):
            k.act(Vaug[:, :, h, 64:65], onesn[:].unsqueeze(2), AF.Identity, scale=ekey[:, h:h + 1])
        lq = [self.load_row_bcast(S, w[n].ap()[l], 32, n) for n in ('diff_lq1', 'diff_lk1', 'diff_lq2', 'diff_lk2')]
        lt = S.sb('lt', [128, 32], F32)
        lv = S.sb('lv', [128, 8], F32)
        for i in range(2):
            k.tt(lt[:], lq[2 * i][:], lq[2 * i + 1][:], ALU.mult)
            k.op('dve', lambda: nc.vector.tensor_reduce(lv[:, i:i + 1], lt[:], AX.X, ALU.add), [lt], [lv])
        k.act(lv[:, 2:4], lv[:, 0:2], AF.Exp)
        k.tt(lv[:, 4:5], lv[:, 2:3], lv[:, 3:4], ALU.subtract)
        k.ts(lv[:, 5:6], lv[:, 4:5], lam_init, ALU.add)
        gd = self.load_row_bcast(S, w['diff_norm_g'].ap()[l], 64, 'gd')
        k.ts(gd[:], gd[:], 1.0 - lam_init, ALU.mult)
        QTm = Rot(S, 'QTm', 4, [128, 512], BF16)
        Pb = Rot(S, 'Pb', 5, [128, 512], BF16)
        rr = S.sb('rr', [128, 16], F32)
        t1 = S.sb('t1', [128, 4, 64], F32)
        t2 = S.sb('t2', [128, 4, 64], F32)
        sq = S.sb('sq', [128, 4, 64], F32)
        omix = Rot(S, 'omix', 2, [128, 4, 256], BF16)
        ostg = Rot(S, 'ostg', 2, [128, 2, 512], BF16)
        Osets = [[self.banks[a_], self.banks[b_]] for (a_, b_) in osets]
        scale = 32.0 ** -0.5
        LA = 2
        gi = 0
        def ldq(qq):
            QT_, qts_ = QTs.next()
            k.dma('sp', QT_[:], self.aq_scr.ap()[:, qq * 512:(qq + 1) * 512].rearrange("(j p) t -> p j t", p=128), qts_)
            return QT_
        nxt_q = ldq(0)
        for qb in range(NQ):
            om, _ = omix.next()
            QT = nxt_q
            if qb + 1 < NQ:
                nxt_q = ldq(qb + 1)
            for h in range(4):
                t2i = h // 2
                O = Osets[gi % len(Osets)]
                gi += 1
                qm = []
                for m in range(2):
                    qt_, _ = QTm.next()
                    k.ts(qt_[:], QT[:, t2i, :], m32[:, (h % 2) * 2 + m:(h % 2) * 2 + m + 1], ALU.mult,
                         eng=('pool' if m else 'dve'))
                    qm.append(qt_)
                for m in range(2):
                    k.memset(O[m][:, 0:260], 0.0)
                blocks = [(s_, 128) for s_ in range(4)] if h == 0 else [(None, 512)]
                steps = []
                for (s_, QB) in blocks:
                    qt_first = 4 * qb + (s_ if s_ is not None else 0)
                    qt_last = 4 * qb + (s_ if s_ is not None else 3)
                    kt_lo = max(0, qt_first - self.ATT_WIN[h])
                    for kt in range(kt_lo, qt_last + 1):
                        for m in range(2):
                            steps.append((s_, QB, qt_first, kt, m))
                live = {}

                def front(i):
                    s_, QB, qt_first, kt, m = steps[i]
                    jd = kt - qt_first
                    ps = self.bank()
                    qcols = slice(s_ * 128, (s_ + 1) * 128) if s_ is not None else slice(0, 512)
                    k.mm(ps[:, 0:QB], KT[:, t2i, kt * 128:(kt + 1) * 128], qm[m][:, qcols])
                    pT, _ = Pb.next()
                    dd = (qt_first - kt) + 3
                    k.act(pT[:, 0:QB], ps[:, 0:QB], AF.Exp, scale=scale, bias=abias[:, h * 67 + dd:h * 67 + dd + 1])
                    if jd >= 0:
                        k.tt(pT[:, 0:QB], pT[:, 0:QB], cmask[:, jd, 0:QB], ALU.mult, eng='pool')
                    live[i] = pT

                def back(i):
                    s_, QB, qt_first, kt, m = steps[i]
                    jd = kt - qt_first
                    pT = live.pop(i)
                    for ss_ in ([s_] if s_ is not None else range(4)):
                        if s_ is None and ss_ < jd:
                            continue
                        col = 0 if s_ is not None else ss_ * 128
                        k.mm(O[m][:, ss_ * 65:(ss_ + 1) * 65], pT[:, col:col + 128], Vaug[:, kt, h, :],
                             start=False, stop=True)
                for i in range(len(steps) + LA):
                    if i < len(steps):
                        front(i)
                    if i >= LA:
                        back(i - LA)
                    yield
                for m in range(2):
                    k.recip(rr[:, 4 * m:4 * m + 4], O[m][:, 0:260].rearrange("p (s e) -> p s e", e=65)[:, :, 64])
                k.ts(rr[:, 4:8], rr[:, 4:8], lv[:, 5:6], ALU.mult)
                k.tt(t1[:], O[0][:, 0:260].rearrange("p (s e) -> p s e", e=65)[:, :, 0:64],
                     rr[:, 0:4].unsqueeze(2).broadcast_to([128, 4, 64]), ALU.mult)
                k.tt(t2[:], O[1][:, 0:260].rearrange("p (s e) -> p s e", e=65)[:, :, 0:64],
                     rr[:, 4:8].unsqueeze(2).broadcast_to([128, 4, 64]), ALU.mult)
                k.tt(t1[:], t1[:], t2[:], ALU.subtract)
                k.tt(sq[:], t1[:], t1[:], ALU.mult)
                k.op('dve', lambda: nc.vector.tensor_reduce(rr[:, 8:12], sq[:], AX.X, ALU.add), [sq], [rr])
                k.ts(rr[:, 8:12], rr[:, 8:12], 1.0 / 64, ALU.mult, EPS, ALU.add)
                k.act(rr[:, 8:12], rr[:, 8:12], AF.Ln)
                k.act(rr[:, 12:16], rr[:, 8:12], AF.Exp, scale=-0.5)
                yield
                k.tt(t1[:], t1[:], rr[:, 12:16].unsqueeze(2).broadcast_to([128, 4, 64]), ALU.mult)
                k.tt(om[:, :, h * 64:(h + 1) * 64], t1[:], gd[:].unsqueeze(1).broadcast_to([128, 4, 64]), ALU.mult)
            og, ogs = ostg.next()
            for s_ in range(4):
                pt = self.bank()
                ptv = pt[:].bitcast(BF16)
                for j in range(2):
                    k.tr(ptv[:, j * 128:(j + 1) * 128], om[:, s_, j * 128:(j + 1) * 128], self.identb[:])
                k.cp(og[:, :, s_ * 128:(s_ + 1) * 128], ptv[:, 0:256].rearrange("p (j t) -> p j t", j=2), eng='act')
            k.dma('sp', self.mix_scr.ap()[3, :, qb * 512:(qb + 1) * 512].rearrange("(j p) t -> p j t", p=128), og[:], ogs)
            yield
        return S

    def run_gens(self, specs, ratio=None):
        st = [{'g': g, 'pool': pool, 'bi': 0, 'alive': True, 'S': None} for (g, pool) in specs]
        ratio = ratio or [1] * len(st)
        while any(x['alive'] for x in st):
            for x, r in zip(st, ratio):
                for _ in range(r):
                    if not x['alive']:
                        break
                    self.bank_pool = x['pool']
                    self.bi = x['bi']
                    try:
                        next(x['g'])
                    except StopIteration as e:
                        x['alive'] = False
                        x['S'] = e.value
                    x['bi'] = self.bi
        self.bank_pool = list(range(8))
        self.bi = 0
        self.k.barrier()
        for x in reversed(st):
            x['S'].es.close()

    def phase_att(self, l):
        self.run_gens([(self.att_gen(l, [(4, 5), (6, 7)]), [0, 1, 2, 3])])

    def phase_dn(self, l):
        self.run_gens([(self.dn_gen(l), list(range(8)))])

    def dn_gen(self, l, nsets=2):
        k = self.k
        nc = k.nc
        L = self.L
        NT = L // 128
        NB = L // 512
        S = Scope(k)
        w = self.w
        onesf = S.sb('onesf', [128, 128], F32)
        k.memset(onesf[:], 1.0)
        nmsl = S.sb('nmsl', [128, 128], F32)
        nmsu = S.sb('nmsu', [128, 128], F32)
        k.dma('sp', nmsl[:], self.c['c_msl'].ap(), 'nmsl')
        k.dma('sp', nmsu[:], self.c['c_msu'].ap(), 'nmsu')
        k.ts(nmsl[:], nmsl[:], -1.0, ALU.mult)
        k.ts(nmsu[:], nmsu[:], -1.0, ALU.mult)
        alog = self.load_row_bcast(S, w['dn_a_log'].ap()[l], 4, 'alog')
        dtb = self.load_row_bcast(S, w['dn_dt_bias'].ap()[l], 4, 'dtb')
        k.act(alog[:], alog[:], AF.Exp)
        k.ts(alog[:], alog[:], -1.0, ALU.mult)
        gn = self.load_row_bcast(S, w['dn_norm_g'].ap()[l], 64, 'gn')
        ab = S.sb('ab', [128, NT, 8], F32)
        k.dma('sp', ab[:].rearrange("p n c -> p (n c)"), self.ab_scr.ap(), 'ab')
        gt = S.sb('gt', [128, NT, 4], F32)
        bt = S.sb('bt', [128, NT, 4], F32)
        Gt = S.sb('Gt', [128, NT, 4], F32)
        GL = S.sb('GL', [128, NT, 4], F32)
        eG = S.sb('eG', [128, NT, 4], F32)
        kds = S.sb('kds', [128, NT, 4], F32)
        gla = S.sb('gla', [128, NT, 4], F32)
        k.tt(gt[:], ab[:, :, 0:4], dtb[:].unsqueeze(1).broadcast_to([128, NT, 4]), ALU.add)
        k.act(gt[:], gt[:], AF.Exp)
        k.ts(gt[:], gt[:], 1.0, ALU.add)
        k.act(gt[:], gt[:], AF.Ln)
        k.tt(gt[:], gt[:], alog[:].unsqueeze(1).broadcast_to([128, NT, 4]), ALU.mult)
        k.act(bt[:], ab[:, :, 4:8], AF.Sigmoid)
        gflat = gt[:].rearrange("p n c -> p (n c)")
        for c0 in range(0, NT * 4, 512):
            m = min(512, NT * 4 - c0)
            pc = self.bank()
            k.mm(pc[:, 0:m], self.mui[:], gflat[:, c0:c0 + m])
            k.cp(Gt[:].rearrange("p n c -> p (n c)")[:, c0:c0 + m], pc[:, 0:m])
            pc2 = self.bank()
            k.mm(pc2[:, 0:m], onesf[:], gflat[:, c0:c0 + m])
            k.cp(GL[:].rearrange("p n c -> p (n c)")[:, c0:c0 + m], pc2[:, 0:m])
        k.act(eG[:], Gt[:], AF.Exp)
        k.tt(kds[:], GL[:], Gt[:], ALU.subtract)
        k.act(kds[:], kds[:], AF.Exp)
        k.act(gla[:], GL[:], AF.Exp)
        Sf = S.sb('Sf', [64, 4, 64], F32)
        Sb = S.sb('Sb', [64, 4, 64], BF16)
        k.memset(Sf[:], 0.0)
        k.memset(Sb[:], 0.0)
        qfs = Rot(S, 'qf', 2, [128, 6, 512], BF16)
        zts = Rot(S, 'zt', 2, [128, 4, 256], BF16)
        zsb = S.sb('zsb', [128, 4, 256], F32)

        def alloc_set(i):
            B = {}
            def a(name, shape, dt):
                B[name] = S.sb('%s_%d' % (name, i), shape, dt)
            a('qkv', [128, 768], F32); a('sq', [128, 512], F32); a('r8', [128, 16], F32)
            a('qn_f', [128, 4, 64], F32); a('kn_f', [128, 4, 64], F32); a('kb_f', [128, 4, 64], F32)
            for n_ in ('qn_b', 'kn_b', 'kb_b', 'qd_b', 'kbg_b', 'kdec_b', 'vb_b'):
                a(n_, [128, 4, 64], BF16)
            for n_ in ('kn_b', 'kb_b', 'qn_b', 'qd_b'):
                a(n_ + 'T', [64, 4, 128], BF16)
            a('dG', [128, 4, 128], F32)
            for n_ in ('Xn', 'Xm', 'Dt', 'Dl', 'Dlm', 'Dtm1', 'Dtm2'):
                a(n_, [128, 128], F32)
            for n_ in ('P0', 'P1', 'PT0', 'PT1', 'TT0', 'TT1'):
                a(n_, [128, 4, 128], F32)
            a('TTb', [128, 4, 128], BF16); a('Aq', [128, 4, 128], BF16)
            a('u_sb', [128, 4, 64], F32); a('wwT', [64, 4, 128], BF16); a('vnew', [128, 4, 64], BF16)
            a('o_sb', [128, 4, 64], F32); a('osq', [128, 4, 64], F32); a('dmix', [128, 256], BF16)
            return B
        sets = [alloc_set(i) for i in range(nsets)]
        dstg = Rot(S, 'dstg', 2, [128, 2, 512], BF16)

        def bc(ap2):
            return ap2.unsqueeze(2).broadcast_to([128, 4, 64])

        def rsq(out, in_, mult):
            k.ts(out, in_, mult, ALU.mult, EPS, ALU.add)
            k.act(out, out, AF.Ln)
            k.act(out, out, AF.Exp, scale=-0.5)

        def dn_tile(B, n, t, qf, dg):
            tok = slice(t * 128, (t + 1) * 128)
            qkv, sq, r8 = B['qkv'], B['sq'], B['r8']
            qn_f, kn_f, kb_f = B['qn_f'], B['kn_f'], B['kb_f']
            pt = self.bank()
            ptv = pt[:].bitcast(BF16)
            for j in range(6):
                k.tr(ptv[:, j * 128:(j + 1) * 128], qf[:, j, tok], self.identb[:])
            k.cp(qkv[:], ptv[:, 0:768], eng='act')
            yield
            k.tt(sq[:], qkv[:, 0:512], qkv[:, 0:512], ALU.mult)
            k.op('dve', lambda: nc.vector.tensor_reduce(r8[:, 0:8], sq[:].rearrange("p (a d) -> p a d", d=64), AX.X, ALU.add),
                 [sq], [r8])
            rsq(r8[:, 8:16], r8[:, 0:8], 1.0)
            k.ts(r8[:, 8:12], r8[:, 8:12], 0.125, ALU.mult)
            yield
            q3 = qkv[:, 0:256].rearrange("p (h d) -> p h d", h=4)
            k3 = qkv[:, 256:512].rearrange("p (h d) -> p h d", h=4)
            v3 = qkv[:, 512:768].rearrange("p (h d) -> p h d", h=4)
            k.tt(qn_f[:], q3, bc(r8[:, 8:12]), ALU.mult)
            k.tt(kn_f[:], k3, bc(r8[:, 12:16]), ALU.mult)
            k.tt(kb_f[:], kn_f[:], bc(bt[:, n, :]), ALU.mult)
            k.cp(B['qn_b'][:], qn_f[:], eng='act')
            k.cp(B['kn_b'][:], kn_f[:], eng='act')
            k.cp(B['kb_b'][:], kb_f[:], eng='act')
            yield
            k.tt(B['qd_b'][:], qn_f[:], bc(eG[:, n, :]), ALU.mult)
            k.tt(B['kbg_b'][:], kb_f[:], bc(eG[:, n, :]), ALU.mult, eng='pool')
            k.tt(B['kdec_b'][:], kn_f[:], bc(kds[:, n, :]), ALU.mult)
            k.tt(B['vb_b'][:], v3, bc(bt[:, n, :]), ALU.mult, eng='pool')
            for gi_, (n1, n2) in enumerate((('kn_b', 'kb_b'), ('qn_b', 'qd_b'))):
                pf = self.bank()
                pfv = pf[:].bitcast(BF16)
                for ii, nm in enumerate((n1, n2)):
                    for h in range(4):
                        k.tr(pfv[0:64, ii * 512 + h * 128:ii * 512 + (h + 1) * 128], B[nm][:, h, :], self.identb[:])
                k.cp(B[n1 + 'T'][:].rearrange("p h t -> p (h t)"), pfv[0:64, 0:512], eng='act')
                k.cp(B[n2 + 'T'][:].rearrange("p h t -> p (h t)"), pfv[0:64, 512:1024])
                yield
            knT, kbT, qnT, qdT = B['kn_bT'], B['kb_bT'], B['qn_bT'], B['qd_bT']
            dG, Xn, Xm, Dt, Dl, Dlm, Dtm1, Dtm2 = (B[x_] for x_ in ('dG', 'Xn', 'Xm', 'Dt', 'Dl', 'Dlm', 'Dtm1', 'Dtm2'))
            for h in range(4):
                k.ts(dG[:, h, :], self.identf[:], Gt[:, n, h:h + 1], ALU.mult, eng='pool')
            Pm = [B['P0'], B['P1']]
            PTm = [B['PT0'], B['PT1']]
            TTm = [B['TT0'], B['TT1']]
            Aq = B['Aq']
            P, PT, TT = Pm[0], PTm[0], TTm[0]
            for h in range(4):
                ps = self.bank()
                k.mm(ps[:, 0:128], kbT[:, h, :], knT[:, h, :])
                k.mm(ps[:, 128:256], knT[:, h, :], kbT[:, h, :])
                k.mm(ps[:, 256:384], knT[:, h, :], qnT[:, h, :])
                k.mm(ps[:, 384:512], onesf[:], dG[:, h, :])
                k.ts(Xn[:], ps[:, 384:512], Gt[:, n, h:h + 1], ALU.subtract, 0.0, ALU.min)
                k.ts(Xm[:], ps[:, 384:512], Gt[:, n, h:h + 1], ALU.subtract, 0.0, ALU.max)
                k.act(Dt[:], Xn[:], AF.Exp)
                k.act(Dl[:], Xm[:], AF.Exp, scale=-1.0)
                k.tt(Dlm[:], Dl[:], nmsl[:], ALU.mult, eng='pool')
                k.tt(Dtm1[:], Dt[:], nmsu[:], ALU.mult, eng='pool')
                k.tt(Dtm2[:], Dt[:], self.mui[:], ALU.mult, eng='pool')
                k.tt(P[:, h, :], ps[:, 0:128], Dlm[:], ALU.mult)
                k.tt(PT[:, h, :], ps[:, 128:256], Dtm1[:], ALU.mult)
                k.tt(Aq[:, h, :], ps[:, 256:384], Dtm2[:], ALU.mult)
                k.tt(TT[:, h, :], PT[:, h, :], self.identf[:], ALU.add, eng='pool')
                yield
            cur = 0
            for lev in range(6):
                nxt = 1 - cur
                pP = self.bank()
                for h in range(4):
                    k.mm(pP[:, h * 128:(h + 1) * 128], PTm[cur][:, h, :], Pm[cur][:, h, :])
                k.cp(Pm[nxt][:].rearrange("p h t -> p (h t)"), pP[:], eng='act')
                if lev < 5:
                    pQ = self.bank()
                    for h in range(4):
                        k.mm(pQ[:, h * 128:(h + 1) * 128], Pm[cur][:, h, :], PTm[cur][:, h, :])
                    k.cp(PTm[nxt][:].rearrange("p h t -> p (h t)"), pQ[:])
                yield
                pT_ = self.bank()
                for h in range(4):
                    k.mm(pT_[:, h * 128:(h + 1) * 128], Pm[nxt][:, h, :], TTm[cur][:, h, :])
                k.tt(TTm[nxt][:].rearrange("p h t -> p (h t)"), pT_[:], TTm[cur][:].rearrange("p h t -> p (h t)"), ALU.add)
                cur = nxt
                yield
            TT = B['TTb']
            k.cp(TT[:], TTm[cur][:], eng='pool')
            u_sb, wwT, vnew, o_sb, osq, dmix = (B[x_] for x_ in ('u_sb', 'wwT', 'vnew', 'o_sb', 'osq', 'dmix'))
            pu = self.bank()
            for h in range(4):
                k.mm(pu[:, h * 64:(h + 1) * 64], TT[:, h, :], B['vb_b'][:, h, :])
            k.cp(u_sb[:].rearrange("p h e -> p (h e)"), pu[:, 0:256], eng='act')
            pw = self.bank()
            for h in range(4):
                k.mm(pw[0:64, h * 128:(h + 1) * 128], B['kbg_b'][:, h, :], TT[:, h, :])
            k.cp(wwT[:].rearrange("p h t -> p (h t)"), pw[0:64, :])
            yield
            pS = self.bank()
            for h in range(4):
                k.mm(pS[:, h * 64:(h + 1) * 64], wwT[:, h, :], Sb[:, h, :])
            k.tt(vnew[:].rearrange("p h e -> p (h e)"), u_sb[:].rearrange("p h e -> p (h e)"), pS[:, 0:256], ALU.subtract)
            po = self.bank()
            for h in range(4):
                k.mm(po[:, h * 64:(h + 1) * 64], qdT[:, h, :], Sb[:, h, :], start=True, stop=False)
                k.mm(po[:, h * 64:(h + 1) * 64], Aq[:, h, :], vnew[:, h, :], start=False, stop=True)
            pK = self.bank()
            for h in range(4):
                k.mm(pK[0:64, h * 64:(h + 1) * 64], B['kdec_b'][:, h, :], vnew[:, h, :])
            k.tt(Sf[:], Sf[:], gla[0:64, n, :].unsqueeze(2).broadcast_to([64, 4, 64]), ALU.mult)
            k.tt(Sf[:].rearrange("p h e -> p (h e)"), Sf[:].rearrange("p h e -> p (h e)"), pK[0:64, 0:256], ALU.add)
            k.cp(Sb[:], Sf[:], eng='act')
            k.cp(o_sb[:].rearrange("p h e -> p (h e)"), po[:, 0:256], eng='act')
            yield
            k.tt(osq[:], o_sb[:], o_sb[:], ALU.mult)
            k.op('dve', lambda: nc.vector.tensor_reduce(r8[:, 0:4], osq[:], AX.X, ALU.add), [osq], [r8])
            rsq(r8[:, 4:8], r8[:, 0:4], 1.0 / 64)
            k.tt(o_sb[:], o_sb[:], bc(r8[:, 4:8]), ALU.mult)
            k.tt(o_sb[:], o_sb[:], gn[:].unsqueeze(1).broadcast_to([128, 4, 64]), ALU.mult, eng='pool')
            k.tt(dmix[:], o_sb[:].rearrange("p h e -> p (h e)"), zsb[:, t, :], ALU.mult)
            yield
            px = self.bank()
            pxv = px[:].bitcast(BF16)
            for j in range(2):
                k.tr(pxv[:, j * 128:(j + 1) * 128], dmix[:, j * 128:(j + 1) * 128], self.identb[:])
            k.cp(dg[:, :, tok], pxv[:, 0:256].rearrange("p (j t) -> p j t", j=2), eng='act')

        def lddn(bb):
            tt0 = bb * 512
            qf_, qfsem_ = qfs.next()
            k.dma('sp', qf_[:], self.qkvc_scr.ap()[:, tt0:tt0 + 512].rearrange("(j p) t -> p j t", p=128), qfsem_)
            zt_, ztsem_ = zts.next()
            k.dma('sp', zt_[:], self.z_scr.ap()[tt0:tt0 + 512, :].rearrange("(t p) c -> p t c", p=128), ztsem_)
            return qf_, zt_
        nxt_dn = lddn(0)
        for b in range(NB):
            t0 = b * 512
            qf, zt = nxt_dn
            if b + 1 < NB:
                nxt_dn = lddn(b + 1)
            k.act(zsb[:], zt[:], AF.Silu)
            dg, dgs = dstg.next()
            for tp in range(4 // nsets):
                gens = [dn_tile(sets[i], b * 4 + tp * nsets + i, tp * nsets + i, qf, dg) for i in range(nsets)]
                alive = [True] * nsets
                while any(alive):
                    for i in range(nsets):
                        if alive[i]:
                            try:
                                next(gens[i])
                            except StopIteration:
                                alive[i] = False
                    yield
            k.dma('sp', self.mix_scr.ap()[2, :, t0:t0 + 512].rearrange("(j p) t -> p j t", p=128), dg[:], dgs)
        return S

    def phase_s5(self, l):
        k = self.k
        nc = k.nc
        L = self.L
        NB = L // 512
        nch = L // 8
        S = Scope(k)
        w = self.w
        nlev = 0
        while (1 << nlev) < nch:
            nlev += 1
        W1 = S.sb('W1pad', [128, 2, 4, 8, 2, 128], BF16)
        Kbd = S.sb('Kbd', [128, 2, 8, 128], BF16)
        W2 = S.sb('W2bd', [128, 2, 4, 8, 2, 128], BF16)
        A8 = [S.sb('A8_%d' % i, [128, 3, 8], F32) for i in range(nlev)]
        wglu = S.sb('wglu', [128, 2, 256], BF16)
        k.dma('pool', wglu[:], w['s5_w_glu'].ap()[l].rearrange("(j p) n -> p j n", p=128), 'wglu')
        dcol = S.sb('dcol', [128, 2], F32)
        bglu = S.sb('bglu', [128, 2], F32)
        SP = Scope(k)
        dcol_t = self.load_cols(SP, w['s5_d'].ap()[l].rearrange("(j p) -> j p", p=128), 2, 'dcolt')
        bglu_t = self.load_cols(SP, w['s5_b_glu'].ap()[l].rearrange("(j p) -> j p", p=128), 2, 'bglut')
        k.cp(dcol[:], dcol_t[:])
        k.cp(bglu[:], bglu_t[:])
        ctr = [0]

        def T(shape=(128, 128), dt=F32):
            ctr[0] += 1
            return SP.sb('s5t%d' % ctr[0], list(shape), dt)

        def cl(name, shape):
            t_ = SP.sb(name, list(shape), F32)
            k.dma('sp', t_[:], self.c[name].ap(), name)
            return t_
        sel8 = cl('c_sel8', [8, 128])
        bm16 = cl('c_bm16', [128, 128])
        w1mask = cl('c_w1mask', [128, 8])
        w2mask = cl('c_w2mask', [128, 4, 128])
        lam8 = SP.sb('lam8', [8, 2, 2, 64], F32)
        k.dma('sp', lam8[:, 0, :, :], w['s5_lam_re'].ap()[l].rearrange("(j g) p -> g j p", g=8), 'lam8')
        k.dma('sp', lam8[:, 1, :, :], w['s5_lam_im'].ap()[l].rearrange("(j g) p -> g j p", g=8), 'lam8')
        ls8 = SP.sb('ls8', [8, 2], F32)
        for j in range(2):
            k.dma('sp', ls8[:, j:j + 1], w['s5_log_step'].ap()[l, j * 8:(j + 1) * 8].unsqueeze(1), 'ls8')
        lr, li, st = T(), T(), T((128, 2))
        pb = self.bank()
        k.mm(pb[:, 0:256], sel8[:], lam8[:].rearrange("g r j p -> g (r j p)"))
        k.mm(pb[:, 256:258], sel8[:], ls8[:])
        k.cp(lr[:], pb[:, 0:128])
        k.cp(li[:], pb[:, 128:256])
        k.act(st[:], pb[:, 256:258], AF.Exp)
        stb = st[:].unsqueeze(2).broadcast_to([128, 2, 64])

        def v3(t_):
            return t_[:].rearrange("q (j p) -> q j p", j=2)
        th, lrs = T(), T()
        k.tt(v3(th), v3(li), stb, ALU.mult)
        k.tt(v3(lrs), v3(lr), stb, ALU.mult)
        er, s32, cs, sn, t1, t2 = T(), T(), T(), T(), T(), T()
        k.act(er[:], lrs[:], AF.Exp)
        k.act(s32[:], th[:], AF.Sin, scale=1.0 / 32)
        k.act(sn[:], th[:], AF.Sin, scale=1.0 / 16)
        k.tt(t1[:], s32[:], s32[:], ALU.mult)
        k.ts(cs[:], t1[:], -2.0, ALU.mult, 1.0, ALU.add)
        for _ in range(4):
            k.tt(t1[:], cs[:], cs[:], ALU.mult)
            k.tt(t2[:], sn[:], sn[:], ALU.mult)
            k.stt(sn[:], cs[:], 2.0, sn[:], ALU.mult, ALU.mult)
            k.tt(cs[:], t1[:], t2[:], ALU.subtract)
        ar, ai = T(), T()
        k.tt(ar[:], er[:], cs[:], ALU.mult)
        k.tt(ai[:], er[:], sn[:], ALU.mult)
        den, nr, cre, cim = T(), T(), T(), T()
        k.tt(t1[:], lr[:], lr[:], ALU.mult)
        k.tt(t2[:], li[:], li[:], ALU.mult)
        k.tt(den[:], t1[:], t2[:], ALU.add)
        k.recip(den[:], den[:])
        k.ts(nr[:], ar[:], -1.0, ALU.add)
        k.tt(t1[:], nr[:], lr[:], ALU.mult)
        k.tt(t2[:], ai[:], li[:], ALU.mult)
        k.tt(cre[:], t1[:], t2[:], ALU.add)
        k.tt(cre[:], cre[:], den[:], ALU.mult)
        k.tt(t1[:], ai[:], lr[:], ALU.mult)
        k.tt(t2[:], nr[:], li[:], ALU.mult)
        k.tt(cim[:], t1[:], t2[:], ALU.subtract)
        k.tt(cim[:], cim[:], den[:], ALU.mult)

        def cmul(o_r, o_i, a_r, a_i, b_r, b_i, neg_im=False):
            k.tt(t1[:], a_r, b_r, ALU.mult)
            k.tt(t2[:], a_i, b_i, ALU.mult)
            k.tt(o_r, t1[:], t2[:], ALU.subtract)
            k.tt(t1[:], a_r, b_i, ALU.mult)
            k.tt(t2[:], a_i, b_r, ALU.mult)
            if neg_im:
                k.stt(o_i, t1[:], -1.0, t2[:], ALU.mult, ALU.subtract)
            else:
                k.tt(o_i, t1[:], t2[:], ALU.add)
        Bre, Bim = T(), T()
        for (nm, dst) in (('s5_b_re', Bre), ('s5_b_im', Bim)):
            bn = SP.sb(nm + 'n', [64, 16, 16], F32)
            k.dma('sp', bn[:], w[nm].ap()[l].rearrange("g p h -> p g h"), nm + 'n')
            for j in range(2):
                pb = self.bank()
                k.tr(pb[:, 0:64], bn[:, j * 8:(j + 1) * 8, :].rearrange("p g h -> p (g h)"), self.identf[0:64, 0:64])
                k.cp(dst[:, j * 64:(j + 1) * 64], pb[:, 0:64])
        Cre, Cim = T(), T()
        k.dma('sp', v3(Cre), w['s5_c_re'].ap()[l].rearrange("(j g) h p -> (g h) j p", g=8), 'Cre')
        k.dma('sp', v3(Cim), w['s5_c_im'].ap()[l].rearrange("(j g) h p -> (g h) j p", g=8), 'Cim')
        Bbr, Bbi = T(), T()
        cmul(Bbr[:], Bbi[:], cre[:], cim[:], Bre[:], Bim[:])
        Pr = [T() for _ in range(9)]
        Pi = [T() for _ in range(9)]
        k.memset(Pr[0][:], 1.0)
        k.memset(Pi[0][:], 0.0)
        k.cp(Pr[1][:], ar[:])
        k.cp(Pi[1][:], ai[:])
        for kk in range(2, 9):
            cmul(Pr[kk][:], Pi[kk][:], Pr[kk - 1][:], Pi[kk - 1][:], ar[:], ai[:])
        AB = [SP.sb('AB%d' % tau, [128, 2, 2, 64], F32) for tau in range(8)]
        abr, abi = T(), T()
        for tau in range(8):
            cmul(abr[:], abi[:], Pr[tau][:], Pi[tau][:], Bbr[:], Bbi[:])
            k.cp(AB[tau][:, :, 0, :], v3(abr), eng='act')
            k.cp(AB[tau][:, :, 1, :], v3(abi), eng='act')
        w1m = w1mask[:].rearrange("q (m h) -> q m h", h=2).unsqueeze(3).broadcast_to([128, 4, 2, 64])
        for s_ in range(8):
            for j in range(2):
                for ri in range(2):
                    src = AB[7 - s_][:, j, ri, :].unsqueeze(1).unsqueeze(1).broadcast_to([128, 4, 2, 64])
                    k.tt(W1[:, j, :, s_, ri, :].rearrange("q m (h p) -> q m h p", h=2), src, w1m, ALU.mult,
                         eng=('pool' if ri else 'dve'))
        CT0 = SP.sb('CT0', [128, 2, 2, 64], F32)
        k.cp(CT0[:, :, 0, :], v3(Cre))
        k.ts(CT0[:, :, 1, :], v3(Cim), -1.0, ALU.mult)
        CT0T = SP.sb('CT0T', [128, 2, 128], F32)
        for j in range(2):
            pb = self.bank()
            k.tr(pb[:, 0:128], CT0[:, j, :, :].rearrange("q r p -> q (r p)"), self.identf[:])
            k.cp(CT0T[:, j, :], pb[:, 0:128])
        abT = Rot(SP, 'abT', 2, [128, 128], F32)
        for tau in range(8):
            for j in range(2):
                pb = self.bank()
                k.tr(pb[:, 0:128], AB[tau][:, j, :, :].rearrange("q r p -> q (r p)"), self.identf[:])
                at_, _ = abT.next()
                k.cp(at_[:], pb[:, 0:128], eng='act')
                k.mm(pb[:, 128:256], at_[:], CT0T[:, j, :])
                k.tt(Kbd[:, j, tau, :], pb[:, 128:256], bm16[:], ALU.mult)
        CA = SP.sb('CA', [128, 2, 2, 2, 64], F32)
        car, cai = T(), T()
        for s_ in range(8):
            cmul(car[:], cai[:], Cre[:], Cim[:], Pr[s_ + 1][:], Pi[s_ + 1][:], neg_im=True)
            for dup in range(2):
                k.cp(CA[:, :, 0, dup, :], v3(car), eng='act')
                k.cp(CA[:, :, 1, dup, :], v3(cai), eng='pool')
            for j in range(2):
                for ri in range(2):
                    pb = self.bank()
                    k.tr(pb[:, 0:128], CA[:, j, ri, :, :].rearrange("q d p -> q (d p)"), self.identf[:])
                    k.tt(W2[:, j, :, s_, ri, :], pb[:, 0:128].unsqueeze(1).broadcast_to([128, 4, 128]), w2mask[:], ALU.mult)
        a8d = SP.sb('a8d', [128, 2, 2, 2, 64], F32)
        for dup in range(2):
            k.cp(a8d[:, :, 0, dup, :], v3(Pr[8]))
            k.cp(a8d[:, :, 1, dup, :], v3(Pi[8]))
        for j in range(2):
            for ri in range(2):
                pb = self.bank()
                k.tr(pb[:, 0:128], a8d[:, j, ri, :, :].rearrange("q d p -> q (d p)"), self.identf[:])
                k.cp(A8[0][0:64, ri, 4 * j:4 * j + 4], pb[0:64, 0:128:32])
                k.cp(A8[0][64:128, ri, 4 * j:4 * j + 4], pb[64:128, 16:128:32])
        s1, s2 = T((128, 8)), T((128, 8))
        for i in range(nlev):
            if i > 0:
                k.tt(s1[:], A8[i - 1][:, 0, :], A8[i - 1][:, 0, :], ALU.mult)
                k.tt(s2[:], A8[i - 1][:, 1, :], A8[i - 1][:, 1, :], ALU.mult)
                k.tt(A8[i][:, 0, :], s1[:], s2[:], ALU.subtract)
                k.stt(A8[i][:, 1, :], A8[i - 1][:, 0, :], 2.0, A8[i - 1][:, 1, :], ALU.mult, ALU.mult)
            k.ts(A8[i][:, 2, :], A8[i][:, 1, :], -1.0, ALU.mult)
        SP.close()
        uT = S.sb('uT', [128, 2, L], BF16)
        for j in range(2):
            k.dma('sp', uT[:, j, :], self.uT_scr.ap()[j * 128:(j + 1) * 128, :], 'uT')
        Xbf = S.sb('Xbf', [128, 2, 8, nch + 1], BF16)
        k.memset(Xbf[:, :, :, 0:1], 0.0)
        XA = [S.sb('XA%d' % i, [128, nch], F32) for i in range(2)]
        XB = [S.sb('XB%d' % i, [128, nch], F32) for i in range(2)]
        XT = S.sb('XT', [128, nch], F32)
        for m in range(8):
            j, mp = m // 4, m % 4
            for ri in range(2):
                for c0 in range(0, nch, 512):
                    n = min(512, nch - c0)
                    ps = self.bank()
                    for s_ in range(8):
                        k.mm(ps[:, 0:n], W1[:, j, mp, s_, ri, :], uT[:, j, 8 * c0 + s_:8 * (c0 + n):8],
                             start=(s_ == 0), stop=(s_ == 7))
                    k.cp(XA[ri][:, c0:c0 + n], ps[:, 0:n], eng=('act' if ri else 'dve'))
            cur, oth = XA, XB
            for lev in range(nlev):
                d = 1 << lev
                Ar = A8[lev][:, 0, m:m + 1]
                Ai = A8[lev][:, 1, m:m + 1]
                nAi = A8[lev][:, 2, m:m + 1]
                k.stt(XT[:, d:], cur[0][:, 0:nch - d], Ar, cur[0][:, d:], ALU.mult, ALU.add)
                k.stt(oth[0][:, d:], cur[1][:, 0:nch - d], nAi, XT[:, d:], ALU.mult, ALU.add)
                k.stt(XT[:, d:], cur[1][:, 0:nch - d], Ar, cur[1][:, d:], ALU.mult, ALU.add)
                k.stt(oth[1][:, d:], cur[0][:, 0:nch - d], Ai, XT[:, d:], ALU.mult, ALU.add)
                k.cp(oth[0][:, 0:d], cur[0][:, 0:d], eng='act')
                k.cp(oth[1][:, 0:d], cur[1][:, 0:d], eng='pool')
                cur, oth = oth, cur
            k.cp(Xbf[:, 0, m, 1:nch + 1], cur[0][:], eng='act')
            k.cp(Xbf[:, 1, m, 1:nch + 1], cur[1][:], eng='pool')
        yv = S.sb('yv', [128, 2, 512], F32)
        zf = S.sb('zf', [128, 2, 512], F32)
        zb = S.sb('zb', [128, 2, 512], BF16)
        sgm = S.sb('sgm', [128, 512], F32)
        sstg = Rot(S, 's5stg', 2, [128, 2, 512], BF16)
        for b in range(NB):
            t0 = b * 512
            cb0 = b * 64
            for j in range(2):
                ps = self.bank()
                k.memset(ps[:], 0.0)
                for s_ in range(8):
                    out = ps[:, s_:512:8]
                    for mp in range(4):
                        for ri in range(2):
                            k.mm(out, W2[:, j, mp, s_, ri, :], Xbf[:, ri, 4 * j + mp, cb0:cb0 + 64], start=False, stop=False)
                    for tau in range(s_ + 1):
                        k.mm(out, Kbd[:, j, tau, :], uT[:, j, t0 + s_ - tau:t0 + 512:8], start=False, stop=(tau == s_))
                k.stt(yv[:, j, :], uT[:, j, t0:t0 + 512], dcol[:, j:j + 1], ps[:], ALU.mult, ALU.add)
                k.act(zf[:, j, :], yv[:, j, :], AF.Gelu_apprx_tanh)
                k.cp(zb[:, j, :], zf[:, j, :], eng='pool')
            sg_, sgs = sstg.next()
            for jo in range(2):
                pg = self.bank()
                for ji in range(2):
                    k.mm(pg[:], wglu[:, ji, jo * 128:(jo + 1) * 128], zb[:, ji, :], start=(ji == 0), stop=(ji == 1))
                k.act(sgm[:], pg[:], AF.Sigmoid, bias=bglu[:, jo:jo + 1])
                k.tt(sg_[:, jo, :], zf[:, jo, :], sgm[:], ALU.mult)
            k.dma('sp', self.mix_scr.ap()[0, :, t0:t0 + 512].rearrange("(j p) t -> p j t", p=128), sg_[:], sgs)
        S.close()

    def zero_branch(self, br):
        k = self.k
        S = Scope(k)
        zt = S.sb('zt', [128, 2, 512], BF16)
        k.memset(zt[:], 0.0)
        for b in range(self.L // 512):
            k.dma('sp', self.mix_scr.ap()[br, :, b * 512:(b + 1) * 512].rearrange("(j p) t -> p j t", p=128), zt[:], 'zt')
        S.close()

    def phase_C(self, l):
        k = self.k
        L = self.L
        NB = L // 512
        S = Scope(k)
        w = self.w
        xsrc = self.x_in if l == 0 else self.y
        Wg = S.sb('Wg', [128, 8, 4096], BF16)
        for kk in range(8):
            k.dma('pool', Wg[:, kk, :], w['w_in'].ap()[l, kk * 128:(kk + 1) * 128, NMIX:NMIX + 4096], 'Wg')
        wbr = S.sb('wbr', [128, 4, 2, D], BF16)
        for i, nm in enumerate(('w_br_s5', 'w_br_sgu', 'w_br_dn', 'w_br_diff')):
            k.dma('pool', wbr[:, i, :, :], w[nm].ap()[l].rearrange("(j p) n -> p j n", p=128), 'wbr')
        wout = S.sb('wout', [128, 8, D], BF16)
        k.dma('pool', wout[:], w['w_out'].ap()[l].rearrange("(j p) n -> p j n", p=128), 'wout')
        xts = Rot(S, 'xt', 2, [128, 4, D], F32)
        hTs = Rot(S, 'hT', 2, [128, 8, 512], BF16)
        mixs = Rot(S, 'mx', 2, [128, 4, 2, 512], BF16)
        sig = Rot(S, 'sig', 2, [128, 512], F32)
        mg = S.sb('mg', [128, 512], F32)
        tmpm = S.sb('tmpm', [128, 512], F32)
        mgT = S.sb('mgT', [128, 8, 512], BF16)
        def ldc(bb):
            tt0 = bb * 512
            xt_, xsem_ = xts.next()
            k.dma('sp', xt_[:], xsrc.ap()[tt0:tt0 + 512, :].rearrange("(t p) d -> p t d", p=128), xsem_)
            hT_, hsem_ = hTs.next()
            k.dma('sp', hT_[:], self.hT_scr.ap()[:, tt0:tt0 + 512].rearrange("(k p) t -> p k t", p=128), hsem_)
            mx_, msem_ = mixs.next()
            for br in range(4):
                k.dma('sp', mx_[:, br, :, :], self.mix_scr.ap()[br, :, tt0:tt0 + 512].rearrange("(j p) t -> p j t", p=128), msem_)
            return xt_, xsem_, hT_, mx_
        nxt_c = ldc(0)
        for b in range(NB):
            t0 = b * 512
            xt, xsem, hT, mx = nxt_c
            if b + 1 < NB:
                nxt_c = ldc(b + 1)
            for mo in range(8):
                for br in range(4):
                    pg = self.bank()
                    c0 = br * D + mo * 128
                    for kk in range(8):
                        k.mm(pg[:], Wg[:, kk, c0:c0 + 128], hT[:, kk, :], start=(kk == 0), stop=(kk == 7))
                    sg, _ = sig.next()
                    k.act(sg[:], pg[:], AF.Sigmoid)
                    py = self.bank()
                    for j in range(2):
                        k.mm(py[:], wbr[:, br, j, mo * 128:(mo + 1) * 128], mx[:, br, j, :], start=(j == 0), stop=(j == 1))
                    if br == 0:
                        k.tt(mg[:], py[:], sg[:], ALU.mult)
                    elif br < 3:
                        k.tt(tmpm[:], py[:], sg[:], ALU.mult)
                        k.tt(mg[:], mg[:], tmpm[:], ALU.add, eng='pool')
                    else:
                        k.tt(tmpm[:], py[:], sg[:], ALU.mult)
                        k.tt(mgT[:, mo, :], mg[:], tmpm[:], ALU.add, eng='pool')
            xn, xns = xt, xsem
            for t in range(4):
                for hf in range(2):
                    po = self.bank()
                    for kk in range(8):
                        k.mm(po[:], mgT[:, kk, t * 128:(t + 1) * 128], wout[:, kk, hf * 512:(hf + 1) * 512],
                             start=(kk == 0), stop=(kk == 7))
                    k.tt(xn[:, t, hf * 512:(hf + 1) * 512], po[:], xt[:, t, hf * 512:(hf + 1) * 512], ALU.add)
            k.dma('sp', self.y.ap()[t0:t0 + 512, :].rearrange("(t p) d -> p t d", p=128), xn[:], xns)
        S.close()

    def phase_D(self, l, p, last):
        k = self.k
        if 'bi0' in self.flags:
            self.bi = 0
        L = self.L
        NB = L // 512
        S = Scope(k)
        w = self.w
        HC = DFF // 2
        Wup = S.sb('Wup', [128, 8, 2, HC], BF16)
        for kk in range(8):
            for which in range(2):
                c0 = which * DFF + p * HC
                k.dma('pool', Wup[:, kk, which, :], w['ffn_w_up'].ap()[l, kk * 128:(kk + 1) * 128, c0:c0 + HC], 'Wup')
        Wdn = S.sb('Wdn', [128, 11, D], BF16)
        k.dma('pool', Wdn[:], w['ffn_w_down'].ap()[l, p * HC:(p + 1) * HC, :].rearrange("(j p) n -> p j n", p=128), 'Wdn')
        g2col = self.load_cols(S, w['norm_ffn_g'].ap()[l].rearrange("(k p) -> k p", p=128), 8, 'g2col')
        cw = self.load_cols(S, w['ffn_conv_w'].ap()[l].rearrange("k (j p) -> (k j) p", p=128), 132, 'cw')
        cb = self.load_cols(S, w['ffn_conv_b'].ap()[l].rearrange("(j p) -> j p", p=128), 44, 'cb')
        fin = last and p == 1
        if fin:
            gfin = self.load_row_bcast(S, w['norm_final_g'].ap(), D, 'gfin')
        xts = Rot(S, 'xt', 2, [128, 4, D], F32)
        hTs = Rot(S, 'hT', 2, [128, 8, 512], BF16)
        hs = S.sb('hs', [128, 4, D], BF16)
        junk = S.sb('junk', [128, D], F32)
        sss = Rot(S, 'ss', 2, [128, 16], F32)
        ups = Rot(S, 'ups', 3, [128, 514], F32)
        halo = S.sb('halo', [128, 22, 2], F32)
        k.memset(halo[:], 0.0)
        cv = Rot(S, 'cv', 3, [128, 512], F32)
        cg = Rot(S, 'cg', 3, [128, 512], F32)
        sgl = Rot(S, 'sgl', 2, [128, 512], F32)
        actT = S.sb('actT', [128, 11, 512], BF16)
        def ldd(bb):
            tt0 = bb * 512
            xt_, xsem_ = xts.next()
            k.dma('sp', xt_[:], self.y.ap()[tt0:tt0 + 512, :].rearrange("(t p) d -> p t d", p=128), xsem_)
            hT_, hsem_ = hTs.next()
            if p == 1:
                k.dma('sp', hT_[:], self.hT_scr.ap()[:, tt0:tt0 + 512].rearrange("(k p) t -> p k t", p=128), hsem_)
            return xt_, xsem_, hT_, hsem_
        nxt_d = ldd(0)
        for b in range(NB):
            t0 = b * 512
            xt, xsem, hT, hsem = nxt_d
            if b + 1 < NB:
                nxt_d = ldd(b + 1)
            ss, _ = sss.next()
            hview = self.hT_scr.ap()[:, t0:t0 + 512].rearrange("(k p) t -> p k t", p=128)
            if p == 0:
                self.rms_block(xt, g2col, hT, junk, ss, hs)
                k.dma('sp', hview, hT[:], hsem)
            pend = []

            def tail(i_, res_):
                sl, _ = sgl.next()
                k.act(sl[:], res_[1][:], AF.Silu)
                k.tt(actT[:, i_, :], sl[:], res_[0][:], ALU.mult, eng='pool')
            for i in range(11):
                res = []
                for which in range(2):
                    j = which * 22 + p * 11 + i
                    hj = which * 11 + i
                    pu = self.bank()
                    for kk in range(8):
                        k.mm(pu[:], Wup[:, kk, which, i * 128:(i + 1) * 128], hT[:, kk, :], start=(kk == 0), stop=(kk == 7))
                    up, _ = ups.next()
                    k.cp(up[:, 2:514], pu[:], eng='act')
                    k.cp(up[:, 0:2], halo[:, hj, :], eng='pool')
                    c, _ = (cv if which == 0 else cg).next()
                    k.act(c[:], pu[:], AF.Identity, scale=cw[:, 2 * 44 + j:2 * 44 + j + 1], bias=cb[:, j:j + 1])
                    k.stt(c[:], up[:, 1:513], cw[:, 44 + j:44 + j + 1], c[:], ALU.mult, ALU.add)
                    k.stt(c[:], up[:, 0:512], cw[:, j:j + 1], c[:], ALU.mult, ALU.add)
                    k.cp(halo[:, hj, :], up[:, 512:514], eng='pool')
                    res.append(c)
                pend.append((i, res))
                if len(pend) > 1:
                    tail(*pend.pop(0))
            while pend:
                tail(*pend.pop(0))
            for t in range(4):
                for hf in range(2):
                    po = self.bank()
                    for i in range(11):
                        k.mm(po[:], actT[:, i, t * 128:(t + 1) * 128], Wdn[:, i, hf * 512:(hf + 1) * 512],
                             start=(i == 0), stop=(i == 10))
                    k.tt(xt[:, t, hf * 512:(hf + 1) * 512], po[:], xt[:, t, hf * 512:(hf + 1) * 512], ALU.add)
            if fin:
                ss2, _ = sss.next()
                for t in range(4):
                    k.sumsq(junk[:], xt[:, t, :], ss2[:, t:t + 1])
                k.ts(ss2[:, 4:8], ss2[:, 0:4], 1.0 / D, ALU.mult, EPS, ALU.add)
                k.act(ss2[:, 8:12], ss2[:, 4:8], AF.Sqrt)
                k.recip(ss2[:, 12:16], ss2[:, 8:12])
                for t in range(4):
                    k.stt(xt[:, t, :], xt[:, t, :], ss2[:, 12 + t:13 + t], gfin[:], ALU.mult, ALU.mult)
            k.dma('sp', self.y.ap()[t0:t0 + 512, :].rearrange("(t p) d -> p t d", p=128), xt[:], xsem)
        S.close()

    def build(self):
        fl = self.flags
        if 'onlyA' in fl:
            self.phase_A(0)
            self.k.barrier()
            return self.k.nc
        for l in range(self.depth):
            self.phase_A(l)
            if 'dn' in fl and 'att' in fl and 'seq' not in fl:
                if 's5' in fl:
                    self.phase_s5(l)
                else:
                    self.zero_branch(0)
                self.run_gens([(self.att_gen(l, [(6, 7)]), [3, 4, 5]), (self.dn_gen(l, 1), [0, 1, 2])], ratio=[3, 1])
            else:
                for br, nm in ((0, 's5'), (2, 'dn'), (3, 'att')):
                    if nm in fl:
                        getattr(self, 'phase_' + nm)(l)
                    else:
                        self.zero_branch(br)
            if 'noCD' in fl:
                continue
            self.phase_C(l)
            if 'stopC' in fl:
                break
            self.phase_D(l, 0, last=(l == self.depth - 1))
            if 'stopD0' in fl:
                break
            self.phase_D(l, 1, last=(l == self.depth - 1))
        self.k.barrier()
        return self.k.nc


_CACHE = {}


def run_cores(xs, weights, L, depth, flags):
    key = (L, depth, tuple(sorted(flags)))
    if key not in _CACHE:
        _CACHE[key] = Prog(L, depth, flags).build()
    nc = _CACHE[key]
    consts = host_consts()
    in_maps = []
    for x in xs:
        m = {"x": np.ascontiguousarray(x, dtype=np.float32)}
        for name in WEIGHT_SHAPES:
            m[name] = np.ascontiguousarray(weights[name], dtype=np.float32)
        m.update(consts)
        in_maps.append(m)
    res = run_bass_kernel_spmd(nc, in_maps, core_ids=list(range(len(xs))))
    if "dbg" in flags:
        return [(r["y"], r["mix_scr"], r["hT_scr"]) for r in res.results]
    return [r["y"] for r in res.results]


def kernel(**inputs):
    x = np.asarray(inputs['x'])
    B, L, _ = x.shape
    weights = {n: np.asarray(inputs[n]) for n in WEIGHT_SHAPES}
    outs = run_cores([x[b] for b in range(B)], weights, L, DEPTH, ('s5', 'dn', 'att'))
    return np.stack(outs, axis=0).astype(np.float32)
```
